# Optimizing a Trainium2 kernel written in Bass

```python
import math
import jax, jax.numpy as jnp
from jax import lax
import numpy as np

D_MODEL = 1024
BATCH = 8
SEQ = 2048
DEPTH = 2
DEC_BATCH = 32
DEC_SEQ = 1
PAST_LEN = 16384
PAGE_SIZE = 128

N_EVEN = (DEPTH + 1) // 2
N_ODD = DEPTH // 2
D_FF = 2816
NORM_EPS = 1e-6
Q_BLOCK = 128
NEG_INF = -1e30
GDN_HEADS = 4
GDN_DK = 128
GDN_DV = 128
GDN_QK = GDN_HEADS * GDN_DK
GDN_VW = GDN_HEADS * GDN_DV
GDN_CONV_DIM = 2 * GDN_QK + GDN_VW
CONV_W = 4
GDN_CHUNK = 64
DIFF_HEADS = 4
DIFF_DQK = 64
DIFF_DV = 2 * DIFF_DQK
DIFF_QK = DIFF_HEADS * 2 * DIFF_DQK
DIFF_VW = DIFF_HEADS * DIFF_DV
DIFF_EPS = 1e-5
EVEN_IN = GDN_CONV_DIM + GDN_VW + 2 * GDN_HEADS + 2 * DIFF_QK + DIFF_VW
EVEN_OUT = GDN_VW + DIFF_VW
MLA_HEADS = 8
MLA_NOPE = 128
MLA_ROPE = 64
MLA_V = 128
Q_LORA = 384
KV_LORA = 256
ROPE_THETA = 10000.0
MLA_IN = Q_LORA + KV_LORA + MLA_ROPE
MLA_OUT = MLA_HEADS * MLA_V
MLA_SCALE = (MLA_NOPE + MLA_ROPE) ** -0.5

kernel_name = 'hybrid_gdn_diff_mla_macaron_step'


def rms_norm(x, w, eps=NORM_EPS):
    xf = x.astype(jnp.float32)
    y = xf * lax.rsqrt(jnp.mean(xf * xf, axis=-1, keepdims=True) + eps)
    return (y * w.astype(jnp.float32)).astype(x.dtype)


def l2norm(x):
    x = x.astype(jnp.float32)
    return x * lax.rsqrt(jnp.sum(x * x, axis=-1, keepdims=True) + 1e-6)


def swiglu(h, w_gate, w_up, w_down):
    return (jax.nn.silu(h @ w_gate) * (h @ w_up)) @ w_down


def causal_conv_silu(x, buf, w):
    T = x.shape[1]
    xx = jnp.concatenate([buf.astype(x.dtype), x], axis=1)
    y = sum(xx[:, i:i + T] * w[i] for i in range(CONV_W))
    return jax.nn.silu(y), xx[:, T:]


def gather_pages(pool, page_table):
    rows = jnp.take(pool, page_table, axis=0)
    return rows.reshape(rows.shape[0], rows.shape[1] * rows.shape[2], *rows.shape[3:])


def rope_cos_sin(pos):
    inv_freq = jnp.exp(-math.log(ROPE_THETA) * jnp.arange(0, MLA_ROPE, 2, dtype=jnp.float32) / MLA_ROPE)
    ang = pos.astype(jnp.float32)[:, None] * inv_freq[None, :]
    return jnp.cos(ang), jnp.sin(ang)


def apply_rope(x, cos, sin):
    half = x.shape[-1] // 2
    x1, x2 = x[..., :half], x[..., half:]
    return jnp.concatenate([x1 * cos - x2 * sin, x2 * cos + x1 * sin], axis=-1).astype(x.dtype)


def over_query_blocks(fn, qs, q_pos):
    T = q_pos.shape[0]
    if T <= Q_BLOCK or T % Q_BLOCK:
        return fn(*qs, q_pos)
    nb = T // Q_BLOCK
    def blk(a):
        return jnp.moveaxis(a.reshape(a.shape[0], nb, Q_BLOCK, *a.shape[2:]), 1, 0)
    out = lax.map(lambda xs: fn(*xs[0], xs[1]), (tuple(blk(a) for a in qs), q_pos.reshape(nb, Q_BLOCK)))
    out = jnp.moveaxis(out, 0, 1)
    return out.reshape(out.shape[0], T, *out.shape[3:])


def gdn_chunked(q, k, v, g, beta, S0):
    bsz, T, H, _ = q.shape
    DV = v.shape[-1]
    n = T // GDN_CHUNK
    def chunks(a):
        a = a.reshape(bsz, n, GDN_CHUNK, H, *a.shape[3:])
        return jnp.moveaxis(jnp.moveaxis(a, 1, 0), 3, 2)
    qc, kc, vc, gc, bc = (chunks(a) for a in (q, k, v, g, beta))
    gcum = jnp.cumsum(gc, axis=-1)
    idx = jnp.arange(GDN_CHUNK)
    incl = idx[:, None] >= idx[None, :]
    strict = idx[:, None] > idx[None, :]
    diff = gcum[..., :, None] - gcum[..., None, :]
    decay = jnp.where(incl, jnp.exp(jnp.where(incl, diff, 0.0)), 0.0)
    kb = kc * bc[..., None]
    L = jnp.where(strict, jnp.einsum('nbhid,nbhjd->nbhij', kb, kc) * decay, 0.0)
    eye = jnp.eye(GDN_CHUNK, dtype=L.dtype)
    rhs = jnp.concatenate([vc * bc[..., None], kb * jnp.exp(gcum)[..., None]], axis=-1)
    sol = lax.linalg.triangular_solve(L + eye, rhs, left_side=True, lower=True, unit_diagonal=True)
    u, w = sol[..., :DV], sol[..., DV:]
    a_qk = jnp.einsum('nbhid,nbhjd->nbhij', qc, kc) * decay
    q_dec = qc * jnp.exp(gcum)[..., None]
    g_last = gcum[..., -1]
    k_dec = kc * jnp.exp(g_last[..., None] - gcum)[..., None]

    def step(S, xs):
        q_i, k_i, u_i, w_i, a_i, gl = xs
        v_new = u_i - jnp.einsum('bhck,bhkv->bhcv', w_i, S)
        o = jnp.einsum('bhck,bhkv->bhcv', q_i, S) + jnp.einsum('bhij,bhjv->bhiv', a_i, v_new)
        S = S * jnp.exp(gl)[..., None, None] + jnp.einsum('bhck,bhcv->bhkv', k_i, v_new)
        return S, o

    S, o = lax.scan(step, S0, (q_dec, k_dec, u, w, a_qk, g_last))
    o = jnp.swapaxes(jnp.moveaxis(o, 0, 1), 2, 3).reshape(bsz, T, H, DV)
    return o, S


def gdn_recurrent(q, k, v, g, beta, S0):
    def step(S, xs):
        q_t, k_t, v_t, g_t, b_t = xs
        S = S * jnp.exp(g_t)[..., None, None]
        delta = (v_t - jnp.einsum('bhk,bhkv->bhv', k_t, S)) * b_t[..., None]
        S = S + jnp.einsum('bhk,bhv->bhkv', k_t, delta)
        return S, jnp.einsum('bhk,bhkv->bhv', q_t, S)
    xs = tuple(jnp.moveaxis(a, 1, 0) for a in (q, k, v, g, beta))
    S, o = lax.scan(step, S0, xs)
    return jnp.moveaxis(o, 0, 1), S


def diff_attend(q, q_pos, segs, lam, lam_init, subln_w):
    scores = []
    for k, _, k_pos in segs:
        s = jnp.einsum('bqhmd,bkhmd->bhmqk', q, k).astype(jnp.float32) * DIFF_DQK ** -0.5
        scores.append(jnp.where(k_pos[None, :] <= q_pos[:, None], s, NEG_INF))
    p = jax.nn.softmax(jnp.concatenate(scores, axis=-1), axis=-1)
    a = p[:, :, 0] - lam * p[:, :, 1]
    outs, off = [], 0
    for k, v, _ in segs:
        n = k.shape[1]
        outs.append(jnp.einsum('bhqk,bkhd->bqhd', a[..., off:off + n].astype(v.dtype), v))
        off += n
    return rms_norm(sum(outs), subln_w, DIFF_EPS) * (1.0 - lam_init)


def mla_attend(q_lat, q_pe, q_pos, segs):
    scores = []
    for ckv, kpe, k_pos in segs:
        s = (jnp.einsum('bqhc,bkc->bhqk', q_lat, ckv) + jnp.einsum('bqhr,bkr->bhqk', q_pe, kpe)).astype(jnp.float32) * MLA_SCALE
        scores.append(jnp.where(k_pos[None, :] <= q_pos[:, None], s, NEG_INF))
    p = jax.nn.softmax(jnp.concatenate(scores, axis=-1), axis=-1)
    outs, off = [], 0
    for ckv, _, _ in segs:
        n = ckv.shape[1]
        outs.append(jnp.einsum('bhqk,bkc->bqhc', p[..., off:off + n].astype(ckv.dtype), ckv))
        off += n
    return sum(outs)


def even_mixer(h, pos, conv_buf, S0, past_k, past_v, w_in, conv_w, a_log, dt_bias, g_norm, lam_vec, subln, w_out, lam_init):
    bsz, T, _ = h.shape
    f32 = jnp.float32
    o1 = GDN_CONV_DIM
    o2 = o1 + GDN_VW
    o3 = o2 + GDN_HEADS
    o4 = o3 + GDN_HEADS
    o5 = o4 + DIFF_QK
    o6 = o5 + DIFF_QK
    qkv, z, b, a, dq, dk, dv = jnp.split(h @ w_in, [o1, o2, o3, o4, o5, o6], axis=-1)
    qkv_c, new_buf = causal_conv_silu(qkv, conv_buf, conv_w)
    gq, gk, gv = jnp.split(qkv_c, [GDN_QK, 2 * GDN_QK], axis=-1)
    gq = l2norm(gq.reshape(bsz, T, GDN_HEADS, GDN_DK)) * GDN_DK ** -0.5
    gk = l2norm(gk.reshape(bsz, T, GDN_HEADS, GDN_DK))
    gv = gv.reshape(bsz, T, GDN_HEADS, GDN_DV).astype(f32)
    beta = jax.nn.sigmoid(b.astype(f32))
    g = -jnp.exp(a_log.astype(f32)) * jax.nn.softplus(a.astype(f32) + dt_bias.astype(f32))
    S0 = S0.astype(f32)
    if T % GDN_CHUNK == 0:
        o_g, S = gdn_chunked(gq, gk, gv, g, beta, S0)
    else:
        o_g, S = gdn_recurrent(gq, gk, gv, g, beta, S0)
    o_g = rms_norm(o_g, g_norm) * jax.nn.silu(z.reshape(bsz, T, GDN_HEADS, GDN_DV).astype(f32))
    q5 = dq.reshape(bsz, T, DIFF_HEADS, 2, DIFF_DQK)
    k_rows = dk.reshape(bsz, T, 2 * DIFF_HEADS, DIFF_DQK)
    v_rows = dv.reshape(bsz, T, DIFF_HEADS, DIFF_DV)
    lv = lam_vec.astype(f32)
    lam = jnp.exp(jnp.sum(lv[0] * lv[1])) - jnp.exp(jnp.sum(lv[2] * lv[3])) + lam_init
    segs = [(k_rows.reshape(bsz, T, DIFF_HEADS, 2, DIFF_DQK), v_rows, pos)]
    if past_k is not None:
        pl = past_k.shape[1]
        segs = [(past_k.reshape(bsz, pl, DIFF_HEADS, 2, DIFF_DQK), past_v, jnp.arange(pl, dtype=jnp.int32))] + segs
    o_d = over_query_blocks(lambda qb, pb: diff_attend(qb, pb, segs, lam, lam_init, subln), [q5], pos)
    mixed = jnp.concatenate([o_g.reshape(bsz, T, GDN_VW).astype(h.dtype), o_d.reshape(bsz, T, DIFF_VW).astype(h.dtype)], axis=-1)
    return mixed @ w_out, new_buf, S.astype(h.dtype), k_rows, v_rows


def odd_mixer(h, pos, past_ckv, past_kpe, w_in, q_norm, w_uq, kv_norm, w_ukv, w_out):
    bsz, T, _ = h.shape
    cq, ckv, kpe = jnp.split(h @ w_in, [Q_LORA, Q_LORA + KV_LORA], axis=-1)
    q = (rms_norm(cq, q_norm) @ w_uq).reshape(bsz, T, MLA_HEADS, MLA_NOPE + MLA_ROPE)
    cos, sin = rope_cos_sin(pos)
    q_pe = apply_rope(q[..., MLA_NOPE:], cos[None, :, None], sin[None, :, None])
    kpe = apply_rope(kpe, cos[None], sin[None])
    ckv = rms_norm(ckv, kv_norm)
    w_ukv = w_ukv.reshape(KV_LORA, MLA_HEADS, MLA_NOPE + MLA_V)
    q_lat = jnp.einsum('bthn,chn->bthc', q[..., :MLA_NOPE], w_ukv[..., :MLA_NOPE])
    segs = [(ckv, kpe, pos)]
    if past_ckv is not None:
        segs = [(past_ckv, past_kpe, jnp.arange(past_ckv.shape[1], dtype=jnp.int32))] + segs
    ctx = over_query_blocks(lambda ql, qr, qp: mla_attend(ql, qr, qp, segs), [q_lat, q_pe], pos)
    out = jnp.einsum('bthc,chv->bthv', ctx, w_ukv[..., MLA_NOPE:]).reshape(bsz, T, MLA_OUT)
    return out @ w_out, ckv, kpe


def setup_inputs(seed: int = 0) -> dict:
    key = jax.random.key(seed)
    ks = iter(jax.random.split(key, 48))
    def nrm(shape, scale=1.0):
        return jax.random.normal(next(ks), shape, jnp.float32) * scale
    def gain(shape):
        return 1.0 + 0.05 * nrm(shape)
    n_pages = PAST_LEN // PAGE_SIZE
    used = DEC_BATCH * n_pages
    n_pool = used + (used + 3) // 4
    inp = {}
    inp['x_prompt'] = nrm((BATCH, SEQ, D_MODEL))
    inp['x_sample'] = nrm((DEC_BATCH, DEC_SEQ, D_MODEL))
    inp['state_gdn_conv'] = nrm((N_EVEN, DEC_BATCH, CONV_W - 1, GDN_CONV_DIM))
    inp['state_gdn_rec'] = nrm((N_EVEN, DEC_BATCH, GDN_HEADS, GDN_DK, GDN_DV), 0.05)
    inp['cache_diff_k'] = nrm((N_EVEN, n_pool, PAGE_SIZE, 2 * DIFF_HEADS, DIFF_DQK))
    inp['cache_diff_v'] = nrm((N_EVEN, n_pool, PAGE_SIZE, DIFF_HEADS, DIFF_DV))
    inp['cache_mla_ckv'] = nrm((N_ODD, n_pool, PAGE_SIZE, KV_LORA))
    inp['cache_mla_kpe'] = nrm((N_ODD, n_pool, PAGE_SIZE, MLA_ROPE))
    inp['page_table'] = jax.random.permutation(next(ks), n_pool)[:used].reshape(DEC_BATCH, n_pages).astype(jnp.int32)
    inp['ffn_norm'] = gain((DEPTH, 2, D_MODEL))
    inp['mix_norm'] = gain((DEPTH, D_MODEL))
    inp['ffn_w_gate'] = nrm((DEPTH, 2, D_MODEL, D_FF), D_MODEL ** -0.5)
    inp['ffn_w_up'] = nrm((DEPTH, 2, D_MODEL, D_FF), D_MODEL ** -0.5)
    inp['ffn_w_down'] = nrm((DEPTH, 2, D_FF, D_MODEL), D_FF ** -0.5)
    inp['even_w_in'] = nrm((N_EVEN, D_MODEL, EVEN_IN), D_MODEL ** -0.5)
    inp['gdn_conv_w'] = nrm((N_EVEN, CONV_W, GDN_CONV_DIM), CONV_W ** -0.5)
    inp['gdn_a_log'] = jnp.log(jax.random.uniform(next(ks), (N_EVEN, GDN_HEADS), jnp.float32, 1.0, 16.0))
    dt = jnp.exp(jax.random.uniform(next(ks), (N_EVEN, GDN_HEADS), jnp.float32, math.log(1e-3), math.log(1e-1)))
    inp['gdn_dt_bias'] = dt + jnp.log(-jnp.expm1(-dt))
    inp['gdn_norm'] = gain((N_EVEN, GDN_DV))
    inp['diff_lambda'] = nrm((N_EVEN, 4, DIFF_DQK), 0.1)
    inp['diff_subln'] = gain((N_EVEN, DIFF_DV))
    inp['even_w_out'] = nrm((N_EVEN, EVEN_OUT, D_MODEL), EVEN_OUT ** -0.5)
    inp['mla_w_in'] = nrm((N_ODD, D_MODEL, MLA_IN), D_MODEL ** -0.5)
    inp['mla_q_norm'] = gain((N_ODD, Q_LORA))
    inp['mla_w_uq'] = nrm((N_ODD, Q_LORA, MLA_HEADS * (MLA_NOPE + MLA_ROPE)), Q_LORA ** -0.5)
    inp['mla_kv_norm'] = gain((N_ODD, KV_LORA))
    inp['mla_w_ukv'] = nrm((N_ODD, KV_LORA, MLA_HEADS * (MLA_NOPE + MLA_V)), KV_LORA ** -0.5)
    inp['mla_w_out'] = nrm((N_ODD, MLA_OUT, D_MODEL), MLA_OUT ** -0.5)
    inp['final_norm'] = gain((D_MODEL,))
    return inp


def reference(x_prompt, x_sample, state_gdn_conv, state_gdn_rec, cache_diff_k, cache_diff_v, cache_mla_ckv, cache_mla_kpe, page_table, ffn_norm, mix_norm, ffn_w_gate, ffn_w_up, ffn_w_down, even_w_in, gdn_conv_w, gdn_a_log, gdn_dt_bias, gdn_norm, diff_lambda, diff_subln, even_w_out, mla_w_in, mla_q_norm, mla_w_uq, mla_kv_norm, mla_w_ukv, mla_w_out, final_norm):
    def run(x, sample):
        bsz, T, _ = x.shape
        if sample:
            pos = page_table.shape[1] * PAGE_SIZE + jnp.arange(T, dtype=jnp.int32)
        else:
            pos = jnp.arange(T, dtype=jnp.int32)
        conv_l, rec_l, k_l, v_l, ckv_l, kpe_l = [], [], [], [], [], []
        for layer in range(DEPTH):
            h = rms_norm(x, ffn_norm[layer, 0])
            x = x + 0.5 * swiglu(h, ffn_w_gate[layer, 0], ffn_w_up[layer, 0], ffn_w_down[layer, 0])
            h = rms_norm(x, mix_norm[layer])
            if layer % 2 == 0:
                e = layer // 2
                lam_init = 0.8 - 0.6 * math.exp(-0.3 * layer)
                if sample:
                    conv_buf, S0 = state_gdn_conv[e], state_gdn_rec[e]
                    past_k = gather_pages(cache_diff_k[e], page_table)
                    past_v = gather_pages(cache_diff_v[e], page_table)
                else:
                    conv_buf = jnp.zeros((bsz, CONV_W - 1, GDN_CONV_DIM), x.dtype)
                    S0 = jnp.zeros((bsz, GDN_HEADS, GDN_DK, GDN_DV), jnp.float32)
                    past_k = past_v = None
                y, buf, S, k_rows, v_rows = even_mixer(h, pos, conv_buf, S0, past_k, past_v, even_w_in[e], gdn_conv_w[e], gdn_a_log[e], gdn_dt_bias[e], gdn_norm[e], diff_lambda[e], diff_subln[e], even_w_out[e], lam_init)
                conv_l.append(buf)
                rec_l.append(S)
                k_l.append(k_rows)
                v_l.append(v_rows)
            else:
                o = layer // 2
                if sample:
                    past_ckv = gather_pages(cache_mla_ckv[o], page_table)
                    past_kpe = gather_pages(cache_mla_kpe[o], page_table)
                else:
                    past_ckv = past_kpe = None
                y, ckv, kpe = odd_mixer(h, pos, past_ckv, past_kpe, mla_w_in[o], mla_q_norm[o], mla_w_uq[o], mla_kv_norm[o], mla_w_ukv[o], mla_w_out[o])
                ckv_l.append(ckv)
                kpe_l.append(kpe)
            x = x + y
            h = rms_norm(x, ffn_norm[layer, 1])
            x = x + 0.5 * swiglu(h, ffn_w_gate[layer, 1], ffn_w_up[layer, 1], ffn_w_down[layer, 1])
        return (rms_norm(x, final_norm), jnp.stack(conv_l), jnp.stack(rec_l), jnp.stack(k_l), jnp.stack(v_l), jnp.stack(ckv_l), jnp.stack(kpe_l))

    y_p, p_conv, p_rec, p_k, p_v, p_ckv, p_kpe = run(x_prompt, False)
    y_s, s_conv, s_rec, s_k, s_v, s_ckv, s_kpe = run(x_sample, True)
    return (y_p, y_s, p_conv, s_conv, p_rec, s_rec, p_k, s_k, p_v, s_v, p_ckv, s_ckv, p_kpe, s_kpe)
```

```python
import contextlib, os
import numpy as np
DBG = int(os.environ.get('KDBG', '9')); DBG2 = int(os.environ.get('KDBG2', '9')); DBG3 = int(os.environ.get('KDBG3', '7')); DBG4 = int(os.environ.get('KDBG4', '3')); DBG5 = int(os.environ.get('KDBG5', '127'))
import concourse.bass as bass
import concourse.mybir as mybir
from concourse.bass_utils import run_bass_kernel_spmd

F32 = mybir.dt.float32; BF16 = mybir.dt.bfloat16; I32 = mybir.dt.int32
AF = mybir.ActivationFunctionType; ALU = mybir.AluOpType; AX = mybir.AxisListType

D = 1024; KC = 8; DFF = 2816; FC = 22
NBS = 4
EVEN_IN = 3592; O1 = 1536; O2 = 2048; O4 = 2056; O5 = 2568; O6 = 3080
MLA_IN = 704


class Tk:
    __slots__ = ("w", "r", "x")
    def __init__(self, x=False): self.w = None; self.r = {}; self.x = x


class KB:
    def __init__(self, nc, es):
        self.nc = nc
        self.E = {'pe': nc.tensor, 'act': nc.scalar, 'dve': nc.vector, 'pool': nc.gpsimd, 'sp': nc.sync}
        self.sem = {e: es.enter_context(nc.semaphore("s_" + e)) for e in self.E}
        self.cnt = {e: 0 for e in self.E}
        self.known = {e: {} for e in self.E}
        self.dq = {}
        for q, n in (('sp', 20), ('pool', 24), ('act', 4)):
            self.dq[q] = dict(sems=[es.enter_context(nc.semaphore(f"d_{q}{i}")) for i in range(n)], i=0, val=[0] * n)
        self.nwait = 0; self.nins = 0
    def _wait(self, e, ev):
        sem, v = ev
        k = id(sem)
        if self.known[e].get(k, 0) >= v: return
        self.E[e].wait_ge(sem, v); self.known[e][k] = v; self.nwait += 1
    def _deps(self, e, reads, writes):
        evs = {}
        def add(ev):
            k = id(ev[0])
            if k not in evs or evs[k][1] < ev[1]: evs[k] = ev
        for t in reads:
            if t.w: add(t.w)
            if t.x:
                for ev in t.r.values(): add(ev)
        for t in writes:
            if t.w: add(t.w)
            for ev in t.r.values(): add(ev)
        for ev in evs.values():
            if e == 'pe' and ev[0] is self.sem['pe']: continue
            self._wait(e, ev)
    def _record(self, ev, reads, writes):
        k = id(ev[0])
        for t in reads: t.r[k] = ev
        for t in writes: t.w = ev; t.r = {}
    def op(self, e, fn, reads=(), writes=()):
        self._deps(e, reads, writes)
        ins = fn(self.E[e])
        self.cnt[e] += 1; self.nins += 1
        ins.then_inc(self.sem[e], 1)
        self._record((self.sem[e], self.cnt[e]), reads, writes)
    def dma(self, q, out, in_, reads=(), writes=(), indirect=None, **kw):
        self._deps(q, reads, writes)
        d = self.dq[q]; i = d['i']; d['i'] = (i + 1) % len(d['sems'])
        sem = d['sems'][i]
        if d['val'][i] > 0: self._wait(q, (sem, d['val'][i]))
        if indirect is not None:
            ins = self.nc.gpsimd.indirect_dma_start(out=out, out_offset=None, in_=in_,
                                                    in_offset=bass.IndirectOffsetOnAxis(ap=indirect, axis=0))
        else:
            ins = self.E[q].dma_start(out=out, in_=in_, **kw)
        d['val'][i] += 16; self.nins += 1
        ins.then_inc(sem, 16)
        self._record((sem, d['val'][i]), reads, writes)
    def barrier(self):
        evs = []
        for q, d in self.dq.items():
            for sem, v in zip(d['sems'], d['val']):
                if v > 0: evs.append((sem, v))
        for e in self.E:
            if self.cnt[e] > 0: evs.append((self.sem[e], self.cnt[e]))
        for e in self.E:
            for ev in evs:
                if ev[0] is self.sem[e]: continue
                self._wait(e, ev)
    def finish(self):
        self.barrier()


def col_tiles(c0, c1, step=512):
    out = []
    c = c0
    while c < c1:
        w = min(step - (c % step), c1 - c)
        out.append((c, w)); c += w
    return out


C_I, C_M, C_NEG, C_NEGT, C_SL, C_SELF, C_SEL, C_SELF2, C_W = 0, 128, 256, 320, 384, 448, 480, 992, 1024

def make_consts(T, past_len):
    c = np.zeros((128, C_W), np.float32)
    c[:, C_I:C_I + 128] = np.eye(128)
    k = np.arange(128)
    c[:, C_M:C_M + 128] = (k[None, :] >= k[:, None])
    i64 = np.arange(64)
    c[:64, C_NEG:C_NEG + 64] = np.where(i64[None, :] <= i64[:, None], 0.0, -1e30)
    c[:64, C_NEGT:C_NEGT + 64] = np.where(i64[None, :] >= i64[:, None], 0.0, -1e30)
    c[:64, C_SL:C_SL + 64] = (i64[None, :] < i64[:, None])
    for b in range(4):
        c[b, C_SELF + 8 * b:C_SELF + 8 * b + 8] = 1.0
        c[b, C_SEL + 128 * b:C_SEL + 128 * (b + 1)] = 1.0
        c[b, C_SELF2 + b:C_SELF2 + 32:4] = 1.0
    NT = T + 4
    pos = np.concatenate([np.arange(T), np.full(4, past_len)]).astype(np.float32)
    inv = np.exp(-np.log(10000.0) * np.arange(0, 64, 2, dtype=np.float32) / 64).astype(np.float32)
    ang = (pos[None, :] * inv[:, None]).astype(np.float32)
    rope = np.zeros((64, 2, NT), np.float32)
    rope[:32, 0] = np.cos(ang); rope[32:, 0] = np.cos(ang)
    rope[:32, 1] = -np.sin(ang); rope[32:, 1] = np.sin(ang)
    return c, rope


def build(T, NPG, NPOOL, stages=("ffn", "even", "odd")):
    NT = T + NBS
    NTT = T // 128
    nc = bass.Bass("TRN2", target_bir_lowering=False)
    def din(name, shape, dt=F32): return nc.dram_tensor(name, list(shape), dt, kind="ExternalInput").ap()
    def dout(name, shape): return nc.dram_tensor(name, list(shape), F32, kind="ExternalOutput").ap()
    xp = din("xp", [T, D]); xs = din("xs", [NBS, D])
    sconv = din("sconv", [NBS, 3, 1536]); srec = din("srec", [NBS * 4 * 128, 128])
    ck = din("ck", [NPOOL * 128, 512]); cv = din("cv", [NPOOL * 128, 512])
    cc = din("cc", [NPOOL * 128, 256]); cp = din("cp", [NPOOL * 128, 64])
    pt = din("pt", [1, NBS * NPG], I32)
    ffn_norm = din("ffn_norm", [4, D]); mix_norm = din("mix_norm", [2, D]); final_norm = din("final_norm", [1, D])
    wg = din("wg", [4 * D, DFF]); wu = din("wu", [4 * D, DFF]); wd = din("wd", [4 * DFF, D])
    ewin = din("ewin", [D, EVEN_IN]); convw = din("convw", [4, 1536])
    alog = din("alog", [1, 4]); dtb = din("dtb", [1, 4]); gnorm = din("gnorm", [1, 128])
    dlam = din("dlam", [1, 256]); subln = din("subln", [1, 128]); ewout = din("ewout", [D, D])
    mwin = din("mwin", [D, MLA_IN]); mqn = din("mqn", [1, 384]); muq = din("muq", [384, 1536])
    mkvn = din("mkvn", [1, 256]); mukv = din("mukv", [256, 2048]); mwout = din("mwout", [D, D])
    cst = din("cst", [128, C_W]); ropeT = din("ropeT", [64, 2 * NT])
    yp = dout("yp", [T, D]); ys = dout("ys", [NBS, D])
    convp = dout("convp", [3, 1536]); convs = dout("convs", [NBS, 3, 1536])
    recp = dout("recp", [4 * 128, 128]); recs = dout("recs", [NBS * 4 * 128, 128])
    kp = dout("kp", [T, 512]); ks = dout("ks", [NBS, 512]); vp = dout("vp", [T, 512]); vs = dout("vs", [NBS, 512])
    ckvp = dout("ckvp", [T, 256]); ckvs = dout("ckvs", [NBS, 256]); kpep = dout("kpep", [T, 64]); kpes = dout("kpes", [NBS, 64])

    es = contextlib.ExitStack()
    with es:
        kb = KB(nc, es)
        uid = [0]
        def sb(stack, shape, dt, name="t"):
            uid[0] += 1
            return stack.enter_context(nc.sbuf_tensor(f"{name}{uid[0]}", list(shape), dt))
        xT = sb(es, [128, KC, NT], F32, "xT")
        NBLK = (NT + 127) // 128
        xTk = [[Tk() for _ in range(NBLK)] for _ in range(KC)]
        HW = max(NT - T // 2, min(512, NT))
        hT = sb(es, [128, KC, HW], BF16, "hT")
        hTk = [Tk() for _ in range((HW + 511) // 512 + 1)]
        SLOT = 4096
        wsl = [sb(es, [128, SLOT], BF16, "wsl") for _ in range(3)]
        wslk = [Tk() for _ in range(3)]
        wsi = [0]
        cF = sb(es, [128, C_W], F32, "cF"); cB = sb(es, [128, 256], BF16, "cB")
        onesF = sb(es, [128, 128], F32, "onesF"); nonesF = sb(es, [128, 128], F32, "nonesF"); onesB = sb(es, [128, 128], BF16, "onesB")
        nw = sb(es, [128, 7, KC], F32, "nw")
        kC = Tk()
        sb_sq8 = sb(es, [128, KC, 512], BF16, "sq8"); sb_sq8k = Tk()
        idx = sb(es, [128, NBS * NPG], I32, "idx"); idxk = Tk()
        es_setup = contextlib.ExitStack()
        ptb = sb(es_setup, [128, NBS * NPG], I32, "ptb"); iot = sb(es_setup, [128, 1], I32, "iot")
        PS = [es.enter_context(nc.psum_tensor(f"ps{i}", [128, 512], F32)) for i in range(8)]
        PSk = [Tk(True) for _ in range(8)]
        pring = [list(range(8)), 0]
        def set_ring(banks): pring[0] = list(banks); pring[1] = 0
        def bank():
            b = pring[0][pring[1] % len(pring[0])]; pring[1] += 1
            return PS[b], PSk[b]
        def xtoks(kcs, c0, w):
            return [xTk[kc][blk] for kc in kcs for blk in range(c0 // 128, (c0 + w - 1) // 128 + 1)]
        identF = cF[:, C_I:C_I + 128]
        identB = cB[:, 0:128]; maskB = cB[:, 128:256]

        kb.dma('sp', cF[:, :], cst, writes=[kC])
        for i, src in enumerate([ffn_norm[0:1, :], ffn_norm[1:2, :], ffn_norm[2:3, :], ffn_norm[3:4, :], mix_norm[0:1, :], mix_norm[1:2, :], final_norm[0:1, :]]):
            kb.dma('sp', nw[:, i, :], src.rearrange("a (kc p) -> p (a kc)", p=128), writes=[kC], allow_slow_non_contiguous=True)
        kb.op('pool', lambda e: e.memset(onesF[:, :], 1.0), writes=[kC])
        kb.op('pool', lambda e: e.memset(nonesF[:, :], -1.0), writes=[kC])
        kb.op('pool', lambda e: e.memset(onesB[:, :], 1.0), writes=[kC])
        kb.op('act', lambda e: e.activation(out=cB[:, :], in_=cF[:, 0:256], func=AF.Copy), reads=[kC], writes=[kC])
        kb.dma('sp', ptb[:, :], pt.partition_broadcast(128), writes=[idxk])
        kb.op('pool', lambda e: e.iota(iot[:, :], pattern=[[0, 1]], base=0, channel_multiplier=1), writes=[idxk])
        kb.op('dve', lambda e: e.tensor_scalar(out=idx[:, :], in0=ptb[:, :], scalar1=128, scalar2=iot[:, 0:1], op0=ALU.mult, op1=ALU.add), reads=[idxk], writes=[idxk])
        kb.barrier()
        es_setup.close()

        def wslot():
            i = wsi[0] % 3; wsi[0] += 1
            return wsl[i], wslk[i]

        def load_lhsT(w_ap, nk, ncol, slot, off=0):
            sl, slk = slot
            dst = sl[:, off:off + nk * ncol].rearrange("p (k n) -> p k n", k=nk)
            kb.dma('pool', dst, w_ap.rearrange("(k p) n -> p k n", p=128), writes=[slk])
            return dst

        def norm_cols(stack_tmp, nk, w, src, src_toks, wcol, dst, dst_toks, eps, dn, extra_scale=1.0):
            sq, sqk, rstd, rstdk = stack_tmp
            for kc in range(nk):
                kb.op('act', lambda e, kc=kc: e.activation(out=sq[:, kc, 0:w], in_=src(kc), func=AF.Square), reads=src_toks, writes=[sqk])
            pb, pbk = bank()
            for kc in range(nk):
                kb.op('pe', lambda e, kc=kc: e.matmul(pb[:, 0:w], lhsT=onesB[:, :], rhs=sq[:, kc, 0:w], start=(kc == 0), stop=(kc == nk - 1)), reads=[sqk, kC], writes=[pbk])
            kb.op('act', lambda e: e.activation(out=rstd[:, 0:w], in_=pb[:, 0:w], func=AF.Sqrt, bias=eps, scale=1.0 / dn), reads=[pbk], writes=[rstdk])
            kb.op('dve', lambda e: e.reciprocal(out=rstd[:, 0:w], in_=rstd[:, 0:w]), reads=[rstdk], writes=[rstdk])
            for kc in range(nk):
                kb.op('dve', lambda e, kc=kc: e.scalar_tensor_tensor(out=dst(kc), in0=src(kc), scalar=wcol(kc), in1=rstd[:, 0:w], op0=ALU.mult, op1=ALU.mult), reads=src_toks + [rstdk, kC], writes=dst_toks)

        with contextlib.ExitStack() as st:
            xin = [sb(st, [128, D], F32, "xin") for _ in range(2)]
            xink = [Tk(), Tk()]
            set_ring(range(8))
            for tt in range(NTT + 1):
                rows = 128 if tt < NTT else NBS
                xi, xik = xin[tt % 2], xink[tt % 2]
                src = xp[tt * 128:(tt + 1) * 128, :] if tt < NTT else xs[:, :]
                kb.dma('sp', xi[0:rows, :], src, writes=[xik])
                for half in range(2):
                    pb, pbk = bank()
                    for j in range(4):
                        c = half * 4 + j
                        kb.op('pe', lambda e, c=c, j=j: e.matmul(pb[:, j * 128:j * 128 + rows], lhsT=xi[0:rows, c * 128:(c + 1) * 128], rhs=identF[0:rows, 0:rows], start=True, stop=True), reads=[xik, kC], writes=[pbk])
                    eng = 'act' if half == 0 else 'dve'
                    o = xT[:, half * 4:half * 4 + 4, tt * 128:tt * 128 + rows]
                    i_ = pb[:, :].rearrange("p (a b) -> p a b", a=4)[:, :, 0:rows]
                    wt = [xTk[half * 4 + j][tt] for j in range(4)]
                    if eng == 'act':
                        kb.op('act', lambda e: e.activation(out=o, in_=i_, func=AF.Copy), reads=[pbk], writes=wt)
                    else:
                        kb.op('dve', lambda e: e.tensor_copy(out=o, in_=i_), reads=[pbk], writes=wt)
            kb.barrier()

        def ffn(idx):
            with contextlib.ExitStack() as st:
                AW = HW
                act = sb(st, [128, FC, AW], BF16, "act"); actk = [Tk() for _ in range(FC)]
                rstd = sb(st, [128, 512], F32, "rstd")
                tmp = (sb_sq8, sb_sq8k, rstd, Tk())
                sg = [sb(st, [128, 512], F32, "sg") for _ in range(2)]; sgk = [Tk(), Tk()]
                set_ring(range(8))
                nsg = 0
                for (s0, s1) in ((0, T // 2), (T // 2, NT)):
                    tiles = col_tiles(s0, s1)
                    for ti, (c0, w) in enumerate(tiles):
                        l0 = c0 - s0
                        norm_cols(tmp, KC, w, lambda kc: xT[:, kc, c0:c0 + w], xtoks(range(KC), c0, w), lambda kc: nw[:, idx, kc:kc + 1],
                                  lambda kc: hT[:, kc, l0:l0 + w], [hTk[ti]], 1e-6, float(D))
                    for fp in range(FC // 2):
                        slot = wslot()
                        g2 = load_lhsT(wg[idx * D:(idx + 1) * D, fp * 256:(fp + 1) * 256], KC, 256, slot, 0)
                        u2 = load_lhsT(wu[idx * D:(idx + 1) * D, fp * 256:(fp + 1) * 256], KC, 256, slot, 2048)
                        for ti, (c0, w) in enumerate(tiles):
                            l0 = c0 - s0
                            for half in range(2):
                                f = 2 * fp + half
                                pg, pgk = bank()
                                for kc in range(KC):
                                    kb.op('pe', lambda e, kc=kc: e.matmul(pg[:, 0:w], lhsT=g2[:, kc, half * 128:(half + 1) * 128], rhs=hT[:, kc, l0:l0 + w], start=(kc == 0), stop=(kc == KC - 1)), reads=[slot[1], hTk[ti]], writes=[pgk])
                                pu, puk = bank()
                                for kc in range(KC):
                                    kb.op('pe', lambda e, kc=kc: e.matmul(pu[:, 0:w], lhsT=u2[:, kc, half * 128:(half + 1) * 128], rhs=hT[:, kc, l0:l0 + w], start=(kc == 0), stop=(kc == KC - 1)), reads=[slot[1], hTk[ti]], writes=[puk])
                                s_, sk_ = sg[nsg % 2], sgk[nsg % 2]; nsg += 1
                                kb.op('act', lambda e: e.activation(out=s_[:, 0:w], in_=pg[:, 0:w], func=AF.Silu), reads=[pgk], writes=[sk_])
                                kb.op('dve', lambda e: e.tensor_tensor(out=act[:, f, l0:l0 + w], in0=s_[:, 0:w], in1=pu[:, 0:w], op=ALU.mult), reads=[sk_, puk], writes=[actk[f]])
                    for dc in range(KC):
                        slot = wslot()
                        d1 = load_lhsT(wd[idx * DFF:(idx + 1) * DFF, dc * 128:(dc + 1) * 128], FC, 128, slot, 0)
                        for ti, (c0, w) in enumerate(tiles):
                            l0 = c0 - s0
                            pb, pbk = bank()
                            for f in range(FC):
                                kb.op('pe', lambda e, f=f: e.matmul(pb[:, 0:w], lhsT=d1[:, f, :], rhs=act[:, f, l0:l0 + w], start=(f == 0), stop=(f == FC - 1)), reads=[slot[1], actk[f]], writes=[pbk])
                            xt_ = xtoks([dc], c0, w)
                            kb.op('dve', lambda e: e.scalar_tensor_tensor(out=xT[:, dc, c0:c0 + w], in0=pb[:, 0:w], scalar=0.5, in1=xT[:, dc, c0:c0 + w], op0=ALU.mult, op1=ALU.add), reads=[pbk] + xt_, writes=xt_)
                kb.barrier()

        def final():
            with contextlib.ExitStack() as st:
                rstd = sb(st, [128, 512], F32, "rstd")
                tmp = (sb_sq8, sb_sq8k, rstd, Tk())
                yT = sb(st, [128, KC, 512], F32, "yT"); yTk = Tk()
                yo = [sb(st, [128, D], F32, "yo") for _ in range(2)]; yok = [Tk(), Tk()]
                set_ring(range(8))
                n = 0
                for (c0, w) in col_tiles(0, NT):
                    norm_cols(tmp, KC, w, lambda kc: xT[:, kc, c0:c0 + w], xtoks(range(KC), c0, w), lambda kc: nw[:, 6, kc:kc + 1],
                              lambda kc: yT[:, kc, 0:w], [yTk], 1e-6, float(D))
                    for t0 in range(0, w, 128):
                        rows = min(128, w - t0)
                        y_, yk_ = yo[n % 2], yok[n % 2]; n += 1
                        for half in range(2):
                            pb, pbk = bank()
                            for j in range(4):
                                c = half * 4 + j
                                kb.op('pe', lambda e, c=c, j=j: e.matmul(pb[0:rows, j * 128:(j + 1) * 128], lhsT=yT[:, c, t0:t0 + rows], rhs=identF[:, :], start=True, stop=True), reads=[yTk, kC], writes=[pbk])
                            if half == 0:
                                kb.op('act', lambda e: e.activation(out=y_[0:rows, 0:512], in_=pb[0:rows, :], func=AF.Copy), reads=[pbk], writes=[yk_])
                            else:
                                kb.op('dve', lambda e: e.tensor_copy(out=y_[0:rows, 512:1024], in_=pb[0:rows, :]), reads=[pbk], writes=[yk_])
                        g0 = c0 + t0
                        dst = yp[g0:g0 + rows, :] if g0 < T else ys[0:rows, :]
                        kb.dma('sp', dst, y_[0:rows, :], reads=[yk_])
                kb.barrier()


        class Ring:
            def __init__(self, stack, n, shape, dt, name="r"):
                self.t = [sb(stack, shape, dt, name) for _ in range(n)]; self.k = [Tk() for _ in range(n)]; self.i = 0
            def next(self):
                i = self.i % len(self.t); self.i += 1
                return self.t[i], self.k[i]

        def evac(eng, out, in_, reads, writes):
            if eng == 'act':
                kb.op('act', lambda e: e.activation(out=out, in_=in_, func=AF.Copy), reads=reads, writes=writes)
            else:
                kb.op(eng, lambda e: e.tensor_copy(out=out, in_=in_), reads=reads, writes=writes)

        def apply_wout(w_ap, srcT, src_toks, c0, w, nk=KC):
            for dp in range(KC // 2):
                slot = wslot()
                s2 = load_lhsT(w_ap[:, dp * 256:(dp + 1) * 256], nk, 256, slot, 0)
                for half in range(2):
                    dc = dp * 2 + half
                    pb, pbk = bank()
                    for kc in range(nk):
                        kb.op('pe', lambda e, kc=kc: e.matmul(pb[:, 0:w], lhsT=s2[:, kc, half * 128:(half + 1) * 128], rhs=srcT[:, kc, 0:w], start=(kc == 0), stop=(kc == nk - 1)), reads=[slot[1]] + src_toks, writes=[pbk])
                    xt_ = xtoks([dc], c0, w)
                    kb.op('dve', lambda e: e.tensor_tensor(out=xT[:, dc, c0:c0 + w], in0=pb[:, 0:w], in1=xT[:, dc, c0:c0 + w], op=ALU.add), reads=[pbk] + xt_, writes=xt_)

        def odd_mixer():
            SCALE = float(192 ** -0.5)
            with contextlib.ExitStack() as st:
                uqN = sb(st, [128, 3, 8, 128], BF16, "uqN"); uqA = sb(st, [128, 3, 8, 64], BF16, "uqA"); uqB = sb(st, [128, 3, 8, 64], BF16, "uqB")
                wkT = sb(st, [128, 8, 256], BF16, "wkT"); wv = sb(st, [128, 2, 8, 128], BF16, "wv")
                qnw = sb(st, [128, 3], F32, "qnw"); kvnw = sb(st, [128, 2], F32, "kvnw")
                kW = Tk()
                ckvnT = sb(st, [128, 2, NT], BF16, "ckvnT"); rkT = sb(st, [64, NT], BF16, "rkT")
                vtok = sb(st, [128, NTT + 1, 256], BF16, "vtok")
                kK = [Tk() for _ in range(NTT + 1)]
                attnT = sb(st, [128, 8, 512], BF16, "attnT"); attnk = Tk()
                rstd = sb(st, [128, 512], F32, "rstd"); tmp = (sb_sq8, sb_sq8k, rstd, Tk())
                rL = rstd; rLk = tmp[3]
                ctxn = Ring(st, 1, [128, 2, 512], BF16, "ctxn")
                qls = sb(st, [128, 2, NBS, 8], BF16, "qls"); rqs = sb(st, [64, NBS, 8], BF16, "rqs"); qsk = Tk()
                st_u = contextlib.ExitStack()
                ukvK = sb(st_u, [128, 2, 8, 128], F32, "ukvK")
                for h in range(8):
                    kb.dma('pool', uqN[:, :, h, :], muq[:, h * 192:h * 192 + 128].rearrange("(k p) n -> p k n", p=128), writes=[kW])
                    kb.dma('pool', uqA[:, :, h, :], muq[:, h * 192 + 128:h * 192 + 192].rearrange("(k p) n -> p k n", p=128), writes=[kW])
                    kb.dma('pool', uqB[:, :, h, 0:32], muq[:, h * 192 + 160:h * 192 + 192].rearrange("(k p) n -> p k n", p=128), writes=[kW])
                    kb.dma('pool', uqB[:, :, h, 32:64], muq[:, h * 192 + 128:h * 192 + 160].rearrange("(k p) n -> p k n", p=128), writes=[kW])
                    kb.dma('sp', ukvK[:, :, h, :], mukv[:, h * 256:h * 256 + 128].rearrange("(k p) n -> p k n", p=128), writes=[kW])
                    kb.dma('pool', wv[:, :, h, :], mukv[:, h * 256 + 128:h * 256 + 256].rearrange("(k p) n -> p k n", p=128), writes=[kW])
                kb.dma('sp', qnw[:, :], mqn.rearrange("a (k p) -> p (a k)", p=128), writes=[kW], allow_slow_non_contiguous=True)
                kb.dma('sp', kvnw[:, :], mkvn.rearrange("a (k p) -> p (a k)", p=128), writes=[kW], allow_slow_non_contiguous=True)
                set_ring([0, 1, 2])
                for h in range(8):
                    for c2 in range(2):
                        pb, pbk = bank()
                        kb.op('pe', lambda e: e.matmul(pb[:, 0:128], lhsT=ukvK[:, c2, h, :], rhs=identF[:, :], start=True, stop=True), reads=[kW, kC], writes=[pbk])
                        evac('act' if c2 == 0 else 'dve', wkT[:, h, c2 * 128:(c2 + 1) * 128], pb[:, 0:128], [pbk], [kW])
                kb.barrier(); st_u.close()
                sp_ = contextlib.ExitStack()
                cqT = sb(sp_, [128, 3, 512], BF16, "cqT"); cqTk = Tk(); cqn = sb(sp_, [128, 3, 512], BF16, "cqn"); cqnk = Tk()
                ckvF = sb(sp_, [128, 2, 512], F32, "ckvF"); ckvFk = Tk(); ckvnF = sb(sp_, [128, 2, 512], F32, "ckvnF"); ckvnFk = Tk()
                rt = sb(sp_, [64, 2, 512], F32, "rt"); rtk = Tk()
                t1r = Ring(sp_, 1, [64, 512], F32, "t1"); t2r = Ring(sp_, 1, [64, 512], F32, "t2")
                rkF = sb(sp_, [64, 512], F32, "rkF"); rkFk = Tk()
                tokst = Ring(sp_, 1, [128, 320], F32, "tokst")
                qnT = Ring(sp_, 1, [128, 512], BF16, "qnT"); qlT = Ring(sp_, 2, [128, 2, 512], BF16, "qlT"); rqT = Ring(sp_, 1, [64, 512], BF16, "rqT")
                Er = Ring(sp_, 2, [128, 512], BF16, "E")
                if DBG < 1:
                    kb.barrier(); return
                sring = [3, 4]; sri = [0]
                def sbank():
                    b = sring[sri[0] % 2]; sri[0] += 1
                    return PS[b], PSk[b]
                CT = [(PS[5], PSk[5]), (PS[6], PSk[6])]; LT = (PS[7], PSk[7])
                ropeV = ropeT.rearrange("p (a n) -> p a n", a=2)

                def rope_apply(pa, pak, pbb, pbbk, w, out_f32=None, out_f32k=None, out_bf=None, out_bfk=None):
                    t1, t1k = t1r.next(); t2, t2k = t2r.next()
                    kb.op('dve', lambda e: e.tensor_tensor(out=t1[:, 0:w], in0=pa[0:64, 0:w], in1=rt[:, 0, 0:w], op=ALU.mult), reads=[pak, rtk], writes=[t1k])
                    kb.op('dve', lambda e: e.tensor_tensor(out=t2[:, 0:w], in0=pbb[0:64, 0:w], in1=rt[:, 1, 0:w], op=ALU.mult), reads=[pbbk, rtk], writes=[t2k])
                    if out_f32 is not None:
                        kb.op('pool', lambda e: e.tensor_tensor(out=out_f32, in0=t1[:, 0:w], in1=t2[:, 0:w], op=ALU.add), reads=[t1k, t2k], writes=[out_f32k])
                        kb.op('act', lambda e: e.activation(out=out_bf, in_=out_f32, func=AF.Copy), reads=[out_f32k], writes=out_bfk)
                    else:
                        kb.op('pool', lambda e: e.tensor_tensor(out=out_bf, in0=t1[:, 0:w], in1=t2[:, 0:w], op=ALU.add), reads=[t1k, t2k], writes=out_bfk)

                for (c0, w) in col_tiles(0, T) + [(T, NBS)]:
                    sample = c0 >= T
                    ktile0 = c0 // 128
                    nsub = (w + 127) // 128
                    ktoks = [kK[ktile0 + i] for i in range(nsub)]
                    norm_cols(tmp, KC, w, lambda kc: xT[:, kc, c0:c0 + w], xtoks(range(KC), c0, w), lambda kc: nw[:, 5, kc:kc + 1],
                              lambda kc: hT[:, kc, 0:w], [hTk[0]], 1e-6, float(D))
                    kb.dma('sp', rt[:, :, 0:w], ropeV[:, :, c0:c0 + w], writes=[rtk])
                    slot = wslot()
                    wq = load_lhsT(mwin[:, 0:384], KC, 384, slot, 0)
                    slot2 = wslot()
                    wc = load_lhsT(mwin[:, 384:640], KC, 256, slot2, 0)
                    wpa = load_lhsT(mwin[:, 640:704], KC, 64, slot2, 2048)
                    wpb = slot2[0][:, 2560:2560 + KC * 64].rearrange("p (k n) -> p k n", k=KC)
                    kb.dma('pool', wpb[:, :, 0:32], mwin[:, 672:704].rearrange("(k p) n -> p k n", p=128), writes=[slot2[1]])
                    kb.dma('pool', wpb[:, :, 32:64], mwin[:, 640:672].rearrange("(k p) n -> p k n", p=128), writes=[slot2[1]])
                    if DBG2 < 2: continue
                    for j in range(3):
                        pb, pbk = bank()
                        for kc in range(KC):
                            kb.op('pe', lambda e, kc=kc: e.matmul(pb[:, 0:w], lhsT=wq[:, kc, j * 128:(j + 1) * 128], rhs=hT[:, kc, 0:w], start=(kc == 0), stop=(kc == KC - 1)), reads=[slot[1], hTk[0]], writes=[pbk])
                        evac('act', cqT[:, j, 0:w], pb[:, 0:w], [pbk], [cqTk])
                    norm_cols(tmp, 3, w, lambda kc: cqT[:, kc, 0:w], [cqTk], lambda kc: qnw[:, kc:kc + 1], lambda kc: cqn[:, kc, 0:w], [cqnk], 1e-6, 384.0)
                    if DBG2 < 3: continue
                    for j in range(2):
                        pb, pbk = bank()
                        for kc in range(KC):
                            kb.op('pe', lambda e, kc=kc: e.matmul(pb[:, 0:w], lhsT=wc[:, kc, j * 128:(j + 1) * 128], rhs=hT[:, kc, 0:w], start=(kc == 0), stop=(kc == KC - 1)), reads=[slot2[1], hTk[0]], writes=[pbk])
                        evac('act', ckvF[:, j, 0:w], pb[:, 0:w], [pbk], [ckvFk])
                    norm_cols(tmp, 2, w, lambda kc: ckvF[:, kc, 0:w], [ckvFk], lambda kc: kvnw[:, kc:kc + 1], lambda kc: ckvnF[:, kc, 0:w], [ckvnFk], 1e-6, 256.0)
                    kb.op('pool', lambda e: e.tensor_copy(out=ckvnT[:, :, c0:c0 + w], in_=ckvnF[:, :, 0:w]), reads=[ckvnFk], writes=ktoks)
                    if DBG2 < 4: continue
                    pa, pak = bank()
                    for kc in range(KC):
                        kb.op('pe', lambda e, kc=kc: e.matmul(pa[0:64, 0:w], lhsT=wpa[:, kc, :], rhs=hT[:, kc, 0:w], start=(kc == 0), stop=(kc == KC - 1)), reads=[slot2[1], hTk[0]], writes=[pak])
                    pbb, pbbk = bank()
                    for kc in range(KC):
                        kb.op('pe', lambda e, kc=kc: e.matmul(pbb[0:64, 0:w], lhsT=wpb[:, kc, :], rhs=hT[:, kc, 0:w], start=(kc == 0), stop=(kc == KC - 1)), reads=[slot2[1], hTk[0]], writes=[pbbk])
                    rope_apply(pa, pak, pbb, pbbk, w, rkF[:, 0:w], rkFk, rkT[:, c0:c0 + w], ktoks)
                    if DBG2 < 5: continue
                    for si in range(nsub):
                        t0 = si * 128; rows = min(128, w - t0)
                        pb, pbk = bank()
                        for c2 in range(2):
                            kb.op('pe', lambda e, c2=c2: e.matmul(pb[0:rows, c2 * 128:(c2 + 1) * 128], lhsT=ckvnF[:, c2, t0:t0 + rows], rhs=identF[:, :], start=True, stop=True), reads=[ckvnFk, kC], writes=[pbk])
                        if DBG3 & 1:
                            kb.op('pe', lambda e: e.matmul(pb[0:rows, 256:320], lhsT=rkF[0:64, t0:t0 + rows], rhs=identF[0:64, 0:64], start=True, stop=True), reads=[rkFk, kC], writes=[pbk])
                        ts_, tsk = tokst.next()
                        evac('act', ts_[0:rows, :], pb[0:rows, 0:320], [pbk], [tsk])
                        kb.op('dve', lambda e: e.tensor_copy(out=vtok[0:rows, ktile0 + si, :], in_=pb[0:rows, 0:256]), reads=[pbk], writes=[ktoks[si]])
                        g0 = c0 + t0
                        if not (DBG3 & 2): continue
                        if not sample:
                            kb.dma('sp', ckvp[g0:g0 + rows, :], ts_[0:rows, 0:256], reads=[tsk])
                            kb.dma('sp', kpep[g0:g0 + rows, :], ts_[0:rows, 256:320], reads=[tsk])
                        else:
                            kb.dma('sp', ckvs[0:rows, :], ts_[0:rows, 0:256], reads=[tsk])
                            kb.dma('sp', kpes[0:rows, :], ts_[0:rows, 256:320], reads=[tsk])
                    if DBG2 < 6: continue
                    for h in range(8 if DBG >= 2 else 0):
                        pb, pbk = bank()
                        for kc in range(3):
                            kb.op('pe', lambda e, kc=kc: e.matmul(pb[:, 0:w], lhsT=uqN[:, kc, h, :], rhs=cqn[:, kc, 0:w], start=(kc == 0), stop=(kc == 2)), reads=[kW, cqnk], writes=[pbk])
                        qn, qnk = qnT.next()
                        evac('act', qn[:, 0:w], pb[:, 0:w], [pbk], [qnk])
                        ql, qlk = qlT.next()
                        for c2 in range(2):
                            pb, pbk = bank()
                            kb.op('pe', lambda e: e.matmul(pb[:, 0:w], lhsT=wkT[:, h, c2 * 128:(c2 + 1) * 128], rhs=qn[:, 0:w], start=True, stop=True), reads=[kW, qnk], writes=[pbk])
                            evac('dve', ql[:, c2, 0:w], pb[:, 0:w], [pbk], [qlk])
                        pa, pak = bank()
                        for kc in range(3):
                            kb.op('pe', lambda e, kc=kc: e.matmul(pa[0:64, 0:w], lhsT=uqA[:, kc, h, :], rhs=cqn[:, kc, 0:w], start=(kc == 0), stop=(kc == 2)), reads=[kW, cqnk], writes=[pak])
                        pbb, pbbk = bank()
                        for kc in range(3):
                            kb.op('pe', lambda e, kc=kc: e.matmul(pbb[0:64, 0:w], lhsT=uqB[:, kc, h, :], rhs=cqn[:, kc, 0:w], start=(kc == 0), stop=(kc == 2)), reads=[kW, cqnk], writes=[pbbk])
                        rq, rqk = rqT.next()
                        rope_apply(pa, pak, pbb, pbbk, w, None, None, rq[:, 0:w], [rqk])
                        if sample:
                            kb.op('pool', lambda e: e.tensor_copy(out=qls[:, :, :, h], in_=ql[:, :, 0:NBS]), reads=[qlk], writes=[qsk])
                            kb.op('pool', lambda e: e.tensor_copy(out=rqs[:, :, h], in_=rq[:, 0:NBS]), reads=[rqk], writes=[qsk])
                            continue
                        nkt = (c0 + w) // 128
                        if DBG < 3: continue
                        for kt in range(nkt):
                            k0 = kt * 128; qs = max(k0 - c0, 0)
                            ps, psk = sbank()
                            kb.op('pe', lambda e: e.matmul(ps[:, qs:w], lhsT=ckvnT[:, 0, k0:k0 + 128], rhs=ql[:, 0, qs:w], start=True, stop=False), reads=[kK[kt], qlk], writes=[psk])
                            kb.op('pe', lambda e: e.matmul(ps[:, qs:w], lhsT=ckvnT[:, 1, k0:k0 + 128], rhs=ql[:, 1, qs:w], start=False, stop=False), reads=[kK[kt], qlk], writes=[psk])
                            kb.op('pe', lambda e: e.matmul(ps[:, qs:w], lhsT=rkT[0:64, k0:k0 + 128], rhs=rq[0:64, qs:w], start=False, stop=True), reads=[kK[kt], rqk], writes=[psk])
                            E, Ek = Er.next()
                            kb.op('act', lambda e: e.activation(out=E[:, qs:w], in_=ps[:, qs:w], func=AF.Exp, scale=SCALE), reads=[psk], writes=[Ek])
                            if k0 >= c0:
                                kb.op('pool', lambda e: e.tensor_tensor(out=E[:, qs:qs + 128], in0=E[:, qs:qs + 128], in1=maskB, op=ALU.mult), reads=[Ek, kC], writes=[Ek])
                            for c2 in range(2):
                                kb.op('pe', lambda e, c2=c2: e.matmul(CT[c2][0][:, qs:w], lhsT=vtok[:, kt, c2 * 128:(c2 + 1) * 128], rhs=E[:, qs:w], start=(kt == 0), stop=(kt == nkt - 1)), reads=[kK[kt], Ek], writes=[CT[c2][1]])
                            kb.op('pe', lambda e: e.matmul(LT[0][:, qs:w], lhsT=onesB[:, :], rhs=E[:, qs:w], start=(kt == 0), stop=(kt == nkt - 1)), reads=[kC, Ek], writes=[LT[1]])
                        kb.op('dve', lambda e: e.reciprocal(out=rL[:, 0:w], in_=LT[0][:, 0:w]), reads=[LT[1]], writes=[rLk])
                        cx, cxk = ctxn.next()
                        for c2 in range(2):
                            kb.op('dve', lambda e, c2=c2: e.tensor_tensor(out=cx[:, c2, 0:w], in0=CT[c2][0][:, 0:w], in1=rL[:, 0:w], op=ALU.mult), reads=[CT[c2][1], rLk], writes=[cxk])
                        pb, pbk = bank()
                        for c2 in range(2):
                            kb.op('pe', lambda e, c2=c2: e.matmul(pb[:, 0:w], lhsT=wv[:, c2, h, :], rhs=cx[:, c2, 0:w], start=(c2 == 0), stop=(c2 == 1)), reads=[kW, cxk], writes=[pbk])
                        evac('act', attnT[:, h, 0:w], pb[:, 0:w], [pbk], [attnk])
                    if not sample:
                        apply_wout(mwout, attnT, [attnk], c0, w)
                kb.barrier(); sp_.close()
                if DBG < 4:
                    kb.barrier(); return
                G = 4
                cpg = Ring(st, 2, [128, G, 256], F32, "cpg"); ppg = Ring(st, 2, [128, G, 64], F32, "ppg")
                kTr = Ring(st, 2, [128, 3, G * 128], BF16, "kTr"); vbf = Ring(st, 2, [128, G, 256], BF16, "vbf")
                Es = Ring(st, 2, [128, G * 8], BF16, "Es")
                Esf = sb(st, [NBS, 32], F32, "Esf"); Esb = sb(st, [NBS, 32], BF16, "Esb"); Esk = Tk()
                ps, psk = sbank()
                for b in range(NBS):
                    kb.op('pe', lambda e: e.matmul(ps[0:NBS, b * 8:b * 8 + 8], lhsT=ckvnT[:, 0, T:T + NBS], rhs=qls[:, 0, b, :], start=True, stop=False), reads=[kK[NTT], qsk], writes=[psk])
                    kb.op('pe', lambda e: e.matmul(ps[0:NBS, b * 8:b * 8 + 8], lhsT=ckvnT[:, 1, T:T + NBS], rhs=qls[:, 1, b, :], start=False, stop=False), reads=[kK[NTT], qsk], writes=[psk])
                    kb.op('pe', lambda e: e.matmul(ps[0:NBS, b * 8:b * 8 + 8], lhsT=rkT[0:64, T:T + NBS], rhs=rqs[0:64, b, :], start=False, stop=True), reads=[kK[NTT], qsk], writes=[psk])
                kb.op('act', lambda e: e.activation(out=Esf[:, :], in_=ps[0:NBS, 0:32], func=AF.Exp, scale=SCALE), reads=[psk], writes=[Esk])
                kb.op('dve', lambda e: e.tensor_tensor(out=Esb[:, :], in0=Esf[:, :], in1=cF[0:NBS, C_SELF:C_SELF + 32], op=ALU.mult), reads=[Esk, kC], writes=[Esk])
                for c2 in range(2):
                    kb.op('pe', lambda e, c2=c2: e.matmul(CT[c2][0][:, 0:32], lhsT=vtok[0:NBS, NTT, c2 * 128:(c2 + 1) * 128], rhs=Esb[:, :], start=True, stop=False), reads=[kK[NTT], Esk], writes=[CT[c2][1]])
                kb.op('pe', lambda e: e.matmul(LT[0][:, 0:32], lhsT=onesB[0:NBS, :], rhs=Esb[:, :], start=True, stop=False), reads=[kC, Esk], writes=[LT[1]])
                for b in range(NBS):
                    for g0 in range(0, NPG, G):
                        ng = min(G, NPG - g0)
                        cp_, cpk = cpg.next(); pp_, ppk = ppg.next()
                        for g in range(ng):
                            col = b * NPG + g0 + g
                            if DBG4 & 1:
                                kb.dma('pool', cp_[:, g, :], cc, reads=[idxk], writes=[cpk], indirect=idx[:, col:col + 1])
                                kb.dma('pool', pp_[:, g, :], cp, reads=[idxk], writes=[ppk], indirect=idx[:, col:col + 1])
                            else:
                                kb.dma('sp', cp_[:, g, :], cc[0:128, :], writes=[cpk])
                                kb.dma('sp', pp_[:, g, :], cp[0:128, :], writes=[ppk])
                        if not (DBG4 & 2): continue
                        kT, kTk = kTr.next()
                        for c2 in range(3):
                            pb, pbk = bank()
                            for g in range(ng):
                                if c2 < 2:
                                    kb.op('pe', lambda e, g=g: e.matmul(pb[:, g * 128:(g + 1) * 128], lhsT=cp_[:, g, c2 * 128:(c2 + 1) * 128], rhs=identF[:, :], start=True, stop=True), reads=[cpk, kC], writes=[pbk])
                                else:
                                    kb.op('pe', lambda e, g=g: e.matmul(pb[0:64, g * 128:(g + 1) * 128], lhsT=pp_[:, g, :], rhs=identF[:, :], start=True, stop=True), reads=[ppk, kC], writes=[pbk])
                            np_ = 128 if c2 < 2 else 64
                            evac('act' if c2 != 1 else 'dve', kT[0:np_, c2, 0:ng * 128], pb[0:np_, 0:ng * 128], [pbk], [kTk])
                        vb, vbk = vbf.next()
                        kb.op('dve', lambda e: e.tensor_copy(out=vb[:, 0:ng, :], in_=cp_[:, 0:ng, :]), reads=[cpk], writes=[vbk])
                        ps, psk = sbank()
                        for g in range(ng):
                            kb.op('pe', lambda e, g=g: e.matmul(ps[:, g * 8:g * 8 + 8], lhsT=kT[:, 0, g * 128:(g + 1) * 128], rhs=qls[:, 0, b, :], start=True, stop=False), reads=[kTk, qsk], writes=[psk])
                            kb.op('pe', lambda e, g=g: e.matmul(ps[:, g * 8:g * 8 + 8], lhsT=kT[:, 1, g * 128:(g + 1) * 128], rhs=qls[:, 1, b, :], start=False, stop=False), reads=[kTk, qsk], writes=[psk])
                            kb.op('pe', lambda e, g=g: e.matmul(ps[:, g * 8:g * 8 + 8], lhsT=kT[0:64, 2, g * 128:(g + 1) * 128], rhs=rqs[0:64, b, :], start=False, stop=True), reads=[kTk, qsk], writes=[psk])
                        E, Ek = Es.next()
                        kb.op('act', lambda e: e.activation(out=E[:, 0:ng * 8], in_=ps[:, 0:ng * 8], func=AF.Exp, scale=SCALE), reads=[psk], writes=[Ek])
                        last = (g0 + G >= NPG)
                        for g in range(ng):
                            fin = last and g == ng - 1 and b == NBS - 1
                            for c2 in range(2):
                                kb.op('pe', lambda e, g=g, c2=c2: e.matmul(CT[c2][0][:, b * 8:b * 8 + 8], lhsT=vb[:, g, c2 * 128:(c2 + 1) * 128], rhs=E[:, g * 8:g * 8 + 8], start=False, stop=fin), reads=[vbk, Ek], writes=[CT[c2][1]])
                            kb.op('pe', lambda e, g=g: e.matmul(LT[0][:, b * 8:b * 8 + 8], lhsT=onesB[:, :], rhs=E[:, g * 8:g * 8 + 8], start=False, stop=fin), reads=[kC, Ek], writes=[LT[1]])
                kb.op('dve', lambda e: e.reciprocal(out=rL[:, 0:32], in_=LT[0][:, 0:32]), reads=[LT[1]], writes=[rLk])
                cx, cxk = ctxn.next()
                for c2 in range(2):
                    kb.op('dve', lambda e, c2=c2: e.tensor_tensor(out=cx[:, c2, 0:32], in0=CT[c2][0][:, 0:32], in1=rL[:, 0:32], op=ALU.mult), reads=[CT[c2][1], rLk], writes=[cxk])
                for h in range(8):
                    pb, pbk = bank()
                    for c2 in range(2):
                        kb.op('pe', lambda e, c2=c2: e.matmul(pb[:, 0:NBS], lhsT=wv[:, c2, h, :], rhs=cx[:, c2, 0:32].rearrange("p (b h) -> p b h", h=8)[:, :, h], start=(c2 == 0), stop=(c2 == 1)), reads=[kW, cxk], writes=[pbk])
                    evac('act', attnT[:, h, 0:NBS], pb[:, 0:NBS], [pbk], [attnk])
                apply_wout(mwout, attnT, [attnk], T, NBS)
                kb.barrier()

        def even_mixer():
            with contextlib.ExitStack() as st:
                mixT = sb(st, [128, KC, NT], BF16, "mixT")
                mtiles = col_tiles(0, T) + [(T, NBS)]
                mixk = [[Tk() for _ in mtiles] for _ in range(KC)]
                if "diff" in stages or True:
                    diff_stage(mixT, mixk, mtiles)
                if DBG >= 5:
                    gdn_stage(mixT, mixk, mtiles)
                else:
                    for ti, (c0, w) in enumerate(mtiles):
                        for kc in range(4):
                            kb.op('pool', lambda e, kc=kc: e.memset(mixT[:, kc, c0:c0 + w], 0.0), writes=[mixk[kc][ti]])
                set_ring(range(8))
                for ti, (c0, w) in enumerate(mtiles):
                    apply_wout(ewout, mixT[:, :, c0:c0 + w], [mixk[kc][ti] for kc in range(KC)], c0, w)
                kb.barrier()

        def diff_stage(mixT, mixk, mtiles):
            with contextlib.ExitStack() as so:
                lamb = sb(so, [128, 256], F32, "lamb"); ltmp = sb(so, [128, 64], F32, "ltmp"); lcol = sb(so, [128, 8], F32, "lcol"); kL = Tk()
                dkS = sb(so, [128, 4, NBS], BF16, "dkS"); dqS = sb(so, [128, 4, NBS], BF16, "dqS"); dvS = sb(so, [NBS, 512], BF16, "dvS"); dqtok = sb(so, [NBS, 512], F32, "dqtok"); kS = Tk()
                rl = sb(so, [128, 512], F32, "rl"); t0 = sb(so, [128, 512], F32, "t0"); t1 = sb(so, [128, 512], F32, "t1"); sq1 = sb(so, [128, 512], BF16, "sq1")
                rlk, t0k, t1k, sq1k = Tk(), Tk(), Tk(), Tk()
                stg = Ring(so, 1, [128, 512], F32, "stg")
                kb.dma('sp', lamb[:, :], dlam.partition_broadcast(128), writes=[kL])
                kb.dma('sp', lcol[:, 4:5], subln.rearrange("a p -> p a"), writes=[kL], allow_slow_non_contiguous=True)
                for i in range(2):
                    kb.op('dve', lambda e: e.tensor_tensor(out=ltmp[:, :], in0=lamb[:, i * 128:i * 128 + 64], in1=lamb[:, i * 128 + 64:i * 128 + 128], op=ALU.mult), reads=[kL], writes=[kL])
                    kb.op('dve', lambda e: e.reduce_sum(out=lcol[:, i:i + 1], in_=ltmp[:, :], axis=AX.X), reads=[kL], writes=[kL])
                    kb.op('act', lambda e: e.activation(out=lcol[:, i:i + 1], in_=lcol[:, i:i + 1], func=AF.Exp), reads=[kL], writes=[kL])
                LAM_INIT = 0.2
                kb.op('dve', lambda e: e.tensor_tensor(out=lcol[:, 2:3], in0=lcol[:, 1:2], in1=lcol[:, 0:1], op=ALU.subtract), reads=[kL], writes=[kL])
                kb.op('dve', lambda e: e.tensor_scalar(out=lcol[:, 2:3], in0=lcol[:, 2:3], scalar1=-LAM_INIT, scalar2=None, op0=ALU.add), reads=[kL], writes=[kL])
                kb.op('dve', lambda e: e.tensor_scalar(out=lcol[:, 5:6], in0=lcol[:, 4:5], scalar1=1.0 - LAM_INIT, scalar2=None, op0=ALU.mult), reads=[kL], writes=[kL])
                neglam = lcol[:, 2:3]; subw = lcol[:, 5:6]
                sring = [0, 1]; sri = [0]
                def sbank():
                    b = sring[sri[0] % 2]; sri[0] += 1
                    return PS[b], PSk[b]
                OL = [[(PS[2], PSk[2]), (PS[3], PSk[3])], [(PS[4], PSk[4]), (PS[5], PSk[5])]]
                set_ring([6, 7])

                def combine(O0, L0, O1, L1, toks, n, out_fn):
                    kb.op('dve', lambda e: e.reciprocal(out=rl[:, 0:n], in_=L0), reads=toks, writes=[rlk])
                    kb.op('dve', lambda e: e.tensor_tensor(out=t0[:, 0:n], in0=O0, in1=rl[:, 0:n], op=ALU.mult), reads=toks + [rlk], writes=[t0k])
                    kb.op('dve', lambda e: e.reciprocal(out=rl[:, 0:n], in_=L1), reads=toks, writes=[rlk])
                    kb.op('dve', lambda e: e.tensor_tensor(out=t1[:, 0:n], in0=O1, in1=rl[:, 0:n], op=ALU.mult), reads=toks + [rlk], writes=[t1k])
                    kb.op('dve', lambda e: e.scalar_tensor_tensor(out=t0[:, 0:n], in0=t1[:, 0:n], scalar=neglam, in1=t0[:, 0:n], op0=ALU.mult, op1=ALU.add), reads=[t1k, t0k, kL], writes=[t0k])
                    kb.op('act', lambda e: e.activation(out=sq1[:, 0:n], in_=t0[:, 0:n], func=AF.Square), reads=[t0k], writes=[sq1k])
                    pn, pnk = bank()
                    kb.op('pe', lambda e: e.matmul(pn[:, 0:n], lhsT=onesB[:, :], rhs=sq1[:, 0:n], start=True, stop=True), reads=[sq1k, kC], writes=[pnk])
                    kb.op('act', lambda e: e.activation(out=rl[:, 0:n], in_=pn[:, 0:n], func=AF.Sqrt, bias=1e-5, scale=1.0 / 128), reads=[pnk], writes=[rlk])
                    kb.op('dve', lambda e: e.reciprocal(out=rl[:, 0:n], in_=rl[:, 0:n]), reads=[rlk], writes=[rlk])
                    out_fn()

                with contextlib.ExitStack() as sa:
                    dkT = sb(sa, [128, 4, T], BF16, "dkT"); dvtok = sb(sa, [128, NTT, 512], BF16, "dvtok"); kK = [Tk() for _ in range(NTT)]
                    dqT = sb(sa, [128, 4, 512], BF16, "dqT"); dqk = Tk()
                    Er = Ring(sa, 3, [128, 512], BF16, "E")
                    for ti, (c0, w) in enumerate(mtiles):
                        sample = c0 >= T
                        nsub = (w + 127) // 128
                        ktile0 = c0 // 128
                        ktoks = [kS] if sample else [kK[ktile0 + i] for i in range(nsub)]
                        norm_cols((sb_sq8, sb_sq8k, rl, rlk), KC, w, lambda kc: xT[:, kc, c0:c0 + w], xtoks(range(KC), c0, w), lambda kc: nw[:, 4, kc:kc + 1],
                                  lambda kc: hT[:, kc, 0:w], [hTk[0]], 1e-6, float(D))
                        for which in range(2):
                            base = O4 if which == 0 else O5
                            for jp in range(2):
                                slot = wslot()
                                s2 = load_lhsT(ewin[:, base + jp * 256:base + (jp + 1) * 256], KC, 256, slot, 0)
                                for half in range(2):
                                    j = jp * 2 + half
                                    pb, pbk = bank()
                                    for kc in range(KC):
                                        kb.op('pe', lambda e, kc=kc: e.matmul(pb[:, 0:w], lhsT=s2[:, kc, half * 128:(half + 1) * 128], rhs=hT[:, kc, 0:w], start=(kc == 0), stop=(kc == KC - 1)), reads=[slot[1], hTk[0]], writes=[pbk])
                                    if which == 0:
                                        evac('act', (dqS[:, j, 0:w] if sample else dqT[:, j, 0:w]), pb[:, 0:w], [pbk], [kS if sample else dqk])
                                    else:
                                        evac('dve', (dkS[:, j, 0:w] if sample else dkT[:, j, c0:c0 + w]), pb[:, 0:w], [pbk], ktoks)
                        for which in range(3 if sample else 2):
                            base = (O5, O6, O4)[which]
                            slot = wslot()
                            s4 = load_lhsT(ewin[:, base:base + 512], KC, 512, slot, 0)
                            for si in range(nsub):
                                r0 = si * 128; rows = min(128, w - r0)
                                pb, pbk = bank()
                                for kc in range(KC):
                                    kb.op('pe', lambda e, kc=kc: e.matmul(pb[0:rows, :], lhsT=hT[:, kc, r0:r0 + rows], rhs=s4[:, kc, :], start=(kc == 0), stop=(kc == KC - 1)), reads=[slot[1], hTk[0]], writes=[pbk])
                                if which == 2:
                                    evac('act', dqtok[0:rows, :], pb[0:rows, :], [pbk], [kS])
                                    continue
                                sg_, sgk_ = stg.next()
                                evac('act', sg_[0:rows, :], pb[0:rows, :], [pbk], [sgk_])
                                if which == 1:
                                    kb.op('dve', lambda e: e.tensor_copy(out=(dvS[0:rows, :] if sample else dvtok[0:rows, ktile0 + si, :]), in_=pb[0:rows, :]), reads=[pbk], writes=[ktoks[0 if sample else si]])
                                g0 = c0 + r0
                                dst = ((ks, vs)[which][0:rows, :]) if sample else ((kp, vp)[which][g0:g0 + rows, :])
                                kb.dma('sp', dst, sg_[0:rows, :], reads=[sgk_])
                        if sample or DBG < 2: continue
                        nkt = (c0 + w) // 128
                        for h in range(4):
                            for kt in range(nkt):
                                k0 = kt * 128; qs = max(k0 - c0, 0)
                                for m in range(2):
                                    r0 = m * 64
                                    ps, psk = sbank()
                                    kb.op('pe', lambda e: e.matmul(ps[:, qs:w], lhsT=dkT[r0:r0 + 64, h, k0:k0 + 128], rhs=dqT[r0:r0 + 64, h, qs:w], start=True, stop=True), reads=[kK[kt], dqk], writes=[psk])
                                    E, Ek = Er.next()
                                    kb.op('act', lambda e: e.activation(out=E[:, qs:w], in_=ps[:, qs:w], func=AF.Exp, scale=0.125), reads=[psk], writes=[Ek])
                                    if k0 >= c0:
                                        kb.op('pool', lambda e: e.tensor_tensor(out=E[:, qs:qs + 128], in0=E[:, qs:qs + 128], in1=maskB, op=ALU.mult), reads=[Ek, kC], writes=[Ek])
                                    (O_, Ok_), (L_, Lk_) = OL[m]
                                    kb.op('pe', lambda e: e.matmul(O_[:, qs:w], lhsT=dvtok[:, kt, h * 128:(h + 1) * 128], rhs=E[:, qs:w], start=(kt == 0), stop=(kt == nkt - 1)), reads=[kK[kt], Ek], writes=[Ok_])
                                    kb.op('pe', lambda e: e.matmul(L_[:, qs:w], lhsT=onesB[:, :], rhs=E[:, qs:w], start=(kt == 0), stop=(kt == nkt - 1)), reads=[kC, Ek], writes=[Lk_])
                            def outp(h=h, c0=c0, w=w, ti=ti):
                                kb.op('dve', lambda e: e.scalar_tensor_tensor(out=mixT[:, 4 + h, c0:c0 + w], in0=t0[:, 0:w], scalar=subw, in1=rl[:, 0:w], op0=ALU.mult, op1=ALU.mult), reads=[t0k, rlk, kL], writes=[mixk[4 + h][ti]])
                            combine(OL[0][0][0][:, 0:w], OL[0][1][0][:, 0:w], OL[1][0][0][:, 0:w], OL[1][1][0][:, 0:w], [OL[0][0][1], OL[0][1][1], OL[1][0][1], OL[1][1][1]], w, outp)
                    kb.barrier()
                if DBG < 3: return
                with contextlib.ExitStack() as sc_:
                    G = 2
                    kpg = Ring(sc_, 2, [128, G, 512], F32, "kpg"); vpg = Ring(sc_, 2, [128, G, 512], F32, "vpg")
                    prod = Ring(sc_, 2, [128, 512], F32, "prod"); scr = Ring(sc_, 2, [128, G * 8], F32, "scr")
                    Eg = Ring(sc_, 2, [128, G * 8], BF16, "Eg"); vbf = Ring(sc_, 2, [128, G, 512], BF16, "vbf")
                    qb = sb(sc_, [128, 512], F32, "qb"); qbk = Tk()
                    Esf = sb(sc_, [NBS, 32], F32, "Esf"); Esb = sb(sc_, [NBS, 32], BF16, "Esb"); Esk = Tk()
                    res = sb(sc_, [128, 16], F32, "res"); resk = Tk()
                    Oacc, Oacck = PS[2], PSk[2]; Lacc, Lacck = PS[3], PSk[3]
                    if DBG5 & 1:
                        psm = [sbank(), sbank()]
                        for hm in range(8):
                            h = hm // 2; m = hm % 2; r0 = m * 64
                            kb.op('pe', lambda e: e.matmul(psm[m][0][0:NBS, h * NBS:(h + 1) * NBS], lhsT=dkS[r0:r0 + 64, h, 0:NBS], rhs=dqS[r0:r0 + 64, h, 0:NBS], start=True, stop=True), reads=[kS], writes=[psm[m][1]])
                        for m in range(2):
                            kb.op('act', lambda e: e.activation(out=Esf[:, m * 16:(m + 1) * 16], in_=psm[m][0][0:NBS, 0:16], func=AF.Exp, scale=0.125), reads=[psm[m][1]], writes=[Esk])
                        kb.op('dve', lambda e: e.tensor_tensor(out=Esb[:, :], in0=Esf[:, :], in1=cF[0:NBS, C_SELF2:C_SELF2 + 32], op=ALU.mult), reads=[Esk, kC], writes=[Esk])
                        Ebm = Esb[:, :].rearrange("p (m h b) -> p b h m", m=2, h=4)
                        for b in range(NBS):
                            for h in range(4):
                                c_ = b * 8 + 2 * h
                                if not (DBG5 & 32): continue
                                kb.op('pe', lambda e: e.matmul(Oacc[:, c_:c_ + 2], lhsT=dvS[0:NBS, h * 128:(h + 1) * 128], rhs=Ebm[:, b, h, :], start=(b == 0 and h == 0), stop=False), reads=[kS, Esk], writes=[Oacck])
                                if not (DBG5 & 64): continue
                                kb.op('pe', lambda e: e.matmul(Lacc[:, c_:c_ + 2], lhsT=onesB[0:NBS, :], rhs=Ebm[:, b, h, :], start=(b == 0 and h == 0), stop=False), reads=[kC, Esk], writes=[Lacck])
                    for b in range(NBS):
                        if DBG5 & 2:
                            pb, pbk = bank()
                            kb.op('pe', lambda e: e.matmul(pb[:, :], lhsT=cF[0:NBS, C_SEL + b * 128:C_SEL + (b + 1) * 128], rhs=dqtok[:, :], start=True, stop=True), reads=[kC, kS], writes=[pbk])
                            evac('act', qb[:, :], pb[:, :], [pbk], [qbk])
                        if not (DBG5 & 4): continue
                        for g0 in range(0, NPG, G):
                            ng = min(G, NPG - g0)
                            kp_, kpk = kpg.next(); vp_, vpk = vpg.next()
                            for g in range(ng):
                                col = b * NPG + g0 + g
                                kb.dma('pool', kp_[:, g, :], ck, reads=[idxk], writes=[kpk], indirect=idx[:, col:col + 1])
                                kb.dma('pool', vp_[:, g, :], cv, reads=[idxk], writes=[vpk], indirect=idx[:, col:col + 1])
                            sc, sck = scr.next()
                            for g in range(ng):
                                pr, prk = prod.next()
                                kb.op('dve', lambda e, g=g: e.tensor_tensor(out=pr[:, :], in0=kp_[:, g, :], in1=qb[:, :], op=ALU.mult), reads=[kpk, qbk], writes=[prk])
                                kb.op('dve', lambda e, g=g: e.reduce_sum(out=sc[:, g * 8:(g + 1) * 8], in_=pr[:, :].rearrange("p (a d) -> p a d", d=64), axis=AX.X), reads=[prk], writes=[sck])
                            E, Ek = Eg.next()
                            kb.op('act', lambda e: e.activation(out=E[:, 0:ng * 8], in_=sc[:, 0:ng * 8], func=AF.Exp, scale=0.125), reads=[sck], writes=[Ek])
                            vb, vbk = vbf.next()
                            kb.op('act', lambda e: e.activation(out=vb[:, 0:ng, :], in_=vp_[:, 0:ng, :], func=AF.Copy), reads=[vpk], writes=[vbk])
                            if not (DBG5 & 8): continue
                            last = (g0 + G >= NPG) and b == NBS - 1
                            for g in range(ng):
                                fin = last and g == ng - 1
                                for h in range(4):
                                    kb.op('pe', lambda e, g=g, h=h: e.matmul(Oacc[:, b * 8 + 2 * h:b * 8 + 2 * h + 2], lhsT=vb[:, g, h * 128:(h + 1) * 128], rhs=E[:, g * 8 + 2 * h:g * 8 + 2 * h + 2], start=False, stop=(fin and h == 3)), reads=[vbk, Ek], writes=[Oacck])
                                kb.op('pe', lambda e, g=g: e.matmul(Lacc[:, b * 8:b * 8 + 8], lhsT=onesB[:, :], rhs=E[:, g * 8:(g + 1) * 8], start=False, stop=fin), reads=[kC, Ek], writes=[Lacck])
                    Ov = Oacc[:, 0:32].rearrange("p (x m) -> p x m", m=2); Lv = Lacc[:, 0:32].rearrange("p (x m) -> p x m", m=2)
                    sidx = len(mtiles) - 1
                    def outs():
                        kb.op('dve', lambda e: e.scalar_tensor_tensor(out=res[:, :], in0=t0[:, 0:16], scalar=subw, in1=rl[:, 0:16], op0=ALU.mult, op1=ALU.mult), reads=[t0k, rlk, kL], writes=[resk])
                        for h in range(4):
                            kb.op('act', lambda e, h=h: e.activation(out=mixT[:, 4 + h, T:T + NBS], in_=res[:, :].rearrange("p (b h) -> p b h", h=4)[:, :, h], func=AF.Copy), reads=[resk], writes=[mixk[4 + h][sidx]])
                    if DBG5 & 16:
                        combine(Ov[:, :, 0], Lv[:, :, 0], Ov[:, :, 1], Lv[:, :, 1], [Oacck, Lacck], 16, outs)
                    kb.barrier()

        def gdn_stage(mixT, mixk, mtiles):
            with contextlib.ExitStack() as sg:
                cw = sb(sg, [128, 12, 4], F32, "cw"); gnw = sb(sg, [128, 1], F32, "gnw"); alb = sb(sg, [128, 4], F32, "alb"); dtbb = sb(sg, [128, 4], F32, "dtbb"); kG = Tk()
                cs = sb(sg, [128, 12, NBS, 4], F32, "cs"); csk = Tk()
                hist = sb(sg, [128, 12, 3], F32, "hist"); histk = [Tk() for _ in range(12)]
                Sf = sb(sg, [128, 4, 128], F32, "Sf"); Sb_ = sb(sg, [128, 4, 128], BF16, "Sb"); Sfk = [Tk() for _ in range(4)]; Sbk = [Tk() for _ in range(4)]
                Ss = sb(sg, [128, NBS * 4, 128], F32, "Ss"); Ssb = sb(sg, [128, NBS * 4, 128], BF16, "Ssb"); Ssk = [Tk() for _ in range(NBS * 4)]; Ssbk = [Tk() for _ in range(NBS * 4)]
                qkvt = sb(sg, [128, 12, 512], BF16, "qkvt"); qk = [Tk() for _ in range(12)]
                zt = sb(sg, [128, 4, 512], BF16, "zt"); zk = [Tk() for _ in range(4)]
                cbr = Ring(sg, 1, [128, 515], F32, "cb"); yr = Ring(sg, 1, [128, 512], F32, "y"); ysr = Ring(sg, 1, [128, 512], F32, "ys")
                sq1 = Ring(sg, 1, [128, 512], BF16, "sq1"); rsr = Ring(sg, 1, [128, 512], F32, "rs")
                stg = Ring(sg, 1, [NBS, 512], F32, "stg")
                bar = Ring(sg, 2, [64, 16], F32, "ba")
                for i in range(4):
                    kb.dma('sp', cw[:, :, i], convw[i:i + 1, :].rearrange("a (j p) -> p (a j)", p=128), writes=[kG], allow_slow_non_contiguous=True)
                kb.dma('sp', gnw[:, :], gnorm.rearrange("a p -> p a"), writes=[kG], allow_slow_non_contiguous=True)
                kb.dma('sp', alb[:, :], alog.partition_broadcast(128), writes=[kG])
                kb.dma('sp', dtbb[:, :], dtb.partition_broadcast(128), writes=[kG])
                kb.op('act', lambda e: e.activation(out=alb[:, :], in_=alb[:, :], func=AF.Exp), reads=[kG], writes=[kG])
                kb.op('dve', lambda e: e.tensor_scalar(out=alb[:, :], in0=alb[:, :], scalar1=-1.0, scalar2=None, op0=ALU.mult), reads=[kG], writes=[kG])
                for b in range(NBS):
                    for r in range(3):
                        kb.dma('sp', cs[:, :, b, r], sconv[b, r:r + 1, :].rearrange("a (j p) -> p (a j)", p=128), writes=[csk], allow_slow_non_contiguous=True)
                kb.dma('sp', convs[:, 0:2, :], sconv[:, 1:3, :])
                kb.op('pool', lambda e: e.memset(hist[:, :, :], 0.0), writes=histk)
                kb.op('pool', lambda e: e.memset(Sf[:, :, :], 0.0), writes=Sfk)
                kb.op('pool', lambda e: e.memset(Sb_[:, :, :], 0.0), writes=Sbk)
                kb.dma('sp', Ss[:, :, :], srec.rearrange("(g p) d -> p g d", p=128), writes=Ssk)
                kb.op('act', lambda e: e.activation(out=Ssb[:, :, :], in_=Ss[:, :, :], func=AF.Copy), reads=Ssk, writes=Ssbk)
                set_ring(range(8))
                triF = cF[:, C_M:C_M + 128]
                def R2(shape, dt, name, n=2): return Ring(sg, n, shape, dt, name)
                rGb = R2([64, 64], F32, "Gb"); rGt = R2([64, 64], F32, "Gt"); rnG = R2([64, 64], F32, "nG")
                rD = R2([64, 64], F32, "D"); rDT = R2([64, 64], F32, "DT"); rDs = R2([64, 64], F32, "Ds"); rL_ = R2([64, 64], F32, "Lneg")
                rM = R2([64, 64], F32, "M"); rMt = R2([64, 64], F32, "Mt"); rP = R2([64, 64], F32, "P")
                rAt = R2([64, 64], BF16, "At"); rXb = R2([64, 64], BF16, "Xb")
                rcol = R2([128, 8], F32, "col")
                rEr = R2([128, 64], F32, "Erow"); rqd = R2([128, 64], BF16, "qd")
                rRk = R2([64, 128], BF16, "Rk"); rkd = R2([64, 128], BF16, "kdec"); rRv = R2([64, 128], BF16, "Rv")
                ru = R2([64, 128], F32, "u"); rwT = R2([128, 64], BF16, "wT"); rvn = R2([64, 128], BF16, "vn")
                ro = R2([128, 64], F32, "o"); rsq = R2([128, 64], BF16, "sqo"); rrs = R2([128, 64], F32, "rso"); rt_ = R2([128, 64], F32, "to")

                def gdn_chunk(C, qT, kT, vT, zT, qkt, zkt, g_col, beta_col, nbeta_col, bak, Sf_a, Sb_a, Sfk_, Sbk_, out_ap, out_tok):
                    def mm(out, lhsT, rhs, reads, pk, start=True, stop=True):
                        kb.op('pe', lambda e: e.matmul(out, lhsT=lhsT, rhs=rhs, start=start, stop=stop), reads=reads, writes=[pk])
                    Gb, Gbk = rGb.next(); Gt, Gtk = rGt.next(); nG, nGk = rnG.next()
                    kb.op('dve', lambda e: e.tensor_scalar(out=Gb[0:C, 0:C], in0=onesF[0:C, 0:C], scalar1=g_col, scalar2=None, op0=ALU.mult), reads=[bak, kC], writes=[Gbk])
                    kb.op('dve', lambda e: e.tensor_scalar(out=Gt[0:C, 0:C], in0=triF[0:C, 0:C], scalar1=g_col, scalar2=None, op0=ALU.mult), reads=[bak, kC], writes=[Gtk])
                    kb.op('act', lambda e: e.activation(out=nG[0:C, 0:C], in_=Gt[0:C, 0:C], func=AF.Copy, scale=-1.0), reads=[Gtk], writes=[nGk])
                    pd, pdk = bank()
                    mm(pd[0:C, 0:C], triF[0:C, 0:C], Gb[0:C, 0:C], [kC, Gbk], pdk, True, False)
                    mm(pd[0:C, 0:C], nonesF[0:C, 0:C], Gt[0:C, 0:C], [kC, Gtk], pdk, False, False)
                    mm(pd[0:C, 0:C], identF[0:C, 0:C], cF[0:C, C_NEG:C_NEG + C], [kC], pdk, False, True)
                    D_, Dk = rD.next()
                    kb.op('act', lambda e: e.activation(out=D_[0:C, 0:C], in_=pd[0:C, 0:C], func=AF.Exp), reads=[pdk], writes=[Dk])
                    pt_, ptk = bank()
                    mm(pt_[0:C, 0:C], onesF[0:C, 0:C], Gt[0:C, 0:C], [kC, Gtk], ptk, True, False)
                    mm(pt_[0:C, 0:C], nG[0:C, 0:C], onesF[0:C, 0:C], [kC, nGk], ptk, False, False)
                    mm(pt_[0:C, 0:C], identF[0:C, 0:C], cF[0:C, C_NEGT:C_NEGT + C], [kC], ptk, False, True)
                    DT, DTk = rDT.next()
                    kb.op('act', lambda e: e.activation(out=DT[0:C, 0:C], in_=pt_[0:C, 0:C], func=AF.Exp), reads=[ptk], writes=[DTk])
                    pc, pck = bank()
                    mm(pc[0:C, 0:1], triF[0:C, 0:C], g_col, [kC, bak], pck)
                    mm(pc[0:128, 1:2], onesF[0:C, 0:128], g_col, [kC, bak], pck)
                    col, colk = rcol.next()
                    kb.op('dve', lambda e: e.tensor_copy(out=col[0:C, 0:1], in_=pc[0:C, 0:1]), reads=[pck], writes=[colk])
                    kb.op('dve', lambda e: e.tensor_copy(out=col[:, 1:2], in_=pc[:, 1:2]), reads=[pck], writes=[colk])
                    kb.op('act', lambda e: e.activation(out=col[0:C, 2:3], in_=col[0:C, 0:1], func=AF.Exp), reads=[colk], writes=[colk])
                    kb.op('act', lambda e: e.activation(out=col[:, 3:4], in_=col[:, 1:2], func=AF.Exp), reads=[colk], writes=[colk])
                    kb.op('act', lambda e: e.activation(out=col[0:C, 4:5], in_=col[0:C, 0:1], func=AF.Exp, scale=-1.0, bias=col[0:C, 1:2]), reads=[colk], writes=[colk])
                    kb.op('dve', lambda e: e.tensor_tensor(out=col[0:C, 5:6], in0=col[0:C, 2:3], in1=beta_col, op=ALU.mult), reads=[colk, bak], writes=[colk])
                    pe_, pek = bank()
                    mm(pe_[0:128, 0:C], onesF[0:C, 0:128], Gt[0:C, 0:C], [kC, Gtk], pek)
                    Er_, Erk = rEr.next()
                    kb.op('act', lambda e: e.activation(out=Er_[:, 0:C], in_=pe_[:, 0:C], func=AF.Exp), reads=[pek], writes=[Erk])
                    pkk, pkkk = bank()
                    mm(pkk[0:C, 0:C], kT, kT, qkt, pkkk)
                    pqk, pqkk = bank()
                    mm(pqk[0:C, 0:C], kT, qT, qkt, pqkk)
                    Ds, Dsk = rDs.next()
                    kb.op('pool', lambda e: e.tensor_tensor(out=Ds[0:C, 0:C], in0=D_[0:C, 0:C], in1=cF[0:C, C_SL:C_SL + C], op=ALU.mult), reads=[Dk, kC], writes=[Dsk])
                    Ln, Lnk = rL_.next()
                    kb.op('dve', lambda e: e.scalar_tensor_tensor(out=Ln[0:C, 0:C], in0=pkk[0:C, 0:C], scalar=nbeta_col, in1=Ds[0:C, 0:C], op0=ALU.mult, op1=ALU.mult), reads=[pkkk, Dsk, bak], writes=[Lnk])
                    At, Atk = rAt.next()
                    kb.op('dve', lambda e: e.tensor_tensor(out=At[0:C, 0:C], in0=pqk[0:C, 0:C], in1=DT[0:C, 0:C], op=ALU.mult), reads=[pqkk, DTk], writes=[Atk])
                    Xb, Xbk = rXb.next()
                    if C > 1:
                        pn_, pnk = bank()
                        mm(pn_[0:C, 0:C], Ln[0:C, 0:C], identF[0:C, 0:C], [Lnk, kC], pnk)
                        M_, Mk = rM.next(); P_, Pk = rP.next()
                        kb.op('act', lambda e: e.activation(out=M_[0:C, 0:C], in_=pn_[0:C, 0:C], func=AF.Copy), reads=[pnk], writes=[Mk])
                        kb.op('dve', lambda e: e.tensor_tensor(out=P_[0:C, 0:C], in0=pn_[0:C, 0:C], in1=identF[0:C, 0:C], op=ALU.add), reads=[pnk, kC], writes=[Pk])
                        Mt_, Mtk = Ln, Lnk
                        nlev = 1
                        while (1 << nlev) < C: nlev += 1
                        for lev in range(1, nlev):
                            p1, p1k = bank()
                            mm(p1[0:C, 0:C], M_[0:C, 0:C], Mt_[0:C, 0:C], [Mk, Mtk], p1k)
                            Mt2, Mt2k = rMt.next()
                            kb.op('act', lambda e: e.activation(out=Mt2[0:C, 0:C], in_=p1[0:C, 0:C], func=AF.Copy), reads=[p1k], writes=[Mt2k])
                            if lev < nlev - 1:
                                p2, p2k = bank()
                                mm(p2[0:C, 0:C], Mt_[0:C, 0:C], M_[0:C, 0:C], [Mk, Mtk], p2k)
                                M2, M2k = rM.next()
                                kb.op('dve', lambda e: e.tensor_copy(out=M2[0:C, 0:C], in_=p2[0:C, 0:C]), reads=[p2k], writes=[M2k])
                            p3, p3k = bank()
                            mm(p3[0:C, 0:C], Mt2[0:C, 0:C], P_[0:C, 0:C], [Mt2k, Pk], p3k)
                            P2, P2k = rP.next()
                            kb.op('dve', lambda e: e.tensor_tensor(out=P2[0:C, 0:C], in0=p3[0:C, 0:C], in1=P_[0:C, 0:C], op=ALU.add), reads=[p3k, Pk], writes=[P2k])
                            if lev < nlev - 1: M_, Mk = M2, M2k
                            Mt_, Mtk = Mt2, Mt2k; P_, Pk = P2, P2k
                        kb.op('act', lambda e: e.activation(out=Xb[0:C, 0:C], in_=P_[0:C, 0:C], func=AF.Copy), reads=[Pk], writes=[Xbk])
                    else:
                        kb.op('act', lambda e: e.activation(out=Xb[0:1, 0:1], in_=identF[0:1, 0:1], func=AF.Copy), reads=[kC], writes=[Xbk])
                    pk_, pkk_ = bank()
                    mm(pk_[0:C, 0:128], kT, identB, qkt + [kC], pkk_)
                    Rk, Rkk = rRk.next(); kd, kdk = rkd.next()
                    kb.op('act', lambda e: e.activation(out=Rk[0:C, :], in_=pk_[0:C, 0:128], func=AF.Copy, scale=col[0:C, 5:6]), reads=[pkk_, colk], writes=[Rkk])
                    kb.op('dve', lambda e: e.tensor_scalar(out=kd[0:C, :], in0=pk_[0:C, 0:128], scalar1=col[0:C, 4:5], scalar2=None, op0=ALU.mult), reads=[pkk_, colk], writes=[kdk])
                    pv_, pvk = bank()
                    mm(pv_[0:C, 0:128], vT, identB, qkt + [kC], pvk)
                    Rv, Rvk = rRv.next()
                    kb.op('act', lambda e: e.activation(out=Rv[0:C, :], in_=pv_[0:C, 0:128], func=AF.Copy, scale=beta_col), reads=[pvk, bak], writes=[Rvk])
                    pu, puk = bank()
                    mm(pu[0:C, 0:128], Xb[0:C, 0:C], Rv[0:C, :], [Xbk, Rvk], puk)
                    u_, uk = ru.next()
                    kb.op('act', lambda e: e.activation(out=u_[0:C, :], in_=pu[0:C, 0:128], func=AF.Copy), reads=[puk], writes=[uk])
                    pw, pwk = bank()
                    mm(pw[0:128, 0:C], Rk[0:C, :], Xb[0:C, 0:C], [Rkk, Xbk], pwk)
                    wT, wTk = rwT.next()
                    kb.op('dve', lambda e: e.tensor_copy(out=wT[:, 0:C], in_=pw[:, 0:C]), reads=[pwk], writes=[wTk])
                    pws, pwsk = bank()
                    mm(pws[0:C, 0:128], wT[:, 0:C], Sb_a, [wTk, Sbk_], pwsk)
                    vn, vnk = rvn.next()
                    kb.op('dve', lambda e: e.tensor_tensor(out=vn[0:C, :], in0=u_[0:C, :], in1=pws[0:C, 0:128], op=ALU.subtract), reads=[uk, pwsk], writes=[vnk])
                    qd, qdk = rqd.next()
                    kb.op('dve', lambda e: e.tensor_tensor(out=qd[:, 0:C], in0=qT, in1=Er_[:, 0:C], op=ALU.mult), reads=qkt + [Erk], writes=[qdk])
                    po, pok = bank()
                    mm(po[0:128, 0:C], Sb_a, qd[:, 0:C], [Sbk_, qdk], pok, True, False)
                    mm(po[0:128, 0:C], vn[0:C, :], At[0:C, 0:C], [vnk, Atk], pok, False, True)
                    pS, pSk = bank()
                    mm(pS[0:128, 0:128], kd[0:C, :], vn[0:C, :], [kdk, vnk], pSk)
                    kb.op('dve', lambda e: e.scalar_tensor_tensor(out=Sf_a, in0=Sf_a, scalar=col[:, 3:4], in1=pS[:, 0:128], op0=ALU.mult, op1=ALU.add), reads=[Sfk_, colk, pSk], writes=[Sfk_])
                    kb.op('pool', lambda e: e.tensor_copy(out=Sb_a, in_=Sf_a), reads=[Sfk_], writes=[Sbk_])
                    o_, ok_ = ro.next(); sq_, sqk_ = rsq.next(); rs_, rsk_ = rrs.next(); t_, tk_ = rt_.next()
                    kb.op('dve', lambda e: e.tensor_copy(out=o_[:, 0:C], in_=po[:, 0:C]), reads=[pok], writes=[ok_])
                    kb.op('act', lambda e: e.activation(out=sq_[:, 0:C], in_=po[:, 0:C], func=AF.Square), reads=[pok], writes=[sqk_])
                    pn2, pn2k = bank()
                    mm(pn2[:, 0:C], onesB[:, :], sq_[:, 0:C], [kC, sqk_], pn2k)
                    kb.op('act', lambda e: e.activation(out=rs_[:, 0:C], in_=pn2[:, 0:C], func=AF.Sqrt, bias=1e-6, scale=1.0 / 128), reads=[pn2k], writes=[rsk_])
                    kb.op('dve', lambda e: e.reciprocal(out=rs_[:, 0:C], in_=rs_[:, 0:C]), reads=[rsk_], writes=[rsk_])
                    kb.op('dve', lambda e: e.scalar_tensor_tensor(out=t_[:, 0:C], in0=o_[:, 0:C], scalar=gnw[:, 0:1], in1=rs_[:, 0:C], op0=ALU.mult, op1=ALU.mult), reads=[ok_, rsk_, kG], writes=[tk_])
                    kb.op('pool', lambda e: e.tensor_tensor(out=out_ap, in0=t_[:, 0:C], in1=zT, op=ALU.mult), reads=[tk_, zkt], writes=[out_tok])

                for ti, (c0, w) in enumerate(mtiles):
                    sample = c0 >= T
                    norm_cols((sb_sq8, sb_sq8k, rsr.t[0], rsr.k[0]), KC, w, lambda kc: xT[:, kc, c0:c0 + w], xtoks(range(KC), c0, w), lambda kc: nw[:, 4, kc:kc + 1],
                              lambda kc: hT[:, kc, 0:w], [hTk[0]], 1e-6, float(D))
                    for jp in range(8):
                        base = jp * 256 if jp < 6 else O1 + (jp - 6) * 256
                        slot = wslot()
                        s2 = load_lhsT(ewin[:, base:base + 256], KC, 256, slot, 0)
                        for half in range(2):
                            pb, pbk = bank()
                            for kc in range(KC):
                                kb.op('pe', lambda e, kc=kc: e.matmul(pb[:, 0:w], lhsT=s2[:, kc, half * 128:(half + 1) * 128], rhs=hT[:, kc, 0:w], start=(kc == 0), stop=(kc == KC - 1)), reads=[slot[1], hTk[0]], writes=[pbk])
                            if jp >= 6:
                                j = (jp - 6) * 2 + half
                                kb.op('act', lambda e: e.activation(out=zt[:, j, 0:w], in_=pb[:, 0:w], func=AF.Silu), reads=[pbk], writes=[zk[j]])
                                continue
                            j = jp * 2 + half
                            y_, yk_ = yr.next()
                            if not sample:
                                cb, cbk = cbr.next()
                                kb.op('pool', lambda e: e.tensor_copy(out=cb[:, 0:3], in_=hist[:, j, :]), reads=[histk[j]], writes=[cbk])
                                evac('act', cb[:, 3:3 + w], pb[:, 0:w], [pbk], [cbk])
                                kb.op('pool', lambda e: e.tensor_copy(out=hist[:, j, :], in_=cb[:, w:w + 3]), reads=[cbk], writes=[histk[j]])
                                src_i = lambda i: cb[:, i:i + w]
                                rd = [cbk]
                            else:
                                evac('act', cs[:, j, :, 3], pb[:, 0:w], [pbk], [csk])
                                src_i = lambda i: cs[:, j, :, i]
                                rd = [csk]
                            kb.op('dve', lambda e: e.tensor_scalar(out=y_[:, 0:w], in0=src_i(0), scalar1=cw[:, j, 0:1], scalar2=None, op0=ALU.mult), reads=rd + [kG], writes=[yk_])
                            for i in range(1, 4):
                                kb.op('dve', lambda e, i=i: e.scalar_tensor_tensor(out=y_[:, 0:w], in0=src_i(i), scalar=cw[:, j, i:i + 1], in1=y_[:, 0:w], op0=ALU.mult, op1=ALU.add), reads=rd + [kG, yk_], writes=[yk_])
                            if j >= 8:
                                kb.op('act', lambda e: e.activation(out=qkvt[:, j, 0:w], in_=y_[:, 0:w], func=AF.Silu), reads=[yk_], writes=[qk[j]])
                                continue
                            ys_, ysk_ = ysr.next(); sq_, sqk_ = sq1.next(); rs_, rsk_ = rsr.next()
                            kb.op('act', lambda e: e.activation(out=ys_[:, 0:w], in_=y_[:, 0:w], func=AF.Silu), reads=[yk_], writes=[ysk_])
                            kb.op('act', lambda e: e.activation(out=sq_[:, 0:w], in_=ys_[:, 0:w], func=AF.Square), reads=[ysk_], writes=[sqk_])
                            pn, pnk = bank()
                            kb.op('pe', lambda e: e.matmul(pn[:, 0:w], lhsT=onesB[:, :], rhs=sq_[:, 0:w], start=True, stop=True), reads=[kC, sqk_], writes=[pnk])
                            kb.op('act', lambda e: e.activation(out=rs_[:, 0:w], in_=pn[:, 0:w], func=AF.Sqrt, bias=1e-6, scale=1.0), reads=[pnk], writes=[rsk_])
                            kb.op('dve', lambda e: e.reciprocal(out=rs_[:, 0:w], in_=rs_[:, 0:w]), reads=[rsk_], writes=[rsk_])
                            scl = float(128 ** -0.5) if j < 4 else 1.0
                            kb.op('dve', lambda e: e.scalar_tensor_tensor(out=qkvt[:, j, 0:w], in0=ys_[:, 0:w], scalar=scl, in1=rs_[:, 0:w], op0=ALU.mult, op1=ALU.mult), reads=[ysk_, rsk_], writes=[qk[j]])
                    if sample or c0 + w == T:
                        M = NBS if sample else 3
                        l0 = 0 if sample else w - 3
                        for n3 in range(3):
                            slot = wslot()
                            s4 = load_lhsT(ewin[:, n3 * 512:(n3 + 1) * 512], KC, 512, slot, 0)
                            pb, pbk = bank()
                            for kc in range(KC):
                                kb.op('pe', lambda e, kc=kc: e.matmul(pb[0:M, :], lhsT=hT[:, kc, l0:l0 + M], rhs=s4[:, kc, :], start=(kc == 0), stop=(kc == KC - 1)), reads=[slot[1], hTk[0]], writes=[pbk])
                            sg_, sgk_ = stg.next()
                            evac('act', sg_[0:M, :], pb[0:M, :], [pbk], [sgk_])
                            dst = convs[:, 2, n3 * 512:(n3 + 1) * 512] if sample else convp[:, n3 * 512:(n3 + 1) * 512]
                            kb.dma('sp', dst, sg_[0:M, :], reads=[sgk_])
                    slot = wslot()
                    bas = load_lhsT(ewin[:, O2:O2 + 8], KC, 8, slot, 0)
                    chunks = [(b, 1) for b in range(NBS)] if sample else [(cc, 64) for cc in range(0, w, 64)]
                    for ci, (cc, C) in enumerate(chunks):
                        pb, pbk = bank()
                        for kc in range(KC):
                            kb.op('pe', lambda e, kc=kc: e.matmul(pb[0:C, 0:8], lhsT=hT[:, kc, cc:cc + C], rhs=bas[:, kc, :], start=(kc == 0), stop=(kc == KC - 1)), reads=[slot[1], hTk[0]], writes=[pbk])
                        ba, bak = bar.next()
                        kb.op('act', lambda e: e.activation(out=ba[0:C, 0:4], in_=pb[0:C, 0:4], func=AF.Sigmoid), reads=[pbk], writes=[bak])
                        kb.op('dve', lambda e: e.tensor_tensor(out=ba[0:C, 4:8], in0=pb[0:C, 4:8], in1=dtbb[0:C, :], op=ALU.add), reads=[pbk, kG], writes=[bak])
                        kb.op('act', lambda e: e.activation(out=ba[0:C, 4:8], in_=ba[0:C, 4:8], func=AF.Exp), reads=[bak], writes=[bak])
                        kb.op('act', lambda e: e.activation(out=ba[0:C, 4:8], in_=ba[0:C, 4:8], func=AF.Ln, bias=1.0, scale=1.0), reads=[bak], writes=[bak])
                        kb.op('dve', lambda e: e.tensor_tensor(out=ba[0:C, 8:12], in0=ba[0:C, 4:8], in1=alb[0:C, :], op=ALU.mult), reads=[bak, kG], writes=[bak])
                        kb.op('dve', lambda e: e.tensor_scalar(out=ba[0:C, 12:16], in0=ba[0:C, 0:4], scalar1=-1.0, scalar2=None, op0=ALU.mult), reads=[bak], writes=[bak])
                        for h in range(4):
                            if sample:
                                si = cc * 4 + h
                                Sfa, Sba, Sfk_, Sbk_ = Ss[:, si, :], Ssb[:, si, :], Ssk[si], Ssbk[si]
                            else:
                                Sfa, Sba, Sfk_, Sbk_ = Sf[:, h, :], Sb_[:, h, :], Sfk[h], Sbk[h]
                            gdn_chunk(C, qkvt[:, h, cc:cc + C], qkvt[:, 4 + h, cc:cc + C], qkvt[:, 8 + h, cc:cc + C], zt[:, h, cc:cc + C],
                                      [qk[h], qk[4 + h], qk[8 + h]], zk[h], ba[0:C, 8 + h:9 + h], ba[0:C, h:h + 1], ba[0:C, 12 + h:13 + h], bak, Sfa, Sba, Sfk_, Sbk_,
                                      mixT[:, h, c0 + cc:c0 + cc + C], mixk[h][ti])
                    if c0 + w == T:
                        kb.dma('sp', recp.rearrange("(h p) d -> p h d", p=128), Sf[:, :, :], reads=Sfk)
                kb.dma('sp', recs.rearrange("(g p) d -> p g d", p=128), Ss[:, :, :], reads=Ssk)
                kb.barrier()
        for layer in range(2):
            if "ffn" in stages: ffn(layer * 2 + 0)
            if layer == 0 and "even" in stages: even_mixer()
            if layer == 1 and "odd" in stages: odd_mixer()
            if "ffn" in stages: ffn(layer * 2 + 1)
        final()
        kb.finish()
        print(f"[build] instructions={kb.nins} waits={kb.nwait} cnt={kb.cnt}", flush=True)
    return nc


_CACHE = {}

def kernel(x_prompt, x_sample, state_gdn_conv, state_gdn_rec, cache_diff_k, cache_diff_v, cache_mla_ckv, cache_mla_kpe, page_table,
           ffn_norm, mix_norm, ffn_w_gate, ffn_w_up, ffn_w_down, even_w_in, gdn_conv_w, gdn_a_log, gdn_dt_bias, gdn_norm, diff_lambda,
           diff_subln, even_w_out, mla_w_in, mla_q_norm, mla_w_uq, mla_kv_norm, mla_w_ukv, mla_w_out, final_norm, _stages=("ffn", "even", "odd")):
    f = lambda a: np.ascontiguousarray(np.asarray(a), dtype=np.float32)
    x_prompt = f(x_prompt); x_sample = f(x_sample)
    B, T, _ = x_prompt.shape
    NPOOL = cache_diff_k.shape[1]
    page_table = np.ascontiguousarray(np.asarray(page_table), dtype=np.int32)
    NPG = page_table.shape[1]
    n = 8
    assert B == n and x_sample.shape[0] == n * NBS
    key = (T, NPG, NPOOL, tuple(_stages))
    if key not in _CACHE:
        _CACHE[key] = build(T, NPG, NPOOL, stages=_stages)
    nc = _CACHE[key]
    cst, rope = make_consts(T, NPG * 128)
    shared = dict(
        ck=f(cache_diff_k).reshape(NPOOL * 128, 512), cv=f(cache_diff_v).reshape(NPOOL * 128, 512),
        cc=f(cache_mla_ckv).reshape(NPOOL * 128, 256), cp=f(cache_mla_kpe).reshape(NPOOL * 128, 64),
        ffn_norm=f(ffn_norm).reshape(4, D), mix_norm=f(mix_norm).reshape(2, D), final_norm=f(final_norm).reshape(1, D),
        wg=f(ffn_w_gate).reshape(4 * D, DFF), wu=f(ffn_w_up).reshape(4 * D, DFF), wd=f(ffn_w_down).reshape(4 * DFF, D),
        ewin=f(even_w_in).reshape(D, EVEN_IN), convw=f(gdn_conv_w).reshape(4, 1536), alog=f(gdn_a_log).reshape(1, 4),
        dtb=f(gdn_dt_bias).reshape(1, 4), gnorm=f(gdn_norm).reshape(1, 128), dlam=f(diff_lambda).reshape(1, 256),
        subln=f(diff_subln).reshape(1, 128), ewout=f(even_w_out).reshape(D, D), mwin=f(mla_w_in).reshape(D, MLA_IN),
        mqn=f(mla_q_norm).reshape(1, 384), muq=f(mla_w_uq).reshape(384, 1536), mkvn=f(mla_kv_norm).reshape(1, 256),
        mukv=f(mla_w_ukv).reshape(256, 2048), mwout=f(mla_w_out).reshape(D, D), cst=cst, ropeT=rope.reshape(64, -1))
    sconv = f(state_gdn_conv); srec = f(state_gdn_rec)
    in_maps = []
    for c in range(n):
        m = dict(shared)
        m["xp"] = x_prompt[c]; m["xs"] = x_sample[c * NBS:(c + 1) * NBS, 0, :]
        m["sconv"] = sconv[0, c * NBS:(c + 1) * NBS]; m["srec"] = srec[0, c * NBS:(c + 1) * NBS].reshape(NBS * 4 * 128, 128)
        m["pt"] = page_table[c * NBS:(c + 1) * NBS].reshape(1, NBS * NPG)
        in_maps.append(m)
    res = run_bass_kernel_spmd(nc, in_maps, core_ids=list(range(n)))
    R = res.results
    cat = lambda k: np.stack([R[c][k] for c in range(n)], 0)
    y_p = cat("yp")
    y_s = cat("ys").reshape(n * NBS, 1, D)
    conv_p = cat("convp")[None]
    conv_s = cat("convs").reshape(1, n * NBS, 3, 1536)
    rec_p = cat("recp").reshape(1, n, 4, 128, 128)
    rec_s = cat("recs").reshape(1, n * NBS, 4, 128, 128)
    k_p = cat("kp").reshape(1, n, T, 8, 64); k_s = cat("ks").reshape(1, n * NBS, 1, 8, 64)
    v_p = cat("vp").reshape(1, n, T, 4, 128); v_s = cat("vs").reshape(1, n * NBS, 1, 4, 128)
    ckv_p = cat("ckvp").reshape(1, n, T, 256); ckv_s = cat("ckvs").reshape(1, n * NBS, 1, 256)
    kpe_p = cat("kpep").reshape(1, n, T, 64); kpe_s = cat("kpes").reshape(1, n * NBS, 1, 64)
    return (y_p, y_s, conv_p, conv_s, rec_p, rec_s, k_p, k_s, v_p, v_s, ckv_p, ckv_s, kpe_p, kpe_s)
```

```python
import contextlib, os
import numpy as np
DBG = int(os.environ.get('KDBG', '9')); DBG2 = int(os.environ.get('KDBG2', '9')); DBG3 = int(os.environ.get('KDBG3', '7')); DBG4 = int(os.environ.get('KDBG4', '3')); DBG5 = int(os.environ.get('KDBG5', '127'))
import concourse.bass as bass
import concourse.mybir as mybir
from concourse.bass_utils import run_bass_kernel_spmd

F32 = mybir.dt.float32; BF16 = mybir.dt.bfloat16; I32 = mybir.dt.int32
AF = mybir.ActivationFunctionType; ALU = mybir.AluOpType; AX = mybir.AxisListType

D = 1024; KC = 8; DFF = 2816; FC = 22
NBS = 4
EVEN_IN = 3592; O1 = 1536; O2 = 2048; O4 = 2056; O5 = 2568; O6 = 3080
MLA_IN = 704


class Tk:
    __slots__ = ("w", "r", "x")
    def __init__(self, x=False): self.w = None; self.r = {}; self.x = x


class KB:
    def __init__(self, nc, es):
        self.nc = nc
        self.E = {'pe': nc.tensor, 'act': nc.scalar, 'dve': nc.vector, 'pool': nc.gpsimd, 'sp': nc.sync}
        self.sem = {e: es.enter_context(nc.semaphore("s_" + e)) for e in self.E}
        self.cnt = {e: 0 for e in self.E}
        self.known = {e: {} for e in self.E}
        self.dq = {}
        for q, n in (('sp', 20), ('pool', 24), ('act', 4)):
            self.dq[q] = dict(sems=[es.enter_context(nc.semaphore(f"d_{q}{i}")) for i in range(n)], i=0, val=[0] * n)
        self.nwait = 0; self.nins = 0
    def _wait(self, e, ev):
        sem, v = ev
        k = id(sem)
        if self.known[e].get(k, 0) >= v: return
        self.E[e].wait_ge(sem, v); self.known[e][k] = v; self.nwait += 1
    def _deps(self, e, reads, writes):
        evs = {}
        def add(ev):
            k = id(ev[0])
            if k not in evs or evs[k][1] < ev[1]: evs[k] = ev
        for t in reads:
            if t.w: add(t.w)
            if t.x:
                for ev in t.r.values(): add(ev)
        for t in writes:
            if t.w: add(t.w)
            for ev in t.r.values(): add(ev)
        for ev in evs.values():
            if e == 'pe' and ev[0] is self.sem['pe']: continue
            self._wait(e, ev)
    def _record(self, ev, reads, writes):
        k = id(ev[0])
        for t in reads: t.r[k] = ev
        for t in writes: t.w = ev; t.r = {}
    def op(self, e, fn, reads=(), writes=()):
        self._deps(e, reads, writes)
        ins = fn(self.E[e])
        self.cnt[e] += 1; self.nins += 1
        ins.then_inc(self.sem[e], 1)
        self._record((self.sem[e], self.cnt[e]), reads, writes)
    def dma(self, q, out, in_, reads=(), writes=(), indirect=None, eoff=0, **kw):
        self._deps(q, reads, writes)
        d = self.dq[q]; i = d['i']; d['i'] = (i + 1) % len(d['sems'])
        sem = d['sems'][i]
        if d['val'][i] > 0: self._wait(q, (sem, d['val'][i]))
        if indirect is not None:
            ins = self.nc.gpsimd.indirect_dma_start(out=out, out_offset=None, in_=in_,
                                                    in_offset=bass.IndirectOffsetOnAxis(ap=indirect, axis=0), element_offset=eoff)
        else:
            ins = self.E[q].dma_start(out=out, in_=in_, **kw)
        d['val'][i] += 16; self.nins += 1
        ins.then_inc(sem, 16)
        self._record((sem, d['val'][i]), reads, writes)
    def barrier(self):
        evs = []
        for q, d in self.dq.items():
            for sem, v in zip(d['sems'], d['val']):
                if v > 0: evs.append((sem, v))
        for e in self.E:
            if self.cnt[e] > 0: evs.append((self.sem[e], self.cnt[e]))
        for e in self.E:
            for ev in evs:
                if ev[0] is self.sem[e]: continue
                self._wait(e, ev)
    def finish(self):
        self.barrier()


def col_tiles(c0, c1, step=512):
    out = []
    c = c0
    while c < c1:
        w = min(step - (c % step), c1 - c)
        out.append((c, w)); c += w
    return out


C_I, C_M, C_NEG, C_NEGT, C_SL, C_SELF, C_SEL, C_SELF2, C_W = 0, 128, 256, 320, 384, 448, 480, 992, 1024

def make_consts(T, past_len):
    c = np.zeros((128, C_W), np.float32)
    c[:, C_I:C_I + 128] = np.eye(128)
    k = np.arange(128)
    c[:, C_M:C_M + 128] = (k[None, :] >= k[:, None])
    i64 = np.arange(64)
    c[:64, C_NEG:C_NEG + 64] = np.where(i64[None, :] <= i64[:, None], 0.0, -1e30)
    c[:64, C_NEGT:C_NEGT + 64] = np.where(i64[None, :] >= i64[:, None], 0.0, -1e30)
    c[:64, C_SL:C_SL + 64] = (i64[None, :] < i64[:, None])
    for b in range(4):
        c[b, C_SELF + 8 * b:C_SELF + 8 * b + 8] = 1.0
        c[b, C_SEL + 128 * b:C_SEL + 128 * (b + 1)] = 1.0
        c[b, C_SELF2 + b:C_SELF2 + 32:4] = 1.0
    NT = T + 4
    pos = np.concatenate([np.arange(T), np.full(4, past_len)]).astype(np.float32)
    inv = np.exp(-np.log(10000.0) * np.arange(0, 64, 2, dtype=np.float32) / 64).astype(np.float32)
    ang = (pos[None, :] * inv[:, None]).astype(np.float32)
    rope = np.zeros((64, 2, NT), np.float32)
    rope[:32, 0] = np.cos(ang); rope[32:, 0] = np.cos(ang)
    rope[:32, 1] = -np.sin(ang); rope[32:, 1] = np.sin(ang)
    return c, rope


def build(T, NPG, NPOOL, stages=("ffn", "even", "odd")):
    NT = T + NBS
    NTT = T // 128
    nc = bass.Bass("TRN2", target_bir_lowering=False)
    def din(name, shape, dt=F32): return nc.dram_tensor(name, list(shape), dt, kind="ExternalInput").ap()
    def dout(name, shape): return nc.dram_tensor(name, list(shape), F32, kind="ExternalOutput").ap()
    xp = din("xp", [T, D]); xs = din("xs", [NBS, D])
    sconv = din("sconv", [NBS, 3, 1536]); srec = din("srec", [NBS * 4 * 128, 128])
    ck = din("ck", [NPOOL * 128, 512]); cv = din("cv", [NPOOL * 128, 512])
    cc = din("cc", [NPOOL * 128, 256]); cp = din("cp", [NPOOL * 128, 64])
    pt = din("pt", [1, NBS * NPG], I32)
    ffn_norm = din("ffn_norm", [4, D]); mix_norm = din("mix_norm", [2, D]); final_norm = din("final_norm", [1, D])
    wg = din("wg", [4 * D, DFF]); wu = din("wu", [4 * D, DFF]); wd = din("wd", [4 * DFF, D])
    ewin = din("ewin", [D, EVEN_IN]); convw = din("convw", [4, 1536])
    alog = din("alog", [1, 4]); dtb = din("dtb", [1, 4]); gnorm = din("gnorm", [1, 128])
    dlam = din("dlam", [1, 256]); subln = din("subln", [1, 128]); ewout = din("ewout", [D, D])
    mwin = din("mwin", [D, MLA_IN]); mqn = din("mqn", [1, 384]); muq = din("muq", [384, 1536])
    mkvn = din("mkvn", [1, 256]); mukv = din("mukv", [256, 2048]); mwout = din("mwout", [D, D])
    cst = din("cst", [128, C_W]); ropeT = din("ropeT", [64, 2 * NT])
    yp = dout("yp", [T, D]); ys = dout("ys", [NBS, D])
    convp = dout("convp", [3, 1536]); convs = dout("convs", [NBS, 3, 1536])
    recp = dout("recp", [4 * 128, 128]); recs = dout("recs", [NBS * 4 * 128, 128])
    kp = dout("kp", [T, 512]); ks = dout("ks", [NBS, 512]); vp = dout("vp", [T, 512]); vs = dout("vs", [NBS, 512])
    ckvp = dout("ckvp", [T, 256]); ckvs = dout("ckvs", [NBS, 256]); kpep = dout("kpep", [T, 64]); kpes = dout("kpes", [NBS, 64])

    es = contextlib.ExitStack()
    with es:
        kb = KB(nc, es)
        uid = [0]
        def sb(stack, shape, dt, name="t"):
            uid[0] += 1
            return stack.enter_context(nc.sbuf_tensor(f"{name}{uid[0]}", list(shape), dt))
        xT = sb(es, [128, KC, NT], F32, "xT")
        NBLK = (NT + 127) // 128
        xTk = [[Tk() for _ in range(NBLK)] for _ in range(KC)]
        HW = max(NT - T // 2, min(512, NT))
        hT = sb(es, [128, KC, HW], BF16, "hT")
        hTk = [Tk() for _ in range((HW + 511) // 512 + 1)]
        SLOT = 4096
        wsl = [sb(es, [128, SLOT], BF16, "wsl") for _ in range(3)]
        wslk = [Tk() for _ in range(3)]
        wsi = [0]
        cF = sb(es, [128, C_W], F32, "cF"); cB = sb(es, [128, 256], BF16, "cB")
        onesF = sb(es, [128, 128], F32, "onesF"); nonesF = sb(es, [128, 128], F32, "nonesF"); onesB = sb(es, [128, 128], BF16, "onesB")
        nw = sb(es, [128, 7, KC], F32, "nw")
        kC = Tk()
        sb_sq8 = sb(es, [128, KC, 512], BF16, "sq8"); sb_sq8k = Tk()
        idxk = Tk()
        idxp = sb(es, [128, NBS], I32, "idxp"); idxq = sb(es, [128, NBS, 32], I32, "idxq"); idxm = sb(es, [128, NBS, 16], I32, "idxm")
        assert NPG <= 128
        es_setup = contextlib.ExitStack()
        ptb = sb(es_setup, [128, NBS * NPG], I32, "ptb"); iot = sb(es_setup, [128, 1], I32, "iot")
        PS = [es.enter_context(nc.psum_tensor(f"ps{i}", [128, 512], F32)) for i in range(8)]
        PSk = [Tk(True) for _ in range(8)]
        pring = [list(range(8)), 0]
        def set_ring(banks): pring[0] = list(banks); pring[1] = 0
        def bank():
            b = pring[0][pring[1] % len(pring[0])]; pring[1] += 1
            return PS[b], PSk[b]
        def xtoks(kcs, c0, w):
            return [xTk[kc][blk] for kc in kcs for blk in range(c0 // 128, (c0 + w - 1) // 128 + 1)]
        identF = cF[:, C_I:C_I + 128]
        identB = cB[:, 0:128]; maskB = cB[:, 128:256]

        kb.dma('sp', cF[:, :], cst, writes=[kC])
        for i, src in enumerate([ffn_norm[0:1, :], ffn_norm[1:2, :], ffn_norm[2:3, :], ffn_norm[3:4, :], mix_norm[0:1, :], mix_norm[1:2, :], final_norm[0:1, :]]):
            kb.dma('sp', nw[:, i, :], src.rearrange("a (kc p) -> p (a kc)", p=128), writes=[kC], allow_slow_non_contiguous=True)
        kb.op('pool', lambda e: e.memset(onesF[:, :], 1.0), writes=[kC])
        kb.op('pool', lambda e: e.memset(nonesF[:, :], -1.0), writes=[kC])
        kb.op('pool', lambda e: e.memset(onesB[:, :], 1.0), writes=[kC])
        kb.op('act', lambda e: e.activation(out=cB[:, :], in_=cF[:, 0:256], func=AF.Copy), reads=[kC], writes=[kC])
        kb.dma('sp', ptb[:, :], pt.partition_broadcast(128), writes=[idxk])
        kb.dma('sp', idxp[0:NPG, :], pt.rearrange("a (b j) -> j (a b)", j=NPG), writes=[idxk], allow_slow_non_contiguous=True)
        kb.op('pool', lambda e: e.iota(iot[:, :], pattern=[[0, 1]], base=0, channel_multiplier=1), writes=[idxk])
        iot32 = sb(es_setup, [128, 32], I32, "iot32"); idxs = sb(es_setup, [128, 2, NBS], I32, "idxs")
        kb.op('pool', lambda e: e.iota(iot32[:, :], pattern=[[1, 32]], base=0, channel_multiplier=0), writes=[idxk])
        kb.op('dve', lambda e: e.tensor_scalar(out=idxs[0:NPG, 0, :], in0=idxp[0:NPG, :], scalar1=32, scalar2=0, op0=ALU.mult, op1=ALU.add), reads=[idxk], writes=[idxk])
        kb.op('dve', lambda e: e.tensor_scalar(out=idxs[0:NPG, 1, :], in0=idxp[0:NPG, :], scalar1=16, scalar2=0, op0=ALU.mult, op1=ALU.add), reads=[idxk], writes=[idxk])
        for b in range(NBS):
            kb.op('dve', lambda e, b=b: e.tensor_scalar(out=idxq[0:NPG, b, :], in0=iot32[0:NPG, :], scalar1=1, scalar2=idxs[0:NPG, 0, b:b + 1], op0=ALU.mult, op1=ALU.add), reads=[idxk], writes=[idxk])
            kb.op('dve', lambda e, b=b: e.tensor_scalar(out=idxm[0:NPG, b, :], in0=iot32[0:NPG, 0:16], scalar1=1, scalar2=idxs[0:NPG, 1, b:b + 1], op0=ALU.mult, op1=ALU.add), reads=[idxk], writes=[idxk])
        kb.barrier()
        es_setup.close()

        def wslot():
            i = wsi[0] % 3; wsi[0] += 1
            return wsl[i], wslk[i]

        def load_lhsT(w_ap, nk, ncol, slot, off=0):
            sl, slk = slot
            dst = sl[:, off:off + nk * ncol].rearrange("p (k n) -> p k n", k=nk)
            kb.dma('pool', dst, w_ap.rearrange("(k p) n -> p k n", p=128), writes=[slk])
            return dst

        def norm_cols(stack_tmp, nk, w, src, src_toks, wcol, dst, dst_toks, eps, dn, extra_scale=1.0):
            sq, sqk, rstd, rstdk = stack_tmp
            for kc in range(nk):
                kb.op('act', lambda e, kc=kc: e.activation(out=sq[:, kc, 0:w], in_=src(kc), func=AF.Square), reads=src_toks, writes=[sqk])
            pb, pbk = bank()
            for kc in range(nk):
                kb.op('pe', lambda e, kc=kc: e.matmul(pb[:, 0:w], lhsT=onesB[:, :], rhs=sq[:, kc, 0:w], start=(kc == 0), stop=(kc == nk - 1)), reads=[sqk, kC], writes=[pbk])
            kb.op('act', lambda e: e.activation(out=rstd[:, 0:w], in_=pb[:, 0:w], func=AF.Sqrt, bias=eps, scale=1.0 / dn), reads=[pbk], writes=[rstdk])
            kb.op('dve', lambda e: e.reciprocal(out=rstd[:, 0:w], in_=rstd[:, 0:w]), reads=[rstdk], writes=[rstdk])
            for kc in range(nk):
                kb.op('dve', lambda e, kc=kc: e.scalar_tensor_tensor(out=dst(kc), in0=src(kc), scalar=wcol(kc), in1=rstd[:, 0:w], op0=ALU.mult, op1=ALU.mult), reads=src_toks + [rstdk, kC], writes=dst_toks)

        with contextlib.ExitStack() as st:
            xin = [sb(st, [128, D], F32, "xin") for _ in range(2)]
            xink = [Tk(), Tk()]
            set_ring(range(8))
            for tt in range(NTT + 1):
                rows = 128 if tt < NTT else NBS
                xi, xik = xin[tt % 2], xink[tt % 2]
                src = xp[tt * 128:(tt + 1) * 128, :] if tt < NTT else xs[:, :]
                kb.dma('sp', xi[0:rows, :], src, writes=[xik])
                for half in range(2):
                    pb, pbk = bank()
                    for j in range(4):
                        c = half * 4 + j
                        kb.op('pe', lambda e, c=c, j=j: e.matmul(pb[:, j * 128:j * 128 + rows], lhsT=xi[0:rows, c * 128:(c + 1) * 128], rhs=identF[0:rows, 0:rows], start=True, stop=True), reads=[xik, kC], writes=[pbk])
                    eng = 'act' if half == 0 else 'dve'
                    o = xT[:, half * 4:half * 4 + 4, tt * 128:tt * 128 + rows]
                    i_ = pb[:, :].rearrange("p (a b) -> p a b", a=4)[:, :, 0:rows]
                    wt = [xTk[half * 4 + j][tt] for j in range(4)]
                    if eng == 'act':
                        kb.op('act', lambda e: e.activation(out=o, in_=i_, func=AF.Copy), reads=[pbk], writes=wt)
                    else:
                        kb.op('dve', lambda e: e.tensor_copy(out=o, in_=i_), reads=[pbk], writes=wt)
            kb.barrier()

        def ffn(idx):
            with contextlib.ExitStack() as st:
                AW = HW
                act = sb(st, [128, FC, AW], BF16, "act"); actk = [Tk() for _ in range(FC)]
                rstd = sb(st, [128, 512], F32, "rstd")
                tmp = (sb_sq8, sb_sq8k, rstd, Tk())
                sg = [sb(st, [128, 512], F32, "sg") for _ in range(2)]; sgk = [Tk(), Tk()]
                set_ring(range(8))
                nsg = 0
                for (s0, s1) in ((0, T // 2), (T // 2, NT)):
                    tiles = col_tiles(s0, s1)
                    for ti, (c0, w) in enumerate(tiles):
                        l0 = c0 - s0
                        norm_cols(tmp, KC, w, lambda kc: xT[:, kc, c0:c0 + w], xtoks(range(KC), c0, w), lambda kc: nw[:, idx, kc:kc + 1],
                                  lambda kc: hT[:, kc, l0:l0 + w], [hTk[ti]], 1e-6, float(D))
                    for fp in range(FC // 2):
                        slot = wslot()
                        g2 = load_lhsT(wg[idx * D:(idx + 1) * D, fp * 256:(fp + 1) * 256], KC, 256, slot, 0)
                        u2 = load_lhsT(wu[idx * D:(idx + 1) * D, fp * 256:(fp + 1) * 256], KC, 256, slot, 2048)
                        for ti, (c0, w) in enumerate(tiles):
                            l0 = c0 - s0
                            for half in range(2):
                                f = 2 * fp + half
                                pg, pgk = bank()
                                for kc in range(KC):
                                    kb.op('pe', lambda e, kc=kc: e.matmul(pg[:, 0:w], lhsT=g2[:, kc, half * 128:(half + 1) * 128], rhs=hT[:, kc, l0:l0 + w], start=(kc == 0), stop=(kc == KC - 1)), reads=[slot[1], hTk[ti]], writes=[pgk])
                                pu, puk = bank()
                                for kc in range(KC):
                                    kb.op('pe', lambda e, kc=kc: e.matmul(pu[:, 0:w], lhsT=u2[:, kc, half * 128:(half + 1) * 128], rhs=hT[:, kc, l0:l0 + w], start=(kc == 0), stop=(kc == KC - 1)), reads=[slot[1], hTk[ti]], writes=[puk])
                                s_, sk_ = sg[nsg % 2], sgk[nsg % 2]; nsg += 1
                                kb.op('act', lambda e: e.activation(out=s_[:, 0:w], in_=pg[:, 0:w], func=AF.Silu), reads=[pgk], writes=[sk_])
                                kb.op('dve', lambda e: e.tensor_tensor(out=act[:, f, l0:l0 + w], in0=s_[:, 0:w], in1=pu[:, 0:w], op=ALU.mult), reads=[sk_, puk], writes=[actk[f]])
                    for dc in range(KC):
                        slot = wslot()
                        d1 = load_lhsT(wd[idx * DFF:(idx + 1) * DFF, dc * 128:(dc + 1) * 128], FC, 128, slot, 0)
                        for ti, (c0, w) in enumerate(tiles):
                            l0 = c0 - s0
                            pb, pbk = bank()
                            for f in range(FC):
                                kb.op('pe', lambda e, f=f: e.matmul(pb[:, 0:w], lhsT=d1[:, f, :], rhs=act[:, f, l0:l0 + w], start=(f == 0), stop=(f == FC - 1)), reads=[slot[1], actk[f]], writes=[pbk])
                            xt_ = xtoks([dc], c0, w)
                            kb.op('dve', lambda e: e.scalar_tensor_tensor(out=xT[:, dc, c0:c0 + w], in0=pb[:, 0:w], scalar=0.5, in1=xT[:, dc, c0:c0 + w], op0=ALU.mult, op1=ALU.add), reads=[pbk] + xt_, writes=xt_)
                kb.barrier()

        def final():
            with contextlib.ExitStack() as st:
                rstd = sb(st, [128, 512], F32, "rstd")
                tmp = (sb_sq8, sb_sq8k, rstd, Tk())
                yT = sb(st, [128, KC, 512], F32, "yT"); yTk = Tk()
                yo = [sb(st, [128, D], F32, "yo") for _ in range(2)]; yok = [Tk(), Tk()]
                set_ring(range(8))
                n = 0
                for (c0, w) in col_tiles(0, NT):
                    norm_cols(tmp, KC, w, lambda kc: xT[:, kc, c0:c0 + w], xtoks(range(KC), c0, w), lambda kc: nw[:, 6, kc:kc + 1],
                              lambda kc: yT[:, kc, 0:w], [yTk], 1e-6, float(D))
                    for t0 in range(0, w, 128):
                        rows = min(128, w - t0)
                        y_, yk_ = yo[n % 2], yok[n % 2]; n += 1
                        for half in range(2):
                            pb, pbk = bank()
                            for j in range(4):
                                c = half * 4 + j
                                kb.op('pe', lambda e, c=c, j=j: e.matmul(pb[0:rows, j * 128:(j + 1) * 128], lhsT=yT[:, c, t0:t0 + rows], rhs=identF[:, :], start=True, stop=True), reads=[yTk, kC], writes=[pbk])
                            if half == 0:
                                kb.op('act', lambda e: e.activation(out=y_[0:rows, 0:512], in_=pb[0:rows, :], func=AF.Copy), reads=[pbk], writes=[yk_])
                            else:
                                kb.op('dve', lambda e: e.tensor_copy(out=y_[0:rows, 512:1024], in_=pb[0:rows, :]), reads=[pbk], writes=[yk_])
                        g0 = c0 + t0
                        dst = yp[g0:g0 + rows, :] if g0 < T else ys[0:rows, :]
                        kb.dma('sp', dst, y_[0:rows, :], reads=[yk_])
                kb.barrier()


        class Ring:
            def __init__(self, stack, n, shape, dt, name="r"):
                self.t = [sb(stack, shape, dt, name) for _ in range(n)]; self.k = [Tk() for _ in range(n)]; self.i = 0
            def next(self):
                i = self.i % len(self.t); self.i += 1
                return self.t[i], self.k[i]

        def evac(eng, out, in_, reads, writes):
            if eng == 'act':
                kb.op('act', lambda e: e.activation(out=out, in_=in_, func=AF.Copy), reads=reads, writes=writes)
            else:
                kb.op(eng, lambda e: e.tensor_copy(out=out, in_=in_), reads=reads, writes=writes)

        def apply_wout(w_ap, srcT, src_toks, c0, w, nk=KC):
            for dp in range(KC // 2):
                slot = wslot()
                s2 = load_lhsT(w_ap[:, dp * 256:(dp + 1) * 256], nk, 256, slot, 0)
                for half in range(2):
                    dc = dp * 2 + half
                    pb, pbk = bank()
                    for kc in range(nk):
                        kb.op('pe', lambda e, kc=kc: e.matmul(pb[:, 0:w], lhsT=s2[:, kc, half * 128:(half + 1) * 128], rhs=srcT[:, kc, 0:w], start=(kc == 0), stop=(kc == nk - 1)), reads=[slot[1]] + src_toks, writes=[pbk])
                    xt_ = xtoks([dc], c0, w)
                    kb.op('dve', lambda e: e.tensor_tensor(out=xT[:, dc, c0:c0 + w], in0=pb[:, 0:w], in1=xT[:, dc, c0:c0 + w], op=ALU.add), reads=[pbk] + xt_, writes=xt_)

        def odd_mixer():
            SCALE = float(192 ** -0.5)
            with contextlib.ExitStack() as st:
                uqN = sb(st, [128, 3, 8, 128], BF16, "uqN"); uqA = sb(st, [128, 3, 8, 64], BF16, "uqA"); uqB = sb(st, [128, 3, 8, 64], BF16, "uqB")
                wkT = sb(st, [128, 8, 256], BF16, "wkT"); wv = sb(st, [128, 2, 8, 128], BF16, "wv")
                qnw = sb(st, [128, 3], F32, "qnw"); kvnw = sb(st, [128, 2], F32, "kvnw")
                kW = Tk()
                ckvnT = sb(st, [128, 2, NT], BF16, "ckvnT"); rkT = sb(st, [64, NT], BF16, "rkT")
                vtok = sb(st, [128, NTT + 1, 256], BF16, "vtok")
                kK = [Tk() for _ in range(NTT + 1)]
                attnT = sb(st, [128, 8, 512], BF16, "attnT"); attnk = Tk()
                rstd = sb(st, [128, 512], F32, "rstd"); tmp = (sb_sq8, sb_sq8k, rstd, Tk())
                rL = rstd; rLk = tmp[3]
                ctxn = Ring(st, 1, [128, 2, 512], BF16, "ctxn")
                qls = sb(st, [128, 2, NBS, 8], BF16, "qls"); rqs = sb(st, [64, NBS, 8], BF16, "rqs"); qsk = Tk()
                st_u = contextlib.ExitStack()
                ukvK = sb(st_u, [128, 2, 8, 128], F32, "ukvK")
                for h in range(8):
                    kb.dma('pool', uqN[:, :, h, :], muq[:, h * 192:h * 192 + 128].rearrange("(k p) n -> p k n", p=128), writes=[kW])
                    kb.dma('pool', uqA[:, :, h, :], muq[:, h * 192 + 128:h * 192 + 192].rearrange("(k p) n -> p k n", p=128), writes=[kW])
                    kb.dma('pool', uqB[:, :, h, 0:32], muq[:, h * 192 + 160:h * 192 + 192].rearrange("(k p) n -> p k n", p=128), writes=[kW])
                    kb.dma('pool', uqB[:, :, h, 32:64], muq[:, h * 192 + 128:h * 192 + 160].rearrange("(k p) n -> p k n", p=128), writes=[kW])
                    kb.dma('sp', ukvK[:, :, h, :], mukv[:, h * 256:h * 256 + 128].rearrange("(k p) n -> p k n", p=128), writes=[kW])
                    kb.dma('pool', wv[:, :, h, :], mukv[:, h * 256 + 128:h * 256 + 256].rearrange("(k p) n -> p k n", p=128), writes=[kW])
                kb.dma('sp', qnw[:, :], mqn.rearrange("a (k p) -> p (a k)", p=128), writes=[kW], allow_slow_non_contiguous=True)
                kb.dma('sp', kvnw[:, :], mkvn.rearrange("a (k p) -> p (a k)", p=128), writes=[kW], allow_slow_non_contiguous=True)
                set_ring([0, 1, 2])
                for h in range(8):
                    for c2 in range(2):
                        pb, pbk = bank()
                        kb.op('pe', lambda e: e.matmul(pb[:, 0:128], lhsT=ukvK[:, c2, h, :], rhs=identF[:, :], start=True, stop=True), reads=[kW, kC], writes=[pbk])
                        evac('act' if c2 == 0 else 'dve', wkT[:, h, c2 * 128:(c2 + 1) * 128], pb[:, 0:128], [pbk], [kW])
                kb.barrier(); st_u.close()
                sp_ = contextlib.ExitStack()
                cqT = sb(sp_, [128, 3, 512], BF16, "cqT"); cqTk = Tk(); cqn = sb(sp_, [128, 3, 512], BF16, "cqn"); cqnk = Tk()
                ckvF = sb(sp_, [128, 2, 512], F32, "ckvF"); ckvFk = Tk(); ckvnF = sb(sp_, [128, 2, 512], F32, "ckvnF"); ckvnFk = Tk()
                rt = sb(sp_, [64, 2, 512], F32, "rt"); rtk = Tk()
                t1r = Ring(sp_, 1, [64, 512], F32, "t1"); t2r = Ring(sp_, 1, [64, 512], F32, "t2")
                rkF = sb(sp_, [64, 512], F32, "rkF"); rkFk = Tk()
                tokst = Ring(sp_, 1, [128, 320], F32, "tokst")
                qnT = Ring(sp_, 1, [128, 512], BF16, "qnT"); qlT = Ring(sp_, 2, [128, 2, 512], BF16, "qlT"); rqT = Ring(sp_, 1, [64, 512], BF16, "rqT")
                Er = Ring(sp_, 2, [128, 512], BF16, "E")
                if DBG < 1:
                    kb.barrier(); return
                sring = [3, 4]; sri = [0]
                def sbank():
                    b = sring[sri[0] % 2]; sri[0] += 1
                    return PS[b], PSk[b]
                CT = [(PS[5], PSk[5]), (PS[6], PSk[6])]; LT = (PS[7], PSk[7])
                ropeV = ropeT.rearrange("p (a n) -> p a n", a=2)

                def rope_apply(pa, pak, pbb, pbbk, w, out_f32=None, out_f32k=None, out_bf=None, out_bfk=None):
                    t1, t1k = t1r.next(); t2, t2k = t2r.next()
                    kb.op('dve', lambda e: e.tensor_tensor(out=t1[:, 0:w], in0=pa[0:64, 0:w], in1=rt[:, 0, 0:w], op=ALU.mult), reads=[pak, rtk], writes=[t1k])
                    kb.op('dve', lambda e: e.tensor_tensor(out=t2[:, 0:w], in0=pbb[0:64, 0:w], in1=rt[:, 1, 0:w], op=ALU.mult), reads=[pbbk, rtk], writes=[t2k])
                    if out_f32 is not None:
                        kb.op('pool', lambda e: e.tensor_tensor(out=out_f32, in0=t1[:, 0:w], in1=t2[:, 0:w], op=ALU.add), reads=[t1k, t2k], writes=[out_f32k])
                        kb.op('act', lambda e: e.activation(out=out_bf, in_=out_f32, func=AF.Copy), reads=[out_f32k], writes=out_bfk)
                    else:
                        kb.op('pool', lambda e: e.tensor_tensor(out=out_bf, in0=t1[:, 0:w], in1=t2[:, 0:w], op=ALU.add), reads=[t1k, t2k], writes=out_bfk)

                for (c0, w) in col_tiles(0, T) + [(T, NBS)]:
                    sample = c0 >= T
                    ktile0 = c0 // 128
                    nsub = (w + 127) // 128
                    ktoks = [kK[ktile0 + i] for i in range(nsub)]
                    norm_cols(tmp, KC, w, lambda kc: xT[:, kc, c0:c0 + w], xtoks(range(KC), c0, w), lambda kc: nw[:, 5, kc:kc + 1],
                              lambda kc: hT[:, kc, 0:w], [hTk[0]], 1e-6, float(D))
                    kb.dma('sp', rt[:, :, 0:w], ropeV[:, :, c0:c0 + w], writes=[rtk])
                    slot = wslot()
                    wq = load_lhsT(mwin[:, 0:384], KC, 384, slot, 0)
                    slot2 = wslot()
                    wc = load_lhsT(mwin[:, 384:640], KC, 256, slot2, 0)
                    wpa = load_lhsT(mwin[:, 640:704], KC, 64, slot2, 2048)
                    wpb = slot2[0][:, 2560:2560 + KC * 64].rearrange("p (k n) -> p k n", k=KC)
                    kb.dma('pool', wpb[:, :, 0:32], mwin[:, 672:704].rearrange("(k p) n -> p k n", p=128), writes=[slot2[1]])
                    kb.dma('pool', wpb[:, :, 32:64], mwin[:, 640:672].rearrange("(k p) n -> p k n", p=128), writes=[slot2[1]])
                    if DBG2 < 2: continue
                    for j in range(3):
                        pb, pbk = bank()
                        for kc in range(KC):
                            kb.op('pe', lambda e, kc=kc: e.matmul(pb[:, 0:w], lhsT=wq[:, kc, j * 128:(j + 1) * 128], rhs=hT[:, kc, 0:w], start=(kc == 0), stop=(kc == KC - 1)), reads=[slot[1], hTk[0]], writes=[pbk])
                        evac('act', cqT[:, j, 0:w], pb[:, 0:w], [pbk], [cqTk])
                    norm_cols(tmp, 3, w, lambda kc: cqT[:, kc, 0:w], [cqTk], lambda kc: qnw[:, kc:kc + 1], lambda kc: cqn[:, kc, 0:w], [cqnk], 1e-6, 384.0)
                    if DBG2 < 3: continue
                    for j in range(2):
                        pb, pbk = bank()
                        for kc in range(KC):
                            kb.op('pe', lambda e, kc=kc: e.matmul(pb[:, 0:w], lhsT=wc[:, kc, j * 128:(j + 1) * 128], rhs=hT[:, kc, 0:w], start=(kc == 0), stop=(kc == KC - 1)), reads=[slot2[1], hTk[0]], writes=[pbk])
                        evac('act', ckvF[:, j, 0:w], pb[:, 0:w], [pbk], [ckvFk])
                    norm_cols(tmp, 2, w, lambda kc: ckvF[:, kc, 0:w], [ckvFk], lambda kc: kvnw[:, kc:kc + 1], lambda kc: ckvnF[:, kc, 0:w], [ckvnFk], 1e-6, 256.0)
                    kb.op('pool', lambda e: e.tensor_copy(out=ckvnT[:, :, c0:c0 + w], in_=ckvnF[:, :, 0:w]), reads=[ckvnFk], writes=ktoks)
                    if DBG2 < 4: continue
                    pa, pak = bank()
                    for kc in range(KC):
                        kb.op('pe', lambda e, kc=kc: e.matmul(pa[0:64, 0:w], lhsT=wpa[:, kc, :], rhs=hT[:, kc, 0:w], start=(kc == 0), stop=(kc == KC - 1)), reads=[slot2[1], hTk[0]], writes=[pak])
                    pbb, pbbk = bank()
                    for kc in range(KC):
                        kb.op('pe', lambda e, kc=kc: e.matmul(pbb[0:64, 0:w], lhsT=wpb[:, kc, :], rhs=hT[:, kc, 0:w], start=(kc == 0), stop=(kc == KC - 1)), reads=[slot2[1], hTk[0]], writes=[pbbk])
                    rope_apply(pa, pak, pbb, pbbk, w, rkF[:, 0:w], rkFk, rkT[:, c0:c0 + w], ktoks)
                    if DBG2 < 5: continue
                    for si in range(nsub):
                        t0 = si * 128; rows = min(128, w - t0)
                        pb, pbk = bank()
                        for c2 in range(2):
                            kb.op('pe', lambda e, c2=c2: e.matmul(pb[0:rows, c2 * 128:(c2 + 1) * 128], lhsT=ckvnF[:, c2, t0:t0 + rows], rhs=identF[:, :], start=True, stop=True), reads=[ckvnFk, kC], writes=[pbk])
                        if DBG3 & 1:
                            kb.op('pe', lambda e: e.matmul(pb[0:rows, 256:320], lhsT=rkF[0:64, t0:t0 + rows], rhs=identF[0:64, 0:64], start=True, stop=True), reads=[rkFk, kC], writes=[pbk])
                        ts_, tsk = tokst.next()
                        evac('act', ts_[0:rows, :], pb[0:rows, 0:320], [pbk], [tsk])
                        kb.op('dve', lambda e: e.tensor_copy(out=vtok[0:rows, ktile0 + si, :], in_=pb[0:rows, 0:256]), reads=[pbk], writes=[ktoks[si]])
                        g0 = c0 + t0
                        if not (DBG3 & 2): continue
                        if not sample:
                            kb.dma('sp', ckvp[g0:g0 + rows, :], ts_[0:rows, 0:256], reads=[tsk])
                            kb.dma('sp', kpep[g0:g0 + rows, :], ts_[0:rows, 256:320], reads=[tsk])
                        else:
                            kb.dma('sp', ckvs[0:rows, :], ts_[0:rows, 0:256], reads=[tsk])
                            kb.dma('sp', kpes[0:rows, :], ts_[0:rows, 256:320], reads=[tsk])
                    if DBG2 < 6: continue
                    for h in range(8 if DBG >= 2 else 0):
                        pb, pbk = bank()
                        for kc in range(3):
                            kb.op('pe', lambda e, kc=kc: e.matmul(pb[:, 0:w], lhsT=uqN[:, kc, h, :], rhs=cqn[:, kc, 0:w], start=(kc == 0), stop=(kc == 2)), reads=[kW, cqnk], writes=[pbk])
                        qn, qnk = qnT.next()
                        evac('act', qn[:, 0:w], pb[:, 0:w], [pbk], [qnk])
                        ql, qlk = qlT.next()
                        for c2 in range(2):
                            pb, pbk = bank()
                            kb.op('pe', lambda e: e.matmul(pb[:, 0:w], lhsT=wkT[:, h, c2 * 128:(c2 + 1) * 128], rhs=qn[:, 0:w], start=True, stop=True), reads=[kW, qnk], writes=[pbk])
                            evac('dve', ql[:, c2, 0:w], pb[:, 0:w], [pbk], [qlk])
                        pa, pak = bank()
                        for kc in range(3):
                            kb.op('pe', lambda e, kc=kc: e.matmul(pa[0:64, 0:w], lhsT=uqA[:, kc, h, :], rhs=cqn[:, kc, 0:w], start=(kc == 0), stop=(kc == 2)), reads=[kW, cqnk], writes=[pak])
                        pbb, pbbk = bank()
                        for kc in range(3):
                            kb.op('pe', lambda e, kc=kc: e.matmul(pbb[0:64, 0:w], lhsT=uqB[:, kc, h, :], rhs=cqn[:, kc, 0:w], start=(kc == 0), stop=(kc == 2)), reads=[kW, cqnk], writes=[pbbk])
                        rq, rqk = rqT.next()
                        rope_apply(pa, pak, pbb, pbbk, w, None, None, rq[:, 0:w], [rqk])
                        if sample:
                            kb.op('pool', lambda e: e.tensor_copy(out=qls[:, :, :, h], in_=ql[:, :, 0:NBS]), reads=[qlk], writes=[qsk])
                            kb.op('pool', lambda e: e.tensor_copy(out=rqs[:, :, h], in_=rq[:, 0:NBS]), reads=[rqk], writes=[qsk])
                            continue
                        nkt = (c0 + w) // 128
                        if DBG < 3: continue
                        for kt in range(nkt):
                            k0 = kt * 128; qs = max(k0 - c0, 0)
                            ps, psk = sbank()
                            kb.op('pe', lambda e: e.matmul(ps[:, qs:w], lhsT=ckvnT[:, 0, k0:k0 + 128], rhs=ql[:, 0, qs:w], start=True, stop=False), reads=[kK[kt], qlk], writes=[psk])
                            kb.op('pe', lambda e: e.matmul(ps[:, qs:w], lhsT=ckvnT[:, 1, k0:k0 + 128], rhs=ql[:, 1, qs:w], start=False, stop=False), reads=[kK[kt], qlk], writes=[psk])
                            kb.op('pe', lambda e: e.matmul(ps[:, qs:w], lhsT=rkT[0:64, k0:k0 + 128], rhs=rq[0:64, qs:w], start=False, stop=True), reads=[kK[kt], rqk], writes=[psk])
                            E, Ek = Er.next()
                            kb.op('act', lambda e: e.activation(out=E[:, qs:w], in_=ps[:, qs:w], func=AF.Exp, scale=SCALE), reads=[psk], writes=[Ek])
                            if k0 >= c0:
                                kb.op('pool', lambda e: e.tensor_tensor(out=E[:, qs:qs + 128], in0=E[:, qs:qs + 128], in1=maskB, op=ALU.mult), reads=[Ek, kC], writes=[Ek])
                            for c2 in range(2):
                                kb.op('pe', lambda e, c2=c2: e.matmul(CT[c2][0][:, qs:w], lhsT=vtok[:, kt, c2 * 128:(c2 + 1) * 128], rhs=E[:, qs:w], start=(kt == 0), stop=(kt == nkt - 1)), reads=[kK[kt], Ek], writes=[CT[c2][1]])
                            kb.op('pe', lambda e: e.matmul(LT[0][:, qs:w], lhsT=onesB[:, :], rhs=E[:, qs:w], start=(kt == 0), stop=(kt == nkt - 1)), reads=[kC, Ek], writes=[LT[1]])
                        kb.op('dve', lambda e: e.reciprocal(out=rL[:, 0:w], in_=LT[0][:, 0:w]), reads=[LT[1]], writes=[rLk])
                        cx, cxk = ctxn.next()
                        for c2 in range(2):
                            kb.op('dve', lambda e, c2=c2: e.tensor_tensor(out=cx[:, c2, 0:w], in0=CT[c2][0][:, 0:w], in1=rL[:, 0:w], op=ALU.mult), reads=[CT[c2][1], rLk], writes=[cxk])
                        pb, pbk = bank()
                        for c2 in range(2):
                            kb.op('pe', lambda e, c2=c2: e.matmul(pb[:, 0:w], lhsT=wv[:, c2, h, :], rhs=cx[:, c2, 0:w], start=(c2 == 0), stop=(c2 == 1)), reads=[kW, cxk], writes=[pbk])
                        evac('act', attnT[:, h, 0:w], pb[:, 0:w], [pbk], [attnk])
                    if not sample:
                        apply_wout(mwout, attnT, [attnk], c0, w)
                kb.barrier(); sp_.close()
                if DBG < 4:
                    kb.barrier(); return
                G = 4; KT = 8
                ccP = cc.rearrange("(n q) d -> n (q d)", q=KT); cpP = cp.rearrange("(n q) d -> n (q d)", q=KT); assert KT == 8
                cpg = Ring(st, 2, [128, KT, 256], F32, "cpg"); ppg = Ring(st, 2, [128, KT, 64], F32, "ppg")
                kTr = Ring(st, 2, [128, 3, G * 128], BF16, "kTr"); vbf = Ring(st, 2, [128, G, 256], BF16, "vbf")
                Es = Ring(st, 2, [128, G * 8], BF16, "Es")
                Esf = sb(st, [NBS, 32], F32, "Esf"); Esb = sb(st, [NBS, 32], BF16, "Esb"); Esk = Tk()
                ps, psk = sbank()
                for b in range(NBS):
                    kb.op('pe', lambda e: e.matmul(ps[0:NBS, b * 8:b * 8 + 8], lhsT=ckvnT[:, 0, T:T + NBS], rhs=qls[:, 0, b, :], start=True, stop=False), reads=[kK[NTT], qsk], writes=[psk])
                    kb.op('pe', lambda e: e.matmul(ps[0:NBS, b * 8:b * 8 + 8], lhsT=ckvnT[:, 1, T:T + NBS], rhs=qls[:, 1, b, :], start=False, stop=False), reads=[kK[NTT], qsk], writes=[psk])
                    kb.op('pe', lambda e: e.matmul(ps[0:NBS, b * 8:b * 8 + 8], lhsT=rkT[0:64, T:T + NBS], rhs=rqs[0:64, b, :], start=False, stop=True), reads=[kK[NTT], qsk], writes=[psk])
                kb.op('act', lambda e: e.activation(out=Esf[:, :], in_=ps[0:NBS, 0:32], func=AF.Exp, scale=SCALE), reads=[psk], writes=[Esk])
                kb.op('dve', lambda e: e.tensor_tensor(out=Esb[:, :], in0=Esf[:, :], in1=cF[0:NBS, C_SELF:C_SELF + 32], op=ALU.mult), reads=[Esk, kC], writes=[Esk])
                for c2 in range(2):
                    kb.op('pe', lambda e, c2=c2: e.matmul(CT[c2][0][:, 0:32], lhsT=vtok[0:NBS, NTT, c2 * 128:(c2 + 1) * 128], rhs=Esb[:, :], start=True, stop=False), reads=[kK[NTT], Esk], writes=[CT[c2][1]])
                kb.op('pe', lambda e: e.matmul(LT[0][:, 0:32], lhsT=onesB[0:NBS, :], rhs=Esb[:, :], start=True, stop=False), reads=[kC, Esk], writes=[LT[1]])
                for b in range(NBS):
                    for t0 in range(0, 128, KT):
                        cp_, cpk = cpg.next(); pp_, ppk = ppg.next()
                        kb.dma('pool', cp_[0:NPG, :, :].rearrange("p a d -> p (a d)"), ccP, reads=[idxk], writes=[cpk], indirect=idxm[0:NPG, b, t0 // KT:t0 // KT + 1])
                        kb.dma('pool', pp_[0:NPG, :, :].rearrange("p a d -> p (a d)"), cpP, reads=[idxk], writes=[ppk], indirect=idxm[0:NPG, b, t0 // KT:t0 // KT + 1])
                        for g0 in range(0, KT, G):
                            ng = G
                            kT, kTk = kTr.next()
                            for c2 in range(3):
                                pb, pbk = bank()
                                for g in range(ng):
                                    if c2 < 2:
                                        kb.op('pe', lambda e, g=g: e.matmul(pb[:, g * 128:g * 128 + NPG], lhsT=cp_[0:NPG, g0 + g, c2 * 128:(c2 + 1) * 128], rhs=identF[0:NPG, 0:NPG], start=True, stop=True), reads=[cpk, kC], writes=[pbk])
                                    else:
                                        kb.op('pe', lambda e, g=g: e.matmul(pb[0:64, g * 128:g * 128 + NPG], lhsT=pp_[0:NPG, g0 + g, :], rhs=identF[0:NPG, 0:NPG], start=True, stop=True), reads=[ppk, kC], writes=[pbk])
                                np_ = 128 if c2 < 2 else 64
                                evac('act' if c2 != 1 else 'dve', kT[0:np_, c2, 0:ng * 128].rearrange("p (g t) -> p g t", t=128)[:, :, 0:NPG], pb[0:np_, 0:ng * 128].rearrange("p (g t) -> p g t", t=128)[:, :, 0:NPG], [pbk], [kTk])
                            vb, vbk = vbf.next()
                            kb.op('dve', lambda e: e.tensor_copy(out=vb[0:NPG, 0:ng, :], in_=cp_[0:NPG, g0:g0 + ng, :]), reads=[cpk], writes=[vbk])
                            ps, psk = sbank()
                            for g in range(ng):
                                kb.op('pe', lambda e, g=g: e.matmul(ps[0:NPG, g * 8:g * 8 + 8], lhsT=kT[:, 0, g * 128:g * 128 + NPG], rhs=qls[:, 0, b, :], start=True, stop=False), reads=[kTk, qsk], writes=[psk])
                                kb.op('pe', lambda e, g=g: e.matmul(ps[0:NPG, g * 8:g * 8 + 8], lhsT=kT[:, 1, g * 128:g * 128 + NPG], rhs=qls[:, 1, b, :], start=False, stop=False), reads=[kTk, qsk], writes=[psk])
                                kb.op('pe', lambda e, g=g: e.matmul(ps[0:NPG, g * 8:g * 8 + 8], lhsT=kT[0:64, 2, g * 128:g * 128 + NPG], rhs=rqs[0:64, b, :], start=False, stop=True), reads=[kTk, qsk], writes=[psk])
                            E, Ek = Es.next()
                            kb.op('act', lambda e: e.activation(out=E[0:NPG, 0:ng * 8], in_=ps[0:NPG, 0:ng * 8], func=AF.Exp, scale=SCALE), reads=[psk], writes=[Ek])
                            last = (t0 + KT >= 128) and (g0 + G >= KT)
                            for g in range(ng):
                                fin = last and g == ng - 1 and b == NBS - 1
                                for c2 in range(2):
                                    kb.op('pe', lambda e, g=g, c2=c2: e.matmul(CT[c2][0][:, b * 8:b * 8 + 8], lhsT=vb[0:NPG, g, c2 * 128:(c2 + 1) * 128], rhs=E[0:NPG, g * 8:g * 8 + 8], start=False, stop=fin), reads=[vbk, Ek], writes=[CT[c2][1]])
                                kb.op('pe', lambda e, g=g: e.matmul(LT[0][:, b * 8:b * 8 + 8], lhsT=onesB[0:NPG, :], rhs=E[0:NPG, g * 8:g * 8 + 8], start=False, stop=fin), reads=[kC, Ek], writes=[LT[1]])
                kb.op('dve', lambda e: e.reciprocal(out=rL[:, 0:32], in_=LT[0][:, 0:32]), reads=[LT[1]], writes=[rLk])
                cx, cxk = ctxn.next()
                for c2 in range(2):
                    kb.op('dve', lambda e, c2=c2: e.tensor_tensor(out=cx[:, c2, 0:32], in0=CT[c2][0][:, 0:32], in1=rL[:, 0:32], op=ALU.mult), reads=[CT[c2][1], rLk], writes=[cxk])
                for h in range(8):
                    pb, pbk = bank()
                    for c2 in range(2):
                        kb.op('pe', lambda e, c2=c2: e.matmul(pb[:, 0:NBS], lhsT=wv[:, c2, h, :], rhs=cx[:, c2, 0:32].rearrange("p (b h) -> p b h", h=8)[:, :, h], start=(c2 == 0), stop=(c2 == 1)), reads=[kW, cxk], writes=[pbk])
                    evac('act', attnT[:, h, 0:NBS], pb[:, 0:NBS], [pbk], [attnk])
                apply_wout(mwout, attnT, [attnk], T, NBS)
                kb.barrier()

        def even_mixer():
            with contextlib.ExitStack() as st:
                mixT = sb(st, [128, KC, NT], BF16, "mixT")
                mtiles = col_tiles(0, T) + [(T, NBS)]
                mixk = [[Tk() for _ in mtiles] for _ in range(KC)]
                if "diff" in stages or True:
                    diff_stage(mixT, mixk, mtiles)
                if DBG >= 5:
                    gdn_stage(mixT, mixk, mtiles)
                else:
                    for ti, (c0, w) in enumerate(mtiles):
                        for kc in range(4):
                            kb.op('pool', lambda e, kc=kc: e.memset(mixT[:, kc, c0:c0 + w], 0.0), writes=[mixk[kc][ti]])
                set_ring(range(8))
                for ti, (c0, w) in enumerate(mtiles):
                    apply_wout(ewout, mixT[:, :, c0:c0 + w], [mixk[kc][ti] for kc in range(KC)], c0, w)
                kb.barrier()

        def diff_stage(mixT, mixk, mtiles):
            with contextlib.ExitStack() as so:
                lamb = sb(so, [128, 256], F32, "lamb"); ltmp = sb(so, [128, 64], F32, "ltmp"); lcol = sb(so, [128, 8], F32, "lcol"); kL = Tk()
                dkS = sb(so, [128, 4, NBS], BF16, "dkS"); dqS = sb(so, [128, 4, NBS], BF16, "dqS"); dvS = sb(so, [NBS, 512], BF16, "dvS"); dqtok = sb(so, [NBS, 512], F32, "dqtok"); kS = Tk()
                rl = sb(so, [128, 512], F32, "rl"); t0 = sb(so, [128, 512], F32, "t0"); t1 = sb(so, [128, 512], F32, "t1"); sq1 = sb(so, [128, 512], BF16, "sq1")
                rlk, t0k, t1k, sq1k = Tk(), Tk(), Tk(), Tk()
                stg = Ring(so, 1, [128, 512], F32, "stg")
                kb.dma('sp', lamb[:, :], dlam.partition_broadcast(128), writes=[kL])
                kb.dma('sp', lcol[:, 4:5], subln.rearrange("a p -> p a"), writes=[kL], allow_slow_non_contiguous=True)
                for i in range(2):
                    kb.op('dve', lambda e: e.tensor_tensor(out=ltmp[:, :], in0=lamb[:, i * 128:i * 128 + 64], in1=lamb[:, i * 128 + 64:i * 128 + 128], op=ALU.mult), reads=[kL], writes=[kL])
                    kb.op('dve', lambda e: e.reduce_sum(out=lcol[:, i:i + 1], in_=ltmp[:, :], axis=AX.X), reads=[kL], writes=[kL])
                    kb.op('act', lambda e: e.activation(out=lcol[:, i:i + 1], in_=lcol[:, i:i + 1], func=AF.Exp), reads=[kL], writes=[kL])
                LAM_INIT = 0.2
                kb.op('dve', lambda e: e.tensor_tensor(out=lcol[:, 2:3], in0=lcol[:, 1:2], in1=lcol[:, 0:1], op=ALU.subtract), reads=[kL], writes=[kL])
                kb.op('dve', lambda e: e.tensor_scalar(out=lcol[:, 2:3], in0=lcol[:, 2:3], scalar1=-LAM_INIT, scalar2=None, op0=ALU.add), reads=[kL], writes=[kL])
                kb.op('dve', lambda e: e.tensor_scalar(out=lcol[:, 5:6], in0=lcol[:, 4:5], scalar1=1.0 - LAM_INIT, scalar2=None, op0=ALU.mult), reads=[kL], writes=[kL])
                neglam = lcol[:, 2:3]; subw = lcol[:, 5:6]
                sring = [0, 1]; sri = [0]
                def sbank():
                    b = sring[sri[0] % 2]; sri[0] += 1
                    return PS[b], PSk[b]
                OL = [[(PS[2], PSk[2]), (PS[3], PSk[3])], [(PS[4], PSk[4]), (PS[5], PSk[5])]]
                set_ring([6, 7])

                def combine(O0, L0, O1, L1, toks, n, out_fn):
                    kb.op('dve', lambda e: e.reciprocal(out=rl[:, 0:n], in_=L0), reads=toks, writes=[rlk])
                    kb.op('dve', lambda e: e.tensor_tensor(out=t0[:, 0:n], in0=O0, in1=rl[:, 0:n], op=ALU.mult), reads=toks + [rlk], writes=[t0k])
                    kb.op('dve', lambda e: e.reciprocal(out=rl[:, 0:n], in_=L1), reads=toks, writes=[rlk])
                    kb.op('dve', lambda e: e.tensor_tensor(out=t1[:, 0:n], in0=O1, in1=rl[:, 0:n], op=ALU.mult), reads=toks + [rlk], writes=[t1k])
                    kb.op('dve', lambda e: e.scalar_tensor_tensor(out=t0[:, 0:n], in0=t1[:, 0:n], scalar=neglam, in1=t0[:, 0:n], op0=ALU.mult, op1=ALU.add), reads=[t1k, t0k, kL], writes=[t0k])
                    kb.op('act', lambda e: e.activation(out=sq1[:, 0:n], in_=t0[:, 0:n], func=AF.Square), reads=[t0k], writes=[sq1k])
                    pn, pnk = bank()
                    kb.op('pe', lambda e: e.matmul(pn[:, 0:n], lhsT=onesB[:, :], rhs=sq1[:, 0:n], start=True, stop=True), reads=[sq1k, kC], writes=[pnk])
                    kb.op('act', lambda e: e.activation(out=rl[:, 0:n], in_=pn[:, 0:n], func=AF.Sqrt, bias=1e-5, scale=1.0 / 128), reads=[pnk], writes=[rlk])
                    kb.op('dve', lambda e: e.reciprocal(out=rl[:, 0:n], in_=rl[:, 0:n]), reads=[rlk], writes=[rlk])
                    out_fn()

                with contextlib.ExitStack() as sa:
                    dkT = sb(sa, [128, 4, T], BF16, "dkT"); dvtok = sb(sa, [128, NTT, 512], BF16, "dvtok"); kK = [Tk() for _ in range(NTT)]
                    dqT = sb(sa, [128, 4, 512], BF16, "dqT"); dqk = Tk()
                    Er = Ring(sa, 3, [128, 512], BF16, "E")
                    for ti, (c0, w) in enumerate(mtiles):
                        sample = c0 >= T
                        nsub = (w + 127) // 128
                        ktile0 = c0 // 128
                        ktoks = [kS] if sample else [kK[ktile0 + i] for i in range(nsub)]
                        norm_cols((sb_sq8, sb_sq8k, rl, rlk), KC, w, lambda kc: xT[:, kc, c0:c0 + w], xtoks(range(KC), c0, w), lambda kc: nw[:, 4, kc:kc + 1],
                                  lambda kc: hT[:, kc, 0:w], [hTk[0]], 1e-6, float(D))
                        for which in range(2):
                            base = O4 if which == 0 else O5
                            for jp in range(2):
                                slot = wslot()
                                s2 = load_lhsT(ewin[:, base + jp * 256:base + (jp + 1) * 256], KC, 256, slot, 0)
                                for half in range(2):
                                    j = jp * 2 + half
                                    pb, pbk = bank()
                                    for kc in range(KC):
                                        kb.op('pe', lambda e, kc=kc: e.matmul(pb[:, 0:w], lhsT=s2[:, kc, half * 128:(half + 1) * 128], rhs=hT[:, kc, 0:w], start=(kc == 0), stop=(kc == KC - 1)), reads=[slot[1], hTk[0]], writes=[pbk])
                                    if which == 0:
                                        evac('act', (dqS[:, j, 0:w] if sample else dqT[:, j, 0:w]), pb[:, 0:w], [pbk], [kS if sample else dqk])
                                    else:
                                        evac('dve', (dkS[:, j, 0:w] if sample else dkT[:, j, c0:c0 + w]), pb[:, 0:w], [pbk], ktoks)
                        for which in range(3 if sample else 2):
                            base = (O5, O6, O4)[which]
                            slot = wslot()
                            s4 = load_lhsT(ewin[:, base:base + 512], KC, 512, slot, 0)
                            for si in range(nsub):
                                r0 = si * 128; rows = min(128, w - r0)
                                pb, pbk = bank()
                                for kc in range(KC):
                                    kb.op('pe', lambda e, kc=kc: e.matmul(pb[0:rows, :], lhsT=hT[:, kc, r0:r0 + rows], rhs=s4[:, kc, :], start=(kc == 0), stop=(kc == KC - 1)), reads=[slot[1], hTk[0]], writes=[pbk])
                                if which == 2:
                                    evac('act', dqtok[0:rows, :], pb[0:rows, :], [pbk], [kS])
                                    continue
                                sg_, sgk_ = stg.next()
                                evac('act', sg_[0:rows, :], pb[0:rows, :], [pbk], [sgk_])
                                if which == 1:
                                    kb.op('dve', lambda e: e.tensor_copy(out=(dvS[0:rows, :] if sample else dvtok[0:rows, ktile0 + si, :]), in_=pb[0:rows, :]), reads=[pbk], writes=[ktoks[0 if sample else si]])
                                g0 = c0 + r0
                                dst = ((ks, vs)[which][0:rows, :]) if sample else ((kp, vp)[which][g0:g0 + rows, :])
                                kb.dma('sp', dst, sg_[0:rows, :], reads=[sgk_])
                        if sample or DBG < 2: continue
                        nkt = (c0 + w) // 128
                        for h in range(4):
                            for kt in range(nkt):
                                k0 = kt * 128; qs = max(k0 - c0, 0)
                                for m in range(2):
                                    r0 = m * 64
                                    ps, psk = sbank()
                                    kb.op('pe', lambda e: e.matmul(ps[:, qs:w], lhsT=dkT[r0:r0 + 64, h, k0:k0 + 128], rhs=dqT[r0:r0 + 64, h, qs:w], start=True, stop=True), reads=[kK[kt], dqk], writes=[psk])
                                    E, Ek = Er.next()
                                    kb.op('act', lambda e: e.activation(out=E[:, qs:w], in_=ps[:, qs:w], func=AF.Exp, scale=0.125), reads=[psk], writes=[Ek])
                                    if k0 >= c0:
                                        kb.op('pool', lambda e: e.tensor_tensor(out=E[:, qs:qs + 128], in0=E[:, qs:qs + 128], in1=maskB, op=ALU.mult), reads=[Ek, kC], writes=[Ek])
                                    (O_, Ok_), (L_, Lk_) = OL[m]
                                    kb.op('pe', lambda e: e.matmul(O_[:, qs:w], lhsT=dvtok[:, kt, h * 128:(h + 1) * 128], rhs=E[:, qs:w], start=(kt == 0), stop=(kt == nkt - 1)), reads=[kK[kt], Ek], writes=[Ok_])
                                    kb.op('pe', lambda e: e.matmul(L_[:, qs:w], lhsT=onesB[:, :], rhs=E[:, qs:w], start=(kt == 0), stop=(kt == nkt - 1)), reads=[kC, Ek], writes=[Lk_])
                            def outp(h=h, c0=c0, w=w, ti=ti):
                                kb.op('dve', lambda e: e.scalar_tensor_tensor(out=mixT[:, 4 + h, c0:c0 + w], in0=t0[:, 0:w], scalar=subw, in1=rl[:, 0:w], op0=ALU.mult, op1=ALU.mult), reads=[t0k, rlk, kL], writes=[mixk[4 + h][ti]])
                            combine(OL[0][0][0][:, 0:w], OL[0][1][0][:, 0:w], OL[1][0][0][:, 0:w], OL[1][1][0][:, 0:w], [OL[0][0][1], OL[0][1][1], OL[1][0][1], OL[1][1][1]], w, outp)
                    kb.barrier()
                if DBG < 3: return
                with contextlib.ExitStack() as sc_:
                    G = 2; KT = 4
                    ckP = ck.rearrange("(n q) d -> n (q d)", q=KT); cvP = cv.rearrange("(n q) d -> n (q d)", q=KT); assert KT == 4
                    kpg = Ring(sc_, 2, [128, KT, 512], F32, "kpg"); vpg = Ring(sc_, 2, [128, KT, 512], F32, "vpg")
                    prod = Ring(sc_, 1, [128, 512], F32, "prod"); scr = Ring(sc_, 2, [128, G * 8], F32, "scr")
                    Eg = Ring(sc_, 2, [128, G * 8], BF16, "Eg"); vbf = Ring(sc_, 1, [128, G, 512], BF16, "vbf")
                    qb = sb(sc_, [128, 512], F32, "qb"); qbk = Tk()
                    Esf = sb(sc_, [NBS, 32], F32, "Esf"); Esb = sb(sc_, [NBS, 32], BF16, "Esb"); Esk = Tk()
                    res = sb(sc_, [128, 16], F32, "res"); resk = Tk()
                    Oacc, Oacck = PS[2], PSk[2]; Lacc, Lacck = PS[3], PSk[3]
                    if DBG5 & 1:
                        psm = [sbank(), sbank()]
                        for hm in range(8):
                            h = hm // 2; m = hm % 2; r0 = m * 64
                            kb.op('pe', lambda e: e.matmul(psm[m][0][0:NBS, h * NBS:(h + 1) * NBS], lhsT=dkS[r0:r0 + 64, h, 0:NBS], rhs=dqS[r0:r0 + 64, h, 0:NBS], start=True, stop=True), reads=[kS], writes=[psm[m][1]])
                        for m in range(2):
                            kb.op('act', lambda e: e.activation(out=Esf[:, m * 16:(m + 1) * 16], in_=psm[m][0][0:NBS, 0:16], func=AF.Exp, scale=0.125), reads=[psm[m][1]], writes=[Esk])
                        kb.op('dve', lambda e: e.tensor_tensor(out=Esb[:, :], in0=Esf[:, :], in1=cF[0:NBS, C_SELF2:C_SELF2 + 32], op=ALU.mult), reads=[Esk, kC], writes=[Esk])
                        Ebm = Esb[:, :].rearrange("p (m h b) -> p b h m", m=2, h=4)
                        for b in range(NBS):
                            for h in range(4):
                                c_ = b * 8 + 2 * h
                                if not (DBG5 & 32): continue
                                kb.op('pe', lambda e: e.matmul(Oacc[:, c_:c_ + 2], lhsT=dvS[0:NBS, h * 128:(h + 1) * 128], rhs=Ebm[:, b, h, :], start=(b == 0 and h == 0), stop=False), reads=[kS, Esk], writes=[Oacck])
                                if not (DBG5 & 64): continue
                                kb.op('pe', lambda e: e.matmul(Lacc[:, c_:c_ + 2], lhsT=onesB[0:NBS, :], rhs=Ebm[:, b, h, :], start=(b == 0 and h == 0), stop=False), reads=[kC, Esk], writes=[Lacck])
                    for b in range(NBS):
                        if DBG5 & 2:
                            pb, pbk = bank()
                            kb.op('pe', lambda e: e.matmul(pb[:, :], lhsT=cF[0:NBS, C_SEL + b * 128:C_SEL + (b + 1) * 128], rhs=dqtok[:, :], start=True, stop=True), reads=[kC, kS], writes=[pbk])
                            evac('act', qb[:, :], pb[:, :], [pbk], [qbk])
                        if not (DBG5 & 4): continue
                        for tq0 in range(0, 128, KT):
                            kp_, kpk = kpg.next(); vp_, vpk = vpg.next()
                            kb.dma('pool', kp_[0:NPG, :, :].rearrange("p a d -> p (a d)"), ckP, reads=[idxk], writes=[kpk], indirect=idxq[0:NPG, b, tq0 // KT:tq0 // KT + 1])
                            kb.dma('pool', vp_[0:NPG, :, :].rearrange("p a d -> p (a d)"), cvP, reads=[idxk], writes=[vpk], indirect=idxq[0:NPG, b, tq0 // KT:tq0 // KT + 1])
                            for g0 in range(0, KT, G):
                                ng = G
                                sc, sck = scr.next()
                                for g in range(ng):
                                    pr, prk = prod.next()
                                    kb.op('dve', lambda e, g=g: e.tensor_tensor(out=pr[0:NPG, :], in0=kp_[0:NPG, g0 + g, :], in1=qb[0:NPG, :], op=ALU.mult), reads=[kpk, qbk], writes=[prk])
                                    kb.op('dve', lambda e, g=g: e.reduce_sum(out=sc[0:NPG, g * 8:(g + 1) * 8], in_=pr[0:NPG, :].rearrange("p (a d) -> p a d", d=64), axis=AX.X), reads=[prk], writes=[sck])
                                E, Ek = Eg.next()
                                kb.op('act', lambda e: e.activation(out=E[0:NPG, 0:ng * 8], in_=sc[0:NPG, 0:ng * 8], func=AF.Exp, scale=0.125), reads=[sck], writes=[Ek])
                                vb, vbk = vbf.next()
                                kb.op('act', lambda e: e.activation(out=vb[0:NPG, 0:ng, :], in_=vp_[0:NPG, g0:g0 + ng, :], func=AF.Copy), reads=[vpk], writes=[vbk])
                                last = (tq0 + KT >= 128) and (g0 + G >= KT) and b == NBS - 1
                                for g in range(ng):
                                    fin = last and g == ng - 1
                                    for h in range(4):
                                        kb.op('pe', lambda e, g=g, h=h: e.matmul(Oacc[:, b * 8 + 2 * h:b * 8 + 2 * h + 2], lhsT=vb[0:NPG, g, h * 128:(h + 1) * 128], rhs=E[0:NPG, g * 8 + 2 * h:g * 8 + 2 * h + 2], start=False, stop=(fin and h == 3)), reads=[vbk, Ek], writes=[Oacck])
                                    kb.op('pe', lambda e, g=g: e.matmul(Lacc[:, b * 8:b * 8 + 8], lhsT=onesB[0:NPG, :], rhs=E[0:NPG, g * 8:(g + 1) * 8], start=False, stop=fin), reads=[kC, Ek], writes=[Lacck])
                    Ov = Oacc[:, 0:32].rearrange("p (x m) -> p x m", m=2); Lv = Lacc[:, 0:32].rearrange("p (x m) -> p x m", m=2)
                    sidx = len(mtiles) - 1
                    def outs():
                        kb.op('dve', lambda e: e.scalar_tensor_tensor(out=res[:, :], in0=t0[:, 0:16], scalar=subw, in1=rl[:, 0:16], op0=ALU.mult, op1=ALU.mult), reads=[t0k, rlk, kL], writes=[resk])
                        for h in range(4):
                            kb.op('act', lambda e, h=h: e.activation(out=mixT[:, 4 + h, T:T + NBS], in_=res[:, :].rearrange("p (b h) -> p b h", h=4)[:, :, h], func=AF.Copy), reads=[resk], writes=[mixk[4 + h][sidx]])
                    if DBG5 & 16:
                        combine(Ov[:, :, 0], Lv[:, :, 0], Ov[:, :, 1], Lv[:, :, 1], [Oacck, Lacck], 16, outs)
                    kb.barrier()

        def gdn_stage(mixT, mixk, mtiles):
            with contextlib.ExitStack() as sg:
                cw = sb(sg, [128, 12, 4], F32, "cw"); gnw = sb(sg, [128, 1], F32, "gnw"); alb = sb(sg, [128, 4], F32, "alb"); dtbb = sb(sg, [128, 4], F32, "dtbb"); kG = Tk()
                cs = sb(sg, [128, 12, NBS, 4], F32, "cs"); csk = Tk()
                hist = sb(sg, [128, 12, 3], F32, "hist"); histk = [Tk() for _ in range(12)]
                Sf = sb(sg, [128, 4, 128], F32, "Sf"); Sb_ = sb(sg, [128, 4, 128], BF16, "Sb"); Sfk = [Tk() for _ in range(4)]; Sbk = [Tk() for _ in range(4)]
                Ss = sb(sg, [128, NBS * 4, 128], F32, "Ss"); Ssb = sb(sg, [128, NBS * 4, 128], BF16, "Ssb"); Ssk = [Tk() for _ in range(NBS * 4)]; Ssbk = [Tk() for _ in range(NBS * 4)]
                qkvt = sb(sg, [128, 12, 512], BF16, "qkvt"); qk = [Tk() for _ in range(12)]
                zt = sb(sg, [128, 4, 512], BF16, "zt"); zk = [Tk() for _ in range(4)]
                cbr = Ring(sg, 1, [128, 515], F32, "cb"); yr = Ring(sg, 1, [128, 512], F32, "y"); ysr = Ring(sg, 1, [128, 512], F32, "ys")
                sq1 = Ring(sg, 1, [128, 512], BF16, "sq1"); rsr = Ring(sg, 1, [128, 512], F32, "rs")
                stg = Ring(sg, 1, [NBS, 512], F32, "stg")
                bar = Ring(sg, 2, [64, 16], F32, "ba")
                for i in range(4):
                    kb.dma('sp', cw[:, :, i], convw[i:i + 1, :].rearrange("a (j p) -> p (a j)", p=128), writes=[kG], allow_slow_non_contiguous=True)
                kb.dma('sp', gnw[:, :], gnorm.rearrange("a p -> p a"), writes=[kG], allow_slow_non_contiguous=True)
                kb.dma('sp', alb[:, :], alog.partition_broadcast(128), writes=[kG])
                kb.dma('sp', dtbb[:, :], dtb.partition_broadcast(128), writes=[kG])
                kb.op('act', lambda e: e.activation(out=alb[:, :], in_=alb[:, :], func=AF.Exp), reads=[kG], writes=[kG])
                kb.op('dve', lambda e: e.tensor_scalar(out=alb[:, :], in0=alb[:, :], scalar1=-1.0, scalar2=None, op0=ALU.mult), reads=[kG], writes=[kG])
                for b in range(NBS):
                    for r in range(3):
                        kb.dma('sp', cs[:, :, b, r], sconv[b, r:r + 1, :].rearrange("a (j p) -> p (a j)", p=128), writes=[csk], allow_slow_non_contiguous=True)
                kb.dma('sp', convs[:, 0:2, :], sconv[:, 1:3, :])
                kb.op('pool', lambda e: e.memset(hist[:, :, :], 0.0), writes=histk)
                kb.op('pool', lambda e: e.memset(Sf[:, :, :], 0.0), writes=Sfk)
                kb.op('pool', lambda e: e.memset(Sb_[:, :, :], 0.0), writes=Sbk)
                kb.dma('sp', Ss[:, :, :], srec.rearrange("(g p) d -> p g d", p=128), writes=Ssk)
                kb.op('act', lambda e: e.activation(out=Ssb[:, :, :], in_=Ss[:, :, :], func=AF.Copy), reads=Ssk, writes=Ssbk)
                set_ring(range(8))
                triF = cF[:, C_M:C_M + 128]
                def R2(shape, dt, name, n=2): return Ring(sg, n, shape, dt, name)
                rGb = R2([64, 64], F32, "Gb"); rGt = R2([64, 64], F32, "Gt"); rnG = R2([64, 64], F32, "nG")
                rD = R2([64, 64], F32, "D"); rDT = R2([64, 64], F32, "DT"); rDs = R2([64, 64], F32, "Ds"); rL_ = R2([64, 64], F32, "Lneg")
                rM = R2([64, 64], F32, "M"); rMt = R2([64, 64], F32, "Mt"); rP = R2([64, 64], F32, "P")
                rAt = R2([64, 64], BF16, "At"); rXb = R2([64, 64], BF16, "Xb")
                rcol = R2([128, 8], F32, "col")
                rEr = R2([128, 64], F32, "Erow"); rqd = R2([128, 64], BF16, "qd")
                rRk = R2([64, 128], BF16, "Rk"); rkd = R2([64, 128], BF16, "kdec"); rRv = R2([64, 128], BF16, "Rv")
                ru = R2([64, 128], F32, "u"); rwT = R2([128, 64], BF16, "wT"); rvn = R2([64, 128], BF16, "vn")
                ro = R2([128, 64], F32, "o"); rsq = R2([128, 64], BF16, "sqo"); rrs = R2([128, 64], F32, "rso"); rt_ = R2([128, 64], F32, "to")

                def gdn_chunk(C, qT, kT, vT, zT, qkt, zkt, g_col, beta_col, nbeta_col, bak, Sf_a, Sb_a, Sfk_, Sbk_, out_ap, out_tok):
                    def mm(out, lhsT, rhs, reads, pk, start=True, stop=True):
                        kb.op('pe', lambda e: e.matmul(out, lhsT=lhsT, rhs=rhs, start=start, stop=stop), reads=reads, writes=[pk])
                    Gb, Gbk = rGb.next(); Gt, Gtk = rGt.next(); nG, nGk = rnG.next()
                    kb.op('dve', lambda e: e.tensor_scalar(out=Gb[0:C, 0:C], in0=onesF[0:C, 0:C], scalar1=g_col, scalar2=None, op0=ALU.mult), reads=[bak, kC], writes=[Gbk])
                    kb.op('dve', lambda e: e.tensor_scalar(out=Gt[0:C, 0:C], in0=triF[0:C, 0:C], scalar1=g_col, scalar2=None, op0=ALU.mult), reads=[bak, kC], writes=[Gtk])
                    kb.op('act', lambda e: e.activation(out=nG[0:C, 0:C], in_=Gt[0:C, 0:C], func=AF.Copy, scale=-1.0), reads=[Gtk], writes=[nGk])
                    pd, pdk = bank()
                    mm(pd[0:C, 0:C], triF[0:C, 0:C], Gb[0:C, 0:C], [kC, Gbk], pdk, True, False)
                    mm(pd[0:C, 0:C], nonesF[0:C, 0:C], Gt[0:C, 0:C], [kC, Gtk], pdk, False, False)
                    mm(pd[0:C, 0:C], identF[0:C, 0:C], cF[0:C, C_NEG:C_NEG + C], [kC], pdk, False, True)
                    D_, Dk = rD.next()
                    kb.op('act', lambda e: e.activation(out=D_[0:C, 0:C], in_=pd[0:C, 0:C], func=AF.Exp), reads=[pdk], writes=[Dk])
                    pt_, ptk = bank()
                    mm(pt_[0:C, 0:C], onesF[0:C, 0:C], Gt[0:C, 0:C], [kC, Gtk], ptk, True, False)
                    mm(pt_[0:C, 0:C], nG[0:C, 0:C], onesF[0:C, 0:C], [kC, nGk], ptk, False, False)
                    mm(pt_[0:C, 0:C], identF[0:C, 0:C], cF[0:C, C_NEGT:C_NEGT + C], [kC], ptk, False, True)
                    DT, DTk = rDT.next()
                    kb.op('act', lambda e: e.activation(out=DT[0:C, 0:C], in_=pt_[0:C, 0:C], func=AF.Exp), reads=[ptk], writes=[DTk])
                    pc, pck = bank()
                    mm(pc[0:C, 0:1], triF[0:C, 0:C], g_col, [kC, bak], pck)
                    mm(pc[0:128, 1:2], onesF[0:C, 0:128], g_col, [kC, bak], pck)
                    col, colk = rcol.next()
                    kb.op('dve', lambda e: e.tensor_copy(out=col[0:C, 0:1], in_=pc[0:C, 0:1]), reads=[pck], writes=[colk])
                    kb.op('dve', lambda e: e.tensor_copy(out=col[:, 1:2], in_=pc[:, 1:2]), reads=[pck], writes=[colk])
                    kb.op('act', lambda e: e.activation(out=col[0:C, 2:3], in_=col[0:C, 0:1], func=AF.Exp), reads=[colk], writes=[colk])
                    kb.op('act', lambda e: e.activation(out=col[:, 3:4], in_=col[:, 1:2], func=AF.Exp), reads=[colk], writes=[colk])
                    kb.op('act', lambda e: e.activation(out=col[0:C, 4:5], in_=col[0:C, 0:1], func=AF.Exp, scale=-1.0, bias=col[0:C, 1:2]), reads=[colk], writes=[colk])
                    kb.op('dve', lambda e: e.tensor_tensor(out=col[0:C, 5:6], in0=col[0:C, 2:3], in1=beta_col, op=ALU.mult), reads=[colk, bak], writes=[colk])
                    pe_, pek = bank()
                    mm(pe_[0:128, 0:C], onesF[0:C, 0:128], Gt[0:C, 0:C], [kC, Gtk], pek)
                    Er_, Erk = rEr.next()
                    kb.op('act', lambda e: e.activation(out=Er_[:, 0:C], in_=pe_[:, 0:C], func=AF.Exp), reads=[pek], writes=[Erk])
                    pkk, pkkk = bank()
                    mm(pkk[0:C, 0:C], kT, kT, qkt, pkkk)
                    pqk, pqkk = bank()
                    mm(pqk[0:C, 0:C], kT, qT, qkt, pqkk)
                    Ds, Dsk = rDs.next()
                    kb.op('pool', lambda e: e.tensor_tensor(out=Ds[0:C, 0:C], in0=D_[0:C, 0:C], in1=cF[0:C, C_SL:C_SL + C], op=ALU.mult), reads=[Dk, kC], writes=[Dsk])
                    Ln, Lnk = rL_.next()
                    kb.op('dve', lambda e: e.scalar_tensor_tensor(out=Ln[0:C, 0:C], in0=pkk[0:C, 0:C], scalar=nbeta_col, in1=Ds[0:C, 0:C], op0=ALU.mult, op1=ALU.mult), reads=[pkkk, Dsk, bak], writes=[Lnk])
                    At, Atk = rAt.next()
                    kb.op('dve', lambda e: e.tensor_tensor(out=At[0:C, 0:C], in0=pqk[0:C, 0:C], in1=DT[0:C, 0:C], op=ALU.mult), reads=[pqkk, DTk], writes=[Atk])
                    Xb, Xbk = rXb.next()
                    if C > 1:
                        pn_, pnk = bank()
                        mm(pn_[0:C, 0:C], Ln[0:C, 0:C], identF[0:C, 0:C], [Lnk, kC], pnk)
                        M_, Mk = rM.next(); P_, Pk = rP.next()
                        kb.op('act', lambda e: e.activation(out=M_[0:C, 0:C], in_=pn_[0:C, 0:C], func=AF.Copy), reads=[pnk], writes=[Mk])
                        kb.op('dve', lambda e: e.tensor_tensor(out=P_[0:C, 0:C], in0=pn_[0:C, 0:C], in1=identF[0:C, 0:C], op=ALU.add), reads=[pnk, kC], writes=[Pk])
                        Mt_, Mtk = Ln, Lnk
                        nlev = 1
                        while (1 << nlev) < C: nlev += 1
                        for lev in range(1, nlev):
                            p1, p1k = bank()
                            mm(p1[0:C, 0:C], M_[0:C, 0:C], Mt_[0:C, 0:C], [Mk, Mtk], p1k)
                            Mt2, Mt2k = rMt.next()
                            kb.op('act', lambda e: e.activation(out=Mt2[0:C, 0:C], in_=p1[0:C, 0:C], func=AF.Copy), reads=[p1k], writes=[Mt2k])
                            if lev < nlev - 1:
                                p2, p2k = bank()
                                mm(p2[0:C, 0:C], Mt_[0:C, 0:C], M_[0:C, 0:C], [Mk, Mtk], p2k)
                                M2, M2k = rM.next()
                                kb.op('dve', lambda e: e.tensor_copy(out=M2[0:C, 0:C], in_=p2[0:C, 0:C]), reads=[p2k], writes=[M2k])
                            p3, p3k = bank()
                            mm(p3[0:C, 0:C], Mt2[0:C, 0:C], P_[0:C, 0:C], [Mt2k, Pk], p3k)
                            P2, P2k = rP.next()
                            kb.op('dve', lambda e: e.tensor_tensor(out=P2[0:C, 0:C], in0=p3[0:C, 0:C], in1=P_[0:C, 0:C], op=ALU.add), reads=[p3k, Pk], writes=[P2k])
                            if lev < nlev - 1: M_, Mk = M2, M2k
                            Mt_, Mtk = Mt2, Mt2k; P_, Pk = P2, P2k
                        kb.op('act', lambda e: e.activation(out=Xb[0:C, 0:C], in_=P_[0:C, 0:C], func=AF.Copy), reads=[Pk], writes=[Xbk])
                    else:
                        kb.op('act', lambda e: e.activation(out=Xb[0:1, 0:1], in_=identF[0:1, 0:1], func=AF.Copy), reads=[kC], writes=[Xbk])
                    pk_, pkk_ = bank()
                    mm(pk_[0:C, 0:128], kT, identB, qkt + [kC], pkk_)
                    Rk, Rkk = rRk.next(); kd, kdk = rkd.next()
                    kb.op('act', lambda e: e.activation(out=Rk[0:C, :], in_=pk_[0:C, 0:128], func=AF.Copy, scale=col[0:C, 5:6]), reads=[pkk_, colk], writes=[Rkk])
                    kb.op('dve', lambda e: e.tensor_scalar(out=kd[0:C, :], in0=pk_[0:C, 0:128], scalar1=col[0:C, 4:5], scalar2=None, op0=ALU.mult), reads=[pkk_, colk], writes=[kdk])
                    pv_, pvk = bank()
                    mm(pv_[0:C, 0:128], vT, identB, qkt + [kC], pvk)
                    Rv, Rvk = rRv.next()
                    kb.op('act', lambda e: e.activation(out=Rv[0:C, :], in_=pv_[0:C, 0:128], func=AF.Copy, scale=beta_col), reads=[pvk, bak], writes=[Rvk])
                    pu, puk = bank()
                    mm(pu[0:C, 0:128], Xb[0:C, 0:C], Rv[0:C, :], [Xbk, Rvk], puk)
                    u_, uk = ru.next()
                    kb.op('act', lambda e: e.activation(out=u_[0:C, :], in_=pu[0:C, 0:128], func=AF.Copy), reads=[puk], writes=[uk])
                    pw, pwk = bank()
                    mm(pw[0:128, 0:C], Rk[0:C, :], Xb[0:C, 0:C], [Rkk, Xbk], pwk)
                    wT, wTk = rwT.next()
                    kb.op('dve', lambda e: e.tensor_copy(out=wT[:, 0:C], in_=pw[:, 0:C]), reads=[pwk], writes=[wTk])
                    pws, pwsk = bank()
                    mm(pws[0:C, 0:128], wT[:, 0:C], Sb_a, [wTk, Sbk_], pwsk)
                    vn, vnk = rvn.next()
                    kb.op('dve', lambda e: e.tensor_tensor(out=vn[0:C, :], in0=u_[0:C, :], in1=pws[0:C, 0:128], op=ALU.subtract), reads=[uk, pwsk], writes=[vnk])
                    qd, qdk = rqd.next()
                    kb.op('dve', lambda e: e.tensor_tensor(out=qd[:, 0:C], in0=qT, in1=Er_[:, 0:C], op=ALU.mult), reads=qkt + [Erk], writes=[qdk])
                    po, pok = bank()
                    mm(po[0:128, 0:C], Sb_a, qd[:, 0:C], [Sbk_, qdk], pok, True, False)
                    mm(po[0:128, 0:C], vn[0:C, :], At[0:C, 0:C], [vnk, Atk], pok, False, True)
                    pS, pSk = bank()
                    mm(pS[0:128, 0:128], kd[0:C, :], vn[0:C, :], [kdk, vnk], pSk)
                    kb.op('dve', lambda e: e.scalar_tensor_tensor(out=Sf_a, in0=Sf_a, scalar=col[:, 3:4], in1=pS[:, 0:128], op0=ALU.mult, op1=ALU.add), reads=[Sfk_, colk, pSk], writes=[Sfk_])
                    kb.op('pool', lambda e: e.tensor_copy(out=Sb_a, in_=Sf_a), reads=[Sfk_], writes=[Sbk_])
                    o_, ok_ = ro.next(); sq_, sqk_ = rsq.next(); rs_, rsk_ = rrs.next(); t_, tk_ = rt_.next()
                    kb.op('dve', lambda e: e.tensor_copy(out=o_[:, 0:C], in_=po[:, 0:C]), reads=[pok], writes=[ok_])
                    kb.op('act', lambda e: e.activation(out=sq_[:, 0:C], in_=po[:, 0:C], func=AF.Square), reads=[pok], writes=[sqk_])
                    pn2, pn2k = bank()
                    mm(pn2[:, 0:C], onesB[:, :], sq_[:, 0:C], [kC, sqk_], pn2k)
                    kb.op('act', lambda e: e.activation(out=rs_[:, 0:C], in_=pn2[:, 0:C], func=AF.Sqrt, bias=1e-6, scale=1.0 / 128), reads=[pn2k], writes=[rsk_])
                    kb.op('dve', lambda e: e.reciprocal(out=rs_[:, 0:C], in_=rs_[:, 0:C]), reads=[rsk_], writes=[rsk_])
                    kb.op('dve', lambda e: e.scalar_tensor_tensor(out=t_[:, 0:C], in0=o_[:, 0:C], scalar=gnw[:, 0:1], in1=rs_[:, 0:C], op0=ALU.mult, op1=ALU.mult), reads=[ok_, rsk_, kG], writes=[tk_])
                    kb.op('pool', lambda e: e.tensor_tensor(out=out_ap, in0=t_[:, 0:C], in1=zT, op=ALU.mult), reads=[tk_, zkt], writes=[out_tok])

                for ti, (c0, w) in enumerate(mtiles):
                    sample = c0 >= T
                    norm_cols((sb_sq8, sb_sq8k, rsr.t[0], rsr.k[0]), KC, w, lambda kc: xT[:, kc, c0:c0 + w], xtoks(range(KC), c0, w), lambda kc: nw[:, 4, kc:kc + 1],
                              lambda kc: hT[:, kc, 0:w], [hTk[0]], 1e-6, float(D))
                    for jp in range(8):
                        base = jp * 256 if jp < 6 else O1 + (jp - 6) * 256
                        slot = wslot()
                        s2 = load_lhsT(ewin[:, base:base + 256], KC, 256, slot, 0)
                        for half in range(2):
                            pb, pbk = bank()
                            for kc in range(KC):
                                kb.op('pe', lambda e, kc=kc: e.matmul(pb[:, 0:w], lhsT=s2[:, kc, half * 128:(half + 1) * 128], rhs=hT[:, kc, 0:w], start=(kc == 0), stop=(kc == KC - 1)), reads=[slot[1], hTk[0]], writes=[pbk])
                            if jp >= 6:
                                j = (jp - 6) * 2 + half
                                kb.op('act', lambda e: e.activation(out=zt[:, j, 0:w], in_=pb[:, 0:w], func=AF.Silu), reads=[pbk], writes=[zk[j]])
                                continue
                            j = jp * 2 + half
                            y_, yk_ = yr.next()
                            if not sample:
                                cb, cbk = cbr.next()
                                kb.op('pool', lambda e: e.tensor_copy(out=cb[:, 0:3], in_=hist[:, j, :]), reads=[histk[j]], writes=[cbk])
                                evac('act', cb[:, 3:3 + w], pb[:, 0:w], [pbk], [cbk])
                                kb.op('pool', lambda e: e.tensor_copy(out=hist[:, j, :], in_=cb[:, w:w + 3]), reads=[cbk], writes=[histk[j]])
                                src_i = lambda i: cb[:, i:i + w]
                                rd = [cbk]
                            else:
                                evac('act', cs[:, j, :, 3], pb[:, 0:w], [pbk], [csk])
                                src_i = lambda i: cs[:, j, :, i]
                                rd = [csk]
                            kb.op('dve', lambda e: e.tensor_scalar(out=y_[:, 0:w], in0=src_i(0), scalar1=cw[:, j, 0:1], scalar2=None, op0=ALU.mult), reads=rd + [kG], writes=[yk_])
                            for i in range(1, 4):
                                kb.op('dve', lambda e, i=i: e.scalar_tensor_tensor(out=y_[:, 0:w], in0=src_i(i), scalar=cw[:, j, i:i + 1], in1=y_[:, 0:w], op0=ALU.mult, op1=ALU.add), reads=rd + [kG, yk_], writes=[yk_])
                            if j >= 8:
                                kb.op('act', lambda e: e.activation(out=qkvt[:, j, 0:w], in_=y_[:, 0:w], func=AF.Silu), reads=[yk_], writes=[qk[j]])
                                continue
                            ys_, ysk_ = ysr.next(); sq_, sqk_ = sq1.next(); rs_, rsk_ = rsr.next()
                            kb.op('act', lambda e: e.activation(out=ys_[:, 0:w], in_=y_[:, 0:w], func=AF.Silu), reads=[yk_], writes=[ysk_])
                            kb.op('act', lambda e: e.activation(out=sq_[:, 0:w], in_=ys_[:, 0:w], func=AF.Square), reads=[ysk_], writes=[sqk_])
                            pn, pnk = bank()
                            kb.op('pe', lambda e: e.matmul(pn[:, 0:w], lhsT=onesB[:, :], rhs=sq_[:, 0:w], start=True, stop=True), reads=[kC, sqk_], writes=[pnk])
                            kb.op('act', lambda e: e.activation(out=rs_[:, 0:w], in_=pn[:, 0:w], func=AF.Sqrt, bias=1e-6, scale=1.0), reads=[pnk], writes=[rsk_])
                            kb.op('dve', lambda e: e.reciprocal(out=rs_[:, 0:w], in_=rs_[:, 0:w]), reads=[rsk_], writes=[rsk_])
                            scl = float(128 ** -0.5) if j < 4 else 1.0
                            kb.op('dve', lambda e: e.scalar_tensor_tensor(out=qkvt[:, j, 0:w], in0=ys_[:, 0:w], scalar=scl, in1=rs_[:, 0:w], op0=ALU.mult, op1=ALU.mult), reads=[ysk_, rsk_], writes=[qk[j]])
                    if sample or c0 + w == T:
                        M = NBS if sample else 3
                        l0 = 0 if sample else w - 3
                        for n3 in range(3):
                            slot = wslot()
                            s4 = load_lhsT(ewin[:, n3 * 512:(n3 + 1) * 512], KC, 512, slot, 0)
                            pb, pbk = bank()
                            for kc in range(KC):
                                kb.op('pe', lambda e, kc=kc: e.matmul(pb[0:M, :], lhsT=hT[:, kc, l0:l0 + M], rhs=s4[:, kc, :], start=(kc == 0), stop=(kc == KC - 1)), reads=[slot[1], hTk[0]], writes=[pbk])
                            sg_, sgk_ = stg.next()
                            evac('act', sg_[0:M, :], pb[0:M, :], [pbk], [sgk_])
                            dst = convs[:, 2, n3 * 512:(n3 + 1) * 512] if sample else convp[:, n3 * 512:(n3 + 1) * 512]
                            kb.dma('sp', dst, sg_[0:M, :], reads=[sgk_])
                    slot = wslot()
                    bas = load_lhsT(ewin[:, O2:O2 + 8], KC, 8, slot, 0)
                    chunks = [(b, 1) for b in range(NBS)] if sample else [(cc, 64) for cc in range(0, w, 64)]
                    for ci, (cc, C) in enumerate(chunks):
                        pb, pbk = bank()
                        for kc in range(KC):
                            kb.op('pe', lambda e, kc=kc: e.matmul(pb[0:C, 0:8], lhsT=hT[:, kc, cc:cc + C], rhs=bas[:, kc, :], start=(kc == 0), stop=(kc == KC - 1)), reads=[slot[1], hTk[0]], writes=[pbk])
                        ba, bak = bar.next()
                        kb.op('act', lambda e: e.activation(out=ba[0:C, 0:4], in_=pb[0:C, 0:4], func=AF.Sigmoid), reads=[pbk], writes=[bak])
                        kb.op('dve', lambda e: e.tensor_tensor(out=ba[0:C, 4:8], in0=pb[0:C, 4:8], in1=dtbb[0:C, :], op=ALU.add), reads=[pbk, kG], writes=[bak])
                        kb.op('act', lambda e: e.activation(out=ba[0:C, 4:8], in_=ba[0:C, 4:8], func=AF.Exp), reads=[bak], writes=[bak])
                        kb.op('act', lambda e: e.activation(out=ba[0:C, 4:8], in_=ba[0:C, 4:8], func=AF.Ln, bias=1.0, scale=1.0), reads=[bak], writes=[bak])
                        kb.op('dve', lambda e: e.tensor_tensor(out=ba[0:C, 8:12], in0=ba[0:C, 4:8], in1=alb[0:C, :], op=ALU.mult), reads=[bak, kG], writes=[bak])
                        kb.op('dve', lambda e: e.tensor_scalar(out=ba[0:C, 12:16], in0=ba[0:C, 0:4], scalar1=-1.0, scalar2=None, op0=ALU.mult), reads=[bak], writes=[bak])
                        for h in range(4):
                            if sample:
                                si = cc * 4 + h
                                Sfa, Sba, Sfk_, Sbk_ = Ss[:, si, :], Ssb[:, si, :], Ssk[si], Ssbk[si]
                            else:
                                Sfa, Sba, Sfk_, Sbk_ = Sf[:, h, :], Sb_[:, h, :], Sfk[h], Sbk[h]
                            gdn_chunk(C, qkvt[:, h, cc:cc + C], qkvt[:, 4 + h, cc:cc + C], qkvt[:, 8 + h, cc:cc + C], zt[:, h, cc:cc + C],
                                      [qk[h], qk[4 + h], qk[8 + h]], zk[h], ba[0:C, 8 + h:9 + h], ba[0:C, h:h + 1], ba[0:C, 12 + h:13 + h], bak, Sfa, Sba, Sfk_, Sbk_,
                                      mixT[:, h, c0 + cc:c0 + cc + C], mixk[h][ti])
                    if c0 + w == T:
                        kb.dma('sp', recp.rearrange("(h p) d -> p h d", p=128), Sf[:, :, :], reads=Sfk)
                kb.dma('sp', recs.rearrange("(g p) d -> p g d", p=128), Ss[:, :, :], reads=Ssk)
                kb.barrier()
        for layer in range(2):
            if "ffn" in stages: ffn(layer * 2 + 0)
            if layer == 0 and "even" in stages: even_mixer()
            if layer == 1 and "odd" in stages: odd_mixer()
            if "ffn" in stages: ffn(layer * 2 + 1)
        final()
        kb.finish()
        print(f"[build] instructions={kb.nins} waits={kb.nwait} cnt={kb.cnt}", flush=True)
    return nc


_CACHE = {}

def kernel(x_prompt, x_sample, state_gdn_conv, state_gdn_rec, cache_diff_k, cache_diff_v, cache_mla_ckv, cache_mla_kpe, page_table,
           ffn_norm, mix_norm, ffn_w_gate, ffn_w_up, ffn_w_down, even_w_in, gdn_conv_w, gdn_a_log, gdn_dt_bias, gdn_norm, diff_lambda,
           diff_subln, even_w_out, mla_w_in, mla_q_norm, mla_w_uq, mla_kv_norm, mla_w_ukv, mla_w_out, final_norm, _stages=("ffn", "even", "odd")):
    f = lambda a: np.ascontiguousarray(np.asarray(a), dtype=np.float32)
    x_prompt = f(x_prompt); x_sample = f(x_sample)
    B, T, _ = x_prompt.shape
    NPOOL = cache_diff_k.shape[1]
    page_table = np.ascontiguousarray(np.asarray(page_table), dtype=np.int32)
    NPG = page_table.shape[1]
    n = 8
    assert B == n and x_sample.shape[0] == n * NBS
    key = (T, NPG, NPOOL, tuple(_stages))
    if key not in _CACHE:
        _CACHE[key] = build(T, NPG, NPOOL, stages=_stages)
    nc = _CACHE[key]
    cst, rope = make_consts(T, NPG * 128)
    shared = dict(
        ck=f(cache_diff_k).reshape(NPOOL * 128, 512), cv=f(cache_diff_v).reshape(NPOOL * 128, 512),
        cc=f(cache_mla_ckv).reshape(NPOOL * 128, 256), cp=f(cache_mla_kpe).reshape(NPOOL * 128, 64),
        ffn_norm=f(ffn_norm).reshape(4, D), mix_norm=f(mix_norm).reshape(2, D), final_norm=f(final_norm).reshape(1, D),
        wg=f(ffn_w_gate).reshape(4 * D, DFF), wu=f(ffn_w_up).reshape(4 * D, DFF), wd=f(ffn_w_down).reshape(4 * DFF, D),
        ewin=f(even_w_in).reshape(D, EVEN_IN), convw=f(gdn_conv_w).reshape(4, 1536), alog=f(gdn_a_log).reshape(1, 4),
        dtb=f(gdn_dt_bias).reshape(1, 4), gnorm=f(gdn_norm).reshape(1, 128), dlam=f(diff_lambda).reshape(1, 256),
        subln=f(diff_subln).reshape(1, 128), ewout=f(even_w_out).reshape(D, D), mwin=f(mla_w_in).reshape(D, MLA_IN),
        mqn=f(mla_q_norm).reshape(1, 384), muq=f(mla_w_uq).reshape(384, 1536), mkvn=f(mla_kv_norm).reshape(1, 256),
        mukv=f(mla_w_ukv).reshape(256, 2048), mwout=f(mla_w_out).reshape(D, D), cst=cst, ropeT=rope.reshape(64, -1))
    sconv = f(state_gdn_conv); srec = f(state_gdn_rec)
    in_maps = []
    for c in range(n):
        m = dict(shared)
        m["xp"] = x_prompt[c]; m["xs"] = x_sample[c * NBS:(c + 1) * NBS, 0, :]
        m["sconv"] = sconv[0, c * NBS:(c + 1) * NBS]; m["srec"] = srec[0, c * NBS:(c + 1) * NBS].reshape(NBS * 4 * 128, 128)
        m["pt"] = page_table[c * NBS:(c + 1) * NBS].reshape(1, NBS * NPG)
        in_maps.append(m)
    res = run_bass_kernel_spmd(nc, in_maps, core_ids=list(range(n)))
    R = res.results
    cat = lambda k: np.stack([R[c][k] for c in range(n)], 0)
    y_p = cat("yp")
    y_s = cat("ys").reshape(n * NBS, 1, D)
    conv_p = cat("convp")[None]
    conv_s = cat("convs").reshape(1, n * NBS, 3, 1536)
    rec_p = cat("recp").reshape(1, n, 4, 128, 128)
    rec_s = cat("recs").reshape(1, n * NBS, 4, 128, 128)
    k_p = cat("kp").reshape(1, n, T, 8, 64); k_s = cat("ks").reshape(1, n * NBS, 1, 8, 64)
    v_p = cat("vp").reshape(1, n, T, 4, 128); v_s = cat("vs").reshape(1, n * NBS, 1, 4, 128)
    ckv_p = cat("ckvp").reshape(1, n, T, 256); ckv_s = cat("ckvs").reshape(1, n * NBS, 1, 256)
    kpe_p = cat("kpep").reshape(1, n, T, 64); kpe_s = cat("kpes").reshape(1, n * NBS, 1, 64)
    return (y_p, y_s, conv_p, conv_s, rec_p, rec_s, k_p, k_s, v_p, v_s, ckv_p, ckv_s, kpe_p, kpe_s)
```

```python
import contextlib, os
import numpy as np
DBG = int(os.environ.get('KDBG', '9')); DBG2 = int(os.environ.get('KDBG2', '9')); DBG3 = int(os.environ.get('KDBG3', '7')); DBG4 = int(os.environ.get('KDBG4', '3')); DBG5 = int(os.environ.get('KDBG5', '127'))
import concourse.bass as bass
import concourse.mybir as mybir
from concourse.bass_utils import run_bass_kernel_spmd

F32 = mybir.dt.float32; BF16 = mybir.dt.bfloat16; I32 = mybir.dt.int32
AF = mybir.ActivationFunctionType; ALU = mybir.AluOpType; AX = mybir.AxisListType

D = 1024; KC = 8; DFF = 2816; FC = 22
NBS = 4
EVEN_IN = 3592; O1 = 1536; O2 = 2048; O4 = 2056; O5 = 2568; O6 = 3080
MLA_IN = 704


class Tk:
    __slots__ = ("w", "r", "x")
    def __init__(self, x=False): self.w = None; self.r = {}; self.x = x


class KB:
    def __init__(self, nc, es):
        self.nc = nc
        self.E = {'pe': nc.tensor, 'act': nc.scalar, 'dve': nc.vector, 'pool': nc.gpsimd, 'sp': nc.sync}
        self.sem = {e: es.enter_context(nc.semaphore("s_" + e)) for e in self.E}
        self.cnt = {e: 0 for e in self.E}
        self.known = {e: {} for e in self.E}
        self.dq = {}
        for q, n in (('sp', 20), ('pool', 24), ('act', 4)):
            self.dq[q] = dict(sems=[es.enter_context(nc.semaphore(f"d_{q}{i}")) for i in range(n)], i=0, val=[0] * n)
        self.nwait = 0; self.nins = 0
    def _wait(self, e, ev):
        sem, v = ev
        k = id(sem)
        if self.known[e].get(k, 0) >= v: return
        self.E[e].wait_ge(sem, v); self.known[e][k] = v; self.nwait += 1
    def _deps(self, e, reads, writes):
        evs = {}
        def add(ev):
            k = id(ev[0])
            if k not in evs or evs[k][1] < ev[1]: evs[k] = ev
        for t in reads:
            if t.w: add(t.w)
            if t.x:
                for ev in t.r.values(): add(ev)
        for t in writes:
            if t.w: add(t.w)
            for ev in t.r.values(): add(ev)
        for ev in evs.values():
            if e == 'pe' and ev[0] is self.sem['pe']: continue
            self._wait(e, ev)
    def _record(self, ev, reads, writes):
        k = id(ev[0])
        for t in reads: t.r[k] = ev
        for t in writes: t.w = ev; t.r = {}
    def op(self, e, fn, reads=(), writes=()):
        self._deps(e, reads, writes)
        ins = fn(self.E[e])
        self.cnt[e] += 1; self.nins += 1
        ins.then_inc(self.sem[e], 1)
        self._record((self.sem[e], self.cnt[e]), reads, writes)
    def dma(self, q, out, in_, reads=(), writes=(), indirect=None, eoff=0, **kw):
        self._deps(q, reads, writes)
        d = self.dq[q]; i = d['i']; d['i'] = (i + 1) % len(d['sems'])
        sem = d['sems'][i]
        if d['val'][i] > 0: self._wait(q, (sem, d['val'][i]))
        if indirect is not None:
            ins = self.nc.gpsimd.indirect_dma_start(out=out, out_offset=None, in_=in_,
                                                    in_offset=bass.IndirectOffsetOnAxis(ap=indirect, axis=0), element_offset=eoff)
        else:
            ins = self.E[q].dma_start(out=out, in_=in_, **kw)
        d['val'][i] += 16; self.nins += 1
        ins.then_inc(sem, 16)
        self._record((sem, d['val'][i]), reads, writes)
    def barrier(self):
        evs = []
        for q, d in self.dq.items():
            for sem, v in zip(d['sems'], d['val']):
                if v > 0: evs.append((sem, v))
        for e in self.E:
            if self.cnt[e] > 0: evs.append((self.sem[e], self.cnt[e]))
        for e in self.E:
            for ev in evs:
                if ev[0] is self.sem[e]: continue
                self._wait(e, ev)
    def finish(self):
        self.barrier()


def col_tiles(c0, c1, step=512):
    out = []
    c = c0
    while c < c1:
        w = min(step - (c % step), c1 - c)
        out.append((c, w)); c += w
    return out


C_I, C_M, C_NEG, C_NEGT, C_SL, C_SELF, C_SEL, C_SELF2, C_W = 0, 128, 256, 320, 384, 448, 480, 992, 1024

def make_consts(T, past_len):
    c = np.zeros((128, C_W), np.float32)
    c[:, C_I:C_I + 128] = np.eye(128)
    k = np.arange(128)
    c[:, C_M:C_M + 128] = (k[None, :] >= k[:, None])
    i64 = np.arange(64)
    c[:64, C_NEG:C_NEG + 64] = np.where(i64[None, :] <= i64[:, None], 0.0, -1e30)
    c[:64, C_NEGT:C_NEGT + 64] = np.where(i64[None, :] >= i64[:, None], 0.0, -1e30)
    c[:64, C_SL:C_SL + 64] = (i64[None, :] < i64[:, None])
    for b in range(4):
        c[b, C_SELF + 8 * b:C_SELF + 8 * b + 8] = 1.0
        c[b, C_SEL + 128 * b:C_SEL + 128 * (b + 1)] = 1.0
        c[b, C_SELF2 + b:C_SELF2 + 32:4] = 1.0
    NT = T + 4
    pos = np.concatenate([np.arange(T), np.full(4, past_len)]).astype(np.float32)
    inv = np.exp(-np.log(10000.0) * np.arange(0, 64, 2, dtype=np.float32) / 64).astype(np.float32)
    ang = (pos[None, :] * inv[:, None]).astype(np.float32)
    rope = np.zeros((64, 2, NT), np.float32)
    rope[:32, 0] = np.cos(ang); rope[32:, 0] = np.cos(ang)
    rope[:32, 1] = -np.sin(ang); rope[32:, 1] = np.sin(ang)
    return c, rope


def build(T, NPG, NPOOL, stages=("ffn", "even", "odd")):
    NT = T + NBS
    NTT = T // 128
    nc = bass.Bass("TRN2", target_bir_lowering=False)
    def din(name, shape, dt=F32): return nc.dram_tensor(name, list(shape), dt, kind="ExternalInput").ap()
    def dout(name, shape): return nc.dram_tensor(name, list(shape), F32, kind="ExternalOutput").ap()
    xp = din("xp", [T, D]); xs = din("xs", [NBS, D])
    sconv = din("sconv", [NBS, 3, 1536]); srec = din("srec", [NBS * 4 * 128, 128])
    ck = din("ck", [NPOOL * 128, 512]); cv = din("cv", [NPOOL * 128, 512])
    cc = din("cc", [NPOOL * 128, 256]); cp = din("cp", [NPOOL * 128, 64])
    pt = din("pt", [1, NBS * NPG], I32)
    ffn_norm = din("ffn_norm", [4, D]); mix_norm = din("mix_norm", [2, D]); final_norm = din("final_norm", [1, D])
    wg = din("wg", [4 * D, DFF]); wu = din("wu", [4 * D, DFF]); wd = din("wd", [4 * DFF, D])
    ewin = din("ewin", [D, EVEN_IN]); convw = din("convw", [4, 1536])
    alog = din("alog", [1, 4]); dtb = din("dtb", [1, 4]); gnorm = din("gnorm", [1, 128])
    dlam = din("dlam", [1, 256]); subln = din("subln", [1, 128]); ewout = din("ewout", [D, D])
    mwin = din("mwin", [D, MLA_IN]); mqn = din("mqn", [1, 384]); muq = din("muq", [384, 1536])
    mkvn = din("mkvn", [1, 256]); mukv = din("mukv", [256, 2048]); mwout = din("mwout", [D, D])
    cst = din("cst", [128, C_W]); ropeT = din("ropeT", [64, 2 * NT])
    yp = dout("yp", [T, D]); ys = dout("ys", [NBS, D])
    convp = dout("convp", [3, 1536]); convs = dout("convs", [NBS, 3, 1536])
    recp = dout("recp", [4 * 128, 128]); recs = dout("recs", [NBS * 4 * 128, 128])
    kp = dout("kp", [T, 512]); ks = dout("ks", [NBS, 512]); vp = dout("vp", [T, 512]); vs = dout("vs", [NBS, 512])
    ckvp = dout("ckvp", [T, 256]); ckvs = dout("ckvs", [NBS, 256]); kpep = dout("kpep", [T, 64]); kpes = dout("kpes", [NBS, 64])

    es = contextlib.ExitStack()
    with es:
        kb = KB(nc, es)
        uid = [0]
        def sb(stack, shape, dt, name="t"):
            uid[0] += 1
            return stack.enter_context(nc.sbuf_tensor(f"{name}{uid[0]}", list(shape), dt))
        xT = sb(es, [128, KC, NT], F32, "xT")
        NBLK = (NT + 127) // 128
        xTk = [[Tk() for _ in range(NBLK)] for _ in range(KC)]
        HW = max(NT - T // 2, min(512, NT))
        hT = sb(es, [128, KC, HW], BF16, "hT")
        hTk = [Tk() for _ in range((HW + 511) // 512 + 1)]
        SLOT = 4096
        wsl = [sb(es, [128, SLOT], BF16, "wsl") for _ in range(3)]
        wslk = [Tk() for _ in range(3)]
        wsi = [0]
        cF = sb(es, [128, C_W], F32, "cF"); cB = sb(es, [128, 256], BF16, "cB")
        onesF = sb(es, [128, 128], F32, "onesF"); nonesF = sb(es, [128, 128], F32, "nonesF"); onesB = sb(es, [128, 128], BF16, "onesB")
        nw = sb(es, [128, 7, KC], F32, "nw")
        kC = Tk()
        sb_sq8 = sb(es, [128, KC, 512], BF16, "sq8"); sb_sq8k = Tk()
        idxk = Tk()
        idxp = sb(es, [128, NBS], I32, "idxp"); idxq = sb(es, [128, NBS, 32], I32, "idxq"); idxm = sb(es, [128, NBS, 16], I32, "idxm")
        assert NPG <= 128
        es_setup = contextlib.ExitStack()
        ptb = sb(es_setup, [128, NBS * NPG], I32, "ptb"); iot = sb(es_setup, [128, 1], I32, "iot")
        PS = [es.enter_context(nc.psum_tensor(f"ps{i}", [128, 512], F32)) for i in range(8)]
        PSk = [Tk(True) for _ in range(8)]
        pring = [list(range(8)), 0]
        def set_ring(banks): pring[0] = list(banks); pring[1] = 0
        def bank():
            b = pring[0][pring[1] % len(pring[0])]; pring[1] += 1
            return PS[b], PSk[b]
        def xtoks(kcs, c0, w):
            return [xTk[kc][blk] for kc in kcs for blk in range(c0 // 128, (c0 + w - 1) // 128 + 1)]
        identF = cF[:, C_I:C_I + 128]
        identB = cB[:, 0:128]; maskB = cB[:, 128:256]

        kb.dma('sp', cF[:, :], cst, writes=[kC])
        for i, src in enumerate([ffn_norm[0:1, :], ffn_norm[1:2, :], ffn_norm[2:3, :], ffn_norm[3:4, :], mix_norm[0:1, :], mix_norm[1:2, :], final_norm[0:1, :]]):
            kb.dma('sp', nw[:, i, :], src.rearrange("a (kc p) -> p (a kc)", p=128), writes=[kC], allow_slow_non_contiguous=True)
        kb.op('pool', lambda e: e.memset(onesF[:, :], 1.0), writes=[kC])
        kb.op('pool', lambda e: e.memset(nonesF[:, :], -1.0), writes=[kC])
        kb.op('pool', lambda e: e.memset(onesB[:, :], 1.0), writes=[kC])
        kb.op('act', lambda e: e.activation(out=cB[:, :], in_=cF[:, 0:256], func=AF.Copy), reads=[kC], writes=[kC])
        kb.dma('sp', ptb[:, :], pt.partition_broadcast(128), writes=[idxk])
        kb.dma('sp', idxp[0:NPG, :], pt.rearrange("a (b j) -> j (a b)", j=NPG), writes=[idxk], allow_slow_non_contiguous=True)
        kb.op('pool', lambda e: e.iota(iot[:, :], pattern=[[0, 1]], base=0, channel_multiplier=1), writes=[idxk])
        iot32 = sb(es_setup, [128, 32], I32, "iot32"); idxs = sb(es_setup, [128, 2, NBS], I32, "idxs")
        kb.op('pool', lambda e: e.iota(iot32[:, :], pattern=[[1, 32]], base=0, channel_multiplier=0), writes=[idxk])
        kb.op('dve', lambda e: e.tensor_scalar(out=idxs[0:NPG, 0, :], in0=idxp[0:NPG, :], scalar1=32, scalar2=0, op0=ALU.mult, op1=ALU.add), reads=[idxk], writes=[idxk])
        kb.op('dve', lambda e: e.tensor_scalar(out=idxs[0:NPG, 1, :], in0=idxp[0:NPG, :], scalar1=16, scalar2=0, op0=ALU.mult, op1=ALU.add), reads=[idxk], writes=[idxk])
        for b in range(NBS):
            kb.op('dve', lambda e, b=b: e.tensor_scalar(out=idxq[0:NPG, b, :], in0=iot32[0:NPG, :], scalar1=1, scalar2=idxs[0:NPG, 0, b:b + 1], op0=ALU.mult, op1=ALU.add), reads=[idxk], writes=[idxk])
            kb.op('dve', lambda e, b=b: e.tensor_scalar(out=idxm[0:NPG, b, :], in0=iot32[0:NPG, 0:16], scalar1=1, scalar2=idxs[0:NPG, 1, b:b + 1], op0=ALU.mult, op1=ALU.add), reads=[idxk], writes=[idxk])
        kb.barrier()
        es_setup.close()

        def wslot():
            i = wsi[0] % 3; wsi[0] += 1
            return wsl[i], wslk[i]

        def load_lhsT(w_ap, nk, ncol, slot, off=0):
            sl, slk = slot
            dst = sl[:, off:off + nk * ncol].rearrange("p (k n) -> p k n", k=nk)
            kb.dma('pool', dst, w_ap.rearrange("(k p) n -> p k n", p=128), writes=[slk])
            return dst

        def norm_cols(stack_tmp, nk, w, src, src_toks, wcol, dst, dst_toks, eps, dn, extra_scale=1.0):
            sq, sqk, rstd, rstdk = stack_tmp
            for kc in range(nk):
                kb.op('act', lambda e, kc=kc: e.activation(out=sq[:, kc, 0:w], in_=src(kc), func=AF.Square), reads=src_toks, writes=[sqk])
            pb, pbk = bank()
            for kc in range(nk):
                kb.op('pe', lambda e, kc=kc: e.matmul(pb[:, 0:w], lhsT=onesB[:, :], rhs=sq[:, kc, 0:w], start=(kc == 0), stop=(kc == nk - 1)), reads=[sqk, kC], writes=[pbk])
            kb.op('act', lambda e: e.activation(out=rstd[:, 0:w], in_=pb[:, 0:w], func=AF.Sqrt, bias=eps, scale=1.0 / dn), reads=[pbk], writes=[rstdk])
            kb.op('dve', lambda e: e.reciprocal(out=rstd[:, 0:w], in_=rstd[:, 0:w]), reads=[rstdk], writes=[rstdk])
            for kc in range(nk):
                kb.op('dve', lambda e, kc=kc: e.scalar_tensor_tensor(out=dst(kc), in0=src(kc), scalar=wcol(kc), in1=rstd[:, 0:w], op0=ALU.mult, op1=ALU.mult), reads=src_toks + [rstdk, kC], writes=dst_toks)

        with contextlib.ExitStack() as st:
            xin = [sb(st, [128, D], F32, "xin") for _ in range(2)]
            xink = [Tk(), Tk()]
            set_ring(range(8))
            for tt in range(NTT + 1):
                rows = 128 if tt < NTT else NBS
                xi, xik = xin[tt % 2], xink[tt % 2]
                src = xp[tt * 128:(tt + 1) * 128, :] if tt < NTT else xs[:, :]
                kb.dma('sp', xi[0:rows, :], src, writes=[xik])
                for half in range(2):
                    pb, pbk = bank()
                    for j in range(4):
                        c = half * 4 + j
                        kb.op('pe', lambda e, c=c, j=j: e.matmul(pb[:, j * 128:j * 128 + rows], lhsT=xi[0:rows, c * 128:(c + 1) * 128], rhs=identF[0:rows, 0:rows], start=True, stop=True), reads=[xik, kC], writes=[pbk])
                    eng = 'act' if half == 0 else 'dve'
                    o = xT[:, half * 4:half * 4 + 4, tt * 128:tt * 128 + rows]
                    i_ = pb[:, :].rearrange("p (a b) -> p a b", a=4)[:, :, 0:rows]
                    wt = [xTk[half * 4 + j][tt] for j in range(4)]
                    if eng == 'act':
                        kb.op('act', lambda e: e.activation(out=o, in_=i_, func=AF.Copy), reads=[pbk], writes=wt)
                    else:
                        kb.op('dve', lambda e: e.tensor_copy(out=o, in_=i_), reads=[pbk], writes=wt)
            kb.barrier()

        def ffn(idx):
            with contextlib.ExitStack() as st:
                AW = HW
                act = sb(st, [128, FC, AW], BF16, "act"); actk = [Tk() for _ in range(FC)]
                rstd = sb(st, [128, 512], F32, "rstd")
                tmp = (sb_sq8, sb_sq8k, rstd, Tk())
                sg = [sb(st, [128, 512], F32, "sg") for _ in range(2)]; sgk = [Tk(), Tk()]
                set_ring(range(8))
                nsg = 0
                for (s0, s1) in ((0, T // 2), (T // 2, NT)):
                    tiles = col_tiles(s0, s1)
                    for ti, (c0, w) in enumerate(tiles):
                        l0 = c0 - s0
                        norm_cols(tmp, KC, w, lambda kc: xT[:, kc, c0:c0 + w], xtoks(range(KC), c0, w), lambda kc: nw[:, idx, kc:kc + 1],
                                  lambda kc: hT[:, kc, l0:l0 + w], [hTk[ti]], 1e-6, float(D))
                    for fp in range(FC // 2):
                        slot = wslot()
                        g2 = load_lhsT(wg[idx * D:(idx + 1) * D, fp * 256:(fp + 1) * 256], KC, 256, slot, 0)
                        u2 = load_lhsT(wu[idx * D:(idx + 1) * D, fp * 256:(fp + 1) * 256], KC, 256, slot, 2048)
                        for ti, (c0, w) in enumerate(tiles):
                            l0 = c0 - s0
                            for half in range(2):
                                f = 2 * fp + half
                                pg, pgk = bank()
                                for kc in range(KC):
                                    kb.op('pe', lambda e, kc=kc: e.matmul(pg[:, 0:w], lhsT=g2[:, kc, half * 128:(half + 1) * 128], rhs=hT[:, kc, l0:l0 + w], start=(kc == 0), stop=(kc == KC - 1)), reads=[slot[1], hTk[ti]], writes=[pgk])
                                pu, puk = bank()
                                for kc in range(KC):
                                    kb.op('pe', lambda e, kc=kc: e.matmul(pu[:, 0:w], lhsT=u2[:, kc, half * 128:(half + 1) * 128], rhs=hT[:, kc, l0:l0 + w], start=(kc == 0), stop=(kc == KC - 1)), reads=[slot[1], hTk[ti]], writes=[puk])
                                s_, sk_ = sg[nsg % 2], sgk[nsg % 2]; nsg += 1
                                kb.op('act', lambda e: e.activation(out=s_[:, 0:w], in_=pg[:, 0:w], func=AF.Silu), reads=[pgk], writes=[sk_])
                                kb.op('dve', lambda e: e.tensor_tensor(out=act[:, f, l0:l0 + w], in0=s_[:, 0:w], in1=pu[:, 0:w], op=ALU.mult), reads=[sk_, puk], writes=[actk[f]])
                    for dc in range(KC):
                        slot = wslot()
                        d1 = load_lhsT(wd[idx * DFF:(idx + 1) * DFF, dc * 128:(dc + 1) * 128], FC, 128, slot, 0)
                        for ti, (c0, w) in enumerate(tiles):
                            l0 = c0 - s0
                            pb, pbk = bank()
                            for f in range(FC):
                                kb.op('pe', lambda e, f=f: e.matmul(pb[:, 0:w], lhsT=d1[:, f, :], rhs=act[:, f, l0:l0 + w], start=(f == 0), stop=(f == FC - 1)), reads=[slot[1], actk[f]], writes=[pbk])
                            xt_ = xtoks([dc], c0, w)
                            kb.op('dve', lambda e: e.scalar_tensor_tensor(out=xT[:, dc, c0:c0 + w], in0=pb[:, 0:w], scalar=0.5, in1=xT[:, dc, c0:c0 + w], op0=ALU.mult, op1=ALU.add), reads=[pbk] + xt_, writes=xt_)
                kb.barrier()

        def final():
            with contextlib.ExitStack() as st:
                rstd = sb(st, [128, 512], F32, "rstd")
                tmp = (sb_sq8, sb_sq8k, rstd, Tk())
                yT = sb(st, [128, KC, 512], F32, "yT"); yTk = Tk()
                yo = [sb(st, [128, D], F32, "yo") for _ in range(2)]; yok = [Tk(), Tk()]
                set_ring(range(8))
                n = 0
                for (c0, w) in col_tiles(0, NT):
                    norm_cols(tmp, KC, w, lambda kc: xT[:, kc, c0:c0 + w], xtoks(range(KC), c0, w), lambda kc: nw[:, 6, kc:kc + 1],
                              lambda kc: yT[:, kc, 0:w], [yTk], 1e-6, float(D))
                    for t0 in range(0, w, 128):
                        rows = min(128, w - t0)
                        y_, yk_ = yo[n % 2], yok[n % 2]; n += 1
                        for half in range(2):
                            pb, pbk = bank()
                            for j in range(4):
                                c = half * 4 + j
                                kb.op('pe', lambda e, c=c, j=j: e.matmul(pb[0:rows, j * 128:(j + 1) * 128], lhsT=yT[:, c, t0:t0 + rows], rhs=identF[:, :], start=True, stop=True), reads=[yTk, kC], writes=[pbk])
                            if half == 0:
                                kb.op('act', lambda e: e.activation(out=y_[0:rows, 0:512], in_=pb[0:rows, :], func=AF.Copy), reads=[pbk], writes=[yk_])
                            else:
                                kb.op('dve', lambda e: e.tensor_copy(out=y_[0:rows, 512:1024], in_=pb[0:rows, :]), reads=[pbk], writes=[yk_])
                        g0 = c0 + t0
                        dst = yp[g0:g0 + rows, :] if g0 < T else ys[0:rows, :]
                        kb.dma('sp', dst, y_[0:rows, :], reads=[yk_])
                kb.barrier()


        class Ring:
            def __init__(self, stack, n, shape, dt, name="r"):
                self.t = [sb(stack, shape, dt, name) for _ in range(n)]; self.k = [Tk() for _ in range(n)]; self.i = 0
            def next(self):
                i = self.i % len(self.t); self.i += 1
                return self.t[i], self.k[i]

        def evac(eng, out, in_, reads, writes):
            if eng == 'act':
                kb.op('act', lambda e: e.activation(out=out, in_=in_, func=AF.Copy), reads=reads, writes=writes)
            else:
                kb.op(eng, lambda e: e.tensor_copy(out=out, in_=in_), reads=reads, writes=writes)

        def apply_wout(w_ap, srcT, src_toks, c0, w, nk=KC):
            for dp in range(KC // 2):
                slot = wslot()
                s2 = load_lhsT(w_ap[:, dp * 256:(dp + 1) * 256], nk, 256, slot, 0)
                for half in range(2):
                    dc = dp * 2 + half
                    pb, pbk = bank()
                    for kc in range(nk):
                        kb.op('pe', lambda e, kc=kc: e.matmul(pb[:, 0:w], lhsT=s2[:, kc, half * 128:(half + 1) * 128], rhs=srcT[:, kc, 0:w], start=(kc == 0), stop=(kc == nk - 1)), reads=[slot[1]] + src_toks, writes=[pbk])
                    xt_ = xtoks([dc], c0, w)
                    kb.op('dve', lambda e: e.tensor_tensor(out=xT[:, dc, c0:c0 + w], in0=pb[:, 0:w], in1=xT[:, dc, c0:c0 + w], op=ALU.add), reads=[pbk] + xt_, writes=xt_)

        def odd_mixer():
            SCALE = float(192 ** -0.5)
            with contextlib.ExitStack() as st:
                uqN = sb(st, [128, 3, 8, 128], BF16, "uqN"); uqA = sb(st, [128, 3, 8, 64], BF16, "uqA"); uqB = sb(st, [128, 3, 8, 64], BF16, "uqB")
                wkT = sb(st, [128, 8, 256], BF16, "wkT"); wv = sb(st, [128, 2, 8, 128], BF16, "wv")
                qnw = sb(st, [128, 3], F32, "qnw"); kvnw = sb(st, [128, 2], F32, "kvnw")
                kW = Tk()
                ckvnT = sb(st, [128, 2, NT], BF16, "ckvnT"); rkT = sb(st, [64, NT], BF16, "rkT")
                vtok = sb(st, [128, NTT + 1, 256], BF16, "vtok")
                kK = [Tk() for _ in range(NTT + 1)]
                attnT = sb(st, [128, 8, 512], BF16, "attnT"); attnk = Tk()
                rstd = sb(st, [128, 512], F32, "rstd"); tmp = (sb_sq8, sb_sq8k, rstd, Tk())
                rL = rstd; rLk = tmp[3]
                ctxn = Ring(st, 1, [128, 2, 512], BF16, "ctxn")
                qls = sb(st, [128, 2, NBS, 8], BF16, "qls"); rqs = sb(st, [64, NBS, 8], BF16, "rqs"); qsk = Tk()
                st_u = contextlib.ExitStack()
                ukvK = sb(st_u, [128, 2, 8, 128], F32, "ukvK")
                for h in range(8):
                    kb.dma('pool', uqN[:, :, h, :], muq[:, h * 192:h * 192 + 128].rearrange("(k p) n -> p k n", p=128), writes=[kW])
                    kb.dma('pool', uqA[:, :, h, :], muq[:, h * 192 + 128:h * 192 + 192].rearrange("(k p) n -> p k n", p=128), writes=[kW])
                    kb.dma('pool', uqB[:, :, h, 0:32], muq[:, h * 192 + 160:h * 192 + 192].rearrange("(k p) n -> p k n", p=128), writes=[kW])
                    kb.dma('pool', uqB[:, :, h, 32:64], muq[:, h * 192 + 128:h * 192 + 160].rearrange("(k p) n -> p k n", p=128), writes=[kW])
                    kb.dma('sp', ukvK[:, :, h, :], mukv[:, h * 256:h * 256 + 128].rearrange("(k p) n -> p k n", p=128), writes=[kW])
                    kb.dma('pool', wv[:, :, h, :], mukv[:, h * 256 + 128:h * 256 + 256].rearrange("(k p) n -> p k n", p=128), writes=[kW])
                kb.dma('sp', qnw[:, :], mqn.rearrange("a (k p) -> p (a k)", p=128), writes=[kW], allow_slow_non_contiguous=True)
                kb.dma('sp', kvnw[:, :], mkvn.rearrange("a (k p) -> p (a k)", p=128), writes=[kW], allow_slow_non_contiguous=True)
                set_ring([0, 1, 2])
                for h in range(8):
                    for c2 in range(2):
                        pb, pbk = bank()
                        kb.op('pe', lambda e: e.matmul(pb[:, 0:128], lhsT=ukvK[:, c2, h, :], rhs=identF[:, :], start=True, stop=True), reads=[kW, kC], writes=[pbk])
                        evac('act' if c2 == 0 else 'dve', wkT[:, h, c2 * 128:(c2 + 1) * 128], pb[:, 0:128], [pbk], [kW])
                kb.barrier(); st_u.close()
                sp_ = contextlib.ExitStack()
                cqT = sb(sp_, [128, 3, 512], BF16, "cqT"); cqTk = Tk(); cqn = sb(sp_, [128, 3, 512], BF16, "cqn"); cqnk = Tk()
                ckvF = sb(sp_, [128, 2, 512], F32, "ckvF"); ckvFk = Tk(); ckvnF = sb(sp_, [128, 2, 512], F32, "ckvnF"); ckvnFk = Tk()
                rt = sb(sp_, [64, 2, 512], F32, "rt"); rtk = Tk()
                t1r = Ring(sp_, 1, [64, 512], F32, "t1"); t2r = Ring(sp_, 1, [64, 512], F32, "t2")
                rkF = sb(sp_, [64, 512], F32, "rkF"); rkFk = Tk()
                tokst = Ring(sp_, 1, [128, 320], F32, "tokst")
                qnT = Ring(sp_, 1, [128, 512], BF16, "qnT"); qlT = Ring(sp_, 2, [128, 2, 512], BF16, "qlT"); rqT = Ring(sp_, 1, [64, 512], BF16, "rqT")
                Er = Ring(sp_, 2, [128, 512], BF16, "E")
                if DBG < 1:
                    kb.barrier(); return
                sring = [3, 4]; sri = [0]
                def sbank():
                    b = sring[sri[0] % 2]; sri[0] += 1
                    return PS[b], PSk[b]
                CT = [(PS[5], PSk[5]), (PS[6], PSk[6])]; LT = (PS[7], PSk[7])
                ropeV = ropeT.rearrange("p (a n) -> p a n", a=2)

                def rope_apply(pa, pak, pbb, pbbk, w, out_f32=None, out_f32k=None, out_bf=None, out_bfk=None):
                    t1, t1k = t1r.next(); t2, t2k = t2r.next()
                    kb.op('dve', lambda e: e.tensor_tensor(out=t1[:, 0:w], in0=pa[0:64, 0:w], in1=rt[:, 0, 0:w], op=ALU.mult), reads=[pak, rtk], writes=[t1k])
                    kb.op('dve', lambda e: e.tensor_tensor(out=t2[:, 0:w], in0=pbb[0:64, 0:w], in1=rt[:, 1, 0:w], op=ALU.mult), reads=[pbbk, rtk], writes=[t2k])
                    if out_f32 is not None:
                        kb.op('pool', lambda e: e.tensor_tensor(out=out_f32, in0=t1[:, 0:w], in1=t2[:, 0:w], op=ALU.add), reads=[t1k, t2k], writes=[out_f32k])
                        kb.op('act', lambda e: e.activation(out=out_bf, in_=out_f32, func=AF.Copy), reads=[out_f32k], writes=out_bfk)
                    else:
                        kb.op('pool', lambda e: e.tensor_tensor(out=out_bf, in0=t1[:, 0:w], in1=t2[:, 0:w], op=ALU.add), reads=[t1k, t2k], writes=out_bfk)

                for (c0, w) in col_tiles(0, T) + [(T, NBS)]:
                    sample = c0 >= T
                    ktile0 = c0 // 128
                    nsub = (w + 127) // 128
                    ktoks = [kK[ktile0 + i] for i in range(nsub)]
                    norm_cols(tmp, KC, w, lambda kc: xT[:, kc, c0:c0 + w], xtoks(range(KC), c0, w), lambda kc: nw[:, 5, kc:kc + 1],
                              lambda kc: hT[:, kc, 0:w], [hTk[0]], 1e-6, float(D))
                    kb.dma('sp', rt[:, :, 0:w], ropeV[:, :, c0:c0 + w], writes=[rtk])
                    slot = wslot()
                    wq = load_lhsT(mwin[:, 0:384], KC, 384, slot, 0)
                    slot2 = wslot()
                    wc = load_lhsT(mwin[:, 384:640], KC, 256, slot2, 0)
                    wpa = load_lhsT(mwin[:, 640:704], KC, 64, slot2, 2048)
                    wpb = slot2[0][:, 2560:2560 + KC * 64].rearrange("p (k n) -> p k n", k=KC)
                    kb.dma('pool', wpb[:, :, 0:32], mwin[:, 672:704].rearrange("(k p) n -> p k n", p=128), writes=[slot2[1]])
                    kb.dma('pool', wpb[:, :, 32:64], mwin[:, 640:672].rearrange("(k p) n -> p k n", p=128), writes=[slot2[1]])
                    if DBG2 < 2: continue
                    for j in range(3):
                        pb, pbk = bank()
                        for kc in range(KC):
                            kb.op('pe', lambda e, kc=kc: e.matmul(pb[:, 0:w], lhsT=wq[:, kc, j * 128:(j + 1) * 128], rhs=hT[:, kc, 0:w], start=(kc == 0), stop=(kc == KC - 1)), reads=[slot[1], hTk[0]], writes=[pbk])
                        evac('act', cqT[:, j, 0:w], pb[:, 0:w], [pbk], [cqTk])
                    norm_cols(tmp, 3, w, lambda kc: cqT[:, kc, 0:w], [cqTk], lambda kc: qnw[:, kc:kc + 1], lambda kc: cqn[:, kc, 0:w], [cqnk], 1e-6, 384.0)
                    if DBG2 < 3: continue
                    for j in range(2):
                        pb, pbk = bank()
                        for kc in range(KC):
                            kb.op('pe', lambda e, kc=kc: e.matmul(pb[:, 0:w], lhsT=wc[:, kc, j * 128:(j + 1) * 128], rhs=hT[:, kc, 0:w], start=(kc == 0), stop=(kc == KC - 1)), reads=[slot2[1], hTk[0]], writes=[pbk])
                        evac('act', ckvF[:, j, 0:w], pb[:, 0:w], [pbk], [ckvFk])
                    norm_cols(tmp, 2, w, lambda kc: ckvF[:, kc, 0:w], [ckvFk], lambda kc: kvnw[:, kc:kc + 1], lambda kc: ckvnF[:, kc, 0:w], [ckvnFk], 1e-6, 256.0)
                    kb.op('pool', lambda e: e.tensor_copy(out=ckvnT[:, :, c0:c0 + w], in_=ckvnF[:, :, 0:w]), reads=[ckvnFk], writes=ktoks)
                    if DBG2 < 4: continue
                    pa, pak = bank()
                    for kc in range(KC):
                        kb.op('pe', lambda e, kc=kc: e.matmul(pa[0:64, 0:w], lhsT=wpa[:, kc, :], rhs=hT[:, kc, 0:w], start=(kc == 0), stop=(kc == KC - 1)), reads=[slot2[1], hTk[0]], writes=[pak])
                    pbb, pbbk = bank()
                    for kc in range(KC):
                        kb.op('pe', lambda e, kc=kc: e.matmul(pbb[0:64, 0:w], lhsT=wpb[:, kc, :], rhs=hT[:, kc, 0:w], start=(kc == 0), stop=(kc == KC - 1)), reads=[slot2[1], hTk[0]], writes=[pbbk])
                    rope_apply(pa, pak, pbb, pbbk, w, rkF[:, 0:w], rkFk, rkT[:, c0:c0 + w], ktoks)
                    if DBG2 < 5: continue
                    for si in range(nsub):
                        t0 = si * 128; rows = min(128, w - t0)
                        pb, pbk = bank()
                        for c2 in range(2):
                            kb.op('pe', lambda e, c2=c2: e.matmul(pb[0:rows, c2 * 128:(c2 + 1) * 128], lhsT=ckvnF[:, c2, t0:t0 + rows], rhs=identF[:, :], start=True, stop=True), reads=[ckvnFk, kC], writes=[pbk])
                        if DBG3 & 1:
                            kb.op('pe', lambda e: e.matmul(pb[0:rows, 256:320], lhsT=rkF[0:64, t0:t0 + rows], rhs=identF[0:64, 0:64], start=True, stop=True), reads=[rkFk, kC], writes=[pbk])
                        ts_, tsk = tokst.next()
                        evac('act', ts_[0:rows, :], pb[0:rows, 0:320], [pbk], [tsk])
                        kb.op('dve', lambda e: e.tensor_copy(out=vtok[0:rows, ktile0 + si, :], in_=pb[0:rows, 0:256]), reads=[pbk], writes=[ktoks[si]])
                        g0 = c0 + t0
                        if not (DBG3 & 2): continue
                        if not sample:
                            kb.dma('sp', ckvp[g0:g0 + rows, :], ts_[0:rows, 0:256], reads=[tsk])
                            kb.dma('sp', kpep[g0:g0 + rows, :], ts_[0:rows, 256:320], reads=[tsk])
                        else:
                            kb.dma('sp', ckvs[0:rows, :], ts_[0:rows, 0:256], reads=[tsk])
                            kb.dma('sp', kpes[0:rows, :], ts_[0:rows, 256:320], reads=[tsk])
                    if DBG2 < 6: continue
                    for h in range(8 if DBG >= 2 else 0):
                        pb, pbk = bank()
                        for kc in range(3):
                            kb.op('pe', lambda e, kc=kc: e.matmul(pb[:, 0:w], lhsT=uqN[:, kc, h, :], rhs=cqn[:, kc, 0:w], start=(kc == 0), stop=(kc == 2)), reads=[kW, cqnk], writes=[pbk])
                        qn, qnk = qnT.next()
                        evac('act', qn[:, 0:w], pb[:, 0:w], [pbk], [qnk])
                        ql, qlk = qlT.next()
                        for c2 in range(2):
                            pb, pbk = bank()
                            kb.op('pe', lambda e: e.matmul(pb[:, 0:w], lhsT=wkT[:, h, c2 * 128:(c2 + 1) * 128], rhs=qn[:, 0:w], start=True, stop=True), reads=[kW, qnk], writes=[pbk])
                            evac('dve', ql[:, c2, 0:w], pb[:, 0:w], [pbk], [qlk])
                        pa, pak = bank()
                        for kc in range(3):
                            kb.op('pe', lambda e, kc=kc: e.matmul(pa[0:64, 0:w], lhsT=uqA[:, kc, h, :], rhs=cqn[:, kc, 0:w], start=(kc == 0), stop=(kc == 2)), reads=[kW, cqnk], writes=[pak])
                        pbb, pbbk = bank()
                        for kc in range(3):
                            kb.op('pe', lambda e, kc=kc: e.matmul(pbb[0:64, 0:w], lhsT=uqB[:, kc, h, :], rhs=cqn[:, kc, 0:w], start=(kc == 0), stop=(kc == 2)), reads=[kW, cqnk], writes=[pbbk])
                        rq, rqk = rqT.next()
                        rope_apply(pa, pak, pbb, pbbk, w, None, None, rq[:, 0:w], [rqk])
                        if sample:
                            kb.op('pool', lambda e: e.tensor_copy(out=qls[:, :, :, h], in_=ql[:, :, 0:NBS]), reads=[qlk], writes=[qsk])
                            kb.op('pool', lambda e: e.tensor_copy(out=rqs[:, :, h], in_=rq[:, 0:NBS]), reads=[rqk], writes=[qsk])
                            continue
                        nkt = (c0 + w) // 128
                        if DBG < 3: continue
                        for kt in range(nkt):
                            k0 = kt * 128; qs = max(k0 - c0, 0)
                            ps, psk = sbank()
                            kb.op('pe', lambda e: e.matmul(ps[:, qs:w], lhsT=ckvnT[:, 0, k0:k0 + 128], rhs=ql[:, 0, qs:w], start=True, stop=False), reads=[kK[kt], qlk], writes=[psk])
                            kb.op('pe', lambda e: e.matmul(ps[:, qs:w], lhsT=ckvnT[:, 1, k0:k0 + 128], rhs=ql[:, 1, qs:w], start=False, stop=False), reads=[kK[kt], qlk], writes=[psk])
                            kb.op('pe', lambda e: e.matmul(ps[:, qs:w], lhsT=rkT[0:64, k0:k0 + 128], rhs=rq[0:64, qs:w], start=False, stop=True), reads=[kK[kt], rqk], writes=[psk])
                            E, Ek = Er.next()
                            kb.op('act', lambda e: e.activation(out=E[:, qs:w], in_=ps[:, qs:w], func=AF.Exp, scale=SCALE), reads=[psk], writes=[Ek])
                            if k0 >= c0:
                                kb.op('pool', lambda e: e.tensor_tensor(out=E[:, qs:qs + 128], in0=E[:, qs:qs + 128], in1=maskB, op=ALU.mult), reads=[Ek, kC], writes=[Ek])
                            for c2 in range(2):
                                kb.op('pe', lambda e, c2=c2: e.matmul(CT[c2][0][:, qs:w], lhsT=vtok[:, kt, c2 * 128:(c2 + 1) * 128], rhs=E[:, qs:w], start=(kt == 0), stop=(kt == nkt - 1)), reads=[kK[kt], Ek], writes=[CT[c2][1]])
                            kb.op('pe', lambda e: e.matmul(LT[0][:, qs:w], lhsT=onesB[:, :], rhs=E[:, qs:w], start=(kt == 0), stop=(kt == nkt - 1)), reads=[kC, Ek], writes=[LT[1]])
                        kb.op('dve', lambda e: e.reciprocal(out=rL[:, 0:w], in_=LT[0][:, 0:w]), reads=[LT[1]], writes=[rLk])
                        cx, cxk = ctxn.next()
                        for c2 in range(2):
                            kb.op('dve', lambda e, c2=c2: e.tensor_tensor(out=cx[:, c2, 0:w], in0=CT[c2][0][:, 0:w], in1=rL[:, 0:w], op=ALU.mult), reads=[CT[c2][1], rLk], writes=[cxk])
                        pb, pbk = bank()
                        for c2 in range(2):
                            kb.op('pe', lambda e, c2=c2: e.matmul(pb[:, 0:w], lhsT=wv[:, c2, h, :], rhs=cx[:, c2, 0:w], start=(c2 == 0), stop=(c2 == 1)), reads=[kW, cxk], writes=[pbk])
                        evac('act', attnT[:, h, 0:w], pb[:, 0:w], [pbk], [attnk])
                    if not sample:
                        apply_wout(mwout, attnT, [attnk], c0, w)
                kb.barrier(); sp_.close()
                if DBG < 4:
                    kb.barrier(); return
                G = 4; KT = 8
                ccP = cc.rearrange("(n q) d -> n (q d)", q=KT); cpP = cp.rearrange("(n q) d -> n (q d)", q=KT); assert KT == 8
                cpg = Ring(st, 2, [128, KT, 256], F32, "cpg"); ppg = Ring(st, 2, [128, KT, 64], F32, "ppg")
                kTr = Ring(st, 2, [128, 3, G * 128], BF16, "kTr"); vbf = Ring(st, 2, [128, G, 256], BF16, "vbf")
                Es = Ring(st, 2, [128, G * 8], BF16, "Es")
                Esf = sb(st, [NBS, 32], F32, "Esf"); Esb = sb(st, [NBS, 32], BF16, "Esb"); Esk = Tk()
                ps, psk = sbank()
                for b in range(NBS):
                    kb.op('pe', lambda e: e.matmul(ps[0:NBS, b * 8:b * 8 + 8], lhsT=ckvnT[:, 0, T:T + NBS], rhs=qls[:, 0, b, :], start=True, stop=False), reads=[kK[NTT], qsk], writes=[psk])
                    kb.op('pe', lambda e: e.matmul(ps[0:NBS, b * 8:b * 8 + 8], lhsT=ckvnT[:, 1, T:T + NBS], rhs=qls[:, 1, b, :], start=False, stop=False), reads=[kK[NTT], qsk], writes=[psk])
                    kb.op('pe', lambda e: e.matmul(ps[0:NBS, b * 8:b * 8 + 8], lhsT=rkT[0:64, T:T + NBS], rhs=rqs[0:64, b, :], start=False, stop=True), reads=[kK[NTT], qsk], writes=[psk])
                kb.op('act', lambda e: e.activation(out=Esf[:, :], in_=ps[0:NBS, 0:32], func=AF.Exp, scale=SCALE), reads=[psk], writes=[Esk])
                kb.op('dve', lambda e: e.tensor_tensor(out=Esb[:, :], in0=Esf[:, :], in1=cF[0:NBS, C_SELF:C_SELF + 32], op=ALU.mult), reads=[Esk, kC], writes=[Esk])
                for c2 in range(2):
                    kb.op('pe', lambda e, c2=c2: e.matmul(CT[c2][0][:, 0:32], lhsT=vtok[0:NBS, NTT, c2 * 128:(c2 + 1) * 128], rhs=Esb[:, :], start=True, stop=False), reads=[kK[NTT], Esk], writes=[CT[c2][1]])
                kb.op('pe', lambda e: e.matmul(LT[0][:, 0:32], lhsT=onesB[0:NBS, :], rhs=Esb[:, :], start=True, stop=False), reads=[kC, Esk], writes=[LT[1]])
                for b in range(NBS):
                    for t0 in range(0, 128, KT):
                        cp_, cpk = cpg.next(); pp_, ppk = ppg.next()
                        kb.dma('pool', cp_[0:NPG, :, :].rearrange("p a d -> p (a d)"), ccP, reads=[idxk], writes=[cpk], indirect=idxm[0:NPG, b, t0 // KT:t0 // KT + 1])
                        kb.dma('pool', pp_[0:NPG, :, :].rearrange("p a d -> p (a d)"), cpP, reads=[idxk], writes=[ppk], indirect=idxm[0:NPG, b, t0 // KT:t0 // KT + 1])
                        for g0 in range(0, KT, G):
                            ng = G
                            kT, kTk = kTr.next()
                            for c2 in range(3):
                                pb, pbk = bank()
                                for g in range(ng):
                                    if c2 < 2:
                                        kb.op('pe', lambda e, g=g: e.matmul(pb[:, g * 128:g * 128 + NPG], lhsT=cp_[0:NPG, g0 + g, c2 * 128:(c2 + 1) * 128], rhs=identF[0:NPG, 0:NPG], start=True, stop=True), reads=[cpk, kC], writes=[pbk])
                                    else:
                                        kb.op('pe', lambda e, g=g: e.matmul(pb[0:64, g * 128:g * 128 + NPG], lhsT=pp_[0:NPG, g0 + g, :], rhs=identF[0:NPG, 0:NPG], start=True, stop=True), reads=[ppk, kC], writes=[pbk])
                                np_ = 128 if c2 < 2 else 64
                                evac('act' if c2 != 1 else 'dve', kT[0:np_, c2, 0:ng * 128].rearrange("p (g t) -> p g t", t=128)[:, :, 0:NPG], pb[0:np_, 0:ng * 128].rearrange("p (g t) -> p g t", t=128)[:, :, 0:NPG], [pbk], [kTk])
                            vb, vbk = vbf.next()
                            kb.op('dve', lambda e: e.tensor_copy(out=vb[0:NPG, 0:ng, :], in_=cp_[0:NPG, g0:g0 + ng, :]), reads=[cpk], writes=[vbk])
                            ps, psk = sbank()
                            for g in range(ng):
                                kb.op('pe', lambda e, g=g: e.matmul(ps[0:NPG, g * 8:g * 8 + 8], lhsT=kT[:, 0, g * 128:g * 128 + NPG], rhs=qls[:, 0, b, :], start=True, stop=False), reads=[kTk, qsk], writes=[psk])
                                kb.op('pe', lambda e, g=g: e.matmul(ps[0:NPG, g * 8:g * 8 + 8], lhsT=kT[:, 1, g * 128:g * 128 + NPG], rhs=qls[:, 1, b, :], start=False, stop=False), reads=[kTk, qsk], writes=[psk])
                                kb.op('pe', lambda e, g=g: e.matmul(ps[0:NPG, g * 8:g * 8 + 8], lhsT=kT[0:64, 2, g * 128:g * 128 + NPG], rhs=rqs[0:64, b, :], start=False, stop=True), reads=[kTk, qsk], writes=[psk])
                            E, Ek = Es.next()
                            kb.op('act', lambda e: e.activation(out=E[0:NPG, 0:ng * 8], in_=ps[0:NPG, 0:ng * 8], func=AF.Exp, scale=SCALE), reads=[psk], writes=[Ek])
                            last = (t0 + KT >= 128) and (g0 + G >= KT)
                            for g in range(ng):
                                fin = last and g == ng - 1 and b == NBS - 1
                                for c2 in range(2):
                                    kb.op('pe', lambda e, g=g, c2=c2: e.matmul(CT[c2][0][:, b * 8:b * 8 + 8], lhsT=vb[0:NPG, g, c2 * 128:(c2 + 1) * 128], rhs=E[0:NPG, g * 8:g * 8 + 8], start=False, stop=fin), reads=[vbk, Ek], writes=[CT[c2][1]])
                                kb.op('pe', lambda e, g=g: e.matmul(LT[0][:, b * 8:b * 8 + 8], lhsT=onesB[0:NPG, :], rhs=E[0:NPG, g * 8:g * 8 + 8], start=False, stop=fin), reads=[kC, Ek], writes=[LT[1]])
                kb.op('dve', lambda e: e.reciprocal(out=rL[:, 0:32], in_=LT[0][:, 0:32]), reads=[LT[1]], writes=[rLk])
                cx, cxk = ctxn.next()
                for c2 in range(2):
                    kb.op('dve', lambda e, c2=c2: e.tensor_tensor(out=cx[:, c2, 0:32], in0=CT[c2][0][:, 0:32], in1=rL[:, 0:32], op=ALU.mult), reads=[CT[c2][1], rLk], writes=[cxk])
                for h in range(8):
                    pb, pbk = bank()
                    for c2 in range(2):
                        kb.op('pe', lambda e, c2=c2: e.matmul(pb[:, 0:NBS], lhsT=wv[:, c2, h, :], rhs=cx[:, c2, 0:32].rearrange("p (b h) -> p b h", h=8)[:, :, h], start=(c2 == 0), stop=(c2 == 1)), reads=[kW, cxk], writes=[pbk])
                    evac('act', attnT[:, h, 0:NBS], pb[:, 0:NBS], [pbk], [attnk])
                apply_wout(mwout, attnT, [attnk], T, NBS)
                kb.barrier()

        def even_mixer():
            with contextlib.ExitStack() as st:
                mixT = sb(st, [128, KC, NT], BF16, "mixT")
                mtiles = col_tiles(0, T) + [(T, NBS)]
                mixk = [[Tk() for _ in mtiles] for _ in range(KC)]
                if "diff" in stages or True:
                    diff_stage(mixT, mixk, mtiles)
                if DBG >= 5:
                    gdn_stage(mixT, mixk, mtiles)
                else:
                    for ti, (c0, w) in enumerate(mtiles):
                        for kc in range(4):
                            kb.op('pool', lambda e, kc=kc: e.memset(mixT[:, kc, c0:c0 + w], 0.0), writes=[mixk[kc][ti]])
                set_ring(range(8))
                for ti, (c0, w) in enumerate(mtiles):
                    apply_wout(ewout, mixT[:, :, c0:c0 + w], [mixk[kc][ti] for kc in range(KC)], c0, w)
                kb.barrier()

        def diff_stage(mixT, mixk, mtiles):
            with contextlib.ExitStack() as so:
                lamb = sb(so, [128, 256], F32, "lamb"); ltmp = sb(so, [128, 64], F32, "ltmp"); lcol = sb(so, [128, 8], F32, "lcol"); kL = Tk()
                dkS = sb(so, [128, 4, NBS], BF16, "dkS"); dqS = sb(so, [128, 4, NBS], BF16, "dqS"); dvS = sb(so, [NBS, 512], BF16, "dvS"); dqtok = sb(so, [NBS, 512], F32, "dqtok"); kS = Tk()
                rl = sb(so, [128, 512], F32, "rl"); t0 = sb(so, [128, 512], F32, "t0"); t1 = sb(so, [128, 512], F32, "t1"); sq1 = sb(so, [128, 512], BF16, "sq1")
                rlk, t0k, t1k, sq1k = Tk(), Tk(), Tk(), Tk()
                stg = Ring(so, 1, [128, 512], F32, "stg")
                kb.dma('sp', lamb[:, :], dlam.partition_broadcast(128), writes=[kL])
                kb.dma('sp', lcol[:, 4:5], subln.rearrange("a p -> p a"), writes=[kL], allow_slow_non_contiguous=True)
                for i in range(2):
                    kb.op('dve', lambda e: e.tensor_tensor(out=ltmp[:, :], in0=lamb[:, i * 128:i * 128 + 64], in1=lamb[:, i * 128 + 64:i * 128 + 128], op=ALU.mult), reads=[kL], writes=[kL])
                    kb.op('dve', lambda e: e.reduce_sum(out=lcol[:, i:i + 1], in_=ltmp[:, :], axis=AX.X), reads=[kL], writes=[kL])
                    kb.op('act', lambda e: e.activation(out=lcol[:, i:i + 1], in_=lcol[:, i:i + 1], func=AF.Exp), reads=[kL], writes=[kL])
                LAM_INIT = 0.2
                kb.op('dve', lambda e: e.tensor_tensor(out=lcol[:, 2:3], in0=lcol[:, 1:2], in1=lcol[:, 0:1], op=ALU.subtract), reads=[kL], writes=[kL])
                kb.op('dve', lambda e: e.tensor_scalar(out=lcol[:, 2:3], in0=lcol[:, 2:3], scalar1=-LAM_INIT, scalar2=None, op0=ALU.add), reads=[kL], writes=[kL])
                kb.op('dve', lambda e: e.tensor_scalar(out=lcol[:, 5:6], in0=lcol[:, 4:5], scalar1=1.0 - LAM_INIT, scalar2=None, op0=ALU.mult), reads=[kL], writes=[kL])
                neglam = lcol[:, 2:3]; subw = lcol[:, 5:6]
                sring = [0, 1]; sri = [0]
                def sbank():
                    b = sring[sri[0] % 2]; sri[0] += 1
                    return PS[b], PSk[b]
                OL = [[(PS[2], PSk[2]), (PS[3], PSk[3])], [(PS[4], PSk[4]), (PS[5], PSk[5])]]
                set_ring([6, 7])

                def combine(O0, L0, O1, L1, toks, n, out_fn):
                    kb.op('dve', lambda e: e.reciprocal(out=rl[:, 0:n], in_=L0), reads=toks, writes=[rlk])
                    kb.op('dve', lambda e: e.tensor_tensor(out=t0[:, 0:n], in0=O0, in1=rl[:, 0:n], op=ALU.mult), reads=toks + [rlk], writes=[t0k])
                    kb.op('dve', lambda e: e.reciprocal(out=rl[:, 0:n], in_=L1), reads=toks, writes=[rlk])
                    kb.op('dve', lambda e: e.tensor_tensor(out=t1[:, 0:n], in0=O1, in1=rl[:, 0:n], op=ALU.mult), reads=toks + [rlk], writes=[t1k])
                    kb.op('dve', lambda e: e.scalar_tensor_tensor(out=t0[:, 0:n], in0=t1[:, 0:n], scalar=neglam, in1=t0[:, 0:n], op0=ALU.mult, op1=ALU.add), reads=[t1k, t0k, kL], writes=[t0k])
                    kb.op('act', lambda e: e.activation(out=sq1[:, 0:n], in_=t0[:, 0:n], func=AF.Square), reads=[t0k], writes=[sq1k])
                    pn, pnk = bank()
                    kb.op('pe', lambda e: e.matmul(pn[:, 0:n], lhsT=onesB[:, :], rhs=sq1[:, 0:n], start=True, stop=True), reads=[sq1k, kC], writes=[pnk])
                    kb.op('act', lambda e: e.activation(out=rl[:, 0:n], in_=pn[:, 0:n], func=AF.Sqrt, bias=1e-5, scale=1.0 / 128), reads=[pnk], writes=[rlk])
                    kb.op('dve', lambda e: e.reciprocal(out=rl[:, 0:n], in_=rl[:, 0:n]), reads=[rlk], writes=[rlk])
                    out_fn()

                with contextlib.ExitStack() as sa:
                    dkT = sb(sa, [128, 4, T], BF16, "dkT"); dvtok = sb(sa, [128, NTT, 512], BF16, "dvtok"); kK = [Tk() for _ in range(NTT)]
                    dqT = sb(sa, [128, 4, 512], BF16, "dqT"); dqk = Tk()
                    Er = Ring(sa, 3, [128, 512], BF16, "E")
                    for ti, (c0, w) in enumerate(mtiles):
                        sample = c0 >= T
                        nsub = (w + 127) // 128
                        ktile0 = c0 // 128
                        ktoks = [kS] if sample else [kK[ktile0 + i] for i in range(nsub)]
                        norm_cols((sb_sq8, sb_sq8k, rl, rlk), KC, w, lambda kc: xT[:, kc, c0:c0 + w], xtoks(range(KC), c0, w), lambda kc: nw[:, 4, kc:kc + 1],
                                  lambda kc: hT[:, kc, 0:w], [hTk[0]], 1e-6, float(D))
                        for which in range(2):
                            base = O4 if which == 0 else O5
                            for jp in range(2):
                                slot = wslot()
                                s2 = load_lhsT(ewin[:, base + jp * 256:base + (jp + 1) * 256], KC, 256, slot, 0)
                                for half in range(2):
                                    j = jp * 2 + half
                                    pb, pbk = bank()
                                    for kc in range(KC):
                                        kb.op('pe', lambda e, kc=kc: e.matmul(pb[:, 0:w], lhsT=s2[:, kc, half * 128:(half + 1) * 128], rhs=hT[:, kc, 0:w], start=(kc == 0), stop=(kc == KC - 1)), reads=[slot[1], hTk[0]], writes=[pbk])
                                    if which == 0:
                                        evac('act', (dqS[:, j, 0:w] if sample else dqT[:, j, 0:w]), pb[:, 0:w], [pbk], [kS if sample else dqk])
                                    else:
                                        evac('dve', (dkS[:, j, 0:w] if sample else dkT[:, j, c0:c0 + w]), pb[:, 0:w], [pbk], ktoks)
                        for which in range(3 if sample else 2):
                            base = (O5, O6, O4)[which]
                            slot = wslot()
                            s4 = load_lhsT(ewin[:, base:base + 512], KC, 512, slot, 0)
                            for si in range(nsub):
                                r0 = si * 128; rows = min(128, w - r0)
                                pb, pbk = bank()
                                for kc in range(KC):
                                    kb.op('pe', lambda e, kc=kc: e.matmul(pb[0:rows, :], lhsT=hT[:, kc, r0:r0 + rows], rhs=s4[:, kc, :], start=(kc == 0), stop=(kc == KC - 1)), reads=[slot[1], hTk[0]], writes=[pbk])
                                if which == 2:
                                    evac('act', dqtok[0:rows, :], pb[0:rows, :], [pbk], [kS])
                                    continue
                                sg_, sgk_ = stg.next()
                                evac('act', sg_[0:rows, :], pb[0:rows, :], [pbk], [sgk_])
                                if which == 1:
                                    kb.op('dve', lambda e: e.tensor_copy(out=(dvS[0:rows, :] if sample else dvtok[0:rows, ktile0 + si, :]), in_=pb[0:rows, :]), reads=[pbk], writes=[ktoks[0 if sample else si]])
                                g0 = c0 + r0
                                dst = ((ks, vs)[which][0:rows, :]) if sample else ((kp, vp)[which][g0:g0 + rows, :])
                                kb.dma('sp', dst, sg_[0:rows, :], reads=[sgk_])
                        if sample or DBG < 2: continue
                        nkt = (c0 + w) // 128
                        for h in range(4):
                            for kt in range(nkt):
                                k0 = kt * 128; qs = max(k0 - c0, 0)
                                for m in range(2):
                                    r0 = m * 64
                                    ps, psk = sbank()
                                    kb.op('pe', lambda e: e.matmul(ps[:, qs:w], lhsT=dkT[r0:r0 + 64, h, k0:k0 + 128], rhs=dqT[r0:r0 + 64, h, qs:w], start=True, stop=True), reads=[kK[kt], dqk], writes=[psk])
                                    E, Ek = Er.next()
                                    kb.op('act', lambda e: e.activation(out=E[:, qs:w], in_=ps[:, qs:w], func=AF.Exp, scale=0.125), reads=[psk], writes=[Ek])
                                    if k0 >= c0:
                                        kb.op('pool', lambda e: e.tensor_tensor(out=E[:, qs:qs + 128], in0=E[:, qs:qs + 128], in1=maskB, op=ALU.mult), reads=[Ek, kC], writes=[Ek])
                                    (O_, Ok_), (L_, Lk_) = OL[m]
                                    kb.op('pe', lambda e: e.matmul(O_[:, qs:w], lhsT=dvtok[:, kt, h * 128:(h + 1) * 128], rhs=E[:, qs:w], start=(kt == 0), stop=(kt == nkt - 1)), reads=[kK[kt], Ek], writes=[Ok_])
                                    kb.op('pe', lambda e: e.matmul(L_[:, qs:w], lhsT=onesB[:, :], rhs=E[:, qs:w], start=(kt == 0), stop=(kt == nkt - 1)), reads=[kC, Ek], writes=[Lk_])
                            def outp(h=h, c0=c0, w=w, ti=ti):
                                kb.op('dve', lambda e: e.scalar_tensor_tensor(out=mixT[:, 4 + h, c0:c0 + w], in0=t0[:, 0:w], scalar=subw, in1=rl[:, 0:w], op0=ALU.mult, op1=ALU.mult), reads=[t0k, rlk, kL], writes=[mixk[4 + h][ti]])
                            combine(OL[0][0][0][:, 0:w], OL[0][1][0][:, 0:w], OL[1][0][0][:, 0:w], OL[1][1][0][:, 0:w], [OL[0][0][1], OL[0][1][1], OL[1][0][1], OL[1][1][1]], w, outp)
                    kb.barrier()
                if DBG < 3: return
                with contextlib.ExitStack() as sc_:
                    G = 2; KT = 4
                    ckP = ck.rearrange("(n q) d -> n (q d)", q=KT); cvP = cv.rearrange("(n q) d -> n (q d)", q=KT); assert KT == 4
                    kpg = Ring(sc_, 2, [128, KT, 512], F32, "kpg"); vpg = Ring(sc_, 2, [128, KT, 512], F32, "vpg")
                    prod = Ring(sc_, 1, [128, 512], F32, "prod"); scr = Ring(sc_, 2, [128, G * 8], F32, "scr")
                    Eg = Ring(sc_, 2, [128, G * 8], BF16, "Eg"); vbf = Ring(sc_, 1, [128, G, 512], BF16, "vbf")
                    qb = sb(sc_, [128, 512], F32, "qb"); qbk = Tk()
                    Esf = sb(sc_, [NBS, 32], F32, "Esf"); Esb = sb(sc_, [NBS, 32], BF16, "Esb"); Esk = Tk()
                    res = sb(sc_, [128, 16], F32, "res"); resk = Tk()
                    Oacc, Oacck = PS[2], PSk[2]; Lacc, Lacck = PS[3], PSk[3]
                    if DBG5 & 1:
                        psm = [sbank(), sbank()]
                        for hm in range(8):
                            h = hm // 2; m = hm % 2; r0 = m * 64
                            kb.op('pe', lambda e: e.matmul(psm[m][0][0:NBS, h * NBS:(h + 1) * NBS], lhsT=dkS[r0:r0 + 64, h, 0:NBS], rhs=dqS[r0:r0 + 64, h, 0:NBS], start=True, stop=True), reads=[kS], writes=[psm[m][1]])
                        for m in range(2):
                            kb.op('act', lambda e: e.activation(out=Esf[:, m * 16:(m + 1) * 16], in_=psm[m][0][0:NBS, 0:16], func=AF.Exp, scale=0.125), reads=[psm[m][1]], writes=[Esk])
                        kb.op('dve', lambda e: e.tensor_tensor(out=Esb[:, :], in0=Esf[:, :], in1=cF[0:NBS, C_SELF2:C_SELF2 + 32], op=ALU.mult), reads=[Esk, kC], writes=[Esk])
                        Ebm = Esb[:, :].rearrange("p (m h b) -> p b h m", m=2, h=4)
                        for b in range(NBS):
                            for h in range(4):
                                c_ = b * 8 + 2 * h
                                if not (DBG5 & 32): continue
                                kb.op('pe', lambda e: e.matmul(Oacc[:, c_:c_ + 2], lhsT=dvS[0:NBS, h * 128:(h + 1) * 128], rhs=Ebm[:, b, h, :], start=(b == 0 and h == 0), stop=False), reads=[kS, Esk], writes=[Oacck])
                                if not (DBG5 & 64): continue
                                kb.op('pe', lambda e: e.matmul(Lacc[:, c_:c_ + 2], lhsT=onesB[0:NBS, :], rhs=Ebm[:, b, h, :], start=(b == 0 and h == 0), stop=False), reads=[kC, Esk], writes=[Lacck])
                    for b in range(NBS):
                        if DBG5 & 2:
                            pb, pbk = bank()
                            kb.op('pe', lambda e: e.matmul(pb[:, :], lhsT=cF[0:NBS, C_SEL + b * 128:C_SEL + (b + 1) * 128], rhs=dqtok[:, :], start=True, stop=True), reads=[kC, kS], writes=[pbk])
                            evac('act', qb[:, :], pb[:, :], [pbk], [qbk])
                        if not (DBG5 & 4): continue
                        for tq0 in range(0, 128, KT):
                            kp_, kpk = kpg.next(); vp_, vpk = vpg.next()
                            kb.dma('pool', kp_[0:NPG, :, :].rearrange("p a d -> p (a d)"), ckP, reads=[idxk], writes=[kpk], indirect=idxq[0:NPG, b, tq0 // KT:tq0 // KT + 1])
                            kb.dma('pool', vp_[0:NPG, :, :].rearrange("p a d -> p (a d)"), cvP, reads=[idxk], writes=[vpk], indirect=idxq[0:NPG, b, tq0 // KT:tq0 // KT + 1])
                            for g0 in range(0, KT, G):
                                ng = G
                                sc, sck = scr.next()
                                for g in range(ng):
                                    pr, prk = prod.next()
                                    kb.op('dve', lambda e, g=g: e.tensor_tensor(out=pr[0:NPG, :], in0=kp_[0:NPG, g0 + g, :], in1=qb[0:NPG, :], op=ALU.mult), reads=[kpk, qbk], writes=[prk])
                                    kb.op('dve', lambda e, g=g: e.reduce_sum(out=sc[0:NPG, g * 8:(g + 1) * 8], in_=pr[0:NPG, :].rearrange("p (a d) -> p a d", d=64), axis=AX.X), reads=[prk], writes=[sck])
                                E, Ek = Eg.next()
                                kb.op('act', lambda e: e.activation(out=E[0:NPG, 0:ng * 8], in_=sc[0:NPG, 0:ng * 8], func=AF.Exp, scale=0.125), reads=[sck], writes=[Ek])
                                vb, vbk = vbf.next()
                                kb.op('act', lambda e: e.activation(out=vb[0:NPG, 0:ng, :], in_=vp_[0:NPG, g0:g0 + ng, :], func=AF.Copy), reads=[vpk], writes=[vbk])
                                last = (tq0 + KT >= 128) and (g0 + G >= KT) and b == NBS - 1
                                for g in range(ng):
                                    fin = last and g == ng - 1
                                    for h in range(4):
                                        kb.op('pe', lambda e, g=g, h=h: e.matmul(Oacc[:, b * 8 + 2 * h:b * 8 + 2 * h + 2], lhsT=vb[0:NPG, g, h * 128:(h + 1) * 128], rhs=E[0:NPG, g * 8 + 2 * h:g * 8 + 2 * h + 2], start=False, stop=(fin and h == 3)), reads=[vbk, Ek], writes=[Oacck])
                                    kb.op('pe', lambda e, g=g: e.matmul(Lacc[:, b * 8:b * 8 + 8], lhsT=onesB[0:NPG, :], rhs=E[0:NPG, g * 8:(g + 1) * 8], start=False, stop=fin), reads=[kC, Ek], writes=[Lacck])
                    Ov = Oacc[:, 0:32].rearrange("p (x m) -> p x m", m=2); Lv = Lacc[:, 0:32].rearrange("p (x m) -> p x m", m=2)
                    sidx = len(mtiles) - 1
                    def outs():
                        kb.op('dve', lambda e: e.scalar_tensor_tensor(out=res[:, :], in0=t0[:, 0:16], scalar=subw, in1=rl[:, 0:16], op0=ALU.mult, op1=ALU.mult), reads=[t0k, rlk, kL], writes=[resk])
                        for h in range(4):
                            kb.op('act', lambda e, h=h: e.activation(out=mixT[:, 4 + h, T:T + NBS], in_=res[:, :].rearrange("p (b h) -> p b h", h=4)[:, :, h], func=AF.Copy), reads=[resk], writes=[mixk[4 + h][sidx]])
                    if DBG5 & 16:
                        combine(Ov[:, :, 0], Lv[:, :, 0], Ov[:, :, 1], Lv[:, :, 1], [Oacck, Lacck], 16, outs)
                    kb.barrier()

        def gdn_stage(mixT, mixk, mtiles):
            with contextlib.ExitStack() as sg:
                cw = sb(sg, [128, 12, 4], F32, "cw"); gnw = sb(sg, [128, 1], F32, "gnw"); alb = sb(sg, [128, 4], F32, "alb"); dtbb = sb(sg, [128, 4], F32, "dtbb"); kG = Tk()
                cs = sb(sg, [128, 12, NBS, 4], F32, "cs"); csk = Tk()
                hist = sb(sg, [128, 12, 3], F32, "hist"); histk = [Tk() for _ in range(12)]
                Sf = sb(sg, [128, 4, 128], F32, "Sf"); Sb_ = sb(sg, [128, 4, 128], BF16, "Sb"); Sfk = [Tk() for _ in range(4)]; Sbk = [Tk() for _ in range(4)]
                Ss = sb(sg, [128, NBS * 4, 128], F32, "Ss"); Ssb = sb(sg, [128, NBS * 4, 128], BF16, "Ssb"); Ssk = [Tk() for _ in range(NBS * 4)]; Ssbk = [Tk() for _ in range(NBS * 4)]
                qkvt = sb(sg, [128, 12, 512], BF16, "qkvt"); qk = [Tk() for _ in range(12)]
                zt = sb(sg, [128, 4, 512], BF16, "zt"); zk = [Tk() for _ in range(4)]
                cbr = Ring(sg, 1, [128, 515], F32, "cb"); yr = Ring(sg, 1, [128, 512], F32, "y"); ysr = Ring(sg, 1, [128, 512], F32, "ys")
                sq1 = Ring(sg, 1, [128, 512], BF16, "sq1"); rsr = Ring(sg, 1, [128, 512], F32, "rs")
                bar = Ring(sg, 2, [64, 16], F32, "ba")
                for i in range(4):
                    kb.dma('sp', cw[:, :, i], convw[i:i + 1, :].rearrange("a (j p) -> p (a j)", p=128), writes=[kG], allow_slow_non_contiguous=True)
                kb.dma('sp', gnw[:, :], gnorm.rearrange("a p -> p a"), writes=[kG], allow_slow_non_contiguous=True)
                kb.dma('sp', alb[:, :], alog.partition_broadcast(128), writes=[kG])
                kb.dma('sp', dtbb[:, :], dtb.partition_broadcast(128), writes=[kG])
                kb.op('act', lambda e: e.activation(out=alb[:, :], in_=alb[:, :], func=AF.Exp), reads=[kG], writes=[kG])
                kb.op('dve', lambda e: e.tensor_scalar(out=alb[:, :], in0=alb[:, :], scalar1=-1.0, scalar2=None, op0=ALU.mult), reads=[kG], writes=[kG])
                for b in range(NBS):
                    for r in range(3):
                        kb.dma('sp', cs[:, :, b, r], sconv[b, r:r + 1, :].rearrange("a (j p) -> p (a j)", p=128), writes=[csk], allow_slow_non_contiguous=True)
                kb.dma('sp', convs[:, 0:2, :], sconv[:, 1:3, :])
                kb.op('pool', lambda e: e.memset(hist[:, :, :], 0.0), writes=histk)
                kb.op('pool', lambda e: e.memset(Sf[:, :, :], 0.0), writes=Sfk)
                kb.op('pool', lambda e: e.memset(Sb_[:, :, :], 0.0), writes=Sbk)
                kb.dma('sp', Ss[:, :, :], srec.rearrange("(g p) d -> p g d", p=128), writes=Ssk)
                kb.op('act', lambda e: e.activation(out=Ssb[:, :, :], in_=Ss[:, :, :], func=AF.Copy), reads=Ssk, writes=Ssbk)
                set_ring(range(8))
                triF = cF[:, C_M:C_M + 128]
                def R2(shape, dt, name, n=2): return Ring(sg, n, shape, dt, name)
                rGb = R2([64, 64], F32, "Gb"); rGt = R2([64, 64], F32, "Gt"); rnG = R2([64, 64], F32, "nG")
                rD = R2([64, 64], F32, "D"); rDT = R2([64, 64], F32, "DT"); rDs = R2([64, 64], F32, "Ds"); rL_ = R2([64, 64], F32, "Lneg")
                rM = R2([64, 64], F32, "M"); rMt = R2([64, 64], F32, "Mt", 4); rP = R2([64, 64], F32, "P")
                rAt = R2([64, 64], BF16, "At"); rXb = R2([64, 64], BF16, "Xb")
                rcol = R2([128, 8], F32, "col")
                rEr = R2([128, 64], F32, "Erow"); rqd = R2([128, 64], BF16, "qd")
                rRk = R2([64, 128], BF16, "Rk"); rkd = R2([64, 128], BF16, "kdec"); rRv = R2([64, 128], BF16, "Rv")
                ru = R2([64, 128], F32, "u"); rwT = R2([128, 64], BF16, "wT"); rvn = R2([64, 128], BF16, "vn")
                ro = R2([128, 64], F32, "o"); rsq = R2([128, 64], BF16, "sqo"); rrs = R2([128, 64], F32, "rso"); rt_ = R2([128, 64], F32, "to")

                def gdn_chunk(C, qT, kT, vT, zT, qkt, zkt, g_col, beta_col, nbeta_col, bak, Sf_a, Sb_a, Sfk_, Sbk_, out_ap, out_tok):
                    def mm(out, lhsT, rhs, reads, pk, start=True, stop=True):
                        kb.op('pe', lambda e: e.matmul(out, lhsT=lhsT, rhs=rhs, start=start, stop=stop), reads=reads, writes=[pk])
                    Gb, Gbk = rGb.next(); Gt, Gtk = rGt.next(); nG, nGk = rnG.next()
                    kb.op('dve', lambda e: e.tensor_scalar(out=Gb[0:C, 0:C], in0=onesF[0:C, 0:C], scalar1=g_col, scalar2=None, op0=ALU.mult), reads=[bak, kC], writes=[Gbk])
                    kb.op('dve', lambda e: e.tensor_scalar(out=Gt[0:C, 0:C], in0=triF[0:C, 0:C], scalar1=g_col, scalar2=None, op0=ALU.mult), reads=[bak, kC], writes=[Gtk])
                    kb.op('act', lambda e: e.activation(out=nG[0:C, 0:C], in_=Gt[0:C, 0:C], func=AF.Copy, scale=-1.0), reads=[Gtk], writes=[nGk])
                    yield
                    pd, pdk = bank()
                    mm(pd[0:C, 0:C], triF[0:C, 0:C], Gb[0:C, 0:C], [kC, Gbk], pdk, True, False)
                    mm(pd[0:C, 0:C], nonesF[0:C, 0:C], Gt[0:C, 0:C], [kC, Gtk], pdk, False, False)
                    mm(pd[0:C, 0:C], identF[0:C, 0:C], cF[0:C, C_NEG:C_NEG + C], [kC], pdk, False, True)
                    D_, Dk = rD.next()
                    kb.op('act', lambda e: e.activation(out=D_[0:C, 0:C], in_=pd[0:C, 0:C], func=AF.Exp), reads=[pdk], writes=[Dk])
                    yield
                    pt_, ptk = bank()
                    mm(pt_[0:C, 0:C], onesF[0:C, 0:C], Gt[0:C, 0:C], [kC, Gtk], ptk, True, False)
                    mm(pt_[0:C, 0:C], nG[0:C, 0:C], onesF[0:C, 0:C], [kC, nGk], ptk, False, False)
                    mm(pt_[0:C, 0:C], identF[0:C, 0:C], cF[0:C, C_NEGT:C_NEGT + C], [kC], ptk, False, True)
                    DT, DTk = rDT.next()
                    kb.op('act', lambda e: e.activation(out=DT[0:C, 0:C], in_=pt_[0:C, 0:C], func=AF.Exp), reads=[ptk], writes=[DTk])
                    yield
                    pc, pck = bank()
                    mm(pc[0:C, 0:1], triF[0:C, 0:C], g_col, [kC, bak], pck)
                    mm(pc[0:128, 1:2], onesF[0:C, 0:128], g_col, [kC, bak], pck)
                    col, colk = rcol.next()
                    kb.op('dve', lambda e: e.tensor_copy(out=col[0:C, 0:1], in_=pc[0:C, 0:1]), reads=[pck], writes=[colk])
                    kb.op('dve', lambda e: e.tensor_copy(out=col[:, 1:2], in_=pc[:, 1:2]), reads=[pck], writes=[colk])
                    kb.op('act', lambda e: e.activation(out=col[0:C, 2:3], in_=col[0:C, 0:1], func=AF.Exp), reads=[colk], writes=[colk])
                    kb.op('act', lambda e: e.activation(out=col[:, 3:4], in_=col[:, 1:2], func=AF.Exp), reads=[colk], writes=[colk])
                    kb.op('act', lambda e: e.activation(out=col[0:C, 4:5], in_=col[0:C, 0:1], func=AF.Exp, scale=-1.0, bias=col[0:C, 1:2]), reads=[colk], writes=[colk])
                    kb.op('dve', lambda e: e.tensor_tensor(out=col[0:C, 5:6], in0=col[0:C, 2:3], in1=beta_col, op=ALU.mult), reads=[colk, bak], writes=[colk])
                    yield
                    pe_, pek = bank()
                    mm(pe_[0:128, 0:C], onesF[0:C, 0:128], Gt[0:C, 0:C], [kC, Gtk], pek)
                    Er_, Erk = rEr.next()
                    kb.op('act', lambda e: e.activation(out=Er_[:, 0:C], in_=pe_[:, 0:C], func=AF.Exp), reads=[pek], writes=[Erk])
                    yield
                    pkk, pkkk = bank()
                    mm(pkk[0:C, 0:C], kT, kT, qkt, pkkk)
                    pqk, pqkk = bank()
                    mm(pqk[0:C, 0:C], kT, qT, qkt, pqkk)
                    Ds, Dsk = rDs.next()
                    kb.op('pool', lambda e: e.tensor_tensor(out=Ds[0:C, 0:C], in0=D_[0:C, 0:C], in1=cF[0:C, C_SL:C_SL + C], op=ALU.mult), reads=[Dk, kC], writes=[Dsk])
                    Ln, Lnk = rL_.next()
                    kb.op('dve', lambda e: e.scalar_tensor_tensor(out=Ln[0:C, 0:C], in0=pkk[0:C, 0:C], scalar=nbeta_col, in1=Ds[0:C, 0:C], op0=ALU.mult, op1=ALU.mult), reads=[pkkk, Dsk, bak], writes=[Lnk])
                    At, Atk = rAt.next()
                    kb.op('dve', lambda e: e.tensor_tensor(out=At[0:C, 0:C], in0=pqk[0:C, 0:C], in1=DT[0:C, 0:C], op=ALU.mult), reads=[pqkk, DTk], writes=[Atk])
                    yield
                    Xb, Xbk = rXb.next()
                    if C > 1:
                        pn_, pnk = bank()
                        mm(pn_[0:C, 0:C], Ln[0:C, 0:C], identF[0:C, 0:C], [Lnk, kC], pnk)
                        M_, Mk = rM.next(); P_, Pk = rP.next()
                        kb.op('act', lambda e: e.activation(out=M_[0:C, 0:C], in_=pn_[0:C, 0:C], func=AF.Copy), reads=[pnk], writes=[Mk])
                        kb.op('dve', lambda e: e.tensor_tensor(out=P_[0:C, 0:C], in0=pn_[0:C, 0:C], in1=identF[0:C, 0:C], op=ALU.add), reads=[pnk, kC], writes=[Pk])
                        yield
                        Mt_, Mtk = Ln, Lnk
                        nlev = 1
                        while (1 << nlev) < C: nlev += 1
                        for lev in range(1, nlev):
                            p1, p1k = bank()
                            mm(p1[0:C, 0:C], M_[0:C, 0:C], Mt_[0:C, 0:C], [Mk, Mtk], p1k)
                            Mt2, Mt2k = rMt.next()
                            kb.op('act', lambda e: e.activation(out=Mt2[0:C, 0:C], in_=p1[0:C, 0:C], func=AF.Copy), reads=[p1k], writes=[Mt2k])
                            if lev < nlev - 1:
                                p2, p2k = bank()
                                mm(p2[0:C, 0:C], Mt_[0:C, 0:C], M_[0:C, 0:C], [Mk, Mtk], p2k)
                                M2, M2k = rM.next()
                                kb.op('dve', lambda e: e.tensor_copy(out=M2[0:C, 0:C], in_=p2[0:C, 0:C]), reads=[p2k], writes=[M2k])
                            p3, p3k = bank()
                            mm(p3[0:C, 0:C], Mt2[0:C, 0:C], P_[0:C, 0:C], [Mt2k, Pk], p3k)
                            P2, P2k = rP.next()
                            kb.op('dve', lambda e: e.tensor_tensor(out=P2[0:C, 0:C], in0=p3[0:C, 0:C], in1=P_[0:C, 0:C], op=ALU.add), reads=[p3k, Pk], writes=[P2k])
                            if lev < nlev - 1: M_, Mk = M2, M2k
                            Mt_, Mtk = Mt2, Mt2k; P_, Pk = P2, P2k
                            yield
                        kb.op('act', lambda e: e.activation(out=Xb[0:C, 0:C], in_=P_[0:C, 0:C], func=AF.Copy), reads=[Pk], writes=[Xbk])
                        yield
                    else:
                        kb.op('act', lambda e: e.activation(out=Xb[0:1, 0:1], in_=identF[0:1, 0:1], func=AF.Copy), reads=[kC], writes=[Xbk])
                    yield
                    pk_, pkk_ = bank()
                    mm(pk_[0:C, 0:128], kT, identB, qkt + [kC], pkk_)
                    Rk, Rkk = rRk.next(); kd, kdk = rkd.next()
                    kb.op('act', lambda e: e.activation(out=Rk[0:C, :], in_=pk_[0:C, 0:128], func=AF.Copy, scale=col[0:C, 5:6]), reads=[pkk_, colk], writes=[Rkk])
                    kb.op('dve', lambda e: e.tensor_scalar(out=kd[0:C, :], in0=pk_[0:C, 0:128], scalar1=col[0:C, 4:5], scalar2=None, op0=ALU.mult), reads=[pkk_, colk], writes=[kdk])
                    yield
                    pv_, pvk = bank()
                    mm(pv_[0:C, 0:128], vT, identB, qkt + [kC], pvk)
                    Rv, Rvk = rRv.next()
                    kb.op('act', lambda e: e.activation(out=Rv[0:C, :], in_=pv_[0:C, 0:128], func=AF.Copy, scale=beta_col), reads=[pvk, bak], writes=[Rvk])
                    yield
                    pu, puk = bank()
                    mm(pu[0:C, 0:128], Xb[0:C, 0:C], Rv[0:C, :], [Xbk, Rvk], puk)
                    u_, uk = ru.next()
                    kb.op('act', lambda e: e.activation(out=u_[0:C, :], in_=pu[0:C, 0:128], func=AF.Copy), reads=[puk], writes=[uk])
                    yield
                    pw, pwk = bank()
                    mm(pw[0:128, 0:C], Rk[0:C, :], Xb[0:C, 0:C], [Rkk, Xbk], pwk)
                    wT, wTk = rwT.next()
                    kb.op('dve', lambda e: e.tensor_copy(out=wT[:, 0:C], in_=pw[:, 0:C]), reads=[pwk], writes=[wTk])
                    yield
                    pws, pwsk = bank()
                    mm(pws[0:C, 0:128], wT[:, 0:C], Sb_a, [wTk, Sbk_], pwsk)
                    vn, vnk = rvn.next()
                    kb.op('dve', lambda e: e.tensor_tensor(out=vn[0:C, :], in0=u_[0:C, :], in1=pws[0:C, 0:128], op=ALU.subtract), reads=[uk, pwsk], writes=[vnk])
                    yield
                    qd, qdk = rqd.next()
                    kb.op('dve', lambda e: e.tensor_tensor(out=qd[:, 0:C], in0=qT, in1=Er_[:, 0:C], op=ALU.mult), reads=qkt + [Erk], writes=[qdk])
                    po, pok = bank()
                    mm(po[0:128, 0:C], Sb_a, qd[:, 0:C], [Sbk_, qdk], pok, True, False)
                    mm(po[0:128, 0:C], vn[0:C, :], At[0:C, 0:C], [vnk, Atk], pok, False, True)
                    o_, ok_ = ro.next(); sq_, sqk_ = rsq.next(); rs_, rsk_ = rrs.next(); t_, tk_ = rt_.next()
                    kb.op('dve', lambda e: e.tensor_copy(out=o_[:, 0:C], in_=po[:, 0:C]), reads=[pok], writes=[ok_])
                    kb.op('act', lambda e: e.activation(out=sq_[:, 0:C], in_=po[:, 0:C], func=AF.Square), reads=[pok], writes=[sqk_])
                    yield
                    pS, pSk = bank()
                    mm(pS[0:128, 0:128], kd[0:C, :], vn[0:C, :], [kdk, vnk], pSk)
                    kb.op('dve', lambda e: e.scalar_tensor_tensor(out=Sf_a, in0=Sf_a, scalar=col[:, 3:4], in1=pS[:, 0:128], op0=ALU.mult, op1=ALU.add), reads=[Sfk_, colk, pSk], writes=[Sfk_])
                    kb.op('pool', lambda e: e.tensor_copy(out=Sb_a, in_=Sf_a), reads=[Sfk_], writes=[Sbk_])
                    yield
                    pn2, pn2k = bank()
                    mm(pn2[:, 0:C], onesB[:, :], sq_[:, 0:C], [kC, sqk_], pn2k)
                    kb.op('act', lambda e: e.activation(out=rs_[:, 0:C], in_=pn2[:, 0:C], func=AF.Sqrt, bias=1e-6, scale=1.0 / 128), reads=[pn2k], writes=[rsk_])
                    kb.op('dve', lambda e: e.reciprocal(out=rs_[:, 0:C], in_=rs_[:, 0:C]), reads=[rsk_], writes=[rsk_])
                    kb.op('dve', lambda e: e.scalar_tensor_tensor(out=t_[:, 0:C], in0=o_[:, 0:C], scalar=gnw[:, 0:1], in1=rs_[:, 0:C], op0=ALU.mult, op1=ALU.mult), reads=[ok_, rsk_, kG], writes=[tk_])
                    kb.op('pool', lambda e: e.tensor_tensor(out=out_ap, in0=t_[:, 0:C], in1=zT, op=ALU.mult), reads=[tk_, zkt], writes=[out_tok])

                for ti, (c0, w) in enumerate(mtiles):
                    sample = c0 >= T
                    norm_cols((sb_sq8, sb_sq8k, rsr.t[0], rsr.k[0]), KC, w, lambda kc: xT[:, kc, c0:c0 + w], xtoks(range(KC), c0, w), lambda kc: nw[:, 4, kc:kc + 1],
                              lambda kc: hT[:, kc, 0:w], [hTk[0]], 1e-6, float(D))
                    for jp in range(8):
                        base = jp * 256 if jp < 6 else O1 + (jp - 6) * 256
                        slot = wslot()
                        s2 = load_lhsT(ewin[:, base:base + 256], KC, 256, slot, 0)
                        for half in range(2):
                            pb, pbk = bank()
                            for kc in range(KC):
                                kb.op('pe', lambda e, kc=kc: e.matmul(pb[:, 0:w], lhsT=s2[:, kc, half * 128:(half + 1) * 128], rhs=hT[:, kc, 0:w], start=(kc == 0), stop=(kc == KC - 1)), reads=[slot[1], hTk[0]], writes=[pbk])
                            if jp >= 6:
                                j = (jp - 6) * 2 + half
                                kb.op('act', lambda e: e.activation(out=zt[:, j, 0:w], in_=pb[:, 0:w], func=AF.Silu), reads=[pbk], writes=[zk[j]])
                                continue
                            j = jp * 2 + half
                            y_, yk_ = yr.next()
                            if not sample:
                                cb, cbk = cbr.next()
                                kb.op('pool', lambda e: e.tensor_copy(out=cb[:, 0:3], in_=hist[:, j, :]), reads=[histk[j]], writes=[cbk])
                                evac('act', cb[:, 3:3 + w], pb[:, 0:w], [pbk], [cbk])
                                kb.op('pool', lambda e: e.tensor_copy(out=hist[:, j, :], in_=cb[:, w:w + 3]), reads=[cbk], writes=[histk[j]])
                                src_i = lambda i: cb[:, i:i + w]
                                rd = [cbk]
                            else:
                                evac('act', cs[:, j, :, 3], pb[:, 0:w], [pbk], [csk])
                                src_i = lambda i: cs[:, j, :, i]
                                rd = [csk]
                            kb.op('dve', lambda e: e.tensor_scalar(out=y_[:, 0:w], in0=src_i(0), scalar1=cw[:, j, 0:1], scalar2=None, op0=ALU.mult), reads=rd + [kG], writes=[yk_])
                            for i in range(1, 4):
                                kb.op('dve', lambda e, i=i: e.scalar_tensor_tensor(out=y_[:, 0:w], in0=src_i(i), scalar=cw[:, j, i:i + 1], in1=y_[:, 0:w], op0=ALU.mult, op1=ALU.add), reads=rd + [kG, yk_], writes=[yk_])
                            if j >= 8:
                                kb.op('act', lambda e: e.activation(out=qkvt[:, j, 0:w], in_=y_[:, 0:w], func=AF.Silu), reads=[yk_], writes=[qk[j]])
                                continue
                            ys_, ysk_ = ysr.next(); sq_, sqk_ = sq1.next(); rs_, rsk_ = rsr.next()
                            kb.op('act', lambda e: e.activation(out=ys_[:, 0:w], in_=y_[:, 0:w], func=AF.Silu), reads=[yk_], writes=[ysk_])
                            kb.op('act', lambda e: e.activation(out=sq_[:, 0:w], in_=ys_[:, 0:w], func=AF.Square), reads=[ysk_], writes=[sqk_])
                            pn, pnk = bank()
                            kb.op('pe', lambda e: e.matmul(pn[:, 0:w], lhsT=onesB[:, :], rhs=sq_[:, 0:w], start=True, stop=True), reads=[kC, sqk_], writes=[pnk])
                            kb.op('act', lambda e: e.activation(out=rs_[:, 0:w], in_=pn[:, 0:w], func=AF.Sqrt, bias=1e-6, scale=1.0), reads=[pnk], writes=[rsk_])
                            kb.op('dve', lambda e: e.reciprocal(out=rs_[:, 0:w], in_=rs_[:, 0:w]), reads=[rsk_], writes=[rsk_])
                            scl = float(128 ** -0.5) if j < 4 else 1.0
                            kb.op('dve', lambda e: e.scalar_tensor_tensor(out=qkvt[:, j, 0:w], in0=ys_[:, 0:w], scalar=scl, in1=rs_[:, 0:w], op0=ALU.mult, op1=ALU.mult), reads=[ysk_, rsk_], writes=[qk[j]])
                    if sample or c0 + w == T:
                        M = NBS if sample else 3
                        l0 = 0 if sample else w - 3
                        for n3 in range(3):
                            slot = wslot()
                            s4 = load_lhsT(ewin[:, n3 * 512:(n3 + 1) * 512], KC, 512, slot, 0)
                            pb, pbk = bank()
                            for kc in range(KC):
                                kb.op('pe', lambda e, kc=kc: e.matmul(pb[0:M, :], lhsT=hT[:, kc, l0:l0 + M], rhs=s4[:, kc, :], start=(kc == 0), stop=(kc == KC - 1)), reads=[slot[1], hTk[0]], writes=[pbk])
                            sg_, sgk_ = ysr.next()
                            evac('act', sg_[0:M, :], pb[0:M, :], [pbk], [sgk_])
                            dst = convs[:, 2, n3 * 512:(n3 + 1) * 512] if sample else convp[:, n3 * 512:(n3 + 1) * 512]
                            kb.dma('sp', dst, sg_[0:M, :], reads=[sgk_])
                    slot = wslot()
                    bas = load_lhsT(ewin[:, O2:O2 + 8], KC, 8, slot, 0)
                    chunks = [(b, 1) for b in range(NBS)] if sample else [(cc, 64) for cc in range(0, w, 64)]
                    for ci, (cc, C) in enumerate(chunks):
                        pb, pbk = bank()
                        for kc in range(KC):
                            kb.op('pe', lambda e, kc=kc: e.matmul(pb[0:C, 0:8], lhsT=hT[:, kc, cc:cc + C], rhs=bas[:, kc, :], start=(kc == 0), stop=(kc == KC - 1)), reads=[slot[1], hTk[0]], writes=[pbk])
                        ba, bak = bar.next()
                        kb.op('act', lambda e: e.activation(out=ba[0:C, 0:4], in_=pb[0:C, 0:4], func=AF.Sigmoid), reads=[pbk], writes=[bak])
                        kb.op('dve', lambda e: e.tensor_tensor(out=ba[0:C, 4:8], in0=pb[0:C, 4:8], in1=dtbb[0:C, :], op=ALU.add), reads=[pbk, kG], writes=[bak])
                        kb.op('act', lambda e: e.activation(out=ba[0:C, 4:8], in_=ba[0:C, 4:8], func=AF.Exp), reads=[bak], writes=[bak])
                        kb.op('act', lambda e: e.activation(out=ba[0:C, 4:8], in_=ba[0:C, 4:8], func=AF.Ln, bias=1.0, scale=1.0), reads=[bak], writes=[bak])
                        kb.op('dve', lambda e: e.tensor_tensor(out=ba[0:C, 8:12], in0=ba[0:C, 4:8], in1=alb[0:C, :], op=ALU.mult), reads=[bak, kG], writes=[bak])
                        kb.op('dve', lambda e: e.tensor_scalar(out=ba[0:C, 12:16], in0=ba[0:C, 0:4], scalar1=-1.0, scalar2=None, op0=ALU.mult), reads=[bak], writes=[bak])
                        for hp in range(2):
                            gens = []
                            for h in (2 * hp, 2 * hp + 1):
                                if sample:
                                    si = cc * 4 + h
                                    Sfa, Sba, Sfk_, Sbk_ = Ss[:, si, :], Ssb[:, si, :], Ssk[si], Ssbk[si]
                                else:
                                    Sfa, Sba, Sfk_, Sbk_ = Sf[:, h, :], Sb_[:, h, :], Sfk[h], Sbk[h]
                                gens.append(gdn_chunk(C, qkvt[:, h, cc:cc + C], qkvt[:, 4 + h, cc:cc + C], qkvt[:, 8 + h, cc:cc + C], zt[:, h, cc:cc + C],
                                                      [qk[h], qk[4 + h], qk[8 + h]], zk[h], ba[0:C, 8 + h:9 + h], ba[0:C, h:h + 1], ba[0:C, 12 + h:13 + h], bak, Sfa, Sba, Sfk_, Sbk_,
                                                      mixT[:, h, c0 + cc:c0 + cc + C], mixk[h][ti]))
                            while gens:
                                for g_ in list(gens):
                                    try:
                                        next(g_)
                                    except StopIteration:
                                        gens.remove(g_)
                    if c0 + w == T:
                        kb.dma('sp', recp.rearrange("(h p) d -> p h d", p=128), Sf[:, :, :], reads=Sfk)
                kb.dma('sp', recs.rearrange("(g p) d -> p g d", p=128), Ss[:, :, :], reads=Ssk)
                kb.barrier()
        for layer in range(2):
            if "ffn" in stages: ffn(layer * 2 + 0)
            if layer == 0 and "even" in stages: even_mixer()
            if layer == 1 and "odd" in stages: odd_mixer()
            if "ffn" in stages: ffn(layer * 2 + 1)
        final()
        kb.finish()
        print(f"[build] instructions={kb.nins} waits={kb.nwait} cnt={kb.cnt}", flush=True)
    return nc


_CACHE = {}

def kernel(x_prompt, x_sample, state_gdn_conv, state_gdn_rec, cache_diff_k, cache_diff_v, cache_mla_ckv, cache_mla_kpe, page_table,
           ffn_norm, mix_norm, ffn_w_gate, ffn_w_up, ffn_w_down, even_w_in, gdn_conv_w, gdn_a_log, gdn_dt_bias, gdn_norm, diff_lambda,
           diff_subln, even_w_out, mla_w_in, mla_q_norm, mla_w_uq, mla_kv_norm, mla_w_ukv, mla_w_out, final_norm, _stages=("ffn", "even", "odd")):
    f = lambda a: np.ascontiguousarray(np.asarray(a), dtype=np.float32)
    x_prompt = f(x_prompt); x_sample = f(x_sample)
    B, T, _ = x_prompt.shape
    NPOOL = cache_diff_k.shape[1]
    page_table = np.ascontiguousarray(np.asarray(page_table), dtype=np.int32)
    NPG = page_table.shape[1]
    n = 8
    assert B == n and x_sample.shape[0] == n * NBS
    key = (T, NPG, NPOOL, tuple(_stages))
    if key not in _CACHE:
        _CACHE[key] = build(T, NPG, NPOOL, stages=_stages)
    nc = _CACHE[key]
    cst, rope = make_consts(T, NPG * 128)
    shared = dict(
        ck=f(cache_diff_k).reshape(NPOOL * 128, 512), cv=f(cache_diff_v).reshape(NPOOL * 128, 512),
        cc=f(cache_mla_ckv).reshape(NPOOL * 128, 256), cp=f(cache_mla_kpe).reshape(NPOOL * 128, 64),
        ffn_norm=f(ffn_norm).reshape(4, D), mix_norm=f(mix_norm).reshape(2, D), final_norm=f(final_norm).reshape(1, D),
        wg=f(ffn_w_gate).reshape(4 * D, DFF), wu=f(ffn_w_up).reshape(4 * D, DFF), wd=f(ffn_w_down).reshape(4 * DFF, D),
        ewin=f(even_w_in).reshape(D, EVEN_IN), convw=f(gdn_conv_w).reshape(4, 1536), alog=f(gdn_a_log).reshape(1, 4),
        dtb=f(gdn_dt_bias).reshape(1, 4), gnorm=f(gdn_norm).reshape(1, 128), dlam=f(diff_lambda).reshape(1, 256),
        subln=f(diff_subln).reshape(1, 128), ewout=f(even_w_out).reshape(D, D), mwin=f(mla_w_in).reshape(D, MLA_IN),
        mqn=f(mla_q_norm).reshape(1, 384), muq=f(mla_w_uq).reshape(384, 1536), mkvn=f(mla_kv_norm).reshape(1, 256),
        mukv=f(mla_w_ukv).reshape(256, 2048), mwout=f(mla_w_out).reshape(D, D), cst=cst, ropeT=rope.reshape(64, -1))
    sconv = f(state_gdn_conv); srec = f(state_gdn_rec)
    in_maps = []
    for c in range(n):
        m = dict(shared)
        m["xp"] = x_prompt[c]; m["xs"] = x_sample[c * NBS:(c + 1) * NBS, 0, :]
        m["sconv"] = sconv[0, c * NBS:(c + 1) * NBS]; m["srec"] = srec[0, c * NBS:(c + 1) * NBS].reshape(NBS * 4 * 128, 128)
        m["pt"] = page_table[c * NBS:(c + 1) * NBS].reshape(1, NBS * NPG)
        in_maps.append(m)
    res = run_bass_kernel_spmd(nc, in_maps, core_ids=list(range(n)))
    R = res.results
    cat = lambda k: np.stack([R[c][k] for c in range(n)], 0)
    y_p = cat("yp")
    y_s = cat("ys").reshape(n * NBS, 1, D)
    conv_p = cat("convp")[None]
    conv_s = cat("convs").reshape(1, n * NBS, 3, 1536)
    rec_p = cat("recp").reshape(1, n, 4, 128, 128)
    rec_s = cat("recs").reshape(1, n * NBS, 4, 128, 128)
    k_p = cat("kp").reshape(1, n, T, 8, 64); k_s = cat("ks").reshape(1, n * NBS, 1, 8, 64)
    v_p = cat("vp").reshape(1, n, T, 4, 128); v_s = cat("vs").reshape(1, n * NBS, 1, 4, 128)
    ckv_p = cat("ckvp").reshape(1, n, T, 256); ckv_s = cat("ckvs").reshape(1, n * NBS, 1, 256)
    kpe_p = cat("kpep").reshape(1, n, T, 64); kpe_s = cat("kpes").reshape(1, n * NBS, 1, 64)
    return (y_p, y_s, conv_p, conv_s, rec_p, rec_s, k_p, k_s, v_p, v_s, ckv_p, ckv_s, kpe_p, kpe_s)
```

```python
import contextlib, os
import numpy as np
DBG = int(os.environ.get('KDBG', '9')); DBG2 = int(os.environ.get('KDBG2', '9')); DBG3 = int(os.environ.get('KDBG3', '7')); DBG4 = int(os.environ.get('KDBG4', '3')); DBG5 = int(os.environ.get('KDBG5', '127'))
import concourse.bass as bass
import concourse.mybir as mybir
from concourse.bass_utils import run_bass_kernel_spmd

F32 = mybir.dt.float32; BF16 = mybir.dt.bfloat16; I32 = mybir.dt.int32
AF = mybir.ActivationFunctionType; ALU = mybir.AluOpType; AX = mybir.AxisListType

D = 1024; KC = 8; DFF = 2816; FC = 22
NBS = 4
EVEN_IN = 3592; O1 = 1536; O2 = 2048; O4 = 2056; O5 = 2568; O6 = 3080
MLA_IN = 704


class Tk:
    __slots__ = ("w", "r", "x")
    def __init__(self, x=False): self.w = None; self.r = {}; self.x = x


class KB:
    def __init__(self, nc, es):
        self.nc = nc
        self.E = {'pe': nc.tensor, 'act': nc.scalar, 'dve': nc.vector, 'pool': nc.gpsimd, 'sp': nc.sync}
        self.sem = {e: es.enter_context(nc.semaphore("s_" + e)) for e in self.E}
        self.cnt = {e: 0 for e in self.E}
        self.known = {e: {} for e in self.E}
        self.dq = {}
        for q, n in (('sp', 20), ('pool', 24), ('act', 4)):
            self.dq[q] = dict(sems=[es.enter_context(nc.semaphore(f"d_{q}{i}")) for i in range(n)], i=0, val=[0] * n)
        self.nwait = 0; self.nins = 0
    def _wait(self, e, ev):
        sem, v = ev
        k = id(sem)
        if self.known[e].get(k, 0) >= v: return
        self.E[e].wait_ge(sem, v); self.known[e][k] = v; self.nwait += 1
    def _deps(self, e, reads, writes):
        evs = {}
        def add(ev):
            k = id(ev[0])
            if k not in evs or evs[k][1] < ev[1]: evs[k] = ev
        for t in reads:
            if t.w: add(t.w)
            if t.x:
                for ev in t.r.values(): add(ev)
        for t in writes:
            if t.w: add(t.w)
            for ev in t.r.values(): add(ev)
        for ev in evs.values():
            if e == 'pe' and ev[0] is self.sem['pe']: continue
            self._wait(e, ev)
    def _record(self, ev, reads, writes):
        k = id(ev[0])
        for t in reads: t.r[k] = ev
        for t in writes: t.w = ev; t.r = {}
    def op(self, e, fn, reads=(), writes=()):
        self._deps(e, reads, writes)
        ins = fn(self.E[e])
        self.cnt[e] += 1; self.nins += 1
        ins.then_inc(self.sem[e], 1)
        self._record((self.sem[e], self.cnt[e]), reads, writes)
    def dma(self, q, out, in_, reads=(), writes=(), indirect=None, eoff=0, **kw):
        self._deps(q, reads, writes)
        d = self.dq[q]; i = d['i']; d['i'] = (i + 1) % len(d['sems'])
        sem = d['sems'][i]
        if d['val'][i] > 0: self._wait(q, (sem, d['val'][i]))
        if indirect is not None:
            ins = self.nc.gpsimd.indirect_dma_start(out=out, out_offset=None, in_=in_,
                                                    in_offset=bass.IndirectOffsetOnAxis(ap=indirect, axis=0), element_offset=eoff)
        else:
            ins = self.E[q].dma_start(out=out, in_=in_, **kw)
        d['val'][i] += 16; self.nins += 1
        ins.then_inc(sem, 16)
        self._record((sem, d['val'][i]), reads, writes)
    def barrier(self):
        evs = []
        for q, d in self.dq.items():
            for sem, v in zip(d['sems'], d['val']):
                if v > 0: evs.append((sem, v))
        for e in self.E:
            if self.cnt[e] > 0: evs.append((self.sem[e], self.cnt[e]))
        for e in self.E:
            for ev in evs:
                if ev[0] is self.sem[e]: continue
                self._wait(e, ev)
    def finish(self):
        self.barrier()


def col_tiles(c0, c1, step=512):
    out = []
    c = c0
    while c < c1:
        w = min(step - (c % step), c1 - c)
        out.append((c, w)); c += w
    return out


C_I, C_M, C_NEG, C_NEGT, C_SL, C_SELF, C_SEL, C_SELF2, C_W = 0, 128, 256, 320, 384, 448, 480, 992, 1024

def make_consts(T, past_len):
    c = np.zeros((128, C_W), np.float32)
    c[:, C_I:C_I + 128] = np.eye(128)
    k = np.arange(128)
    c[:, C_M:C_M + 128] = (k[None, :] >= k[:, None])
    i64 = np.arange(64)
    c[:64, C_NEG:C_NEG + 64] = np.where(i64[None, :] <= i64[:, None], 0.0, -1e30)
    c[:64, C_NEGT:C_NEGT + 64] = np.where(i64[None, :] >= i64[:, None], 0.0, -1e30)
    c[:64, C_SL:C_SL + 64] = (i64[None, :] < i64[:, None])
    for b in range(4):
        c[b, C_SELF + 8 * b:C_SELF + 8 * b + 8] = 1.0
        c[b, C_SEL + 128 * b:C_SEL + 128 * (b + 1)] = 1.0
        c[b, C_SELF2 + b:C_SELF2 + 32:4] = 1.0
    NT = T + 4
    pos = np.concatenate([np.arange(T), np.full(4, past_len)]).astype(np.float32)
    inv = np.exp(-np.log(10000.0) * np.arange(0, 64, 2, dtype=np.float32) / 64).astype(np.float32)
    ang = (pos[None, :] * inv[:, None]).astype(np.float32)
    rope = np.zeros((64, 2, NT), np.float32)
    rope[:32, 0] = np.cos(ang); rope[32:, 0] = np.cos(ang)
    rope[:32, 1] = -np.sin(ang); rope[32:, 1] = np.sin(ang)
    return c, rope


def build(T, NPG, NPOOL, stages=("ffn", "even", "odd")):
    NT = T + NBS
    NTT = T // 128
    nc = bass.Bass("TRN2", target_bir_lowering=False)
    def din(name, shape, dt=F32): return nc.dram_tensor(name, list(shape), dt, kind="ExternalInput").ap()
    def dout(name, shape): return nc.dram_tensor(name, list(shape), F32, kind="ExternalOutput").ap()
    xp = din("xp", [T, D]); xs = din("xs", [NBS, D])
    sconv = din("sconv", [NBS, 3, 1536]); srec = din("srec", [NBS * 4 * 128, 128])
    ck = din("ck", [NPOOL * 128, 512]); cv = din("cv", [NPOOL * 128, 512])
    cc = din("cc", [NPOOL * 128, 256]); cp = din("cp", [NPOOL * 128, 64])
    pt = din("pt", [1, NBS * NPG], I32)
    ffn_norm = din("ffn_norm", [4, D]); mix_norm = din("mix_norm", [2, D]); final_norm = din("final_norm", [1, D])
    wg = din("wg", [4 * D, DFF]); wu = din("wu", [4 * D, DFF]); wd = din("wd", [4 * DFF, D])
    ewin = din("ewin", [D, EVEN_IN]); convw = din("convw", [4, 1536])
    alog = din("alog", [1, 4]); dtb = din("dtb", [1, 4]); gnorm = din("gnorm", [1, 128])
    dlam = din("dlam", [1, 256]); subln = din("subln", [1, 128]); ewout = din("ewout", [D, D])
    mwin = din("mwin", [D, MLA_IN]); mqn = din("mqn", [1, 384]); muq = din("muq", [384, 1536])
    mkvn = din("mkvn", [1, 256]); mukv = din("mukv", [256, 2048]); mwout = din("mwout", [D, D])
    cst = din("cst", [128, C_W]); ropeT = din("ropeT", [64, 2 * NT])
    yp = dout("yp", [T, D]); ys = dout("ys", [NBS, D])
    convp = dout("convp", [3, 1536]); convs = dout("convs", [NBS, 3, 1536])
    recp = dout("recp", [4 * 128, 128]); recs = dout("recs", [NBS * 4 * 128, 128])
    kp = dout("kp", [T, 512]); ks = dout("ks", [NBS, 512]); vp = dout("vp", [T, 512]); vs = dout("vs", [NBS, 512])
    ckvp = dout("ckvp", [T, 256]); ckvs = dout("ckvs", [NBS, 256]); kpep = dout("kpep", [T, 64]); kpes = dout("kpes", [NBS, 64])

    es = contextlib.ExitStack()
    with es:
        kb = KB(nc, es)
        uid = [0]
        def sb(stack, shape, dt, name="t"):
            uid[0] += 1
            return stack.enter_context(nc.sbuf_tensor(f"{name}{uid[0]}", list(shape), dt))
        xT = sb(es, [128, KC, NT], F32, "xT")
        NBLK = (NT + 127) // 128
        xTk = [[Tk() for _ in range(NBLK)] for _ in range(KC)]
        HW = max(NT - T // 2, min(512, NT))
        hT = sb(es, [128, KC, HW], BF16, "hT")
        hTk = [Tk() for _ in range((HW + 511) // 512 + 1)]
        SLOT = 4096
        wsl = [sb(es, [128, SLOT], BF16, "wsl") for _ in range(3)]
        wslk = [Tk() for _ in range(3)]
        wsi = [0]
        cF = sb(es, [128, C_W], F32, "cF"); cB = sb(es, [128, 256], BF16, "cB")
        onesF = sb(es, [128, 128], F32, "onesF"); nonesF = sb(es, [128, 128], F32, "nonesF"); onesB = sb(es, [128, 128], BF16, "onesB")
        nw = sb(es, [128, 7, KC], F32, "nw")
        kC = Tk()
        sb_sq8 = sb(es, [128, KC, 512], BF16, "sq8"); sb_sq8k = Tk()
        idxk = Tk()
        idxp = sb(es, [128, NBS], I32, "idxp"); idxq = sb(es, [128, NBS, 32], I32, "idxq"); idxm = sb(es, [128, NBS, 16], I32, "idxm")
        assert NPG <= 128
        es_setup = contextlib.ExitStack()
        ptb = sb(es_setup, [128, NBS * NPG], I32, "ptb"); iot = sb(es_setup, [128, 1], I32, "iot")
        PS = [es.enter_context(nc.psum_tensor(f"ps{i}", [128, 512], F32)) for i in range(8)]
        PSk = [Tk(True) for _ in range(8)]
        pring = [list(range(8)), 0]
        def set_ring(banks): pring[0] = list(banks); pring[1] = 0
        def bank():
            b = pring[0][pring[1] % len(pring[0])]; pring[1] += 1
            return PS[b], PSk[b]
        def xtoks(kcs, c0, w):
            return [xTk[kc][blk] for kc in kcs for blk in range(c0 // 128, (c0 + w - 1) // 128 + 1)]
        identF = cF[:, C_I:C_I + 128]
        identB = cB[:, 0:128]; maskB = cB[:, 128:256]

        kb.dma('sp', cF[:, :], cst, writes=[kC])
        for i, src in enumerate([ffn_norm[0:1, :], ffn_norm[1:2, :], ffn_norm[2:3, :], ffn_norm[3:4, :], mix_norm[0:1, :], mix_norm[1:2, :], final_norm[0:1, :]]):
            kb.dma('sp', nw[:, i, :], src.rearrange("a (kc p) -> p (a kc)", p=128), writes=[kC], allow_slow_non_contiguous=True)
        kb.op('pool', lambda e: e.memset(onesF[:, :], 1.0), writes=[kC])
        kb.op('pool', lambda e: e.memset(nonesF[:, :], -1.0), writes=[kC])
        kb.op('pool', lambda e: e.memset(onesB[:, :], 1.0), writes=[kC])
        kb.op('act', lambda e: e.activation(out=cB[:, :], in_=cF[:, 0:256], func=AF.Copy), reads=[kC], writes=[kC])
        kb.dma('sp', ptb[:, :], pt.partition_broadcast(128), writes=[idxk])
        kb.dma('sp', idxp[0:NPG, :], pt.rearrange("a (b j) -> j (a b)", j=NPG), writes=[idxk], allow_slow_non_contiguous=True)
        kb.op('pool', lambda e: e.iota(iot[:, :], pattern=[[0, 1]], base=0, channel_multiplier=1), writes=[idxk])
        iot32 = sb(es_setup, [128, 32], I32, "iot32"); idxs = sb(es_setup, [128, 2, NBS], I32, "idxs")
        kb.op('pool', lambda e: e.iota(iot32[:, :], pattern=[[1, 32]], base=0, channel_multiplier=0), writes=[idxk])
        kb.op('dve', lambda e: e.tensor_scalar(out=idxs[0:NPG, 0, :], in0=idxp[0:NPG, :], scalar1=32, scalar2=0, op0=ALU.mult, op1=ALU.add), reads=[idxk], writes=[idxk])
        kb.op('dve', lambda e: e.tensor_scalar(out=idxs[0:NPG, 1, :], in0=idxp[0:NPG, :], scalar1=16, scalar2=0, op0=ALU.mult, op1=ALU.add), reads=[idxk], writes=[idxk])
        for b in range(NBS):
            kb.op('dve', lambda e, b=b: e.tensor_scalar(out=idxq[0:NPG, b, :], in0=iot32[0:NPG, :], scalar1=1, scalar2=idxs[0:NPG, 0, b:b + 1], op0=ALU.mult, op1=ALU.add), reads=[idxk], writes=[idxk])
            kb.op('dve', lambda e, b=b: e.tensor_scalar(out=idxm[0:NPG, b, :], in0=iot32[0:NPG, 0:16], scalar1=1, scalar2=idxs[0:NPG, 1, b:b + 1], op0=ALU.mult, op1=ALU.add), reads=[idxk], writes=[idxk])
        kb.barrier()
        es_setup.close()

        def wslot():
            i = wsi[0] % 3; wsi[0] += 1
            return wsl[i], wslk[i]

        def load_lhsT(w_ap, nk, ncol, slot, off=0):
            sl, slk = slot
            dst = sl[:, off:off + nk * ncol].rearrange("p (k n) -> p k n", k=nk)
            kb.dma('pool', dst, w_ap.rearrange("(k p) n -> p k n", p=128), writes=[slk])
            return dst

        def norm_cols(stack_tmp, nk, w, src, src_toks, wcol, dst, dst_toks, eps, dn, extra_scale=1.0):
            sq, sqk, rstd, rstdk = stack_tmp
            for kc in range(nk):
                kb.op('act', lambda e, kc=kc: e.activation(out=sq[:, kc, 0:w], in_=src(kc), func=AF.Square), reads=src_toks, writes=[sqk])
            pb, pbk = bank()
            for kc in range(nk):
                kb.op('pe', lambda e, kc=kc: e.matmul(pb[:, 0:w], lhsT=onesB[:, :], rhs=sq[:, kc, 0:w], start=(kc == 0), stop=(kc == nk - 1)), reads=[sqk, kC], writes=[pbk])
            kb.op('act', lambda e: e.activation(out=rstd[:, 0:w], in_=pb[:, 0:w], func=AF.Sqrt, bias=eps, scale=1.0 / dn), reads=[pbk], writes=[rstdk])
            kb.op('dve', lambda e: e.reciprocal(out=rstd[:, 0:w], in_=rstd[:, 0:w]), reads=[rstdk], writes=[rstdk])
            for kc in range(nk):
                kb.op('dve', lambda e, kc=kc: e.scalar_tensor_tensor(out=dst(kc), in0=src(kc), scalar=wcol(kc), in1=rstd[:, 0:w], op0=ALU.mult, op1=ALU.mult), reads=src_toks + [rstdk, kC], writes=dst_toks)

        with contextlib.ExitStack() as st:
            xin = [sb(st, [128, D], F32, "xin") for _ in range(2)]
            xink = [Tk(), Tk()]
            set_ring(range(8))
            for tt in range(NTT + 1):
                rows = 128 if tt < NTT else NBS
                xi, xik = xin[tt % 2], xink[tt % 2]
                src = xp[tt * 128:(tt + 1) * 128, :] if tt < NTT else xs[:, :]
                kb.dma('sp', xi[0:rows, :], src, writes=[xik])
                for half in range(2):
                    pb, pbk = bank()
                    for j in range(4):
                        c = half * 4 + j
                        kb.op('pe', lambda e, c=c, j=j: e.matmul(pb[:, j * 128:j * 128 + rows], lhsT=xi[0:rows, c * 128:(c + 1) * 128], rhs=identF[0:rows, 0:rows], start=True, stop=True), reads=[xik, kC], writes=[pbk])
                    eng = 'act' if half == 0 else 'dve'
                    o = xT[:, half * 4:half * 4 + 4, tt * 128:tt * 128 + rows]
                    i_ = pb[:, :].rearrange("p (a b) -> p a b", a=4)[:, :, 0:rows]
                    wt = [xTk[half * 4 + j][tt] for j in range(4)]
                    if eng == 'act':
                        kb.op('act', lambda e: e.activation(out=o, in_=i_, func=AF.Copy), reads=[pbk], writes=wt)
                    else:
                        kb.op('dve', lambda e: e.tensor_copy(out=o, in_=i_), reads=[pbk], writes=wt)
            kb.barrier()

        def ffn(idx):
            with contextlib.ExitStack() as st:
                AW = HW
                act = sb(st, [128, FC, AW], BF16, "act"); actk = [Tk() for _ in range(FC)]
                rstd = sb(st, [128, 512], F32, "rstd")
                tmp = (sb_sq8, sb_sq8k, rstd, Tk())
                sg = [sb(st, [128, 512], F32, "sg") for _ in range(2)]; sgk = [Tk(), Tk()]
                set_ring(range(8))
                nsg = 0
                for (s0, s1) in ((0, T // 2), (T // 2, NT)):
                    tiles = col_tiles(s0, s1)
                    for ti, (c0, w) in enumerate(tiles):
                        l0 = c0 - s0
                        norm_cols(tmp, KC, w, lambda kc: xT[:, kc, c0:c0 + w], xtoks(range(KC), c0, w), lambda kc: nw[:, idx, kc:kc + 1],
                                  lambda kc: hT[:, kc, l0:l0 + w], [hTk[ti]], 1e-6, float(D))
                    for fp in range(FC // 2):
                        slot = wslot()
                        g2 = load_lhsT(wg[idx * D:(idx + 1) * D, fp * 256:(fp + 1) * 256], KC, 256, slot, 0)
                        u2 = load_lhsT(wu[idx * D:(idx + 1) * D, fp * 256:(fp + 1) * 256], KC, 256, slot, 2048)
                        for ti, (c0, w) in enumerate(tiles):
                            l0 = c0 - s0
                            for half in range(2):
                                f = 2 * fp + half
                                pg, pgk = bank()
                                for kc in range(KC):
                                    kb.op('pe', lambda e, kc=kc: e.matmul(pg[:, 0:w], lhsT=g2[:, kc, half * 128:(half + 1) * 128], rhs=hT[:, kc, l0:l0 + w], start=(kc == 0), stop=(kc == KC - 1)), reads=[slot[1], hTk[ti]], writes=[pgk])
                                pu, puk = bank()
                                for kc in range(KC):
                                    kb.op('pe', lambda e, kc=kc: e.matmul(pu[:, 0:w], lhsT=u2[:, kc, half * 128:(half + 1) * 128], rhs=hT[:, kc, l0:l0 + w], start=(kc == 0), stop=(kc == KC - 1)), reads=[slot[1], hTk[ti]], writes=[puk])
                                s_, sk_ = sg[nsg % 2], sgk[nsg % 2]; nsg += 1
                                kb.op('act', lambda e: e.activation(out=s_[:, 0:w], in_=pg[:, 0:w], func=AF.Silu), reads=[pgk], writes=[sk_])
                                kb.op('dve', lambda e: e.tensor_tensor(out=act[:, f, l0:l0 + w], in0=s_[:, 0:w], in1=pu[:, 0:w], op=ALU.mult), reads=[sk_, puk], writes=[actk[f]])
                    for dc in range(KC):
                        slot = wslot()
                        d1 = load_lhsT(wd[idx * DFF:(idx + 1) * DFF, dc * 128:(dc + 1) * 128], FC, 128, slot, 0)
                        for ti, (c0, w) in enumerate(tiles):
                            l0 = c0 - s0
                            pb, pbk = bank()
                            for f in range(FC):
                                kb.op('pe', lambda e, f=f: e.matmul(pb[:, 0:w], lhsT=d1[:, f, :], rhs=act[:, f, l0:l0 + w], start=(f == 0), stop=(f == FC - 1)), reads=[slot[1], actk[f]], writes=[pbk])
                            xt_ = xtoks([dc], c0, w)
                            kb.op('dve', lambda e: e.scalar_tensor_tensor(out=xT[:, dc, c0:c0 + w], in0=pb[:, 0:w], scalar=0.5, in1=xT[:, dc, c0:c0 + w], op0=ALU.mult, op1=ALU.add), reads=[pbk] + xt_, writes=xt_)
                kb.barrier()

        def final():
            with contextlib.ExitStack() as st:
                rstd = sb(st, [128, 512], F32, "rstd")
                tmp = (sb_sq8, sb_sq8k, rstd, Tk())
                yT = sb(st, [128, KC, 512], F32, "yT"); yTk = Tk()
                yo = [sb(st, [128, D], F32, "yo") for _ in range(2)]; yok = [Tk(), Tk()]
                set_ring(range(8))
                n = 0
                for (c0, w) in col_tiles(0, NT):
                    norm_cols(tmp, KC, w, lambda kc: xT[:, kc, c0:c0 + w], xtoks(range(KC), c0, w), lambda kc: nw[:, 6, kc:kc + 1],
                              lambda kc: yT[:, kc, 0:w], [yTk], 1e-6, float(D))
                    for t0 in range(0, w, 128):
                        rows = min(128, w - t0)
                        y_, yk_ = yo[n % 2], yok[n % 2]; n += 1
                        for half in range(2):
                            pb, pbk = bank()
                            for j in range(4):
                                c = half * 4 + j
                                kb.op('pe', lambda e, c=c, j=j: e.matmul(pb[0:rows, j * 128:(j + 1) * 128], lhsT=yT[:, c, t0:t0 + rows], rhs=identF[:, :], start=True, stop=True), reads=[yTk, kC], writes=[pbk])
                            if half == 0:
                                kb.op('act', lambda e: e.activation(out=y_[0:rows, 0:512], in_=pb[0:rows, :], func=AF.Copy), reads=[pbk], writes=[yk_])
                            else:
                                kb.op('dve', lambda e: e.tensor_copy(out=y_[0:rows, 512:1024], in_=pb[0:rows, :]), reads=[pbk], writes=[yk_])
                        g0 = c0 + t0
                        dst = yp[g0:g0 + rows, :] if g0 < T else ys[0:rows, :]
                        kb.dma('sp', dst, y_[0:rows, :], reads=[yk_])
                kb.barrier()


        class Ring:
            def __init__(self, stack, n, shape, dt, name="r"):
                self.t = [sb(stack, shape, dt, name) for _ in range(n)]; self.k = [Tk() for _ in range(n)]; self.i = 0
            def next(self):
                i = self.i % len(self.t); self.i += 1
                return self.t[i], self.k[i]

        def evac(eng, out, in_, reads, writes):
            if eng == 'act':
                kb.op('act', lambda e: e.activation(out=out, in_=in_, func=AF.Copy), reads=reads, writes=writes)
            else:
                kb.op(eng, lambda e: e.tensor_copy(out=out, in_=in_), reads=reads, writes=writes)

        def apply_wout(w_ap, srcT, src_toks, c0, w, nk=KC):
            for dp in range(KC // 2):
                slot = wslot()
                s2 = load_lhsT(w_ap[:, dp * 256:(dp + 1) * 256], nk, 256, slot, 0)
                for half in range(2):
                    dc = dp * 2 + half
                    pb, pbk = bank()
                    for kc in range(nk):
                        kb.op('pe', lambda e, kc=kc: e.matmul(pb[:, 0:w], lhsT=s2[:, kc, half * 128:(half + 1) * 128], rhs=srcT[:, kc, 0:w], start=(kc == 0), stop=(kc == nk - 1)), reads=[slot[1]] + src_toks, writes=[pbk])
                    xt_ = xtoks([dc], c0, w)
                    kb.op('dve', lambda e: e.tensor_tensor(out=xT[:, dc, c0:c0 + w], in0=pb[:, 0:w], in1=xT[:, dc, c0:c0 + w], op=ALU.add), reads=[pbk] + xt_, writes=xt_)

        def odd_mixer():
            SCALE = float(192 ** -0.5)
            with contextlib.ExitStack() as st:
                uqN = sb(st, [128, 3, 8, 128], BF16, "uqN"); uqA = sb(st, [128, 3, 8, 64], BF16, "uqA"); uqB = sb(st, [128, 3, 8, 64], BF16, "uqB")
                wkT = sb(st, [128, 8, 256], BF16, "wkT"); wv = sb(st, [128, 2, 8, 128], BF16, "wv")
                qnw = sb(st, [128, 3], F32, "qnw"); kvnw = sb(st, [128, 2], F32, "kvnw")
                kW = Tk()
                ckvnT = sb(st, [128, 2, NT], BF16, "ckvnT"); rkT = sb(st, [64, NT], BF16, "rkT")
                vtok = sb(st, [128, NTT + 1, 256], BF16, "vtok")
                kK = [Tk() for _ in range(NTT + 1)]
                attnT = sb(st, [128, 8, 512], BF16, "attnT"); attnk = Tk()
                rstd = sb(st, [128, 512], F32, "rstd"); tmp = (sb_sq8, sb_sq8k, rstd, Tk())
                rL = rstd; rLk = tmp[3]
                ctxn = Ring(st, 1, [128, 2, 512], BF16, "ctxn")
                qls = sb(st, [128, 2, NBS, 8], BF16, "qls"); rqs = sb(st, [64, NBS, 8], BF16, "rqs"); qsk = Tk()
                st_u = contextlib.ExitStack()
                ukvK = sb(st_u, [128, 2, 8, 128], F32, "ukvK")
                for h in range(8):
                    kb.dma('pool', uqN[:, :, h, :], muq[:, h * 192:h * 192 + 128].rearrange("(k p) n -> p k n", p=128), writes=[kW])
                    kb.dma('pool', uqA[:, :, h, :], muq[:, h * 192 + 128:h * 192 + 192].rearrange("(k p) n -> p k n", p=128), writes=[kW])
                    kb.dma('pool', uqB[:, :, h, 0:32], muq[:, h * 192 + 160:h * 192 + 192].rearrange("(k p) n -> p k n", p=128), writes=[kW])
                    kb.dma('pool', uqB[:, :, h, 32:64], muq[:, h * 192 + 128:h * 192 + 160].rearrange("(k p) n -> p k n", p=128), writes=[kW])
                    kb.dma('sp', ukvK[:, :, h, :], mukv[:, h * 256:h * 256 + 128].rearrange("(k p) n -> p k n", p=128), writes=[kW])
                    kb.dma('pool', wv[:, :, h, :], mukv[:, h * 256 + 128:h * 256 + 256].rearrange("(k p) n -> p k n", p=128), writes=[kW])
                kb.dma('sp', qnw[:, :], mqn.rearrange("a (k p) -> p (a k)", p=128), writes=[kW], allow_slow_non_contiguous=True)
                kb.dma('sp', kvnw[:, :], mkvn.rearrange("a (k p) -> p (a k)", p=128), writes=[kW], allow_slow_non_contiguous=True)
                set_ring([0, 1, 2])
                for h in range(8):
                    for c2 in range(2):
                        pb, pbk = bank()
                        kb.op('pe', lambda e: e.matmul(pb[:, 0:128], lhsT=ukvK[:, c2, h, :], rhs=identF[:, :], start=True, stop=True), reads=[kW, kC], writes=[pbk])
                        evac('act' if c2 == 0 else 'dve', wkT[:, h, c2 * 128:(c2 + 1) * 128], pb[:, 0:128], [pbk], [kW])
                kb.barrier(); st_u.close()
                sp_ = contextlib.ExitStack()
                cqT = sb(sp_, [128, 3, 512], BF16, "cqT"); cqTk = Tk(); cqn = sb(sp_, [128, 3, 512], BF16, "cqn"); cqnk = Tk()
                ckvF = sb(sp_, [128, 2, 512], F32, "ckvF"); ckvFk = Tk(); ckvnF = sb(sp_, [128, 2, 512], F32, "ckvnF"); ckvnFk = Tk()
                rt = sb(sp_, [64, 2, 512], F32, "rt"); rtk = Tk()
                t1r = Ring(sp_, 1, [64, 512], F32, "t1"); t2r = Ring(sp_, 1, [64, 512], F32, "t2")
                rkF = sb(sp_, [64, 512], F32, "rkF"); rkFk = Tk()
                tokst = Ring(sp_, 1, [128, 320], F32, "tokst")
                qnT = Ring(sp_, 1, [128, 512], BF16, "qnT"); qlT = Ring(sp_, 2, [128, 2, 512], BF16, "qlT"); rqT = Ring(sp_, 1, [64, 512], BF16, "rqT")
                Er = Ring(sp_, 2, [128, 512], BF16, "E")
                if DBG < 1:
                    kb.barrier(); return
                sring = [3, 4]; sri = [0]
                def sbank():
                    b = sring[sri[0] % 2]; sri[0] += 1
                    return PS[b], PSk[b]
                CT = [(PS[5], PSk[5]), (PS[6], PSk[6])]; LT = (PS[7], PSk[7])
                ropeV = ropeT.rearrange("p (a n) -> p a n", a=2)

                def rope_apply(pa, pak, pbb, pbbk, w, out_f32=None, out_f32k=None, out_bf=None, out_bfk=None):
                    t1, t1k = t1r.next(); t2, t2k = t2r.next()
                    kb.op('dve', lambda e: e.tensor_tensor(out=t1[:, 0:w], in0=pa[0:64, 0:w], in1=rt[:, 0, 0:w], op=ALU.mult), reads=[pak, rtk], writes=[t1k])
                    kb.op('dve', lambda e: e.tensor_tensor(out=t2[:, 0:w], in0=pbb[0:64, 0:w], in1=rt[:, 1, 0:w], op=ALU.mult), reads=[pbbk, rtk], writes=[t2k])
                    if out_f32 is not None:
                        kb.op('pool', lambda e: e.tensor_tensor(out=out_f32, in0=t1[:, 0:w], in1=t2[:, 0:w], op=ALU.add), reads=[t1k, t2k], writes=[out_f32k])
                        kb.op('act', lambda e: e.activation(out=out_bf, in_=out_f32, func=AF.Copy), reads=[out_f32k], writes=out_bfk)
                    else:
                        kb.op('pool', lambda e: e.tensor_tensor(out=out_bf, in0=t1[:, 0:w], in1=t2[:, 0:w], op=ALU.add), reads=[t1k, t2k], writes=out_bfk)

                for (c0, w) in col_tiles(0, T) + [(T, NBS)]:
                    sample = c0 >= T
                    ktile0 = c0 // 128
                    nsub = (w + 127) // 128
                    ktoks = [kK[ktile0 + i] for i in range(nsub)]
                    norm_cols(tmp, KC, w, lambda kc: xT[:, kc, c0:c0 + w], xtoks(range(KC), c0, w), lambda kc: nw[:, 5, kc:kc + 1],
                              lambda kc: hT[:, kc, 0:w], [hTk[0]], 1e-6, float(D))
                    kb.dma('sp', rt[:, :, 0:w], ropeV[:, :, c0:c0 + w], writes=[rtk])
                    slot = wslot()
                    wq = load_lhsT(mwin[:, 0:384], KC, 384, slot, 0)
                    slot2 = wslot()
                    wc = load_lhsT(mwin[:, 384:640], KC, 256, slot2, 0)
                    wpa = load_lhsT(mwin[:, 640:704], KC, 64, slot2, 2048)
                    wpb = slot2[0][:, 2560:2560 + KC * 64].rearrange("p (k n) -> p k n", k=KC)
                    kb.dma('pool', wpb[:, :, 0:32], mwin[:, 672:704].rearrange("(k p) n -> p k n", p=128), writes=[slot2[1]])
                    kb.dma('pool', wpb[:, :, 32:64], mwin[:, 640:672].rearrange("(k p) n -> p k n", p=128), writes=[slot2[1]])
                    if DBG2 < 2: continue
                    for j in range(3):
                        pb, pbk = bank()
                        for kc in range(KC):
                            kb.op('pe', lambda e, kc=kc: e.matmul(pb[:, 0:w], lhsT=wq[:, kc, j * 128:(j + 1) * 128], rhs=hT[:, kc, 0:w], start=(kc == 0), stop=(kc == KC - 1)), reads=[slot[1], hTk[0]], writes=[pbk])
                        evac('act', cqT[:, j, 0:w], pb[:, 0:w], [pbk], [cqTk])
                    norm_cols(tmp, 3, w, lambda kc: cqT[:, kc, 0:w], [cqTk], lambda kc: qnw[:, kc:kc + 1], lambda kc: cqn[:, kc, 0:w], [cqnk], 1e-6, 384.0)
                    if DBG2 < 3: continue
                    for j in range(2):
                        pb, pbk = bank()
                        for kc in range(KC):
                            kb.op('pe', lambda e, kc=kc: e.matmul(pb[:, 0:w], lhsT=wc[:, kc, j * 128:(j + 1) * 128], rhs=hT[:, kc, 0:w], start=(kc == 0), stop=(kc == KC - 1)), reads=[slot2[1], hTk[0]], writes=[pbk])
                        evac('act', ckvF[:, j, 0:w], pb[:, 0:w], [pbk], [ckvFk])
                    norm_cols(tmp, 2, w, lambda kc: ckvF[:, kc, 0:w], [ckvFk], lambda kc: kvnw[:, kc:kc + 1], lambda kc: ckvnF[:, kc, 0:w], [ckvnFk], 1e-6, 256.0)
                    kb.op('pool', lambda e: e.tensor_copy(out=ckvnT[:, :, c0:c0 + w], in_=ckvnF[:, :, 0:w]), reads=[ckvnFk], writes=ktoks)
                    if DBG2 < 4: continue
                    pa, pak = bank()
                    for kc in range(KC):
                        kb.op('pe', lambda e, kc=kc: e.matmul(pa[0:64, 0:w], lhsT=wpa[:, kc, :], rhs=hT[:, kc, 0:w], start=(kc == 0), stop=(kc == KC - 1)), reads=[slot2[1], hTk[0]], writes=[pak])
                    pbb, pbbk = bank()
                    for kc in range(KC):
                        kb.op('pe', lambda e, kc=kc: e.matmul(pbb[0:64, 0:w], lhsT=wpb[:, kc, :], rhs=hT[:, kc, 0:w], start=(kc == 0), stop=(kc == KC - 1)), reads=[slot2[1], hTk[0]], writes=[pbbk])
                    rope_apply(pa, pak, pbb, pbbk, w, rkF[:, 0:w], rkFk, rkT[:, c0:c0 + w], ktoks)
                    if DBG2 < 5: continue
                    for si in range(nsub):
                        t0 = si * 128; rows = min(128, w - t0)
                        pb, pbk = bank()
                        for c2 in range(2):
                            kb.op('pe', lambda e, c2=c2: e.matmul(pb[0:rows, c2 * 128:(c2 + 1) * 128], lhsT=ckvnF[:, c2, t0:t0 + rows], rhs=identF[:, :], start=True, stop=True), reads=[ckvnFk, kC], writes=[pbk])
                        if DBG3 & 1:
                            kb.op('pe', lambda e: e.matmul(pb[0:rows, 256:320], lhsT=rkF[0:64, t0:t0 + rows], rhs=identF[0:64, 0:64], start=True, stop=True), reads=[rkFk, kC], writes=[pbk])
                        ts_, tsk = tokst.next()
                        evac('act', ts_[0:rows, :], pb[0:rows, 0:320], [pbk], [tsk])
                        kb.op('dve', lambda e: e.tensor_copy(out=vtok[0:rows, ktile0 + si, :], in_=pb[0:rows, 0:256]), reads=[pbk], writes=[ktoks[si]])
                        g0 = c0 + t0
                        if not (DBG3 & 2): continue
                        if not sample:
                            kb.dma('sp', ckvp[g0:g0 + rows, :], ts_[0:rows, 0:256], reads=[tsk])
                            kb.dma('sp', kpep[g0:g0 + rows, :], ts_[0:rows, 256:320], reads=[tsk])
                        else:
                            kb.dma('sp', ckvs[0:rows, :], ts_[0:rows, 0:256], reads=[tsk])
                            kb.dma('sp', kpes[0:rows, :], ts_[0:rows, 256:320], reads=[tsk])
                    if DBG2 < 6: continue
                    for h in range(8 if DBG >= 2 else 0):
                        pb, pbk = bank()
                        for kc in range(3):
                            kb.op('pe', lambda e, kc=kc: e.matmul(pb[:, 0:w], lhsT=uqN[:, kc, h, :], rhs=cqn[:, kc, 0:w], start=(kc == 0), stop=(kc == 2)), reads=[kW, cqnk], writes=[pbk])
                        qn, qnk = qnT.next()
                        evac('act', qn[:, 0:w], pb[:, 0:w], [pbk], [qnk])
                        ql, qlk = qlT.next()
                        for c2 in range(2):
                            pb, pbk = bank()
                            kb.op('pe', lambda e: e.matmul(pb[:, 0:w], lhsT=wkT[:, h, c2 * 128:(c2 + 1) * 128], rhs=qn[:, 0:w], start=True, stop=True), reads=[kW, qnk], writes=[pbk])
                            evac('dve', ql[:, c2, 0:w], pb[:, 0:w], [pbk], [qlk])
                        pa, pak = bank()
                        for kc in range(3):
                            kb.op('pe', lambda e, kc=kc: e.matmul(pa[0:64, 0:w], lhsT=uqA[:, kc, h, :], rhs=cqn[:, kc, 0:w], start=(kc == 0), stop=(kc == 2)), reads=[kW, cqnk], writes=[pak])
                        pbb, pbbk = bank()
                        for kc in range(3):
                            kb.op('pe', lambda e, kc=kc: e.matmul(pbb[0:64, 0:w], lhsT=uqB[:, kc, h, :], rhs=cqn[:, kc, 0:w], start=(kc == 0), stop=(kc == 2)), reads=[kW, cqnk], writes=[pbbk])
                        rq, rqk = rqT.next()
                        rope_apply(pa, pak, pbb, pbbk, w, None, None, rq[:, 0:w], [rqk])
                        if sample:
                            kb.op('pool', lambda e: e.tensor_copy(out=qls[:, :, :, h], in_=ql[:, :, 0:NBS]), reads=[qlk], writes=[qsk])
                            kb.op('pool', lambda e: e.tensor_copy(out=rqs[:, :, h], in_=rq[:, 0:NBS]), reads=[rqk], writes=[qsk])
                            continue
                        nkt = (c0 + w) // 128
                        if DBG < 3: continue
                        for kt in range(nkt):
                            k0 = kt * 128; qs = max(k0 - c0, 0)
                            ps, psk = sbank()
                            kb.op('pe', lambda e: e.matmul(ps[:, qs:w], lhsT=ckvnT[:, 0, k0:k0 + 128], rhs=ql[:, 0, qs:w], start=True, stop=False), reads=[kK[kt], qlk], writes=[psk])
                            kb.op('pe', lambda e: e.matmul(ps[:, qs:w], lhsT=ckvnT[:, 1, k0:k0 + 128], rhs=ql[:, 1, qs:w], start=False, stop=False), reads=[kK[kt], qlk], writes=[psk])
                            kb.op('pe', lambda e: e.matmul(ps[:, qs:w], lhsT=rkT[0:64, k0:k0 + 128], rhs=rq[0:64, qs:w], start=False, stop=True), reads=[kK[kt], rqk], writes=[psk])
                            E, Ek = Er.next()
                            kb.op('act', lambda e: e.activation(out=E[:, qs:w], in_=ps[:, qs:w], func=AF.Exp, scale=SCALE), reads=[psk], writes=[Ek])
                            if k0 >= c0:
                                kb.op('pool', lambda e: e.tensor_tensor(out=E[:, qs:qs + 128], in0=E[:, qs:qs + 128], in1=maskB, op=ALU.mult), reads=[Ek, kC], writes=[Ek])
                            for c2 in range(2):
                                kb.op('pe', lambda e, c2=c2: e.matmul(CT[c2][0][:, qs:w], lhsT=vtok[:, kt, c2 * 128:(c2 + 1) * 128], rhs=E[:, qs:w], start=(kt == 0), stop=(kt == nkt - 1)), reads=[kK[kt], Ek], writes=[CT[c2][1]])
                            kb.op('pe', lambda e: e.matmul(LT[0][:, qs:w], lhsT=onesB[:, :], rhs=E[:, qs:w], start=(kt == 0), stop=(kt == nkt - 1)), reads=[kC, Ek], writes=[LT[1]])
                        kb.op('dve', lambda e: e.reciprocal(out=rL[:, 0:w], in_=LT[0][:, 0:w]), reads=[LT[1]], writes=[rLk])
                        cx, cxk = ctxn.next()
                        for c2 in range(2):
                            kb.op('dve', lambda e, c2=c2: e.tensor_tensor(out=cx[:, c2, 0:w], in0=CT[c2][0][:, 0:w], in1=rL[:, 0:w], op=ALU.mult), reads=[CT[c2][1], rLk], writes=[cxk])
                        pb, pbk = bank()
                        for c2 in range(2):
                            kb.op('pe', lambda e, c2=c2: e.matmul(pb[:, 0:w], lhsT=wv[:, c2, h, :], rhs=cx[:, c2, 0:w], start=(c2 == 0), stop=(c2 == 1)), reads=[kW, cxk], writes=[pbk])
                        evac('act', attnT[:, h, 0:w], pb[:, 0:w], [pbk], [attnk])
                    if not sample:
                        apply_wout(mwout, attnT, [attnk], c0, w)
                kb.barrier(); sp_.close()
                if DBG < 4:
                    kb.barrier(); return
                G = 4; KT = 8
                ccP = cc.rearrange("(n q) d -> n (q d)", q=KT); cpP = cp.rearrange("(n q) d -> n (q d)", q=KT); assert KT == 8
                cpg = Ring(st, 2, [128, KT, 256], F32, "cpg"); ppg = Ring(st, 2, [128, KT, 64], F32, "ppg")
                kTr = Ring(st, 2, [128, 3, G * 128], BF16, "kTr"); vbf = Ring(st, 2, [128, G, 256], BF16, "vbf")
                Es = Ring(st, 2, [128, G * 8], BF16, "Es")
                Esf = sb(st, [NBS, 32], F32, "Esf"); Esb = sb(st, [NBS, 32], BF16, "Esb"); Esk = Tk()
                ps, psk = sbank()
                for b in range(NBS):
                    kb.op('pe', lambda e: e.matmul(ps[0:NBS, b * 8:b * 8 + 8], lhsT=ckvnT[:, 0, T:T + NBS], rhs=qls[:, 0, b, :], start=True, stop=False), reads=[kK[NTT], qsk], writes=[psk])
                    kb.op('pe', lambda e: e.matmul(ps[0:NBS, b * 8:b * 8 + 8], lhsT=ckvnT[:, 1, T:T + NBS], rhs=qls[:, 1, b, :], start=False, stop=False), reads=[kK[NTT], qsk], writes=[psk])
                    kb.op('pe', lambda e: e.matmul(ps[0:NBS, b * 8:b * 8 + 8], lhsT=rkT[0:64, T:T + NBS], rhs=rqs[0:64, b, :], start=False, stop=True), reads=[kK[NTT], qsk], writes=[psk])
                kb.op('act', lambda e: e.activation(out=Esf[:, :], in_=ps[0:NBS, 0:32], func=AF.Exp, scale=SCALE), reads=[psk], writes=[Esk])
                kb.op('dve', lambda e: e.tensor_tensor(out=Esb[:, :], in0=Esf[:, :], in1=cF[0:NBS, C_SELF:C_SELF + 32], op=ALU.mult), reads=[Esk, kC], writes=[Esk])
                for c2 in range(2):
                    kb.op('pe', lambda e, c2=c2: e.matmul(CT[c2][0][:, 0:32], lhsT=vtok[0:NBS, NTT, c2 * 128:(c2 + 1) * 128], rhs=Esb[:, :], start=True, stop=False), reads=[kK[NTT], Esk], writes=[CT[c2][1]])
                kb.op('pe', lambda e: e.matmul(LT[0][:, 0:32], lhsT=onesB[0:NBS, :], rhs=Esb[:, :], start=True, stop=False), reads=[kC, Esk], writes=[LT[1]])
                for b in range(NBS):
                    for t0 in range(0, 128, KT):
                        cp_, cpk = cpg.next(); pp_, ppk = ppg.next()
                        kb.dma('pool', cp_[0:NPG, :, :].rearrange("p a d -> p (a d)"), ccP, reads=[idxk], writes=[cpk], indirect=idxm[0:NPG, b, t0 // KT:t0 // KT + 1])
                        kb.dma('pool', pp_[0:NPG, :, :].rearrange("p a d -> p (a d)"), cpP, reads=[idxk], writes=[ppk], indirect=idxm[0:NPG, b, t0 // KT:t0 // KT + 1])
                        for g0 in range(0, KT, G):
                            ng = G
                            kT, kTk = kTr.next()
                            vb, vbk = vbf.next()
                            kb.op('dve', lambda e: e.tensor_copy(out=vb[0:NPG, 0:ng, :], in_=cp_[0:NPG, g0:g0 + ng, :]), reads=[cpk], writes=[vbk])
                            for c2 in range(3):
                                pb, pbk = bank()
                                for g in range(ng):
                                    if c2 < 2:
                                        kb.op('pe', lambda e, g=g: e.matmul(pb[:, g * 128:g * 128 + NPG], lhsT=vb[0:NPG, g, c2 * 128:(c2 + 1) * 128], rhs=identB[0:NPG, 0:NPG], start=True, stop=True), reads=[vbk, kC], writes=[pbk])
                                    else:
                                        kb.op('pe', lambda e, g=g: e.matmul(pb[0:64, g * 128:g * 128 + NPG], lhsT=pp_[0:NPG, g0 + g, :], rhs=identF[0:NPG, 0:NPG], start=True, stop=True), reads=[ppk, kC], writes=[pbk])
                                np_ = 128 if c2 < 2 else 64
                                evac('act' if c2 != 1 else 'dve', kT[0:np_, c2, 0:ng * 128].rearrange("p (g t) -> p g t", t=128)[:, :, 0:NPG], pb[0:np_, 0:ng * 128].rearrange("p (g t) -> p g t", t=128)[:, :, 0:NPG], [pbk], [kTk])
                            ps, psk = sbank()
                            for g in range(ng):
                                kb.op('pe', lambda e, g=g: e.matmul(ps[0:NPG, g * 8:g * 8 + 8], lhsT=kT[:, 0, g * 128:g * 128 + NPG], rhs=qls[:, 0, b, :], start=True, stop=False), reads=[kTk, qsk], writes=[psk])
                                kb.op('pe', lambda e, g=g: e.matmul(ps[0:NPG, g * 8:g * 8 + 8], lhsT=kT[:, 1, g * 128:g * 128 + NPG], rhs=qls[:, 1, b, :], start=False, stop=False), reads=[kTk, qsk], writes=[psk])
                                kb.op('pe', lambda e, g=g: e.matmul(ps[0:NPG, g * 8:g * 8 + 8], lhsT=kT[0:64, 2, g * 128:g * 128 + NPG], rhs=rqs[0:64, b, :], start=False, stop=True), reads=[kTk, qsk], writes=[psk])
                            E, Ek = Es.next()
                            kb.op('act', lambda e: e.activation(out=E[0:NPG, 0:ng * 8], in_=ps[0:NPG, 0:ng * 8], func=AF.Exp, scale=SCALE), reads=[psk], writes=[Ek])
                            last = (t0 + KT >= 128) and (g0 + G >= KT)
                            for g in range(ng):
                                fin = last and g == ng - 1 and b == NBS - 1
                                for c2 in range(2):
                                    kb.op('pe', lambda e, g=g, c2=c2: e.matmul(CT[c2][0][:, b * 8:b * 8 + 8], lhsT=vb[0:NPG, g, c2 * 128:(c2 + 1) * 128], rhs=E[0:NPG, g * 8:g * 8 + 8], start=False, stop=fin), reads=[vbk, Ek], writes=[CT[c2][1]])
                                kb.op('pe', lambda e, g=g: e.matmul(LT[0][:, b * 8:b * 8 + 8], lhsT=onesB[0:NPG, :], rhs=E[0:NPG, g * 8:g * 8 + 8], start=False, stop=fin), reads=[kC, Ek], writes=[LT[1]])
                kb.op('dve', lambda e: e.reciprocal(out=rL[:, 0:32], in_=LT[0][:, 0:32]), reads=[LT[1]], writes=[rLk])
                cx, cxk = ctxn.next()
                for c2 in range(2):
                    kb.op('dve', lambda e, c2=c2: e.tensor_tensor(out=cx[:, c2, 0:32], in0=CT[c2][0][:, 0:32], in1=rL[:, 0:32], op=ALU.mult), reads=[CT[c2][1], rLk], writes=[cxk])
                for h in range(8):
                    pb, pbk = bank()
                    for c2 in range(2):
                        kb.op('pe', lambda e, c2=c2: e.matmul(pb[:, 0:NBS], lhsT=wv[:, c2, h, :], rhs=cx[:, c2, 0:32].rearrange("p (b h) -> p b h", h=8)[:, :, h], start=(c2 == 0), stop=(c2 == 1)), reads=[kW, cxk], writes=[pbk])
                    evac('act', attnT[:, h, 0:NBS], pb[:, 0:NBS], [pbk], [attnk])
                apply_wout(mwout, attnT, [attnk], T, NBS)
                kb.barrier()

        def even_mixer():
            with contextlib.ExitStack() as st:
                mixT = sb(st, [128, KC, NT], BF16, "mixT")
                mtiles = col_tiles(0, T) + [(T, NBS)]
                mixk = [[Tk() for _ in mtiles] for _ in range(KC)]
                if "diff" in stages or True:
                    diff_stage(mixT, mixk, mtiles)
                if DBG >= 5:
                    gdn_stage(mixT, mixk, mtiles)
                else:
                    for ti, (c0, w) in enumerate(mtiles):
                        for kc in range(4):
                            kb.op('pool', lambda e, kc=kc: e.memset(mixT[:, kc, c0:c0 + w], 0.0), writes=[mixk[kc][ti]])
                set_ring(range(8))
                for ti, (c0, w) in enumerate(mtiles):
                    apply_wout(ewout, mixT[:, :, c0:c0 + w], [mixk[kc][ti] for kc in range(KC)], c0, w)
                kb.barrier()

        def diff_stage(mixT, mixk, mtiles):
            with contextlib.ExitStack() as so:
                lamb = sb(so, [128, 256], F32, "lamb"); ltmp = sb(so, [128, 64], F32, "ltmp"); lcol = sb(so, [128, 8], F32, "lcol"); kL = Tk()
                dkS = sb(so, [128, 4, NBS], BF16, "dkS"); dqS = sb(so, [128, 4, NBS], BF16, "dqS"); dvS = sb(so, [NBS, 512], BF16, "dvS"); dqtok = sb(so, [NBS, 512], F32, "dqtok"); kS = Tk()
                rl = sb(so, [128, 512], F32, "rl"); t0 = sb(so, [128, 512], F32, "t0"); t1 = sb(so, [128, 512], F32, "t1"); sq1 = sb(so, [128, 512], BF16, "sq1")
                rlk, t0k, t1k, sq1k = Tk(), Tk(), Tk(), Tk()
                stg = Ring(so, 1, [128, 512], F32, "stg")
                kb.dma('sp', lamb[:, :], dlam.partition_broadcast(128), writes=[kL])
                kb.dma('sp', lcol[:, 4:5], subln.rearrange("a p -> p a"), writes=[kL], allow_slow_non_contiguous=True)
                for i in range(2):
                    kb.op('dve', lambda e: e.tensor_tensor(out=ltmp[:, :], in0=lamb[:, i * 128:i * 128 + 64], in1=lamb[:, i * 128 + 64:i * 128 + 128], op=ALU.mult), reads=[kL], writes=[kL])
                    kb.op('dve', lambda e: e.reduce_sum(out=lcol[:, i:i + 1], in_=ltmp[:, :], axis=AX.X), reads=[kL], writes=[kL])
                    kb.op('act', lambda e: e.activation(out=lcol[:, i:i + 1], in_=lcol[:, i:i + 1], func=AF.Exp), reads=[kL], writes=[kL])
                LAM_INIT = 0.2
                kb.op('dve', lambda e: e.tensor_tensor(out=lcol[:, 2:3], in0=lcol[:, 1:2], in1=lcol[:, 0:1], op=ALU.subtract), reads=[kL], writes=[kL])
                kb.op('dve', lambda e: e.tensor_scalar(out=lcol[:, 2:3], in0=lcol[:, 2:3], scalar1=-LAM_INIT, scalar2=None, op0=ALU.add), reads=[kL], writes=[kL])
                kb.op('dve', lambda e: e.tensor_scalar(out=lcol[:, 5:6], in0=lcol[:, 4:5], scalar1=1.0 - LAM_INIT, scalar2=None, op0=ALU.mult), reads=[kL], writes=[kL])
                neglam = lcol[:, 2:3]; subw = lcol[:, 5:6]
                sring = [0, 1]; sri = [0]
                def sbank():
                    b = sring[sri[0] % 2]; sri[0] += 1
                    return PS[b], PSk[b]
                OL = [[(PS[2], PSk[2]), (PS[3], PSk[3])], [(PS[4], PSk[4]), (PS[5], PSk[5])]]
                set_ring([6, 7])

                def combine(O0, L0, O1, L1, toks, n, out_fn):
                    kb.op('dve', lambda e: e.reciprocal(out=rl[:, 0:n], in_=L0), reads=toks, writes=[rlk])
                    kb.op('dve', lambda e: e.tensor_tensor(out=t0[:, 0:n], in0=O0, in1=rl[:, 0:n], op=ALU.mult), reads=toks + [rlk], writes=[t0k])
                    kb.op('dve', lambda e: e.reciprocal(out=rl[:, 0:n], in_=L1), reads=toks, writes=[rlk])
                    kb.op('dve', lambda e: e.tensor_tensor(out=t1[:, 0:n], in0=O1, in1=rl[:, 0:n], op=ALU.mult), reads=toks + [rlk], writes=[t1k])
                    kb.op('dve', lambda e: e.scalar_tensor_tensor(out=t0[:, 0:n], in0=t1[:, 0:n], scalar=neglam, in1=t0[:, 0:n], op0=ALU.mult, op1=ALU.add), reads=[t1k, t0k, kL], writes=[t0k])
                    kb.op('act', lambda e: e.activation(out=sq1[:, 0:n], in_=t0[:, 0:n], func=AF.Square), reads=[t0k], writes=[sq1k])
                    pn, pnk = bank()
                    kb.op('pe', lambda e: e.matmul(pn[:, 0:n], lhsT=onesB[:, :], rhs=sq1[:, 0:n], start=True, stop=True), reads=[sq1k, kC], writes=[pnk])
                    kb.op('act', lambda e: e.activation(out=rl[:, 0:n], in_=pn[:, 0:n], func=AF.Sqrt, bias=1e-5, scale=1.0 / 128), reads=[pnk], writes=[rlk])
                    kb.op('dve', lambda e: e.reciprocal(out=rl[:, 0:n], in_=rl[:, 0:n]), reads=[rlk], writes=[rlk])
                    out_fn()

                with contextlib.ExitStack() as sa:
                    dkT = sb(sa, [128, 4, T], BF16, "dkT"); dvtok = sb(sa, [128, NTT, 512], BF16, "dvtok"); kK = [Tk() for _ in range(NTT)]
                    dqT = sb(sa, [128, 4, 512], BF16, "dqT"); dqk = Tk()
                    Er = Ring(sa, 3, [128, 512], BF16, "E")
                    for ti, (c0, w) in enumerate(mtiles):
                        sample = c0 >= T
                        nsub = (w + 127) // 128
                        ktile0 = c0 // 128
                        ktoks = [kS] if sample else [kK[ktile0 + i] for i in range(nsub)]
                        norm_cols((sb_sq8, sb_sq8k, rl, rlk), KC, w, lambda kc: xT[:, kc, c0:c0 + w], xtoks(range(KC), c0, w), lambda kc: nw[:, 4, kc:kc + 1],
                                  lambda kc: hT[:, kc, 0:w], [hTk[0]], 1e-6, float(D))
                        for which in range(2):
                            base = O4 if which == 0 else O5
                            for jp in range(2):
                                slot = wslot()
                                s2 = load_lhsT(ewin[:, base + jp * 256:base + (jp + 1) * 256], KC, 256, slot, 0)
                                for half in range(2):
                                    j = jp * 2 + half
                                    pb, pbk = bank()
                                    for kc in range(KC):
                                        kb.op('pe', lambda e, kc=kc: e.matmul(pb[:, 0:w], lhsT=s2[:, kc, half * 128:(half + 1) * 128], rhs=hT[:, kc, 0:w], start=(kc == 0), stop=(kc == KC - 1)), reads=[slot[1], hTk[0]], writes=[pbk])
                                    if which == 0:
                                        evac('act', (dqS[:, j, 0:w] if sample else dqT[:, j, 0:w]), pb[:, 0:w], [pbk], [kS if sample else dqk])
                                    else:
                                        evac('dve', (dkS[:, j, 0:w] if sample else dkT[:, j, c0:c0 + w]), pb[:, 0:w], [pbk], ktoks)
                        for which in range(3 if sample else 2):
                            base = (O5, O6, O4)[which]
                            slot = wslot()
                            s4 = load_lhsT(ewin[:, base:base + 512], KC, 512, slot, 0)
                            for si in range(nsub):
                                r0 = si * 128; rows = min(128, w - r0)
                                pb, pbk = bank()
                                for kc in range(KC):
                                    kb.op('pe', lambda e, kc=kc: e.matmul(pb[0:rows, :], lhsT=hT[:, kc, r0:r0 + rows], rhs=s4[:, kc, :], start=(kc == 0), stop=(kc == KC - 1)), reads=[slot[1], hTk[0]], writes=[pbk])
                                if which == 2:
                                    evac('act', dqtok[0:rows, :], pb[0:rows, :], [pbk], [kS])
                                    continue
                                sg_, sgk_ = stg.next()
                                evac('act', sg_[0:rows, :], pb[0:rows, :], [pbk], [sgk_])
                                if which == 1:
                                    kb.op('dve', lambda e: e.tensor_copy(out=(dvS[0:rows, :] if sample else dvtok[0:rows, ktile0 + si, :]), in_=pb[0:rows, :]), reads=[pbk], writes=[ktoks[0 if sample else si]])
                                g0 = c0 + r0
                                dst = ((ks, vs)[which][0:rows, :]) if sample else ((kp, vp)[which][g0:g0 + rows, :])
                                kb.dma('sp', dst, sg_[0:rows, :], reads=[sgk_])
                        if sample or DBG < 2: continue
                        nkt = (c0 + w) // 128
                        for h in range(4):
                            for kt in range(nkt):
                                k0 = kt * 128; qs = max(k0 - c0, 0)
                                for m in range(2):
                                    r0 = m * 64
                                    ps, psk = sbank()
                                    kb.op('pe', lambda e: e.matmul(ps[:, qs:w], lhsT=dkT[r0:r0 + 64, h, k0:k0 + 128], rhs=dqT[r0:r0 + 64, h, qs:w], start=True, stop=True), reads=[kK[kt], dqk], writes=[psk])
                                    E, Ek = Er.next()
                                    kb.op('act', lambda e: e.activation(out=E[:, qs:w], in_=ps[:, qs:w], func=AF.Exp, scale=0.125), reads=[psk], writes=[Ek])
                                    if k0 >= c0:
                                        kb.op('pool', lambda e: e.tensor_tensor(out=E[:, qs:qs + 128], in0=E[:, qs:qs + 128], in1=maskB, op=ALU.mult), reads=[Ek, kC], writes=[Ek])
                                    (O_, Ok_), (L_, Lk_) = OL[m]
                                    kb.op('pe', lambda e: e.matmul(O_[:, qs:w], lhsT=dvtok[:, kt, h * 128:(h + 1) * 128], rhs=E[:, qs:w], start=(kt == 0), stop=(kt == nkt - 1)), reads=[kK[kt], Ek], writes=[Ok_])
                                    kb.op('pe', lambda e: e.matmul(L_[:, qs:w], lhsT=onesB[:, :], rhs=E[:, qs:w], start=(kt == 0), stop=(kt == nkt - 1)), reads=[kC, Ek], writes=[Lk_])
                            def outp(h=h, c0=c0, w=w, ti=ti):
                                kb.op('dve', lambda e: e.scalar_tensor_tensor(out=mixT[:, 4 + h, c0:c0 + w], in0=t0[:, 0:w], scalar=subw, in1=rl[:, 0:w], op0=ALU.mult, op1=ALU.mult), reads=[t0k, rlk, kL], writes=[mixk[4 + h][ti]])
                            combine(OL[0][0][0][:, 0:w], OL[0][1][0][:, 0:w], OL[1][0][0][:, 0:w], OL[1][1][0][:, 0:w], [OL[0][0][1], OL[0][1][1], OL[1][0][1], OL[1][1][1]], w, outp)
                    kb.barrier()
                if DBG < 3: return
                with contextlib.ExitStack() as sc_:
                    G = 2; KT = 4
                    ckP = ck.rearrange("(n q) d -> n (q d)", q=KT); cvP = cv.rearrange("(n q) d -> n (q d)", q=KT); assert KT == 4
                    kpg = Ring(sc_, 2, [128, KT, 512], F32, "kpg"); vpg = Ring(sc_, 2, [128, KT, 512], F32, "vpg")
                    prod = Ring(sc_, 1, [128, 512], F32, "prod"); scr = Ring(sc_, 2, [128, G * 8], F32, "scr")
                    Eg = Ring(sc_, 2, [128, G * 8], BF16, "Eg"); vbf = Ring(sc_, 1, [128, G, 512], BF16, "vbf")
                    qb = sb(sc_, [128, 512], F32, "qb"); qbk = Tk()
                    Esf = sb(sc_, [NBS, 32], F32, "Esf"); Esb = sb(sc_, [NBS, 32], BF16, "Esb"); Esk = Tk()
                    res = sb(sc_, [128, 16], F32, "res"); resk = Tk()
                    Oacc, Oacck = PS[2], PSk[2]; Lacc, Lacck = PS[3], PSk[3]
                    if DBG5 & 1:
                        psm = [sbank(), sbank()]
                        for hm in range(8):
                            h = hm // 2; m = hm % 2; r0 = m * 64
                            kb.op('pe', lambda e: e.matmul(psm[m][0][0:NBS, h * NBS:(h + 1) * NBS], lhsT=dkS[r0:r0 + 64, h, 0:NBS], rhs=dqS[r0:r0 + 64, h, 0:NBS], start=True, stop=True), reads=[kS], writes=[psm[m][1]])
                        for m in range(2):
                            kb.op('act', lambda e: e.activation(out=Esf[:, m * 16:(m + 1) * 16], in_=psm[m][0][0:NBS, 0:16], func=AF.Exp, scale=0.125), reads=[psm[m][1]], writes=[Esk])
                        kb.op('dve', lambda e: e.tensor_tensor(out=Esb[:, :], in0=Esf[:, :], in1=cF[0:NBS, C_SELF2:C_SELF2 + 32], op=ALU.mult), reads=[Esk, kC], writes=[Esk])
                        Ebm = Esb[:, :].rearrange("p (m h b) -> p b h m", m=2, h=4)
                        for b in range(NBS):
                            for h in range(4):
                                c_ = b * 8 + 2 * h
                                if not (DBG5 & 32): continue
                                kb.op('pe', lambda e: e.matmul(Oacc[:, c_:c_ + 2], lhsT=dvS[0:NBS, h * 128:(h + 1) * 128], rhs=Ebm[:, b, h, :], start=(b == 0 and h == 0), stop=False), reads=[kS, Esk], writes=[Oacck])
                                if not (DBG5 & 64): continue
                                kb.op('pe', lambda e: e.matmul(Lacc[:, c_:c_ + 2], lhsT=onesB[0:NBS, :], rhs=Ebm[:, b, h, :], start=(b == 0 and h == 0), stop=False), reads=[kC, Esk], writes=[Lacck])
                    for b in range(NBS):
                        if DBG5 & 2:
                            pb, pbk = bank()
                            kb.op('pe', lambda e: e.matmul(pb[:, :], lhsT=cF[0:NBS, C_SEL + b * 128:C_SEL + (b + 1) * 128], rhs=dqtok[:, :], start=True, stop=True), reads=[kC, kS], writes=[pbk])
                            evac('act', qb[:, :], pb[:, :], [pbk], [qbk])
                        if not (DBG5 & 4): continue
                        for tq0 in range(0, 128, KT):
                            kp_, kpk = kpg.next(); vp_, vpk = vpg.next()
                            kb.dma('pool', kp_[0:NPG, :, :].rearrange("p a d -> p (a d)"), ckP, reads=[idxk], writes=[kpk], indirect=idxq[0:NPG, b, tq0 // KT:tq0 // KT + 1])
                            kb.dma('pool', vp_[0:NPG, :, :].rearrange("p a d -> p (a d)"), cvP, reads=[idxk], writes=[vpk], indirect=idxq[0:NPG, b, tq0 // KT:tq0 // KT + 1])
                            for g0 in range(0, KT, G):
                                ng = G
                                sc, sck = scr.next()
                                for g in range(ng):
                                    pr, prk = prod.next()
                                    kb.op('dve', lambda e, g=g: e.tensor_tensor(out=pr[0:NPG, :], in0=kp_[0:NPG, g0 + g, :], in1=qb[0:NPG, :], op=ALU.mult), reads=[kpk, qbk], writes=[prk])
                                    kb.op('dve', lambda e, g=g: e.reduce_sum(out=sc[0:NPG, g * 8:(g + 1) * 8], in_=pr[0:NPG, :].rearrange("p (a d) -> p a d", d=64), axis=AX.X), reads=[prk], writes=[sck])
                                E, Ek = Eg.next()
                                kb.op('act', lambda e: e.activation(out=E[0:NPG, 0:ng * 8], in_=sc[0:NPG, 0:ng * 8], func=AF.Exp, scale=0.125), reads=[sck], writes=[Ek])
                                vb, vbk = vbf.next()
                                kb.op('act', lambda e: e.activation(out=vb[0:NPG, 0:ng, :], in_=vp_[0:NPG, g0:g0 + ng, :], func=AF.Copy), reads=[vpk], writes=[vbk])
                                last = (tq0 + KT >= 128) and (g0 + G >= KT) and b == NBS - 1
                                for g in range(ng):
                                    fin = last and g == ng - 1
                                    for h in range(4):
                                        kb.op('pe', lambda e, g=g, h=h: e.matmul(Oacc[:, b * 8 + 2 * h:b * 8 + 2 * h + 2], lhsT=vb[0:NPG, g, h * 128:(h + 1) * 128], rhs=E[0:NPG, g * 8 + 2 * h:g * 8 + 2 * h + 2], start=False, stop=(fin and h == 3)), reads=[vbk, Ek], writes=[Oacck])
                                    kb.op('pe', lambda e, g=g: e.matmul(Lacc[:, b * 8:b * 8 + 8], lhsT=onesB[0:NPG, :], rhs=E[0:NPG, g * 8:(g + 1) * 8], start=False, stop=fin), reads=[kC, Ek], writes=[Lacck])
                    Ov = Oacc[:, 0:32].rearrange("p (x m) -> p x m", m=2); Lv = Lacc[:, 0:32].rearrange("p (x m) -> p x m", m=2)
                    sidx = len(mtiles) - 1
                    def outs():
                        kb.op('dve', lambda e: e.scalar_tensor_tensor(out=res[:, :], in0=t0[:, 0:16], scalar=subw, in1=rl[:, 0:16], op0=ALU.mult, op1=ALU.mult), reads=[t0k, rlk, kL], writes=[resk])
                        for h in range(4):
                            kb.op('act', lambda e, h=h: e.activation(out=mixT[:, 4 + h, T:T + NBS], in_=res[:, :].rearrange("p (b h) -> p b h", h=4)[:, :, h], func=AF.Copy), reads=[resk], writes=[mixk[4 + h][sidx]])
                    if DBG5 & 16:
                        combine(Ov[:, :, 0], Lv[:, :, 0], Ov[:, :, 1], Lv[:, :, 1], [Oacck, Lacck], 16, outs)
                    kb.barrier()

        def gdn_stage(mixT, mixk, mtiles):
            with contextlib.ExitStack() as sg:
                cw = sb(sg, [128, 12, 4], F32, "cw"); gnw = sb(sg, [128, 1], F32, "gnw"); alb = sb(sg, [128, 4], F32, "alb"); dtbb = sb(sg, [128, 4], F32, "dtbb"); kG = Tk()
                cs = sb(sg, [128, 12, NBS, 4], F32, "cs"); csk = Tk()
                hist = sb(sg, [128, 12, 3], F32, "hist"); histk = [Tk() for _ in range(12)]
                Sf = sb(sg, [128, 4, 128], F32, "Sf"); Sb_ = sb(sg, [128, 4, 128], BF16, "Sb"); Sfk = [Tk() for _ in range(4)]; Sbk = [Tk() for _ in range(4)]
                Ss = sb(sg, [128, NBS * 4, 128], F32, "Ss"); Ssb = sb(sg, [128, NBS * 4, 128], BF16, "Ssb"); Ssk = [Tk() for _ in range(NBS * 4)]; Ssbk = [Tk() for _ in range(NBS * 4)]
                qkvt = sb(sg, [128, 12, 512], BF16, "qkvt"); qk = [Tk() for _ in range(12)]
                zt = sb(sg, [128, 4, 512], BF16, "zt"); zk = [Tk() for _ in range(4)]
                cbr = Ring(sg, 1, [128, 515], F32, "cb"); yr = Ring(sg, 1, [128, 512], F32, "y"); ysr = Ring(sg, 1, [128, 512], F32, "ys")
                sq1 = Ring(sg, 1, [128, 512], BF16, "sq1"); rsr = Ring(sg, 1, [128, 512], F32, "rs")
                bar = Ring(sg, 2, [64, 16], F32, "ba")
                for i in range(4):
                    kb.dma('sp', cw[:, :, i], convw[i:i + 1, :].rearrange("a (j p) -> p (a j)", p=128), writes=[kG], allow_slow_non_contiguous=True)
                kb.dma('sp', gnw[:, :], gnorm.rearrange("a p -> p a"), writes=[kG], allow_slow_non_contiguous=True)
                kb.dma('sp', alb[:, :], alog.partition_broadcast(128), writes=[kG])
                kb.dma('sp', dtbb[:, :], dtb.partition_broadcast(128), writes=[kG])
                kb.op('act', lambda e: e.activation(out=alb[:, :], in_=alb[:, :], func=AF.Exp), reads=[kG], writes=[kG])
                kb.op('dve', lambda e: e.tensor_scalar(out=alb[:, :], in0=alb[:, :], scalar1=-1.0, scalar2=None, op0=ALU.mult), reads=[kG], writes=[kG])
                for b in range(NBS):
                    for r in range(3):
                        kb.dma('sp', cs[:, :, b, r], sconv[b, r:r + 1, :].rearrange("a (j p) -> p (a j)", p=128), writes=[csk], allow_slow_non_contiguous=True)
                kb.dma('sp', convs[:, 0:2, :], sconv[:, 1:3, :])
                kb.op('pool', lambda e: e.memset(hist[:, :, :], 0.0), writes=histk)
                kb.op('pool', lambda e: e.memset(Sf[:, :, :], 0.0), writes=Sfk)
                kb.op('pool', lambda e: e.memset(Sb_[:, :, :], 0.0), writes=Sbk)
                kb.dma('sp', Ss[:, :, :], srec.rearrange("(g p) d -> p g d", p=128), writes=Ssk)
                kb.op('act', lambda e: e.activation(out=Ssb[:, :, :], in_=Ss[:, :, :], func=AF.Copy), reads=Ssk, writes=Ssbk)
                set_ring(range(8))
                triF = cF[:, C_M:C_M + 128]
                def R2(shape, dt, name, n=2): return Ring(sg, n, shape, dt, name)
                rGb = R2([64, 64], F32, "Gb"); rGt = R2([64, 64], F32, "Gt"); rnG = R2([64, 64], F32, "nG")
                rD = R2([64, 64], F32, "D"); rDT = R2([64, 64], F32, "DT"); rDs = R2([64, 64], F32, "Ds"); rL_ = R2([64, 64], F32, "Lneg")
                rM = R2([64, 64], F32, "M"); rMt = R2([64, 64], F32, "Mt", 4); rP = R2([64, 64], F32, "P")
                rAt = R2([64, 64], BF16, "At"); rXb = R2([64, 64], BF16, "Xb")
                rcol = R2([128, 8], F32, "col")
                rEr = R2([128, 64], F32, "Erow"); rqd = R2([128, 64], BF16, "qd")
                rRk = R2([64, 128], BF16, "Rk"); rkd = R2([64, 128], BF16, "kdec"); rRv = R2([64, 128], BF16, "Rv")
                ru = R2([64, 128], F32, "u"); rwT = R2([128, 64], BF16, "wT"); rvn = R2([64, 128], BF16, "vn")
                ro = R2([128, 64], F32, "o"); rsq = R2([128, 64], BF16, "sqo"); rrs = R2([128, 64], F32, "rso"); rt_ = R2([128, 64], F32, "to")

                def gdn_chunk(C, qT, kT, vT, zT, qkt, zkt, g_col, beta_col, nbeta_col, bak, Sf_a, Sb_a, Sfk_, Sbk_, out_ap, out_tok):
                    def mm(out, lhsT, rhs, reads, pk, start=True, stop=True):
                        kb.op('pe', lambda e: e.matmul(out, lhsT=lhsT, rhs=rhs, start=start, stop=stop), reads=reads, writes=[pk])
                    Gb, Gbk = rGb.next(); Gt, Gtk = rGt.next(); nG, nGk = rnG.next()
                    kb.op('dve', lambda e: e.tensor_scalar(out=Gb[0:C, 0:C], in0=onesF[0:C, 0:C], scalar1=g_col, scalar2=None, op0=ALU.mult), reads=[bak, kC], writes=[Gbk])
                    kb.op('dve', lambda e: e.tensor_scalar(out=Gt[0:C, 0:C], in0=triF[0:C, 0:C], scalar1=g_col, scalar2=None, op0=ALU.mult), reads=[bak, kC], writes=[Gtk])
                    kb.op('act', lambda e: e.activation(out=nG[0:C, 0:C], in_=Gt[0:C, 0:C], func=AF.Copy, scale=-1.0), reads=[Gtk], writes=[nGk])
                    yield
                    pd, pdk = bank()
                    mm(pd[0:C, 0:C], triF[0:C, 0:C], Gb[0:C, 0:C], [kC, Gbk], pdk, True, False)
                    mm(pd[0:C, 0:C], nonesF[0:C, 0:C], Gt[0:C, 0:C], [kC, Gtk], pdk, False, False)
                    mm(pd[0:C, 0:C], identF[0:C, 0:C], cF[0:C, C_NEG:C_NEG + C], [kC], pdk, False, True)
                    D_, Dk = rD.next()
                    kb.op('act', lambda e: e.activation(out=D_[0:C, 0:C], in_=pd[0:C, 0:C], func=AF.Exp), reads=[pdk], writes=[Dk])
                    yield
                    pt_, ptk = bank()
                    mm(pt_[0:C, 0:C], onesF[0:C, 0:C], Gt[0:C, 0:C], [kC, Gtk], ptk, True, False)
                    mm(pt_[0:C, 0:C], nG[0:C, 0:C], onesF[0:C, 0:C], [kC, nGk], ptk, False, False)
                    mm(pt_[0:C, 0:C], identF[0:C, 0:C], cF[0:C, C_NEGT:C_NEGT + C], [kC], ptk, False, True)
                    DT, DTk = rDT.next()
                    kb.op('act', lambda e: e.activation(out=DT[0:C, 0:C], in_=pt_[0:C, 0:C], func=AF.Exp), reads=[ptk], writes=[DTk])
                    yield
                    pc, pck = bank()
                    mm(pc[0:C, 0:1], triF[0:C, 0:C], g_col, [kC, bak], pck)
                    mm(pc[0:128, 1:2], onesF[0:C, 0:128], g_col, [kC, bak], pck)
                    col, colk = rcol.next()
                    kb.op('dve', lambda e: e.tensor_copy(out=col[0:C, 0:1], in_=pc[0:C, 0:1]), reads=[pck], writes=[colk])
                    kb.op('dve', lambda e: e.tensor_copy(out=col[:, 1:2], in_=pc[:, 1:2]), reads=[pck], writes=[colk])
                    kb.op('act', lambda e: e.activation(out=col[0:C, 2:3], in_=col[0:C, 0:1], func=AF.Exp), reads=[colk], writes=[colk])
                    kb.op('act', lambda e: e.activation(out=col[:, 3:4], in_=col[:, 1:2], func=AF.Exp), reads=[colk], writes=[colk])
                    kb.op('act', lambda e: e.activation(out=col[0:C, 4:5], in_=col[0:C, 0:1], func=AF.Exp, scale=-1.0, bias=col[0:C, 1:2]), reads=[colk], writes=[colk])
                    kb.op('dve', lambda e: e.tensor_tensor(out=col[0:C, 5:6], in0=col[0:C, 2:3], in1=beta_col, op=ALU.mult), reads=[colk, bak], writes=[colk])
                    yield
                    pe_, pek = bank()
                    mm(pe_[0:128, 0:C], onesF[0:C, 0:128], Gt[0:C, 0:C], [kC, Gtk], pek)
                    Er_, Erk = rEr.next()
                    kb.op('act', lambda e: e.activation(out=Er_[:, 0:C], in_=pe_[:, 0:C], func=AF.Exp), reads=[pek], writes=[Erk])
                    yield
                    pkk, pkkk = bank()
                    mm(pkk[0:C, 0:C], kT, kT, qkt, pkkk)
                    pqk, pqkk = bank()
                    mm(pqk[0:C, 0:C], kT, qT, qkt, pqkk)
                    Ds, Dsk = rDs.next()
                    kb.op('pool', lambda e: e.tensor_tensor(out=Ds[0:C, 0:C], in0=D_[0:C, 0:C], in1=cF[0:C, C_SL:C_SL + C], op=ALU.mult), reads=[Dk, kC], writes=[Dsk])
                    Ln, Lnk = rL_.next()
                    kb.op('dve', lambda e: e.scalar_tensor_tensor(out=Ln[0:C, 0:C], in0=pkk[0:C, 0:C], scalar=nbeta_col, in1=Ds[0:C, 0:C], op0=ALU.mult, op1=ALU.mult), reads=[pkkk, Dsk, bak], writes=[Lnk])
                    At, Atk = rAt.next()
                    kb.op('dve', lambda e: e.tensor_tensor(out=At[0:C, 0:C], in0=pqk[0:C, 0:C], in1=DT[0:C, 0:C], op=ALU.mult), reads=[pqkk, DTk], writes=[Atk])
                    yield
                    Xb, Xbk = rXb.next()
                    if C > 1:
                        pn_, pnk = bank()
                        mm(pn_[0:C, 0:C], Ln[0:C, 0:C], identF[0:C, 0:C], [Lnk, kC], pnk)
                        M_, Mk = rM.next(); P_, Pk = rP.next()
                        kb.op('act', lambda e: e.activation(out=M_[0:C, 0:C], in_=pn_[0:C, 0:C], func=AF.Copy), reads=[pnk], writes=[Mk])
                        kb.op('dve', lambda e: e.tensor_tensor(out=P_[0:C, 0:C], in0=pn_[0:C, 0:C], in1=identF[0:C, 0:C], op=ALU.add), reads=[pnk, kC], writes=[Pk])
                        yield
                        Mt_, Mtk = Ln, Lnk
                        nlev = 1
                        while (1 << nlev) < C: nlev += 1
                        for lev in range(1, nlev):
                            p1, p1k = bank()
                            mm(p1[0:C, 0:C], M_[0:C, 0:C], Mt_[0:C, 0:C], [Mk, Mtk], p1k)
                            Mt2, Mt2k = rMt.next()
                            kb.op('act', lambda e: e.activation(out=Mt2[0:C, 0:C], in_=p1[0:C, 0:C], func=AF.Copy), reads=[p1k], writes=[Mt2k])
                            if lev < nlev - 1:
                                p2, p2k = bank()
                                mm(p2[0:C, 0:C], Mt_[0:C, 0:C], M_[0:C, 0:C], [Mk, Mtk], p2k)
                                M2, M2k = rM.next()
                                kb.op('dve', lambda e: e.tensor_copy(out=M2[0:C, 0:C], in_=p2[0:C, 0:C]), reads=[p2k], writes=[M2k])
                            p3, p3k = bank()
                            mm(p3[0:C, 0:C], Mt2[0:C, 0:C], P_[0:C, 0:C], [Mt2k, Pk], p3k)
                            P2, P2k = rP.next()
                            kb.op('dve', lambda e: e.tensor_tensor(out=P2[0:C, 0:C], in0=p3[0:C, 0:C], in1=P_[0:C, 0:C], op=ALU.add), reads=[p3k, Pk], writes=[P2k])
                            if lev < nlev - 1: M_, Mk = M2, M2k
                            Mt_, Mtk = Mt2, Mt2k; P_, Pk = P2, P2k
                            yield
                        kb.op('act', lambda e: e.activation(out=Xb[0:C, 0:C], in_=P_[0:C, 0:C], func=AF.Copy), reads=[Pk], writes=[Xbk])
                        yield
                    else:
                        kb.op('act', lambda e: e.activation(out=Xb[0:1, 0:1], in_=identF[0:1, 0:1], func=AF.Copy), reads=[kC], writes=[Xbk])
                    yield
                    pk_, pkk_ = bank()
                    mm(pk_[0:C, 0:128], kT, identB, qkt + [kC], pkk_)
                    Rk, Rkk = rRk.next(); kd, kdk = rkd.next()
                    kb.op('act', lambda e: e.activation(out=Rk[0:C, :], in_=pk_[0:C, 0:128], func=AF.Copy, scale=col[0:C, 5:6]), reads=[pkk_, colk], writes=[Rkk])
                    kb.op('dve', lambda e: e.tensor_scalar(out=kd[0:C, :], in0=pk_[0:C, 0:128], scalar1=col[0:C, 4:5], scalar2=None, op0=ALU.mult), reads=[pkk_, colk], writes=[kdk])
                    yield
                    pv_, pvk = bank()
                    mm(pv_[0:C, 0:128], vT, identB, qkt + [kC], pvk)
                    Rv, Rvk = rRv.next()
                    kb.op('act', lambda e: e.activation(out=Rv[0:C, :], in_=pv_[0:C, 0:128], func=AF.Copy, scale=beta_col), reads=[pvk, bak], writes=[Rvk])
                    yield
                    pu, puk = bank()
                    mm(pu[0:C, 0:128], Xb[0:C, 0:C], Rv[0:C, :], [Xbk, Rvk], puk)
                    u_, uk = ru.next()
                    kb.op('act', lambda e: e.activation(out=u_[0:C, :], in_=pu[0:C, 0:128], func=AF.Copy), reads=[puk], writes=[uk])
                    yield
                    pw, pwk = bank()
                    mm(pw[0:128, 0:C], Rk[0:C, :], Xb[0:C, 0:C], [Rkk, Xbk], pwk)
                    wT, wTk = rwT.next()
                    kb.op('dve', lambda e: e.tensor_copy(out=wT[:, 0:C], in_=pw[:, 0:C]), reads=[pwk], writes=[wTk])
                    yield
                    pws, pwsk = bank()
                    mm(pws[0:C, 0:128], wT[:, 0:C], Sb_a, [wTk, Sbk_], pwsk)
                    vn, vnk = rvn.next()
                    kb.op('dve', lambda e: e.tensor_tensor(out=vn[0:C, :], in0=u_[0:C, :], in1=pws[0:C, 0:128], op=ALU.subtract), reads=[uk, pwsk], writes=[vnk])
                    yield
                    qd, qdk = rqd.next()
                    kb.op('dve', lambda e: e.tensor_tensor(out=qd[:, 0:C], in0=qT, in1=Er_[:, 0:C], op=ALU.mult), reads=qkt + [Erk], writes=[qdk])
                    po, pok = bank()
                    mm(po[0:128, 0:C], Sb_a, qd[:, 0:C], [Sbk_, qdk], pok, True, False)
                    mm(po[0:128, 0:C], vn[0:C, :], At[0:C, 0:C], [vnk, Atk], pok, False, True)
                    o_, ok_ = ro.next(); sq_, sqk_ = rsq.next(); rs_, rsk_ = rrs.next(); t_, tk_ = rt_.next()
                    kb.op('dve', lambda e: e.tensor_copy(out=o_[:, 0:C], in_=po[:, 0:C]), reads=[pok], writes=[ok_])
                    kb.op('act', lambda e: e.activation(out=sq_[:, 0:C], in_=po[:, 0:C], func=AF.Square), reads=[pok], writes=[sqk_])
                    yield
                    pS, pSk = bank()
                    mm(pS[0:128, 0:128], kd[0:C, :], vn[0:C, :], [kdk, vnk], pSk)
                    kb.op('dve', lambda e: e.scalar_tensor_tensor(out=Sf_a, in0=Sf_a, scalar=col[:, 3:4], in1=pS[:, 0:128], op0=ALU.mult, op1=ALU.add), reads=[Sfk_, colk, pSk], writes=[Sfk_])
                    kb.op('pool', lambda e: e.tensor_copy(out=Sb_a, in_=Sf_a), reads=[Sfk_], writes=[Sbk_])
                    yield
                    pn2, pn2k = bank()
                    mm(pn2[:, 0:C], onesB[:, :], sq_[:, 0:C], [kC, sqk_], pn2k)
                    kb.op('act', lambda e: e.activation(out=rs_[:, 0:C], in_=pn2[:, 0:C], func=AF.Sqrt, bias=1e-6, scale=1.0 / 128), reads=[pn2k], writes=[rsk_])
                    kb.op('dve', lambda e: e.reciprocal(out=rs_[:, 0:C], in_=rs_[:, 0:C]), reads=[rsk_], writes=[rsk_])
                    kb.op('dve', lambda e: e.scalar_tensor_tensor(out=t_[:, 0:C], in0=o_[:, 0:C], scalar=gnw[:, 0:1], in1=rs_[:, 0:C], op0=ALU.mult, op1=ALU.mult), reads=[ok_, rsk_, kG], writes=[tk_])
                    kb.op('pool', lambda e: e.tensor_tensor(out=out_ap, in0=t_[:, 0:C], in1=zT, op=ALU.mult), reads=[tk_, zkt], writes=[out_tok])

                for ti, (c0, w) in enumerate(mtiles):
                    sample = c0 >= T
                    norm_cols((sb_sq8, sb_sq8k, rsr.t[0], rsr.k[0]), KC, w, lambda kc: xT[:, kc, c0:c0 + w], xtoks(range(KC), c0, w), lambda kc: nw[:, 4, kc:kc + 1],
                              lambda kc: hT[:, kc, 0:w], [hTk[0]], 1e-6, float(D))
                    for jp in range(8):
                        base = jp * 256 if jp < 6 else O1 + (jp - 6) * 256
                        slot = wslot()
                        s2 = load_lhsT(ewin[:, base:base + 256], KC, 256, slot, 0)
                        for half in range(2):
                            pb, pbk = bank()
                            for kc in range(KC):
                                kb.op('pe', lambda e, kc=kc: e.matmul(pb[:, 0:w], lhsT=s2[:, kc, half * 128:(half + 1) * 128], rhs=hT[:, kc, 0:w], start=(kc == 0), stop=(kc == KC - 1)), reads=[slot[1], hTk[0]], writes=[pbk])
                            if jp >= 6:
                                j = (jp - 6) * 2 + half
                                kb.op('act', lambda e: e.activation(out=zt[:, j, 0:w], in_=pb[:, 0:w], func=AF.Silu), reads=[pbk], writes=[zk[j]])
                                continue
                            j = jp * 2 + half
                            y_, yk_ = yr.next()
                            if not sample:
                                cb, cbk = cbr.next()
                                kb.op('pool', lambda e: e.tensor_copy(out=cb[:, 0:3], in_=hist[:, j, :]), reads=[histk[j]], writes=[cbk])
                                evac('act', cb[:, 3:3 + w], pb[:, 0:w], [pbk], [cbk])
                                kb.op('pool', lambda e: e.tensor_copy(out=hist[:, j, :], in_=cb[:, w:w + 3]), reads=[cbk], writes=[histk[j]])
                                src_i = lambda i: cb[:, i:i + w]
                                rd = [cbk]
                            else:
                                evac('act', cs[:, j, :, 3], pb[:, 0:w], [pbk], [csk])
                                src_i = lambda i: cs[:, j, :, i]
                                rd = [csk]
                            kb.op('dve', lambda e: e.tensor_scalar(out=y_[:, 0:w], in0=src_i(0), scalar1=cw[:, j, 0:1], scalar2=None, op0=ALU.mult), reads=rd + [kG], writes=[yk_])
                            for i in range(1, 4):
                                kb.op('dve', lambda e, i=i: e.scalar_tensor_tensor(out=y_[:, 0:w], in0=src_i(i), scalar=cw[:, j, i:i + 1], in1=y_[:, 0:w], op0=ALU.mult, op1=ALU.add), reads=rd + [kG, yk_], writes=[yk_])
                            if j >= 8:
                                kb.op('act', lambda e: e.activation(out=qkvt[:, j, 0:w], in_=y_[:, 0:w], func=AF.Silu), reads=[yk_], writes=[qk[j]])
                                continue
                            ys_, ysk_ = ysr.next(); sq_, sqk_ = sq1.next(); rs_, rsk_ = rsr.next()
                            kb.op('act', lambda e: e.activation(out=ys_[:, 0:w], in_=y_[:, 0:w], func=AF.Silu), reads=[yk_], writes=[ysk_])
                            kb.op('act', lambda e: e.activation(out=sq_[:, 0:w], in_=ys_[:, 0:w], func=AF.Square), reads=[ysk_], writes=[sqk_])
                            pn, pnk = bank()
                            kb.op('pe', lambda e: e.matmul(pn[:, 0:w], lhsT=onesB[:, :], rhs=sq_[:, 0:w], start=True, stop=True), reads=[kC, sqk_], writes=[pnk])
                            kb.op('act', lambda e: e.activation(out=rs_[:, 0:w], in_=pn[:, 0:w], func=AF.Sqrt, bias=1e-6, scale=1.0), reads=[pnk], writes=[rsk_])
                            kb.op('dve', lambda e: e.reciprocal(out=rs_[:, 0:w], in_=rs_[:, 0:w]), reads=[rsk_], writes=[rsk_])
                            scl = float(128 ** -0.5) if j < 4 else 1.0
                            kb.op('dve', lambda e: e.scalar_tensor_tensor(out=qkvt[:, j, 0:w], in0=ys_[:, 0:w], scalar=scl, in1=rs_[:, 0:w], op0=ALU.mult, op1=ALU.mult), reads=[ysk_, rsk_], writes=[qk[j]])
                    if sample or c0 + w == T:
                        M = NBS if sample else 3
                        l0 = 0 if sample else w - 3
                        for n3 in range(3):
                            slot = wslot()
                            s4 = load_lhsT(ewin[:, n3 * 512:(n3 + 1) * 512], KC, 512, slot, 0)
                            pb, pbk = bank()
                            for kc in range(KC):
                                kb.op('pe', lambda e, kc=kc: e.matmul(pb[0:M, :], lhsT=hT[:, kc, l0:l0 + M], rhs=s4[:, kc, :], start=(kc == 0), stop=(kc == KC - 1)), reads=[slot[1], hTk[0]], writes=[pbk])
                            sg_, sgk_ = ysr.next()
                            evac('act', sg_[0:M, :], pb[0:M, :], [pbk], [sgk_])
                            dst = convs[:, 2, n3 * 512:(n3 + 1) * 512] if sample else convp[:, n3 * 512:(n3 + 1) * 512]
                            kb.dma('sp', dst, sg_[0:M, :], reads=[sgk_])
                    slot = wslot()
                    bas = load_lhsT(ewin[:, O2:O2 + 8], KC, 8, slot, 0)
                    chunks = [(b, 1) for b in range(NBS)] if sample else [(cc, 64) for cc in range(0, w, 64)]
                    for ci, (cc, C) in enumerate(chunks):
                        pb, pbk = bank()
                        for kc in range(KC):
                            kb.op('pe', lambda e, kc=kc: e.matmul(pb[0:C, 0:8], lhsT=hT[:, kc, cc:cc + C], rhs=bas[:, kc, :], start=(kc == 0), stop=(kc == KC - 1)), reads=[slot[1], hTk[0]], writes=[pbk])
                        ba, bak = bar.next()
                        kb.op('act', lambda e: e.activation(out=ba[0:C, 0:4], in_=pb[0:C, 0:4], func=AF.Sigmoid), reads=[pbk], writes=[bak])
                        kb.op('dve', lambda e: e.tensor_tensor(out=ba[0:C, 4:8], in0=pb[0:C, 4:8], in1=dtbb[0:C, :], op=ALU.add), reads=[pbk, kG], writes=[bak])
                        kb.op('act', lambda e: e.activation(out=ba[0:C, 4:8], in_=ba[0:C, 4:8], func=AF.Exp), reads=[bak], writes=[bak])
                        kb.op('act', lambda e: e.activation(out=ba[0:C, 4:8], in_=ba[0:C, 4:8], func=AF.Ln, bias=1.0, scale=1.0), reads=[bak], writes=[bak])
                        kb.op('dve', lambda e: e.tensor_tensor(out=ba[0:C, 8:12], in0=ba[0:C, 4:8], in1=alb[0:C, :], op=ALU.mult), reads=[bak, kG], writes=[bak])
                        kb.op('dve', lambda e: e.tensor_scalar(out=ba[0:C, 12:16], in0=ba[0:C, 0:4], scalar1=-1.0, scalar2=None, op0=ALU.mult), reads=[bak], writes=[bak])
                        for hp in range(2):
                            gens = []
                            for h in (2 * hp, 2 * hp + 1):
                                if sample:
                                    si = cc * 4 + h
                                    Sfa, Sba, Sfk_, Sbk_ = Ss[:, si, :], Ssb[:, si, :], Ssk[si], Ssbk[si]
                                else:
                                    Sfa, Sba, Sfk_, Sbk_ = Sf[:, h, :], Sb_[:, h, :], Sfk[h], Sbk[h]
                                gens.append(gdn_chunk(C, qkvt[:, h, cc:cc + C], qkvt[:, 4 + h, cc:cc + C], qkvt[:, 8 + h, cc:cc + C], zt[:, h, cc:cc + C],
                                                      [qk[h], qk[4 + h], qk[8 + h]], zk[h], ba[0:C, 8 + h:9 + h], ba[0:C, h:h + 1], ba[0:C, 12 + h:13 + h], bak, Sfa, Sba, Sfk_, Sbk_,
                                                      mixT[:, h, c0 + cc:c0 + cc + C], mixk[h][ti]))
                            while gens:
                                for g_ in list(gens):
                                    try:
                                        next(g_)
                                    except StopIteration:
                                        gens.remove(g_)
                    if c0 + w == T:
                        kb.dma('sp', recp.rearrange("(h p) d -> p h d", p=128), Sf[:, :, :], reads=Sfk)
                kb.dma('sp', recs.rearrange("(g p) d -> p g d", p=128), Ss[:, :, :], reads=Ssk)
                kb.barrier()
        for layer in range(2):
            if "ffn" in stages: ffn(layer * 2 + 0)
            if layer == 0 and "even" in stages: even_mixer()
            if layer == 1 and "odd" in stages: odd_mixer()
            if "ffn" in stages: ffn(layer * 2 + 1)
        final()
        kb.finish()
        print(f"[build] instructions={kb.nins} waits={kb.nwait} cnt={kb.cnt}", flush=True)
    return nc


_CACHE = {}

def kernel(x_prompt, x_sample, state_gdn_conv, state_gdn_rec, cache_diff_k, cache_diff_v, cache_mla_ckv, cache_mla_kpe, page_table,
           ffn_norm, mix_norm, ffn_w_gate, ffn_w_up, ffn_w_down, even_w_in, gdn_conv_w, gdn_a_log, gdn_dt_bias, gdn_norm, diff_lambda,
           diff_subln, even_w_out, mla_w_in, mla_q_norm, mla_w_uq, mla_kv_norm, mla_w_ukv, mla_w_out, final_norm, _stages=("ffn", "even", "odd")):
    f = lambda a: np.ascontiguousarray(np.asarray(a), dtype=np.float32)
    x_prompt = f(x_prompt); x_sample = f(x_sample)
    B, T, _ = x_prompt.shape
    NPOOL = cache_diff_k.shape[1]
    page_table = np.ascontiguousarray(np.asarray(page_table), dtype=np.int32)
    NPG = page_table.shape[1]
    n = 8
    assert B == n and x_sample.shape[0] == n * NBS
    key = (T, NPG, NPOOL, tuple(_stages))
    if key not in _CACHE:
        _CACHE[key] = build(T, NPG, NPOOL, stages=_stages)
    nc = _CACHE[key]
    cst, rope = make_consts(T, NPG * 128)
    shared = dict(
        ck=f(cache_diff_k).reshape(NPOOL * 128, 512), cv=f(cache_diff_v).reshape(NPOOL * 128, 512),
        cc=f(cache_mla_ckv).reshape(NPOOL * 128, 256), cp=f(cache_mla_kpe).reshape(NPOOL * 128, 64),
        ffn_norm=f(ffn_norm).reshape(4, D), mix_norm=f(mix_norm).reshape(2, D), final_norm=f(final_norm).reshape(1, D),
        wg=f(ffn_w_gate).reshape(4 * D, DFF), wu=f(ffn_w_up).reshape(4 * D, DFF), wd=f(ffn_w_down).reshape(4 * DFF, D),
        ewin=f(even_w_in).reshape(D, EVEN_IN), convw=f(gdn_conv_w).reshape(4, 1536), alog=f(gdn_a_log).reshape(1, 4),
        dtb=f(gdn_dt_bias).reshape(1, 4), gnorm=f(gdn_norm).reshape(1, 128), dlam=f(diff_lambda).reshape(1, 256),
        subln=f(diff_subln).reshape(1, 128), ewout=f(even_w_out).reshape(D, D), mwin=f(mla_w_in).reshape(D, MLA_IN),
        mqn=f(mla_q_norm).reshape(1, 384), muq=f(mla_w_uq).reshape(384, 1536), mkvn=f(mla_kv_norm).reshape(1, 256),
        mukv=f(mla_w_ukv).reshape(256, 2048), mwout=f(mla_w_out).reshape(D, D), cst=cst, ropeT=rope.reshape(64, -1))
    sconv = f(state_gdn_conv); srec = f(state_gdn_rec)
    in_maps = []
    for c in range(n):
        m = dict(shared)
        m["xp"] = x_prompt[c]; m["xs"] = x_sample[c * NBS:(c + 1) * NBS, 0, :]
        m["sconv"] = sconv[0, c * NBS:(c + 1) * NBS]; m["srec"] = srec[0, c * NBS:(c + 1) * NBS].reshape(NBS * 4 * 128, 128)
        m["pt"] = page_table[c * NBS:(c + 1) * NBS].reshape(1, NBS * NPG)
        in_maps.append(m)
    res = run_bass_kernel_spmd(nc, in_maps, core_ids=list(range(n)))
    R = res.results
    cat = lambda k: np.stack([R[c][k] for c in range(n)], 0)
    y_p = cat("yp")
    y_s = cat("ys").reshape(n * NBS, 1, D)
    conv_p = cat("convp")[None]
    conv_s = cat("convs").reshape(1, n * NBS, 3, 1536)
    rec_p = cat("recp").reshape(1, n, 4, 128, 128)
    rec_s = cat("recs").reshape(1, n * NBS, 4, 128, 128)
    k_p = cat("kp").reshape(1, n, T, 8, 64); k_s = cat("ks").reshape(1, n * NBS, 1, 8, 64)
    v_p = cat("vp").reshape(1, n, T, 4, 128); v_s = cat("vs").reshape(1, n * NBS, 1, 4, 128)
    ckv_p = cat("ckvp").reshape(1, n, T, 256); ckv_s = cat("ckvs").reshape(1, n * NBS, 1, 256)
    kpe_p = cat("kpep").reshape(1, n, T, 64); kpe_s = cat("kpes").reshape(1, n * NBS, 1, 64)
    return (y_p, y_s, conv_p, conv_s, rec_p, rec_s, k_p, k_s, v_p, v_s, ckv_p, ckv_s, kpe_p, kpe_s)
```
